# Optimizing a Trainium2 kernel written in Bass

```python
import math
import jax, jax.numpy as jnp
from jax import lax
import numpy as np


D_MODEL = 1024
BATCH = 4
SEQ = 8192
DEPTH = 1

MLSTM_HEADS = 4
MLSTM_WIDTH = D_MODEL
MLSTM_VDIM = MLSTM_WIDTH // MLSTM_HEADS
MLSTM_QKDIM = MLSTM_VDIM // 2
MLSTM_CONV = 4
MLSTM_CHUNK = 64
S5_WIDTH = D_MODEL // 2
S5_GROUP = 16
S5_GROUPS = S5_WIDTH // S5_GROUP
S5_STATE = 64
S5_DT_MIN = 1e-3
S5_DT_MAX = 1e-1
FFN_HIDDEN = ((8 * D_MODEL // 3 + 127) // 128) * 128
FFN_CONV = 3
ALPHA = (2.0 * DEPTH) ** 0.25
BETA = (8.0 * DEPTH) ** -0.25
LN_EPS = 1e-5
IN_SIZES = (MLSTM_WIDTH, MLSTM_WIDTH, MLSTM_HEADS, MLSTM_HEADS, S5_WIDTH, D_MODEL, D_MODEL)
IN_WIDTH = sum(IN_SIZES)
IN_SPLITS = tuple(int(s) for s in np.cumsum(IN_SIZES)[:-1])
F_OFF = 2 * MLSTM_WIDTH + MLSTM_HEADS

kernel_name = 'hybrid_mlstm_s5_convglu_deepnorm_adaln'


def _standardize(x):
    xf = x.astype(jnp.float32)
    mu = jnp.mean(xf, axis=-1, keepdims=True)
    var = jnp.mean(jnp.square(xf - mu), axis=-1, keepdims=True)
    return ((xf - mu) * lax.rsqrt(var + LN_EPS)).astype(x.dtype)


def _layer_norm(x, gain, bias):
    return _standardize(x) * gain + bias


def _causal_dwconv(x, w, b):
    K = w.shape[0]
    S = x.shape[1]
    xp = jnp.pad(x, ((0, 0), (K - 1, 0), (0, 0)))
    y = b
    for j in range(K):
        y = y + w[j] * xp[:, j:j + S]
    return y


def _mlstm_chunkwise(q, k, v, i_pre, f_pre):
    Bsz, H, S, dk = q.shape
    dv = v.shape[-1]
    L = MLSTM_CHUNK
    nc = S // L
    f32 = jnp.float32
    q = q.astype(f32) * (MLSTM_QKDIM ** -0.5)
    k = k.astype(f32)
    v = v.astype(f32)
    ig = i_pre.astype(f32)
    logf = jax.nn.log_sigmoid(f_pre.astype(f32))

    def to_chunks(a):
        return jnp.moveaxis(a.reshape((Bsz, H, nc, L) + a.shape[3:]), 2, 0)

    causal = jnp.tril(jnp.ones((L, L), dtype=bool))

    def step(carry, xs):
        C, n, m = carry
        qb, kb, vb, ib, fb = xs
        b = jnp.cumsum(fb, axis=-1)
        D = b[..., :, None] - b[..., None, :] + ib[..., None, :]
        D = jnp.where(causal, D, -jnp.inf)
        inter = b + m[..., None]
        m_t = jnp.maximum(inter, jnp.max(D, axis=-1))
        wmat = jnp.exp(D - m_t[..., None])
        sc_inter = jnp.exp(inter - m_t)
        s = jnp.einsum('bhtd,bhsd->bhts', qb, kb) * wmat
        num = jnp.einsum('bhts,bhsv->bhtv', s, vb) + sc_inter[..., None] * jnp.einsum('bhtd,bhdv->bhtv', qb, C)
        den = jnp.sum(s, axis=-1) + sc_inter * jnp.einsum('bhtd,bhd->bht', qb, n)
        h = num / jnp.maximum(jnp.abs(den), jnp.exp(-m_t))[..., None]
        b_last = b[..., -1]
        g = b_last[..., None] - b + ib
        m_new = jnp.maximum(b_last + m, jnp.max(g, axis=-1))
        wk = jnp.exp(g - m_new[..., None])
        decay = jnp.exp(b_last + m - m_new)
        kw = kb * wk[..., None]
        C_new = decay[..., None, None] * C + jnp.einsum('bhsd,bhsv->bhdv', kw, vb)
        n_new = decay[..., None] * n + jnp.sum(kw, axis=2)
        return (C_new, n_new, m_new), h

    init = (jnp.zeros((Bsz, H, dk, dv), f32), jnp.zeros((Bsz, H, dk), f32), jnp.zeros((Bsz, H), f32))
    _, hc = lax.scan(step, init, (to_chunks(q), to_chunks(k), to_chunks(v), to_chunks(ig), to_chunks(logf)))
    return jnp.moveaxis(hc, 0, 2).reshape(Bsz, H, S, dv)


def _s5_ssm(u, lam_re, lam_im, log_dt, b_re, b_im, c_re, c_im, d):
    Bsz, S, W = u.shape
    f32 = jnp.float32
    uf = u.astype(f32)
    lam_re = lam_re.astype(f32)
    lam_im = lam_im.astype(f32)
    dt = jnp.exp(log_dt.astype(f32))[:, None]
    mag = jnp.exp(lam_re * dt)
    ar = mag * jnp.cos(lam_im * dt)
    ai = mag * jnp.sin(lam_im * dt)
    den = lam_re * lam_re + lam_im * lam_im
    zr = ((ar - 1.0) * lam_re + ai * lam_im) / den
    zi = (ai * lam_re - (ar - 1.0) * lam_im) / den
    bbr = zr[..., None] * b_re - zi[..., None] * b_im
    bbi = zr[..., None] * b_im + zi[..., None] * b_re
    ug = uf.reshape(Bsz, S, S5_GROUPS, S5_GROUP)
    bu_r = jnp.einsum('bsgc,gpc->bsgp', ug, bbr)
    bu_i = jnp.einsum('bsgc,gpc->bsgp', ug, bbi)
    a_r = jnp.broadcast_to(ar, bu_r.shape)
    a_i = jnp.broadcast_to(ai, bu_r.shape)

    def combine(e1, e2):
        a1r, a1i, b1r, b1i = e1
        a2r, a2i, b2r, b2i = e2
        return (a1r * a2r - a1i * a2i,
                a1r * a2i + a1i * a2r,
                a2r * b1r - a2i * b1i + b2r,
                a2r * b1i + a2i * b1r + b2i)

    _, _, xr, xi = lax.associative_scan(combine, (a_r, a_i, bu_r, bu_i), axis=1)
    y = jnp.einsum('gcp,bsgp->bsgc', c_re, xr) - jnp.einsum('gcp,bsgp->bsgc', c_im, xi)
    y = y.reshape(Bsz, S, W) + d * uf
    return y.astype(u.dtype)


def _hybrid_mixer(h, w_in, b_in, w_mlstm_conv, b_mlstm_conv, w_mlstm_q, w_mlstm_k, mlstm_norm_gain,
                  w_mlstm_down, s5_lam_re, s5_lam_im, s5_log_dt, s5_b_re, s5_b_im, s5_c_re, s5_c_im,
                  s5_d, w_s5_glu, w_mix_out):
    Bsz, S, _ = h.shape
    proj = h @ w_in + b_in
    xm, om, ip, fp, us, ga, gb = jnp.split(proj, IN_SPLITS, axis=-1)
    xc = jax.nn.silu(_causal_dwconv(xm, w_mlstm_conv, b_mlstm_conv)).reshape(Bsz, S, MLSTM_HEADS, MLSTM_VDIM)
    q = jnp.einsum('bshc,hcd->bhsd', xc, w_mlstm_q)
    k = jnp.einsum('bshc,hcd->bhsd', xc, w_mlstm_k)
    v = xm.reshape(Bsz, S, MLSTM_HEADS, MLSTM_VDIM).transpose(0, 2, 1, 3)
    hm = _mlstm_chunkwise(q, k, v, ip.transpose(0, 2, 1), fp.transpose(0, 2, 1))
    hm = _standardize(hm).transpose(0, 2, 1, 3).reshape(Bsz, S, MLSTM_WIDTH).astype(h.dtype)
    y_a = (hm * mlstm_norm_gain * jax.nn.sigmoid(om)) @ w_mlstm_down
    ys = jax.nn.gelu(_s5_ssm(us, s5_lam_re, s5_lam_im, s5_log_dt, s5_b_re, s5_b_im, s5_c_re, s5_c_im, s5_d))
    val, gate = jnp.split(ys @ w_s5_glu, 2, axis=-1)
    y_b = val * jax.nn.sigmoid(gate)
    y = jax.nn.sigmoid(ga) * y_a + jax.nn.sigmoid(gb) * y_b
    return y @ w_mix_out


def _conv_glu_ffn(h, w_ffn_up, w_ffn_conv, b_ffn_conv, w_ffn_down):
    val, gate = jnp.split(h @ w_ffn_up, 2, axis=-1)
    gate = jax.nn.gelu(_causal_dwconv(gate, w_ffn_conv, b_ffn_conv))
    return (gate * val) @ w_ffn_down


def setup_inputs(seed: int = 0) -> dict:
    key = jax.random.key(seed)
    ks = iter(jax.random.split(key, 40))
    f32 = jnp.float32
    L = DEPTH

    def nrm(shape, scale):
        return scale * jax.random.normal(next(ks), shape, f32)

    x = nrm((BATCH, SEQ, D_MODEL), 1.0)
    c = nrm((BATCH, D_MODEL), 1.0)
    w_ada = nrm((L, D_MODEL, 6 * D_MODEL), 0.5 * D_MODEL ** -0.5)
    b_ada = nrm((L, 6 * D_MODEL), 0.02)
    w_in = nrm((L, D_MODEL, IN_WIDTH), D_MODEL ** -0.5)
    f_bias = jnp.linspace(3.0, 6.0, MLSTM_HEADS, dtype=f32)
    b_in = nrm((L, IN_WIDTH), 0.02).at[:, F_OFF:F_OFF + MLSTM_HEADS].add(f_bias)
    w_mlstm_conv = nrm((L, MLSTM_CONV, MLSTM_WIDTH), MLSTM_CONV ** -0.5)
    b_mlstm_conv = nrm((L, MLSTM_WIDTH), 0.02)
    w_mlstm_q = nrm((L, MLSTM_HEADS, MLSTM_VDIM, MLSTM_QKDIM), MLSTM_VDIM ** -0.5)
    w_mlstm_k = nrm((L, MLSTM_HEADS, MLSTM_VDIM, MLSTM_QKDIM), MLSTM_VDIM ** -0.5)
    mlstm_norm_gain = 1.0 + nrm((L, MLSTM_WIDTH), 0.02)
    w_mlstm_down = nrm((L, MLSTM_WIDTH, D_MODEL), MLSTM_WIDTH ** -0.5)
    s5_lam_re = -0.5 + nrm((L, S5_GROUPS, S5_STATE), 0.01)
    s5_lam_im = math.pi * jnp.arange(S5_STATE, dtype=f32) + nrm((L, S5_GROUPS, S5_STATE), 0.01)
    s5_log_dt = math.log(S5_DT_MIN) + jax.random.uniform(next(ks), (L, S5_GROUPS), f32) * (math.log(S5_DT_MAX) - math.log(S5_DT_MIN))
    s5_b_re = nrm((L, S5_GROUPS, S5_STATE, S5_GROUP), (2.0 * S5_GROUP) ** -0.5)
    s5_b_im = nrm((L, S5_GROUPS, S5_STATE, S5_GROUP), (2.0 * S5_GROUP) ** -0.5)
    s5_c_re = nrm((L, S5_GROUPS, S5_GROUP, S5_STATE), (2.0 * S5_STATE) ** -0.5)
    s5_c_im = nrm((L, S5_GROUPS, S5_GROUP, S5_STATE), (2.0 * S5_STATE) ** -0.5)
    s5_d = nrm((L, S5_WIDTH), 1.0)
    w_s5_glu = nrm((L, S5_WIDTH, 2 * D_MODEL), S5_WIDTH ** -0.5)
    w_mix_out = nrm((L, D_MODEL, D_MODEL), BETA * D_MODEL ** -0.5)
    ln1_gain = 1.0 + nrm((L, D_MODEL), 0.02)
    ln1_bias = nrm((L, D_MODEL), 0.02)
    w_ffn_up = nrm((L, D_MODEL, 2 * FFN_HIDDEN), D_MODEL ** -0.5)
    w_ffn_conv = nrm((L, FFN_CONV, FFN_HIDDEN), FFN_CONV ** -0.5)
    b_ffn_conv = nrm((L, FFN_HIDDEN), 0.02)
    w_ffn_down = nrm((L, FFN_HIDDEN, D_MODEL), BETA * FFN_HIDDEN ** -0.5)
    ln2_gain = 1.0 + nrm((L, D_MODEL), 0.02)
    ln2_bias = nrm((L, D_MODEL), 0.02)
    return {'x': x, 'c': c, 'w_ada': w_ada, 'b_ada': b_ada, 'w_in': w_in, 'b_in': b_in,
            'w_mlstm_conv': w_mlstm_conv, 'b_mlstm_conv': b_mlstm_conv, 'w_mlstm_q': w_mlstm_q,
            'w_mlstm_k': w_mlstm_k, 'mlstm_norm_gain': mlstm_norm_gain, 'w_mlstm_down': w_mlstm_down,
            's5_lam_re': s5_lam_re, 's5_lam_im': s5_lam_im, 's5_log_dt': s5_log_dt, 's5_b_re': s5_b_re,
            's5_b_im': s5_b_im, 's5_c_re': s5_c_re, 's5_c_im': s5_c_im, 's5_d': s5_d, 'w_s5_glu': w_s5_glu,
            'w_mix_out': w_mix_out, 'ln1_gain': ln1_gain, 'ln1_bias': ln1_bias, 'w_ffn_up': w_ffn_up,
            'w_ffn_conv': w_ffn_conv, 'b_ffn_conv': b_ffn_conv, 'w_ffn_down': w_ffn_down,
            'ln2_gain': ln2_gain, 'ln2_bias': ln2_bias}


def reference(x, c, w_ada, b_ada, w_in, b_in, w_mlstm_conv, b_mlstm_conv, w_mlstm_q, w_mlstm_k,
              mlstm_norm_gain, w_mlstm_down, s5_lam_re, s5_lam_im, s5_log_dt, s5_b_re, s5_b_im,
              s5_c_re, s5_c_im, s5_d, w_s5_glu, w_mix_out, ln1_gain, ln1_bias, w_ffn_up, w_ffn_conv,
              b_ffn_conv, w_ffn_down, ln2_gain, ln2_bias):
    c_act = jax.nn.silu(c)
    for l in range(DEPTH):
        mod = c_act @ w_ada[l] + b_ada[l]
        sh1, sc1, g1, sh2, sc2, g2 = [m[:, None, :] for m in jnp.split(mod, 6, axis=-1)]
        h = _standardize(x) * (1.0 + sc1) + sh1
        mix = _hybrid_mixer(h, w_in[l], b_in[l], w_mlstm_conv[l], b_mlstm_conv[l], w_mlstm_q[l],
                            w_mlstm_k[l], mlstm_norm_gain[l], w_mlstm_down[l], s5_lam_re[l],
                            s5_lam_im[l], s5_log_dt[l], s5_b_re[l], s5_b_im[l], s5_c_re[l],
                            s5_c_im[l], s5_d[l], w_s5_glu[l], w_mix_out[l])
        x = _layer_norm(ALPHA * x + (1.0 + g1) * mix, ln1_gain[l], ln1_bias[l])
        h = _standardize(x) * (1.0 + sc2) + sh2
        f = _conv_glu_ffn(h, w_ffn_up[l], w_ffn_conv[l], b_ffn_conv[l], w_ffn_down[l])
        x = _layer_norm(ALPHA * x + (1.0 + g2) * f, ln2_gain[l], ln2_bias[l])
    return x
```

```python
import math
from contextlib import ExitStack
import numpy as np
import concourse.bass as bass
import concourse.mybir as mybir
from concourse.bass_utils import run_bass_kernel_spmd

F32 = mybir.dt.float32
BF16 = mybir.dt.bfloat16
AF = mybir.ActivationFunctionType
ALU = mybir.AluOpType
AX = mybir.AxisListType

N_DMA_SEMS = 24
COMPUTE = ("pe", "act", "dve", "pool")
ENG = {"pe": "tensor", "act": "scalar", "dve": "vector", "pool": "gpsimd", "sp": "sync"}

D = 1024
NH = 4
DV = 256
DK = 128
S5W = 512
FH = 2816
INW = 4616
NKT = 8
ALPHA = 2.0 ** 0.25
LN_EPS = 1e-5
TB = 8
ARENA_F32 = 50688


class Prog:
    def __init__(self, nc, stack):
        self.nc = nc
        self.stack = stack
        self.ops = []
        self.arena = stack.enter_context(nc.sbuf_tensor("arena", [128, ARENA_F32], F32))
        self.top = 0
        self.peak = 0
        self.banks = [stack.enter_context(nc.psum_tensor(f"bank{i}", [128, 512], F32)) for i in range(8)]
        self.bank_i = 0
        self.uid = 0

    def alloc(self, shape, dtype=F32):
        n = 1
        for s in shape[1:]:
            n *= s
        words = (n + 1) // 2 if dtype == BF16 else n
        words = (words + 7) // 8 * 8
        a = self.top
        self.top += words
        self.peak = max(self.peak, self.top)
        assert self.top <= ARENA_F32, f"SBUF arena overflow {self.top}"
        v = self.arena[:, a:a + words]
        if dtype == BF16:
            v = v.bitcast(BF16)
        v = v[:, 0:n]
        if len(shape) == 3:
            v = v.rearrange("p (a b) -> p a b", b=shape[2])
        elif len(shape) == 4:
            v = v.rearrange("p (a b c) -> p a b c", b=shape[2], c=shape[3])
        elif len(shape) == 5:
            v = v.rearrange("p (a b c d) -> p a b c d", b=shape[2], c=shape[3], d=shape[4])
        return v[0:shape[0]] if shape[0] != 128 else v

    def key(self, prefix="k"):
        self.uid += 1
        return f"{prefix}{self.uid}"

    def bank(self):
        i = self.bank_i
        self.bank_i = (i + 1) % 8
        return self.banks[i], f"bank{i}"

    def op(self, eng, fn, reads=(), writes=()):
        self.ops.append(dict(eng=eng, fn=fn, reads=tuple(reads), writes=tuple(writes), dma=False, bar=False))

    def dma(self, out, in_, reads=(), writes=(), q="sp", **kw):
        def fn(e, out=out, in_=in_, kw=kw):
            return e.dma_start(out=out, in_=in_, **kw)
        rd, wr = list(reads), list(writes)
        for ap, lst in ((in_, rd), (out, wr)):
            if isinstance(ap, bass.AP) and ap.tensor.name == "arena":
                lst.append(ap)
        self.ops.append(dict(eng=q, fn=fn, reads=tuple(rd), writes=tuple(wr), dma=True, bar=False))

    def barrier(self):
        self.ops.append(dict(eng=None, fn=None, reads=(), writes=(), dma=False, bar=True))

    @staticmethod
    def _res(x):
        if isinstance(x, str):
            return ("key", x)
        name = x.tensor.name
        if name != "arena":
            return ("key", name)
        esz = 2 if x.dtype == BF16 else 4
        aps = x.ap
        pstride = ARENA_F32 * 4 // esz
        off = x.offset
        p0 = off // pstride
        lo = (off % pstride) * esz
        if aps[0][0] == 0:
            npart = 1
        else:
            npart = aps[0][1]
        span = 1
        for (st_, cnt) in aps[1:]:
            span += (cnt - 1) * abs(st_)
        return ("box", p0, p0 + npart, lo, lo + span * esz)

    def mmg(self, items, extra_reads=()):
        def f(e, items=items):
            for it in items:
                kw = it[5] if len(it) > 5 else {}
                ins = e.matmul(it[0], it[1], it[2], start=it[3], stop=it[4], **kw)
            return ins
        rd, wr = list(extra_reads), []
        for it in items:
            rd += [it[1], it[2]]
            wr.append(it[0])
        self.op("pe", f, rd, wr)

    def trg(self, items):
        def f(e, items=items):
            for (o, i_, idn) in items:
                ins = e.transpose(o, i_, idn)
            return ins
        self.op("pe", f, [x for it in items for x in (it[1], it[2])], [it[0] for it in items])

    def act(self, out, in_, func, bias=None, scale=1.0):
        rd = [in_] + [x for x in (bias, scale) if isinstance(x, bass.AP)]
        kw = {} if bias is None else {"bias": bias}
        self.op("act", lambda e: e.activation(out, in_, func, scale=scale, **kw), rd, [out])

    def tt(self, eng, out, a, b, op):
        self.op(eng, lambda e: e.tensor_tensor(out, a, b, op), [a, b], [out])

    def ts(self, eng, out, a, s1, s2, op0, op1=None):
        rd = [a] + [x for x in (s1, s2) if isinstance(x, bass.AP)]
        if op1 is None:
            self.op(eng, lambda e: e.tensor_scalar(out, a, s1, s2, op0), rd, [out])
        else:
            self.op(eng, lambda e: e.tensor_scalar(out, a, s1, s2, op0, op1), rd, [out])

    def stt(self, out, a, sc, b, op0, op1):
        rd = [a, b] + ([sc] if isinstance(sc, bass.AP) else [])
        self.op("dve", lambda e: e.scalar_tensor_tensor(out, a, sc, b, op0, op1), rd, [out])

    def cp(self, eng, out, in_):
        if eng == "act":
            self.op("act", lambda e: e.copy(out, in_), [in_], [out])
        else:
            self.op(eng, lambda e: e.tensor_copy(out, in_), [in_], [out])

    def view(self, a, shape, dtype=F32):
        top = self.top
        self.top = a
        v = self.alloc(shape, dtype)
        used = self.top
        self.top = max(top, used)
        return v

    def emit(self, final_wait_eng="sp"):
        nc = self.nc
        ops = self.ops
        n = len(ops)
        last_w, readers = {}, {}
        boxes = []
        deps = [set() for _ in ops]
        since_bar = []
        pending = {}
        for i, o in enumerate(ops):
            if o["bar"]:
                last_per_eng, dmas = {}, []
                for p in since_bar:
                    po = ops[p]
                    if po["dma"]:
                        dmas.append(p)
                    else:
                        last_per_eng[po["eng"]] = p
                pre = set(last_per_eng.values()) | set(dmas)
                for e in list(COMPUTE) + ["sp"]:
                    pending[e] = set(pre) | pending.get(e, set())
                since_bar = []
                last_w, readers, boxes = {}, {}, []
                continue
            d = set()
            rres = [self._res(x) for x in o["reads"]]
            wres = [self._res(x) for x in o["writes"]]
            for r in rres:
                if r[0] == "key":
                    if r[1] in last_w:
                        d.add(last_w[r[1]])
                    if r[1].startswith("bank"):
                        for r_ in readers.get(r[1], ()):
                            if ops[r_]["eng"] != o["eng"]:
                                d.add(r_)
                else:
                    _, p0, p1, lo, hi = r
                    for bx in boxes:
                        if bx[5] and bx[1] < p1 and p0 < bx[2] and bx[3] < hi and lo < bx[4]:
                            d.add(bx[0])
            for w in wres:
                if w[0] == "key":
                    if w[1] in last_w:
                        d.add(last_w[w[1]])
                    for r_ in readers.get(w[1], ()):
                        d.add(r_)
                else:
                    _, p0, p1, lo, hi = w
                    for bx in boxes:
                        if bx[1] < p1 and p0 < bx[2] and bx[3] < hi and lo < bx[4]:
                            d.add(bx[0])
            for p in d:
                if p == i:
                    continue
                po = ops[p]
                if not po["dma"] and not o["dma"] and po["eng"] == o["eng"] and o["eng"] == "pe":
                    continue
                deps[i].add(p)
            if pending.get(o["eng"]):
                for p in pending[o["eng"]]:
                    po = ops[p]
                    if (not po["dma"]) and (not o["dma"]) and po["eng"] == o["eng"]:
                        continue
                    deps[i].add(p)
                pending[o["eng"]] = set()
            ek = ("dma", i) if o["dma"] else o["eng"]
            for w in wres:
                if w[0] == "key":
                    last_w[w[1]] = i
                    readers[w[1]] = []
                else:
                    _, p0, p1, lo, hi = w
                    boxes = [bx for bx in boxes if not (p0 <= bx[1] and bx[2] <= p1 and lo <= bx[3] and bx[4] <= hi)]
                    boxes.append([i, p0, p1, lo, hi, True, ek])
            for r in rres:
                if r[0] == "key":
                    readers.setdefault(r[1], []).append(i)
                else:
                    _, p0, p1, lo, hi = r
                    boxes = [bx for bx in boxes if not (not bx[5] and bx[6] == ek and bx[1] == p0 and bx[2] == p1 and bx[3] == lo and bx[4] == hi)]
                    boxes.append([i, p0, p1, lo, hi, False, ek])
            since_bar.append(i)
        needed = set()
        for i in range(n):
            needed |= deps[i]
        sig_no, cnt = {}, {e: 0 for e in COMPUTE}
        dma_slot, dma_tot, ndma = {}, [0] * N_DMA_SEMS, 0
        nq = {"sp": 0, "pool": 0, "act": 0}
        NSP = N_DMA_SEMS - 8
        for i, o in enumerate(ops):
            if o["bar"]:
                continue
            if o["dma"]:
                if o["eng"] == "pool":
                    s = NSP + nq["pool"] % 2
                    nq["pool"] += 1
                else:
                    s = nq["sp"] % NSP
                    nq["sp"] += 1
                ndma += 1
                prev = dma_tot[s]
                dma_tot[s] += 16
                dma_slot[i] = (s, prev, dma_tot[s])
            elif i in needed:
                cnt[o["eng"]] += 1
                sig_no[i] = cnt[o["eng"]]
        st = self.stack
        csem = {e: st.enter_context(nc.semaphore(f"s_{e}")) for e in COMPUTE}
        dsem = [st.enter_context(nc.semaphore(f"s_dma{j}")) for j in range(N_DMA_SEMS)]
        used = sorted({o["eng"] for o in ops if not o["bar"]} | {final_wait_eng})
        with nc.Block() as block:
            for ename in used:
                def body(e, ename=ename):
                    seen = {c: 0 for c in csem}
                    seen_dma = [0] * N_DMA_SEMS
                    for i, o in enumerate(ops):
                        if o["bar"] or o["eng"] != ename:
                            continue
                        for p in sorted(deps[i]):
                            po = ops[p]
                            if po["dma"]:
                                s, _, tgt = dma_slot[p]
                                if seen_dma[s] < tgt:
                                    e.wait_ge(dsem[s], tgt)
                                    seen_dma[s] = tgt
                            else:
                                pe_ = po["eng"]
                                nn = sig_no[p]
                                if seen[pe_] < nn:
                                    e.wait_ge(csem[pe_], nn)
                                    seen[pe_] = nn
                        if o["dma"]:
                            s, prev, tgt = dma_slot[i]
                            if prev > 0 and seen_dma[s] < prev:
                                e.wait_ge(dsem[s], prev)
                                seen_dma[s] = prev
                            o["fn"](e).then_inc(dsem[s], 16)
                        else:
                            ins = o["fn"](e)
                            if i in sig_no:
                                ins.then_inc(csem[ename], 1)
                    if ename == final_wait_eng:
                        for s in range(N_DMA_SEMS):
                            if dma_tot[s] > seen_dma[s]:
                                e.wait_ge(dsem[s], dma_tot[s])
                getattr(block, ENG[ename])(body)
        return dict(n_ops=n, sig=cnt, ndma=ndma, peak_kb=self.peak * 4 / 1024)


WA_SRC = [0, 512, 1024, 1536, 2056, 2568, 3080, 3592, 4104]
BIN_GROUPS = [(0, 8), (1024, 8), (2056, 4), (2568, 8), (3592, 8)]
WF_GROUPS = [(0, 8), (8, 8), (16, 6)]


def dram_in(nc, name, shape, dtype=F32):
    return nc.dram_tensor(name, list(shape), dtype, kind="ExternalInput").ap()


class Builder:
    def __init__(self, state_ns, full_ns, skip_sub, dbg=()):
        self.state_ns = list(state_ns)
        self.full_ns = list(full_ns)
        self.skip_sub = skip_sub
        self.dbg = set(dbg)
        self.ntok = 128 * (sum(state_ns) + sum(full_ns))
        self.nout = 128 * (sum(full_ns) - skip_sub)
        self.nc = bass.Bass("TRN2", target_bir_lowering=False)
        self.dbg_out = {}

    def dbg_dump(self, name, ap, keys, dtype=F32):
        if name not in self.dbg:
            return
        shape = list(ap.shape)
        o = self.nc.dram_tensor("dbg_" + name, shape, dtype, kind="ExternalOutput").ap()
        self.P.dma(o, ap, reads=keys)

    def build(self):
        nc = self.nc
        I = {}
        def inp(name, shape):
            I[name] = dram_in(nc, name, shape)
        inp("xin", [self.ntok, D]); inp("cvec", [D]); inp("flagv", [128, 1])
        inp("w_ada", [D, 6 * D]); inp("b_ada", [6 * D]); inp("w_in", [D, INW]); inp("b_in", [INW])
        inp("w_mlstm_conv", [4, D]); inp("b_mlstm_conv", [D]); inp("w_mlstm_q", [NH, DV, DK]); inp("w_mlstm_k", [NH, DV, DK])
        inp("mlstm_norm_gain", [D]); inp("w_mlstm_down", [D, D])
        inp("s5_lam_re", [32, 64]); inp("s5_lam_im", [32, 64]); inp("s5_log_dt", [32])
        inp("s5_b_re", [32, 64, 16]); inp("s5_b_im", [32, 64, 16]); inp("s5_c_re", [32, 16, 64]); inp("s5_c_im", [32, 16, 64])
        inp("s5_d", [S5W]); inp("w_s5_glu", [S5W, 2 * D]); inp("w_mix_out", [D, D])
        inp("ln1_gain", [D]); inp("ln1_bias", [D]); inp("w_ffn_up", [D, 2 * FH]); inp("w_ffn_conv", [3, FH]); inp("b_ffn_conv", [FH])
        inp("w_ffn_down", [FH, D]); inp("ln2_gain", [D]); inp("ln2_bias", [D])
        inp("c_ident", [128, 128]); inp("c_tri", [128, 128]); inp("c_bmask", [128, 128])
        self.I = I
        self.yout = nc.dram_tensor("yout", [self.nout, D], F32, kind="ExternalOutput").ap()
        S = {}
        def scr(name, shape, dtype=BF16):
            S[name] = nc.dram_tensor("scr_" + name, list(shape), dtype, kind="Internal").ap()
        scr("wA", [9, 128, 8, 512]); scr("wD", [2, 128, 8, 512]); scr("wG", [4, 128, 4, 512]); scr("wM", [2, 128, 8, 512])
        scr("wU", [11, 128, 8, 512]); scr("wF", [6, 128, 8, 512]); scr("fdiag", [22, 128, 3, 128])
        scr("s5K", [128, 4, 8, 128]); scr("s5X", [2, 128, 2, 8, 2, 128]); scr("s5Y", [2, 128, 8, 8, 2, 32])
        self.S = S
        st = ExitStack()
        with st:
            self.P = P = Prog(nc, st)
            self.persistent()
            self.prologue()
            self.main()
            import os as _os
            if _os.environ.get("KMAX"):
                P.ops = P.ops[:int(_os.environ["KMAX"])]
            info = P.emit()
        self.info = info
        return nc

    def persistent(self):
        P, I = self.P, self.I
        A = P.alloc
        T = self.T = {}
        T["ident_f"] = A([128, 128]); T["tri_f"] = A([128, 128]); T["bmask_f"] = A([128, 128]); T["ones_f"] = A([128, 128])
        T["ident_b"] = A([128, 128], BF16); T["mask_b"] = A([128, 128], BF16)
        P.dma(T["ident_f"], I["c_ident"], writes=["ident_f"])
        P.dma(T["tri_f"], I["c_tri"], writes=["tri_f"])
        P.dma(T["bmask_f"], I["c_bmask"], writes=["bmask_f"])
        P.op("pool", lambda e: e.memset(T["ones_f"], 1.0), writes=["ones_f"])
        P.op("dve", lambda e: e.tensor_copy(T["ident_b"], T["ident_f"]), reads=["ident_f"], writes=["ident_b"])
        P.op("dve", lambda e: e.tensor_copy(T["mask_b"], T["tri_f"]), reads=["tri_f"], writes=["mask_b"])
        T["mhalf"] = A([128, 4]); T["flag"] = A([128, 1]); T["xmh"] = A([128, 8, 3], BF16)
        P.op("pool", lambda e: e.memset(T["mhalf"], -0.5), writes=["mhalf"])
        P.dma(T["flag"], I["flagv"], writes=["flag"])
        T["modT"] = A([128, 48])
        T["binT"] = A([128, 36]); T["bgate"] = A([128, 8])
        T["wgt"] = A([128, 8, 8], BF16); T["wq"] = A([128, 4, 2, 128], BF16); T["wk"] = A([128, 4, 2, 128], BF16)
        T["cdiag"] = A([128, 8, 4, 128], BF16); T["cb"] = A([128, 8]); T["gainT"] = A([128, 8])
        T["fcb"] = A([128, 22]); T["dT"] = A([128, 4])
        T["C32"] = A([128, 4, 260]); T["Cb"] = A([128, 4, 260], BF16)
        T["rotc"] = A([128, 16, 64]); T["rots"] = A([128, 16, 64]); T["rho8"] = A([128, 16]); T["u8"] = A([128, 2, 16])
        T["xcar"] = A([128, 2, 16])
        self.d0 = {}
        for nb in sorted({ns * 16 for ns in self.state_ns + self.full_ns}):
            self.d0[nb] = A([128, 16, nb])
        T["fhalo"] = A([128, 22, 2], BF16)

    def ap_cols(self, vec, off, ntile):
        return bass.AP(vec.tensor, off, [[1, 128], [128, ntile]])

    def ap_bcast(self, vec, off, n):
        return bass.AP(vec.tensor, off, [[0, 128], [1, n]])


def bc(ap, n):
    return bass.AP(ap.tensor, ap.offset, [list(x) for x in ap.ap] + [[0, n]])


def bc_mid(ap, n):
    a = [list(x) for x in ap.ap]
    return bass.AP(ap.tensor, ap.offset, [a[0], [0, n]] + a[1:])


SLOW = dict(allow_slow_non_contiguous=True)


def _prologue(self):
    P, I, T, S = self.P, self.I, self.T, self.S
    A = P.alloc
    mark = P.top
    TT = lambda eng, out, a, b, op, rd, wr: P.op(eng, lambda e: e.tensor_tensor(out, a, b, op), reads=rd, writes=wr)
    w_in_v = I["w_in"].rearrange("(kt p) c -> p kt c", p=128)
    for c in range(9):
        P.dma(S["wA"][c], w_in_v[:, :, WA_SRC[c]:WA_SRC[c] + 512], writes=[f"wA{c}"], q="pool")
    P.dma(T["wgt"], w_in_v[:, :, 2048:2056], writes=["wgt"], q="pool")
    P.dma(T["wq"], I["w_mlstm_q"].rearrange("h (kt p) d -> p h kt d", p=128), writes=["wq"], q="pool")
    P.dma(T["wk"], I["w_mlstm_k"].rearrange("h (kt p) d -> p h kt d", p=128), writes=["wk"], q="pool")
    wd_v = I["w_mlstm_down"].rearrange("(kt p) c -> p kt c", p=128)
    for c in range(2):
        P.dma(S["wD"][c], wd_v[:, :, 512 * c:512 * c + 512], writes=[f"wD{c}"], q="pool")
    wg_v = I["w_s5_glu"].rearrange("(kt p) c -> p kt c", p=128)
    for c in range(4):
        P.dma(S["wG"][c], wg_v[:, :, 512 * c:512 * c + 512], writes=[f"wG{c}"], q="pool")
    wu_v = I["w_ffn_up"].rearrange("(kt p) c -> p kt c", p=128)
    for c in range(11):
        P.dma(S["wU"][c], wu_v[:, :, 512 * c:512 * c + 512], writes=[f"wU{c}"], q="pool")
    colstg = [A([128, 128]) for _ in range(2)]
    mark2 = P.top
    cact = A([128, 8]); badaT = A([128, 48]); cw = A([128, 8, 4]); fcw = A([128, 22, 3])
    self.cl_i = 0
    for i_ in range(2):
        P.op("pool", lambda e, i_=i_: e.memset(colstg[i_], 0.0), writes=[f"colstg{i_}"])

    def load_cols(dst, vec, off, nt, dkey, dst_is_3d=None):
        sg = colstg[self.cl_i % 2]; sk = f"colstg{self.cl_i % 2}"; self.cl_i += 1
        P.dma(sg[0:nt, :], bass.AP(vec.tensor, off, [[128, nt], [1, 128]]), reads=[sk], writes=[sk])
        bk_, bkk = P.bank()
        P.op("pe", lambda e, sg=sg, bk_=bk_: e.transpose(bk_[:, 0:128], sg, T["ident_f"]), reads=[sk, "ident_f"], writes=[bkk])
        P.op("dve", lambda e, dst=dst, bk_=bk_, nt=nt: e.tensor_copy(dst, bk_[:, 0:nt] if dst_is_3d is None else dst_is_3d(bk_)), reads=[bkk], writes=[dkey])

    load_cols(cact, I["cvec"], 0, 8, "cact")
    load_cols(badaT, I["b_ada"], 0, 48, "badaT")
    c0 = 0
    for gi, (off, nt) in enumerate(BIN_GROUPS):
        load_cols(T["binT"][:, c0:c0 + nt], I["b_in"], off, nt, f"binT{gi}")
        c0 += nt
    P.dma(T["bgate"], self.ap_bcast(I["b_in"], 2048, 8), writes=["bgate"])
    load_cols(cw.rearrange("p c j -> p j c"), I["w_mlstm_conv"], 0, 32, "cw", dst_is_3d=lambda b_: b_[:, 0:32].rearrange("p (j c) -> p j c", c=8))
    load_cols(T["cb"], I["b_mlstm_conv"], 0, 8, "cb")
    load_cols(T["gainT"], I["mlstm_norm_gain"], 0, 8, "gainT")
    load_cols(fcw.rearrange("p t j -> p j t"), I["w_ffn_conv"], 0, 66, "fcw", dst_is_3d=lambda b_: b_[:, 0:66].rearrange("p (j t) -> p j t", t=22))
    load_cols(T["fcb"], I["b_ffn_conv"], 0, 22, "fcb")
    load_cols(T["dT"], I["s5_d"], 0, 4, "dT")
    self.load_cols = load_cols
    n = 0
    for ct in range(8):
        for j in range(4):
            eng = "dve" if n % 2 == 0 else "pool"; n += 1
            P.op(eng, lambda e, ct=ct, j=j: e.tensor_scalar(T["cdiag"][:, ct, j, :], T["ident_f"], cw[:, ct, j:j + 1], None, ALU.mult),
                 reads=["cw", "ident_f"], writes=[f"cdiag{ct}_{j}"])
    fd = A([128, 22, 3, 128], BF16)
    for t in range(22):
        for j in range(3):
            eng = "dve" if n % 2 == 0 else "pool"; n += 1
            P.op(eng, lambda e, t=t, j=j: e.tensor_scalar(fd[:, t, j, :], T["ident_f"], fcw[:, t, j:j + 1], None, ALU.mult),
                 reads=["fcw", "ident_f"], writes=[f"fd{t}_{j}"])
    P.dma(S["fdiag"].rearrange("t p j d -> p t j d"), fd, reads=[f"fd{t}_{j}" for t in range(22) for j in range(3)], writes=["fdiag"])
    P.op("act", lambda e: e.activation(cact, cact, AF.Silu), reads=["cact"], writes=["cact"])
    cact2 = A([128, 8, 2])
    P.op("dve", lambda e: e.tensor_copy(cact2, bc(cact, 2)), reads=["cact"], writes=["cact2"])
    cactB = A([128, 8, 128])
    P.op("dve", lambda e: e.tensor_copy(cactB, bc(cact, 128)), reads=["cact"], writes=["cactB"])
    g1b = A([128, D]); g2b = A([128, D])
    P.dma(g1b, self.ap_bcast(I["b_ada"], 2 * D, D), writes=["g1b0", "g1b1"])
    P.dma(g2b, self.ap_bcast(I["b_ada"], 5 * D, D), writes=["g2b0", "g2b1"])
    stg = [A([128, 8, 512]) for _ in range(2)]
    wada_v = I["w_ada"].rearrange("(kt p) c -> p kt c", p=128)
    mb, mbk = P.bank()
    for c in range(12):
        sg, sk = stg[c % 2], f"stg{c % 2}"
        P.dma(sg, wada_v[:, :, 512 * c:512 * c + 512], writes=[sk])
        kind, half = c // 2, c % 2
        if kind in (2, 5):
            gb_, gk = (g1b, f"g1b{half}") if kind == 2 else (g2b, f"g2b{half}")
            bk_, bkk = P.bank()
            def f(e, sg=sg, bk_=bk_):
                for kt in range(8):
                    r = e.matmul(bk_[:, 0:512], cactB[:, kt, :], sg[:, kt, :], start=(kt == 0), stop=(kt == 7))
                return r
            P.op("pe", f, reads=[sk, "cactB"], writes=[bkk])
            dst = gb_[:, 512 * half:512 * half + 512]
            P.op("dve", lambda e, dst=dst, bk_=bk_: e.scalar_tensor_tensor(dst, bk_[:, 0:512], 1.0, dst, ALU.add, ALU.add), reads=[bkk, gk], writes=[gk])
        else:
            def f(e, sg=sg, kind=kind, half=half):
                for j in range(4):
                    ct = kind * 8 + half * 4 + j
                    for kt in range(8):
                        r = e.matmul(mb[:, 2 * ct:2 * ct + 2], sg[:, kt, 128 * j:128 * j + 128], cact2[:, kt, :], start=(kt == 0), stop=(kt == 7))
                return r
            P.op("pe", f, reads=[sk, "cact2"], writes=[mbk])
    P.op("pool", lambda e: e.memset(T["modT"], 0.0), writes=["modT0", "modT24"])
    for (a, b) in ((0, 16), (24, 40)):
        P.op("dve", lambda e, a=a, b=b: e.tensor_tensor(T["modT"][:, a:b], mb[:, 2 * a:2 * b:2], badaT[:, a:b], ALU.add), reads=[mbk, "badaT"], writes=[f"modT{a}"])
    for a, kk in ((8, "modT0"), (32, "modT24")):
        P.op("dve", lambda e, a=a: e.tensor_scalar(T["modT"][:, a:a + 8], T["modT"][:, a:a + 8], 1.0, None, ALU.add), reads=[kk], writes=[kk])
    self.dbg_dump("modT", T["modT"], ["modT0", "modT24"])
    self.dbg_dump("g1b", g1b, ["g1b0", "g1b1"])
    ob = [A([128, 8, 512], BF16) for _ in range(2)]
    wm_v = I["w_mix_out"].rearrange("(kt p) c -> p kt c", p=128)
    wf_v = I["w_ffn_down"].rearrange("(kt p) c -> p kt c", p=128)
    jobs = [(wm_v, 0, 8, hf, g1b, f"g1b{hf}", S["wM"][hf], f"wM{hf}") for hf in range(2)]
    for gi, (k0, nk) in enumerate(WF_GROUPS):
        for hf in range(2):
            jobs.append((wf_v, k0, nk, hf, g2b, f"g2b{hf}", S["wF"][gi * 2 + hf], f"wF{gi * 2 + hf}"))
    for ji, (src, k0, nk, hf, gt, gk, dst, dk) in enumerate(jobs):
        sg, sk = stg[ji % 2], f"stg{ji % 2}"
        o_, ok_ = ob[ji % 2], f"ob{ji % 2}"
        P.dma(sg[:, 0:nk, :], src[:, k0:k0 + nk, 512 * hf:512 * hf + 512], writes=[sk])
        eng = "dve" if ji % 2 == 0 else "pool"
        P.op(eng, lambda e, o_=o_, sg=sg, nk=nk, gt=gt, hf=hf: e.tensor_tensor(o_[:, 0:nk, :], sg[:, 0:nk, :], bc_mid(gt[:, 512 * hf:512 * hf + 512], nk), ALU.mult),
             reads=[sk, gk], writes=[ok_])
        P.dma(dst[:, 0:nk, :], o_[:, 0:nk, :], reads=[ok_], writes=[dk])
    P.barrier()
    P.top = mark2
    self.s5_prologue()
    P.op("pool", lambda e: e.memset(T["C32"], 0.0), writes=["C32"])
    P.op("pool", lambda e: e.memset(T["Cb"], 0.0), writes=["Cb"])
    P.op("pool", lambda e: e.memset(T["xcar"], 0.0), writes=["xcar"])
    P.op("pool", lambda e: e.memset(T["fhalo"], 0.0), writes=["fhalo"])
    P.op("pool", lambda e: e.memset(T["xmh"], 0.0), writes=["xmh"])
    P.barrier()
    P.top = mark
    print("ops after prologue", len(P.ops))


Builder.prologue = _prologue


def _s5_prologue(self):
    P, I, T, S = self.P, self.I, self.T, self.S
    A = P.alloc
    V = "dve"

    def tk(t):
        return "tmp_" + t.tensor.name + str(t.offset)

    def tt(out, a, b, op, rd, wr, eng=V):
        P.op(eng, lambda e: e.tensor_tensor(out, a, b, op), reads=rd, writes=wr)

    def cmul(outr, outi, ar, ai, br, bi, t1, t2, rd, wr):
        k1, k2 = "tmp_" + t1.tensor.name + str(t1.offset), "tmp_" + t2.tensor.name + str(t2.offset)
        tt(t1, ar, br, ALU.mult, rd, [k1])
        tt(t2, ai, bi, ALU.mult, rd, [k2])
        tt(outr, t1, t2, ALU.subtract, [k1, k2], [wr + "r"])
        tt(t1, ar, bi, ALU.mult, rd + [wr + "r"], [k1])
        tt(t2, ai, br, ALU.mult, rd + [wr + "r"], [k2])
        tt(outi, t1, t2, ALU.add, [k1, k2], [wr + "i"])

    lamr = A([128, 16]); lami = A([128, 16]); ldt = A([128, 16]); dt = A([128, 16]); phi = A([128, 16]); aa = A([128, 16])
    cc = A([128, 16]); ss = A([128, 16]); t1 = A([128, 16]); t2 = A([128, 16])
    self.load_cols(lamr, I["s5_lam_re"], 0, 16, "lamr")
    self.load_cols(lami, I["s5_lam_im"], 0, 16, "lami")
    ldtb = A([128, 32])
    P.dma(ldtb, self.ap_bcast(I["s5_log_dt"], 0, 32), writes=["ldtb"])
    for h in range(2):
        P.op(V, lambda e, h=h: e.tensor_copy(ldt[64 * h:64 * h + 64, :], ldtb[64 * h:64 * h + 64, h:32:2]), reads=["ldtb"], writes=[f"ldt{h}"])
    def taylor_exp(out, x, deg, xk, ok, tmp):
        P.op(V, lambda e: e.tensor_scalar(out, x, 1.0 / deg, 1.0, ALU.mult, ALU.add), reads=xk, writes=[ok])
        for n_ in range(deg - 1, 0, -1):
            tt(tmp, x, out, ALU.mult, xk + [ok], [tk(tmp)])
            P.op(V, lambda e, n_=n_: e.tensor_scalar(out, tmp, 1.0 / n_, 1.0, ALU.mult, ALU.add), reads=[tk(tmp)], writes=[ok])

    P.op(V, lambda e: e.tensor_scalar(ldt, ldt, 1.0 / 16, None, ALU.mult), reads=["ldt0", "ldt1"], writes=["ldt0", "ldt1"])
    taylor_exp(dt, ldt, 10, ["ldt0", "ldt1"], "dt", t1)
    for _ in range(4):
        tt(t2, dt, dt, ALU.mult, ["dt"], [tk(t2)])
        P.op(V, lambda e: e.tensor_copy(dt, t2), reads=[tk(t2)], writes=["dt"])
    tt(phi, lami, dt, ALU.mult, ["lami", "dt"], ["phi"])
    tt(aa, lamr, dt, ALU.mult, ["lamr", "dt"], ["aa"])
    P.op("act", lambda e: e.activation(ss, phi, AF.Sin, scale=1.0 / 32), reads=["phi"], writes=["ss"])
    hp = A([128, 1])
    P.op("pool", lambda e: e.memset(hp, math.pi / 2), writes=["hp"])
    P.op("act", lambda e: e.activation(cc, phi, AF.Sin, scale=1.0 / 32, bias=hp), reads=["phi", "hp"], writes=["cc"])
    for it in range(5):
        tt(t1, cc, cc, ALU.mult, ["cc"], [tk(t1)])
        tt(t2, ss, ss, ALU.mult, ["ss"], [tk(t2)])
        P.op(V, lambda e: e.scalar_tensor_tensor(ss, cc, 2.0, ss, ALU.mult, ALU.mult), reads=["cc", "ss", tk(t2)], writes=["ss"])
        tt(cc, t1, t2, ALU.subtract, [tk(t1), tk(t2), "ss"], ["cc"])
    UPr = A([128, 9, 16]); UPi = A([128, 9, 16]); MG = A([128, 9, 16]); PWr = A([128, 9, 16]); PWi = A([128, 9, 16])
    P.op("pool", lambda e: e.memset(UPr[:, 0, :], 1.0), writes=["UP0r"])
    P.op("pool", lambda e: e.memset(UPi[:, 0, :], 0.0), writes=["UP0i"])
    P.op(V, lambda e: e.tensor_copy(UPr[:, 1, :], cc), reads=["cc"], writes=["UP1r"])
    P.op(V, lambda e: e.tensor_copy(UPi[:, 1, :], ss), reads=["ss"], writes=["UP1i"])
    for k in range(2, 9):
        cmul(UPr[:, k, :], UPi[:, k, :], UPr[:, k - 1, :], UPi[:, k - 1, :], UPr[:, 1, :], UPi[:, 1, :], t1, t2,
             [f"UP{k - 1}r", f"UP{k - 1}i", "UP1r", "UP1i"], f"UP{k}")
    P.op("pool", lambda e: e.memset(MG[:, 0, :], 1.0), writes=["MG0"])
    taylor_exp(MG[:, 1, :], aa, 7, ["aa"], "MG1", t1)
    for k in range(2, 9):
        tt(MG[:, k, :], MG[:, k - 1, :], MG[:, 1, :], ALU.mult, [f"MG{k - 1}", "MG1"], [f"MG{k}"])
    allup = [f"UP{k}{c}" for k in range(9) for c in "ri"] + [f"MG{k}" for k in range(9)]
    tt(PWr, MG, UPr, ALU.mult, allup, ["PWr"])
    tt(PWi, MG, UPi, ALU.mult, allup, ["PWi"])
    PW = ["PWr", "PWi"]
    den = A([128, 16]); am1 = A([128, 16]); zr = A([128, 16]); zi = A([128, 16])
    tt(t1, lamr, lamr, ALU.mult, ["lamr"] + PW, [tk(t1)])
    tt(t2, lami, lami, ALU.mult, ["lami"] + PW, [tk(t2)])
    tt(den, t1, t2, ALU.add, [tk(t1), tk(t2)], ["den"])
    P.op(V, lambda e: e.reciprocal(den, den), reads=["den"], writes=["den"])
    P.op(V, lambda e: e.tensor_scalar(am1, PWr[:, 1, :], -1.0, None, ALU.add), reads=PW, writes=["am1"])
    tt(t1, am1, lamr, ALU.mult, ["am1", "lamr", "den"], [tk(t1)])
    tt(t2, PWi[:, 1, :], lami, ALU.mult, PW + ["lami", "den"], [tk(t2)])
    tt(zr, t1, t2, ALU.add, [tk(t1), tk(t2)], ["zr"])
    tt(zr, zr, den, ALU.mult, ["zr", "den"], ["zr"])
    tt(t1, PWi[:, 1, :], lamr, ALU.mult, PW + ["lamr", "zr"], [tk(t1)])
    tt(t2, am1, lami, ALU.mult, ["am1", "lami", "zr"], [tk(t2)])
    tt(zi, t1, t2, ALU.subtract, [tk(t1), tk(t2)], ["zi"])
    tt(zi, zi, den, ALU.mult, ["zi", "den"], ["zi"])
    Bre = A([128, 16, 16]); Bim = A([128, 16, 16]); Cre = A([128, 16, 16]); Cim = A([128, 16, 16])
    b_ap = lambda v: bass.AP(v.tensor, 0, [[16, 128], [2048, 16], [1, 16]])
    P.dma(Bre, b_ap(I["s5_b_re"]), writes=["Bre"])
    P.dma(Bim, b_ap(I["s5_b_im"]), writes=["Bim"])
    Cl = A([128, 16, 128])
    P.op("pool", lambda e: e.memset(Cl, 0.0), writes=["Cl"])
    for nm, dstC in (("s5_c_re", Cre), ("s5_c_im", Cim)):
        for q in range(16):
            P.dma(Cl[0:16, q, :].rearrange("c (g p) -> c g p", p=64), bass.AP(I[nm].tensor, 2048 * q, [[64, 16], [1024, 2], [1, 64]]), reads=["Cl"], writes=[f"Cl{q}"])
        for b4 in range(4):
            bk_, bkk = P.bank()
            def f(e, bk_=bk_, b4=b4):
                for qq in range(4):
                    r = e.transpose(bk_[:, 128 * qq:128 * qq + 128], Cl[:, 4 * b4 + qq, :], T["ident_f"])
                return r
            P.op("pe", f, reads=[f"Cl{4 * b4 + qq}" for qq in range(4)] + ["ident_f"], writes=[bkk])
            P.op(V, lambda e, dstC=dstC, bk_=bk_, b4=b4: e.tensor_copy(dstC[:, 4 * b4:4 * b4 + 4, :], bk_[:, 0:512].rearrange("p (q x) -> p q x", x=128)[:, :, 0:16]),
                 reads=[bkk], writes=["C" + nm[-2:] + str(b4)])
    CK = [f"C{x}{b}" for x in ("re", "im") for b in range(4)]
    BBr = A([128, 16, 16]); BBi = A([128, 16, 16]); W1 = A([128, 16, 16]); W2 = A([128, 16, 16])
    cmul(BBr, BBi, bc(zr, 16), bc(zi, 16), Bre, Bim, W1, W2, ["zr", "zi", "Bre", "Bim"], "BB")
    PBr = A([128, 8, 16, 16]); PBi = A([128, 8, 16, 16])
    for k in range(8):
        cmul(PBr[:, k], PBi[:, k], bc(PWr[:, k, :], 16), bc(PWi[:, k, :], 16), BBr, BBi, W1, W2, PW + ["BBr", "BBi"], f"PB{k}")
    CBD = A([128, 2, 16, 32])
    P.op("pool", lambda e: e.memset(CBD, 0.0), writes=["CBD"])
    for h in range(2):
        sl = slice(64 * h, 64 * h + 64)
        P.op(V, lambda e, sl=sl, h=h: e.tensor_copy(CBD[sl, 0, :, 16 * h:16 * h + 16], Cre[sl]), reads=[f"Cre{b}" for b in range(4)] + ["CBD"], writes=["CBD"])
        P.op(V, lambda e, sl=sl, h=h: e.tensor_scalar(CBD[sl, 1, :, 16 * h:16 * h + 16], Cim[sl], -1.0, None, ALU.mult), reads=[f"Cim{b}" for b in range(4)] + ["CBD"], writes=["CBD"])
    Mb = [A([128, 4, 2, 16]) for _ in range(4)]
    for b in range(4):
        P.op("pool", lambda e, b=b: e.memset(Mb[b], 0.0), writes=[f"Mb{b}"])
    XwS = A([128, 4, 8, 2, 128], BF16)
    KcS = A([128, 4, 8, 128], BF16)
    dtmp = A([128, 128]); ktmp = A([128, 128])
    nb_ = 0
    for ct in range(4):
        for k in range(8):
            tau = 7 - k
            kb, kbk = P.bank()
            mfs = []
            for ri in range(2):
                m, mk = Mb[nb_ % 4], f"Mb{nb_ % 4}"; nb_ += 1
                src = (PBr if ri == 0 else PBi)
                for h in range(2):
                    sl = slice(64 * h, 64 * h + 64)
                    P.op(V if h == 0 else "pool", lambda e, m=m, sl=sl, h=h, src=src, k=k, ct=ct: e.tensor_copy(m[sl, :, h, :], src[sl, k, 4 * ct:4 * ct + 4, :]),
                         reads=[f"PB{k}r", f"PB{k}i", mk], writes=[mk])
                mf = m.rearrange("p a b c -> p (a b c)")
                mfs.append((mf, mk))
                P.op("pe", lambda e, mf=mf, kb=kb, ri=ri: e.transpose(kb[:, 128 + 128 * ri:256 + 128 * ri], mf, T["ident_f"]), reads=[mk, "ident_f"], writes=[kbk])
            for ri in range(2):
                mf, mk = mfs[ri]
                P.op("pe", lambda e, mf=mf, kb=kb, ri=ri, ct=ct: e.matmul(kb[:, 0:128], mf, CBD[:, ri, 4 * ct:4 * ct + 4, :].rearrange("p a b -> p (a b)"), start=(ri == 0), stop=(ri == 1)),
                     reads=[mk, "CBD"], writes=[kbk])
            P.op("act", lambda e, kb=kb, ct=ct, tau=tau: e.copy(XwS[:, ct, tau].rearrange("p a b -> p (a b)"), kb[:, 128:384]), reads=[kbk], writes=["XwS"])
            if k == 0:
                P.op(V, lambda e, ct=ct: e.tensor_scalar(dtmp, T["ident_f"], T["dT"][:, ct:ct + 1], None, ALU.mult), reads=["dT", "ident_f", "KcS"], writes=["dtmp"])
                P.op(V, lambda e, kb=kb: e.tensor_tensor(ktmp, kb[:, 0:128], T["bmask_f"], ALU.mult), reads=[kbk, "bmask_f", "KcS"], writes=["ktmp"])
                P.op(V, lambda e, ct=ct, k=k: e.tensor_tensor(KcS[:, ct, k, :], ktmp, dtmp, ALU.add), reads=["ktmp", "dtmp"], writes=["KcS"])
            else:
                P.op(V, lambda e, kb=kb, ct=ct, k=k: e.tensor_tensor(KcS[:, ct, k, :], kb[:, 0:128], T["bmask_f"], ALU.mult), reads=[kbk, "bmask_f"], writes=["KcS"])
    P.dma(S["s5K"], KcS, reads=["KcS"], writes=["s5K"])
    for c in range(2):
        P.dma(S["s5X"][c], XwS[:, 2 * c:2 * c + 2], reads=["XwS"], writes=[f"s5X{c}"])
    self.dbg_dump("KcS", KcS, ["KcS"], BF16)
    self.dbg_dump("XwS", XwS, ["XwS"], BF16)
    YwS = A([128, 16, 8, 2, 32], BF16)
    P.op("pool", lambda e: e.memset(YwS, 0.0), writes=["YwS"])
    YR = A([128, 16, 16]); YI = A([128, 16, 16])
    for tau in range(8):
        pr, pi = bc(PWr[:, tau + 1, :], 16), bc(PWi[:, tau + 1, :], 16)
        cmul(YR, YI, Cre, Cim, pr, pi, W1, W2, CK + PW + ["YwS"], "YY")
        for h in range(2):
            sl = slice(64 * h, 64 * h + 64)
            P.op(V, lambda e, sl=sl, h=h, tau=tau: e.tensor_copy(YwS[sl, :, tau, 0, 16 * h:16 * h + 16], YR[sl]), reads=["YYr", "YwS"], writes=["YwS"])
            P.op(V, lambda e, sl=sl, h=h, tau=tau: e.tensor_scalar(YwS[sl, :, tau, 1, 16 * h:16 * h + 16], YI[sl], -1.0, None, ALU.mult), reads=["YYi", "YwS"], writes=["YwS"])
    for c in range(2):
        P.dma(S["s5Y"][c], YwS[:, 8 * c:8 * c + 8], reads=["YwS"], writes=[f"s5Y{c}"])
    self.dbg_dump("YwS", YwS, ["YwS"], BF16)
    rc, rs = T["rotc"], T["rots"]
    P.op("pool", lambda e: e.memset(rc[:, :, 0], 1.0), writes=["rot"])
    P.op("pool", lambda e: e.memset(rs[:, :, 0], 0.0), reads=["rot"], writes=["rot"])
    P.op(V, lambda e: e.tensor_copy(T["u8"][:, 0, :], UPr[:, 8, :]), reads=["UP8r"], writes=["u8"])
    P.op(V, lambda e: e.tensor_copy(T["u8"][:, 1, :], UPi[:, 8, :]), reads=["UP8i", "u8"], writes=["u8"])
    r1r = A([128, 16]); r1i = A([128, 16]); wr_ = A([128, 16]); wi_ = A([128, 16])
    P.op(V, lambda e: e.tensor_copy(r1r, UPr[:, 8, :]), reads=["UP8r"], writes=["r1r"])
    P.op(V, lambda e: e.tensor_scalar(r1i, UPi[:, 8, :], -1.0, None, ALU.mult), reads=["UP8i"], writes=["r1i"])
    ln = 1
    R1 = A([128, 16, 32]); R2 = A([128, 16, 32])
    while ln < 64:
        cmul(wr_, wi_, rc[:, :, ln - 1], rs[:, :, ln - 1], r1r, r1i, t1, t2, ["rot", "r1r", "r1i"], "W")
        a_r, a_i = rc[:, :, 0:ln], rs[:, :, 0:ln]
        w_r, w_i = bc(wr_, ln), bc(wi_, ln)
        x1, x2 = R1[:, :, 0:ln], R2[:, :, 0:ln]
        tt(x1, a_r, w_r, ALU.mult, ["rot", "Wr", "Wi"], ["x1"])
        tt(x2, a_i, w_i, ALU.mult, ["rot", "Wr", "Wi"], ["x2"])
        tt(rc[:, :, ln:2 * ln], x1, x2, ALU.subtract, ["x1", "x2"], ["rot"])
        tt(x1, a_r, w_i, ALU.mult, ["rot", "Wr", "Wi"], ["x1"])
        tt(x2, a_i, w_r, ALU.mult, ["rot", "Wr", "Wi"], ["x2"])
        tt(rs[:, :, ln:2 * ln], x1, x2, ALU.add, ["x1", "x2", "rot"], ["rot"])
        ln *= 2
    P.op(V, lambda e: e.tensor_copy(T["rho8"], MG[:, 8, :]), reads=["MG8"], writes=["rho8"])
    for nb, d0 in self.d0.items():
        P.op(V, lambda e, d0=d0, nb=nb: e.tensor_copy(d0, bc(T["rho8"], nb)), reads=["rho8"], writes=[f"d0_{nb}"])
        P.op(V, lambda e, d0=d0: e.memset(d0[:, :, 0], 0.0), reads=[f"d0_{nb}"], writes=[f"d0_{nb}"])
    self.dbg_dump("rotc", rc, ["rot"])
    self.dbg_dump("rots", rs, ["rot"])


Builder.s5_prologue = _s5_prologue


def _main(self):
    P, I, T, S = self.P, self.I, self.T, self.S
    A = P.alloc
    NSM = max(self.state_ns + self.full_ns)
    NM = 128 * NSM
    NBM = 16 * NSM
    B = self.B = {}
    B["XR"] = [A([128, D]) for _ in range(NSM + 1)]
    B["xn"] = [A([128, D], BF16) for _ in range(2)]
    B["hT"] = A([128, 8, NM], BF16)
    B["hm"] = A([128, NSM, D], BF16)
    B["omS"] = A([128, 8, NM], BF16)
    B["prodT"] = A([128, 8, NM], BF16)
    B["yT"] = A([128, 8, NM], BF16)
    B["gtmp"] = A([128, 3, NM], BF16)
    B["uT"] = A([128, 4, NM], BF16)
    B["ysT"] = A([128, 4, NM], BF16)
    B["gsm"] = A([128, NSM, 48])
    B["lnt"] = A([128, 2, D])
    B["ring"] = [A([128, 8, 512], BF16) for _ in range(3)]
    B["smr"] = A([128, 512])
    self.sm_i = 0
    r0 = P.top
    B["xmT"] = A([128, 8, 3 + NM], BF16)
    B["xcT"] = A([128, 8, NM], BF16)
    B["QT"] = A([128, 4, NM], BF16)
    B["KT"] = A([128, 4, NM], BF16)
    B["Vx"] = A([128, NSM, 4, 258], BF16)
    B["KW"] = A([128, 2, 4, 128], BF16)
    B["SW"] = A([128, 2, 4, 128], BF16)
    r1 = P.top
    P.top = r0
    B["Xin"] = A([128, 2, 16, NBM])
    B["Zr"] = A([128, 16 * NBM]); B["Zi"] = A([128, 16 * NBM])
    B["Ta"] = A([128, 16 * NBM]); B["Tb"] = A([128, 16 * NBM])
    B["XS"] = A([128, 2, 16, NBM])
    B["xprev"] = A([128, 2, 16, NBM], BF16)
    r2 = P.top
    P.top = r0
    B["prodF"] = A([128, 22, NM], BF16)
    B["gpre"] = A([128, 2, 2 + NM], BF16)
    B["gact"] = A([128, 2, NM], BF16)
    r3 = P.top
    P.top = max(r1, r2, r3)
    self.ring_i = 0
    self.ring_tag = [None, None, None]
    self.ring_use = [0, 0, 0]
    self.ring_shape = [None, None, None]
    self.ring_clock = 0
    self.xr_i = 0
    tok = 0
    out_row = 0
    n_pre = len(self.state_ns) + (1 if self.skip_sub > 0 else 0)
    tiles = [("state", ns) for ns in self.state_ns] + [("full", ns) for ns in self.full_ns]
    skipped = 0
    for ti, (mode, ns) in enumerate(tiles):
        store = None
        if mode == "full":
            if skipped < self.skip_sub:
                assert ns <= self.skip_sub - skipped
                skipped += ns
            else:
                store = out_row
                out_row += 128 * ns
        self.tile(mode, tok, ns, store)
        tok += 128 * ns
        if ti == n_pre - 1:
            self.apply_flag()
    assert out_row == self.nout


def _sm(self, n):
    if self.sm_i + n > 512:
        self.sm_i = 0
    v = self.B["smr"][:, self.sm_i:self.sm_i + n]
    self.sm_i += n
    return v


def _wchunk(self, name, idx, shape=None, src=None):
    tag = (name, idx)
    self.ring_clock += 1
    for i in range(3):
        if self.ring_tag[i] == tag:
            self.ring_use[i] = self.ring_clock
            return self.chunk_view(i, shape if shape is not None else self.ring_shape[i])
    i = min(range(3), key=lambda k: self.ring_use[k])
    self.ring_tag[i] = tag
    self.ring_use[i] = self.ring_clock
    if src is None:
        src = self.S[name] if idx is None else self.S[name][idx]
    self.ring_shape[i] = list(src.shape)
    dst = self.chunk_view(i, list(src.shape))
    self.P.dma(dst, src)
    return self.chunk_view(i, shape if shape is not None else list(src.shape))


def _chunk_view(self, i, shape):
    slot = self.B["ring"][i]
    if shape is None or list(shape) == [128, 8, 512]:
        return slot
    flat = slot.rearrange("p a b -> p (a b)")
    n = 1
    for x in shape[1:]:
        n *= x
    v = flat[:, 0:n]
    if len(shape) == 3:
        return v.rearrange("p (a b) -> p a b", b=shape[2])
    if len(shape) == 4:
        return v.rearrange("p (a b c) -> p a b c", b=shape[2], c=shape[3])
    if len(shape) == 5:
        return v.rearrange("p (a b c d) -> p a b c d", b=shape[2], c=shape[3], d=shape[4])
    return v


def _apply_flag(self):
    P, T, B = self.P, self.T, self.B
    fl = T["flag"][:, 0:1]
    P.ts("dve", T["C32"], T["C32"], fl, None, ALU.mult)
    P.cp("pool", T["Cb"], T["C32"])
    P.ts("dve", T["xcar"], T["xcar"], fl, None, ALU.mult)
    P.ts("dve", T["xmh"], T["xmh"], fl, None, ALU.mult)
    P.ts("dve", T["fhalo"], T["fhalo"], fl, None, ALU.mult)


def _ln_stats(self, x):
    P, T = self.P, self.T
    st6 = self.sm(12).rearrange("p (a b) -> p a b", b=6)
    mv = self.sm(2); tmp = self.sm(1); rstd = self.sm(1); nmr = self.sm(1)
    P.op("dve", lambda e: e.bn_stats(st6[:, 0, :], x[:, 0:512]), [x[:, 0:512]], [st6[:, 0, :]])
    P.op("dve", lambda e: e.bn_stats(st6[:, 1, :], x[:, 512:1024]), [x[:, 512:1024]], [st6[:, 1, :]])
    P.op("dve", lambda e: e.bn_aggr(mv, st6.rearrange("p a b -> p (a b)")), [st6], [mv])
    P.ts("pool", tmp, mv[:, 1:2], LN_EPS, None, ALU.add)
    P.tt("pool", rstd, tmp, T["mhalf"][:, 0:1], ALU.pow)
    P.stt(nmr, mv[:, 0:1], -1.0, rstd, ALU.mult, ALU.mult)
    return rstd, nmr


def _ln_to_T(self, x, s, sc0, sh0):
    P, T, B = self.P, self.T, self.B
    rstd, nmr = self.ln_stats(x)
    xn = B["xn"][s % 2]
    P.act(xn, x, AF.Identity, bias=nmr, scale=rstd)
    bk, _ = P.bank()
    bb = bk[:, :].bitcast(BF16)
    P.trg([(bb[:, 128 * kt:128 * kt + 128], xn[:, 128 * kt:128 * kt + 128], T["ident_b"]) for kt in range(8)])
    for kt in range(8):
        P.act(B["hT"][:, kt, 128 * s:128 * s + 128], bb[:, 128 * kt:128 * kt + 128], AF.Identity,
              bias=T["modT"][:, sh0 + kt:sh0 + kt + 1], scale=T["modT"][:, sc0 + kt:sc0 + kt + 1])


def _inproj_tile(self, ptile, N):
    P, B = self.P, self.B
    w = self.wchunk("wA", ptile // 4)
    j = ptile % 4
    bk, _ = P.bank()
    P.mmg([(bk[:, 0:N], w[:, kt, 128 * j:128 * j + 128], B["hT"][:, kt, 0:N], kt == 0, kt == 7) for kt in range(8)])
    return bk


Builder.main = _main
Builder.sm = _sm
Builder.wchunk = _wchunk
Builder.chunk_view = _chunk_view
Builder.apply_flag = _apply_flag
Builder.ln_stats = _ln_stats
Builder.ln_to_T = _ln_to_T
Builder.inproj_tile = _inproj_tile


def _tile(self, mode, tok0, ns, store):
    P, I, T, S, B = self.P, self.I, self.T, self.S, self.B
    full = (mode == "full")
    N = 128 * ns
    nb = 16 * ns
    hT, xmT, xcT, QT, KT, Vx = B["hT"], B["xmT"], B["xcT"], B["QT"], B["KT"], B["Vx"]
    cs = lambda s: slice(128 * s, 128 * s + 128)
    xs = []
    for s in range(ns):
        x = B["XR"][self.xr_i]
        self.xr_i = (self.xr_i + 1) % len(B["XR"])
        xs.append(x)
        r0 = tok0 + 128 * s
        P.dma(x, I["xin"][r0:r0 + 128, :], q="pool")
        self.ln_to_T(x, s, 8, 0)
    P.cp("pool", xmT[:, :, 0:3], T["xmh"])
    P.op("pool", lambda e: e.memset(Vx[:, 0:ns, :, 256:257], 1.0), [], [Vx[:, 0:ns, :, 256:257]])
    for ct in range(8):
        bk = self.inproj_tile(ct, N)
        P.act(xmT[:, ct, 3:3 + N], bk[:, 0:N], AF.Identity, bias=T["binT"][:, ct:ct + 1])
    G = B["gsm"]
    for s in range(ns):
        g = G[:, s, :]
        bk, _ = P.bank()
        P.mmg([(bk[:, 0:8], hT[:, kt, cs(s)], T["wgt"][:, kt, :], kt == 0, kt == 7) for kt in range(8)])
        P.tt("dve", g[:, 0:8], bk[:, 0:8], T["bgate"], ALU.add)
        P.act(g[:, 8:12], g[:, 4:8], AF.Exp, scale=-1.0)
        P.act(g[:, 12:16], g[:, 8:12], AF.Ln, bias=T["ones_f"][:, 0:1])
        b2, _ = P.bank()
        P.mmg([(b2[:, 0:4], T["tri_f"], g[:, 12:16], True, True), (b2[:, 4:8], T["ones_f"], g[:, 12:16], True, True)])
        P.tt("dve", g[:, 32:36], g[:, 0:4], b2[:, 0:4], ALU.add)
        P.tt("dve", g[:, 36:40], g[:, 32:36], b2[:, 4:8], ALU.subtract)
        P.act(g[:, 20:24], g[:, 36:40], AF.Exp)
        P.act(g[:, 24:28], b2[:, 4:8], AF.Exp, scale=-1.0)
        if full:
            P.act(g[:, 16:20], g[:, 32:36], AF.Exp)
            P.act(g[:, 28:32], b2[:, 0:4], AF.Exp)
    for ct in range(8):
        bk, _ = P.bank()
        P.mmg([(bk[:, 0:N], T["cdiag"][:, ct, j, :], xmT[:, ct, j:j + N], j == 0, j == 3) for j in range(4)])
        P.act(xcT[:, ct, 0:N], bk[:, 0:N], AF.Silu, bias=T["cb"][:, ct:ct + 1])
    if full:
        for h in range(4):
            bk, _ = P.bank()
            P.mmg([(bk[:, 0:N], T["wq"][:, h, kt, :], xcT[:, 2 * h + kt, 0:N], kt == 0, kt == 1) for kt in range(2)])
            P.op("act", lambda e, h=h, bk=bk: e.mul(QT[:, h, 0:N], bk[:, 0:N], DK ** -0.5), [bk[:, 0:N]], [QT[:, h, 0:N]])
            bk2, _ = P.bank()
            P.mmg([(bk2[:, 0:N], T["wk"][:, h, kt, :], xcT[:, 2 * h + kt, 0:N], kt == 0, kt == 1) for kt in range(2)])
            P.cp("dve", KT[:, h, 0:N], bk2[:, 0:N])
    for s in range(ns):
        g = G[:, s, :]
        par = s % 2
        KW, SW = B["KW"][:, par], B["SW"][:, par]
        kb, _ = P.bank()
        items = []
        for h in range(4):
            for kt in range(2):
                items.append((kb[:, 128 * h:128 * h + 128], xcT[:, 2 * h + kt, cs(s)], T["wk"][:, h, kt, :], kt == 0, kt == 1))
        P.mmg(items)
        P.tt("dve", KW, kb[:, 0:512].rearrange("p (h d) -> p h d", d=128), bc(g[:, 20:24], 128), ALU.mult)
        vb, _ = P.bank()
        vbb = vb[:, :].bitcast(BF16)
        P.trg([(vbb[:, 128 * ct:128 * ct + 128], xmT[:, ct, 3 + 128 * s:3 + 128 * s + 128], T["ident_b"]) for ct in range(8)])
        P.cp("act", Vx[:, s, :, 0:256], vbb[:, 0:1024].rearrange("p (h v) -> p h v", v=256))
        if full:
            sb_, _ = P.bank()
            P.mmg([(sb_[:, 128 * h:128 * h + 128], KT[:, h, cs(s)], QT[:, h, cs(s)], True, True) for h in range(4)])
            for h in range(4):
                P.stt(SW[:, h, :], sb_[:, 128 * h:128 * h + 128], g[:, 16 + h:17 + h], T["mask_b"], ALU.mult, ALU.mult)
            nbs = []
            for h in range(4):
                nbk, _ = P.bank()
                nbs.append(nbk)
                P.mmg([(nbk[:, 0:257], SW[:, h, :], Vx[:, s, h, 0:257], True, False),
                       (nbk[:, 0:257], QT[:, h, cs(s)], T["Cb"][:, h, 0:257], False, True)])
            a1 = self.sm(4); rd = self.sm(4); st6 = self.sm(24).rearrange("p (h b) -> p h b", b=6); mv = self.sm(8).rearrange("p (h b) -> p h b", b=2)
            t1 = self.sm(4); aa = self.sm(4); nbv = self.sm(4)
            for h in range(4):
                P.act(a1[:, h:h + 1], nbs[h][:, 256:257], AF.Abs)
            P.tt("dve", a1, a1, g[:, 28:32], ALU.max)
            P.op("dve", lambda e, rd=rd, a1=a1: e.reciprocal(rd, a1), [a1], [rd])
            for h in range(4):
                P.op("dve", lambda e, h=h, st6=st6, nbs=nbs: e.bn_stats(st6[:, h, :], nbs[h][:, 0:256]), [nbs[h][:, 0:256]], [st6[:, h, :]])
            for h in range(4):
                P.op("dve", lambda e, h=h, st6=st6, mv=mv: e.bn_aggr(mv[:, h, :], st6[:, h, :]), [st6[:, h, :]], [mv[:, h, :]])
            P.tt("pool", t1, mv[:, :, 1], rd, ALU.mult)
            P.tt("pool", t1, t1, rd, ALU.mult)
            P.ts("pool", t1, t1, LN_EPS, None, ALU.add)
            P.tt("pool", t1, t1, T["mhalf"], ALU.pow)
            P.tt("pool", aa, t1, rd, ALU.mult)
            P.stt(nbv, mv[:, :, 0], -1.0, aa, ALU.mult, ALU.mult)
            for h in range(4):
                P.act(B["hm"][:, s, 256 * h:256 * h + 256], nbs[h][:, 0:256], AF.Identity, bias=nbv[:, h:h + 1], scale=aa[:, h:h + 1])
        for h in range(4):
            cbk, _ = P.bank()
            P.mmg([(cbk[:, 0:257], KW[:, h, :], Vx[:, s, h, 0:257], True, True)])
            P.stt(T["C32"][:, h, 0:257], T["C32"][:, h, 0:257], g[:, 24 + h:25 + h], cbk[:, 0:257], ALU.mult, ALU.add)
            P.cp("pool", T["Cb"][:, h, 0:257], T["C32"][:, h, 0:257])
    P.cp("pool", T["xmh"], xmT[:, :, N:N + 3])
    if full:
        self.tile_mix_out(ns, xs)
    self.tile_s5(full, ns)
    if full:
        self.tile_post(ns, xs, store)


Builder.tile = _tile


def _tile_mix_out(self, ns, xs):
    P, T, B = self.P, self.T, self.B
    N = 128 * ns
    hm, omS, prodT, yT, gtmp = B["hm"], B["omS"], B["prodT"], B["yT"], B["gtmp"]
    for ct in range(8):
        bk = self.inproj_tile(8 + ct, N)
        P.act(omS[:, ct, 0:N], bk[:, 0:N], AF.Sigmoid, bias=T["binT"][:, 8 + ct:9 + ct])
    for vp in range(4):
        bk, _ = P.bank()
        bb = bk[:, :].bitcast(BF16)
        items = []
        for j in range(2):
            vt = 2 * vp + j
            for s in range(ns):
                items.append((bb[:, 512 * j + 128 * s:512 * j + 128 * s + 128], hm[:, s, 128 * vt:128 * vt + 128], T["ident_b"]))
        P.trg(items)
        for j in range(2):
            vt = 2 * vp + j
            P.stt(prodT[:, vt, 0:N], bb[:, 512 * j:512 * j + N], T["gainT"][:, vt:vt + 1], omS[:, vt, 0:N], ALU.mult, ALU.mult)
    for dt_ in range(8):
        w = self.wchunk("wD", dt_ // 4)
        j = dt_ % 4
        bk, _ = P.bank()
        P.mmg([(bk[:, 0:N], w[:, kt, 128 * j:128 * j + 128], prodT[:, kt, 0:N], kt == 0, kt == 7) for kt in range(8)])
        b2 = self.inproj_tile(20 + dt_, N)
        P.act(gtmp[:, 0, 0:N], b2[:, 0:N], AF.Sigmoid, bias=T["binT"][:, 20 + dt_:21 + dt_])
        P.tt("dve", yT[:, dt_, 0:N], bk[:, 0:N], gtmp[:, 0, 0:N], ALU.mult)


def _tile_s5(self, full, ns):
    P, T, B = self.P, self.T, self.B
    N = 128 * ns
    nb = 16 * ns
    uT, ysT, yT, gtmp = B["uT"], B["ysT"], B["yT"], B["gtmp"]
    for ct in range(4):
        bk = self.inproj_tile(16 + ct, N)
        P.act(uT[:, ct, 0:N], bk[:, 0:N], AF.Identity, bias=T["binT"][:, 16 + ct:17 + ct])
    xb = [P.bank()[0] for _ in range(4)]
    for ct in range(4):
        w = self.wchunk("s5X", ct // 2)
        for q in range(4):
            items = []
            kw = dict(tile_position=(96, 0)) if q == 3 else {}
            for ri in range(2):
                c0 = (ct * 2 + ri) * nb
                for tau in range(8):
                    items.append((xb[q][:, c0:c0 + nb], w[32 * q:32 * q + 32, ct % 2, tau, ri, :], uT[32 * q:32 * q + 32, ct, tau:N:8], tau == 0, tau == 7, kw))
            P.mmg(items)
    Xin = B["Xin"]
    for q in range(4):
        src = xb[q][:, 0:8 * nb].rearrange("p (c r n) -> p r c n", c=4, r=2)
        P.cp("act" if q % 2 == 0 else "dve", Xin[:, :, q:16:4, 0:nb], src)
    cj, sj = T["rotc"][:, :, 0:nb], T["rots"][:, :, 0:nb]
    v3 = lambda t: t[:, 0:16 * nb].rearrange("p (q n) -> p q n", n=nb)
    Zr, Zi, Ta, Tb = v3(B["Zr"]), v3(B["Zi"]), v3(B["Ta"]), v3(B["Tb"])
    Xr, Xi = Xin[:, 0, :, 0:nb], Xin[:, 1, :, 0:nb]
    P.tt("dve", Ta, cj, Xr, ALU.mult); P.tt("pool", Tb, sj, Xi, ALU.mult); P.tt("dve", Zr, Ta, Tb, ALU.subtract)
    P.tt("dve", Ta, cj, Xi, ALU.mult); P.tt("pool", Tb, sj, Xr, ALU.mult); P.tt("dve", Zi, Ta, Tb, ALU.add)
    xc = T["xcar"]; u8 = T["u8"]
    i_r = self.sm(16); i_i = self.sm(16); ta = self.sm(16); tb = self.sm(16)
    P.tt("dve", ta, u8[:, 0, :], xc[:, 0, :], ALU.mult); P.tt("dve", tb, u8[:, 1, :], xc[:, 1, :], ALU.mult); P.tt("dve", i_r, ta, tb, ALU.subtract)
    P.tt("dve", ta, u8[:, 0, :], xc[:, 1, :], ALU.mult); P.tt("dve", tb, u8[:, 1, :], xc[:, 0, :], ALU.mult); P.tt("dve", i_i, ta, tb, ALU.add)
    P.tt("dve", i_r, i_r, T["rho8"], ALU.mult); P.tt("dve", i_i, i_i, T["rho8"], ALU.mult)
    P.tt("dve", Zr[:, :, 0], Zr[:, :, 0], i_r, ALU.add); P.tt("dve", Zi[:, :, 0], Zi[:, :, 0], i_i, ALU.add)
    d0 = self.d0[nb].rearrange("p q n -> p (q n)")
    fr, fi = B["Ta"][:, 0:16 * nb], B["Tb"][:, 0:16 * nb]
    P.op("dve", lambda e: e.tensor_tensor_scan(fr, d0, B["Zr"][:, 0:16 * nb], 0.0, ALU.mult, ALU.add), [d0, B["Zr"][:, 0:16 * nb]], [fr])
    P.op("dve", lambda e: e.tensor_tensor_scan(fi, d0, B["Zi"][:, 0:16 * nb], 0.0, ALU.mult, ALU.add), [d0, B["Zi"][:, 0:16 * nb]], [fi])
    XS = B["XS"]
    xr_o, xi_o = XS[:, 0, :, 0:nb], XS[:, 1, :, 0:nb]
    P.tt("dve", Zr, cj, Ta, ALU.mult); P.tt("pool", Zi, sj, Tb, ALU.mult); P.tt("dve", xr_o, Zr, Zi, ALU.add)
    P.tt("dve", Zr, cj, Tb, ALU.mult); P.tt("pool", Zi, sj, Ta, ALU.mult); P.tt("dve", xi_o, Zr, Zi, ALU.subtract)
    xp = B["xprev"]
    if full:
        P.cp("pool", xp[:, :, :, 0], xc)
        if nb > 1:
            P.cp("pool", xp[:, :, :, 1:nb], XS[:, :, :, 0:nb - 1])
    P.cp("dve", xc, XS[:, :, :, nb - 1])
    if not full:
        return
    kc = None
    for ct in range(4):
        kc = self.wchunk("s5K", None)
        yb, _ = P.bank()
        items = []
        for tp in range(8):
            for tau in range(tp, 8):
                items.append((yb[:, tau:N:8], kc[:, ct, tp, :], uT[:, ct, tau - tp:N:8], (tp == 0 and tau == 0), False, dict(skip_group_check=True)))
        P.mmg(items)
        items = []
        yw = self.wchunk("s5Y", ct // 2)
        for q in range(4):
            for tau in range(8):
                for ri in range(2):
                    last = (q == 3 and tau == 7 and ri == 1)
                    items.append((yb[32 * q:32 * q + 32, tau:N:8], yw[:, (4 * ct + q) % 8, tau, ri, :], xp[:, ri, 4 * ct + q, 0:nb],
                                  False, last, dict(tile_position=(0, 32 * q), skip_group_check=True)))
        P.mmg(items)
        P.act(ysT[:, ct, 0:N], yb[:, 0:N], AF.Gelu_apprx_tanh)
    for dt_ in range(8):
        j = dt_ % 4
        wv = self.wchunk("wG", dt_ // 4)
        bv, _ = P.bank()
        P.mmg([(bv[:, 0:N], wv[:, kt, 128 * j:128 * j + 128], ysT[:, kt, 0:N], kt == 0, kt == 3) for kt in range(4)])
        wg = self.wchunk("wG", 2 + dt_ // 4)
        bg, _ = P.bank()
        P.mmg([(bg[:, 0:N], wg[:, kt, 128 * j:128 * j + 128], ysT[:, kt, 0:N], kt == 0, kt == 3) for kt in range(4)])
        P.act(gtmp[:, 1, 0:N], bg[:, 0:N], AF.Sigmoid)
        P.tt("dve", gtmp[:, 2, 0:N], bv[:, 0:N], gtmp[:, 1, 0:N], ALU.mult)
        b2 = self.inproj_tile(28 + dt_, N)
        P.act(gtmp[:, 0, 0:N], b2[:, 0:N], AF.Sigmoid, bias=T["binT"][:, 28 + dt_:29 + dt_])
        P.tt("dve", gtmp[:, 2, 0:N], gtmp[:, 2, 0:N], gtmp[:, 0, 0:N], ALU.mult)
        P.tt("dve", yT[:, dt_, 0:N], yT[:, dt_, 0:N], gtmp[:, 2, 0:N], ALU.add)


Builder.tile_mix_out = _tile_mix_out
Builder.tile_s5 = _tile_s5


def _post_ln(self, x, gb_key):
    P, B = self.P, self.B
    rstd, nmr = self.ln_stats(x)
    P.act(x, x, AF.Identity, bias=nmr, scale=rstd)
    P.tt("dve", x, x, B["lnt"][:, 0, :], ALU.mult)
    P.tt("pool", x, x, B["lnt"][:, 1, :], ALU.add)


def _tile_post(self, ns, xs, store):
    P, I, T, S, B = self.P, self.I, self.T, self.S, self.B
    N = 128 * ns
    yT, hT, prodF = B["yT"], B["hT"], B["prodF"]
    cs = lambda s: slice(128 * s, 128 * s + 128)
    lnt = B["lnt"]
    P.dma(lnt[:, 0, :], self.ap_bcast(I["ln1_gain"], 0, D))
    P.dma(lnt[:, 1, :], self.ap_bcast(I["ln1_bias"], 0, D))
    for hf in range(2):
        w = self.wchunk("wM", hf)
        for s in range(ns):
            bk, _ = P.bank()
            P.mmg([(bk[:, 0:512], yT[:, kt, cs(s)], w[:, kt, :], kt == 0, kt == 7) for kt in range(8)])
            xh = xs[s][:, 512 * hf:512 * hf + 512]
            P.stt(xh, xh, ALPHA, bk[:, 0:512], ALU.mult, ALU.add)
    for s in range(ns):
        self.post_ln(xs[s], "ln1")
        self.ln_to_T(xs[s], s, 32, 24)
    fh = T["fhalo"]
    gpre, gact = B["gpre"], B["gact"]
    pend = None

    def conv_stage(t, par, bv):
        fd = self.wchunk("fdiag", t // 10, src=S["fdiag"][10 * (t // 10):min(22, 10 * (t // 10) + 10)].rearrange("t p j d -> p t j d"))
        tl = t % 10
        bc_, _ = P.bank()
        P.mmg([(bc_[:, 0:N], fd[:, tl, j, :], gpre[:, par, j:j + N], j == 0, j == 2) for j in range(3)])
        P.act(gact[:, par, 0:N], bc_[:, 0:N], AF.Gelu_apprx_tanh, bias=T["fcb"][:, t:t + 1])
        P.tt("dve", prodF[:, t, 0:N], bv[:, 0:N], gact[:, par, 0:N], ALU.mult)

    for t in range(22):
        par = t % 2
        wv = self.wchunk("wU", t // 4)
        bv, _ = P.bank()
        P.mmg([(bv[:, 0:N], wv[:, kt, 128 * (t % 4):128 * (t % 4) + 128], hT[:, kt, 0:N], kt == 0, kt == 7) for kt in range(8)])
        gt_ = 22 + t
        wg = self.wchunk("wU", gt_ // 4)
        bg, _ = P.bank()
        P.mmg([(bg[:, 0:N], wg[:, kt, 128 * (gt_ % 4):128 * (gt_ % 4) + 128], hT[:, kt, 0:N], kt == 0, kt == 7) for kt in range(8)])
        P.cp("pool", gpre[:, par, 0:2], fh[:, t, :])
        P.cp("act", gpre[:, par, 2:2 + N], bg[:, 0:N])
        P.cp("pool", fh[:, t, :], gpre[:, par, N:N + 2])
        if pend is not None:
            conv_stage(*pend)
        pend = (t, par, bv)
    conv_stage(*pend)
    P.dma(lnt[:, 0, :], self.ap_bcast(I["ln2_gain"], 0, D))
    P.dma(lnt[:, 1, :], self.ap_bcast(I["ln2_bias"], 0, D))
    for hf in range(2):
        acc = [P.bank()[0] for _ in range(ns)]
        for gi, (k0, nk) in enumerate(WF_GROUPS):
            w = self.wchunk("wF", gi * 2 + hf, src=S["wF"][gi * 2 + hf][:, 0:nk, :])
            for s in range(ns):
                P.mmg([(acc[s][:, 0:512], prodF[:, k0 + kk, cs(s)], w[:, kk, :], (gi == 0 and kk == 0), (gi == 2 and kk == nk - 1)) for kk in range(nk)])
        for s in range(ns):
            xh = xs[s][:, 512 * hf:512 * hf + 512]
            P.stt(xh, xh, ALPHA, acc[s][:, 0:512], ALU.mult, ALU.add)
    for s in range(ns):
        self.post_ln(xs[s], "ln2")
        if store is not None:
            P.dma(self.yout[store + 128 * s:store + 128 * s + 128, :], xs[s], q="pool")
    self.dbg_tile = True


Builder.post_ln = _post_ln
Builder.tile_post = _tile_post


_CACHE = {}


def _consts():
    bm = np.zeros((128, 128), np.float32)
    for q in range(4):
        bm[32 * q:32 * q + 32, 32 * q:32 * q + 32] = 1.0
    return np.eye(128, dtype=np.float32), np.triu(np.ones((128, 128), np.float32)), bm


def core_map(inputs, b, xin, flag):
    ident, tri, bm = _consts()
    m = {"xin": np.ascontiguousarray(xin, dtype=np.float32), "cvec": np.ascontiguousarray(inputs["c"][b], dtype=np.float32),
         "flagv": np.full((128, 1), flag, np.float32), "c_ident": ident, "c_tri": tri, "c_bmask": bm}
    for k, v in inputs.items():
        if k in ("x", "c"):
            continue
        m[k] = np.ascontiguousarray(np.asarray(v)[0], dtype=np.float32)
    return m


def kernel(**inputs):
    inputs = {k: np.asarray(v) for k, v in inputs.items()}
    x = inputs["x"]
    Bn, Sq, _ = x.shape
    half = Sq // 2
    n_state = (half - 128) // 128
    state_ns = [4] * (n_state // 4) + ([n_state % 4] if n_state % 4 else [])
    full_ns = [1] + [4] * (half // 512)
    key = (tuple(state_ns), tuple(full_ns))
    if key not in _CACHE:
        bld = Builder(state_ns, full_ns, 1)
        _CACHE[key] = bld.build()
    nc = _CACHE[key]
    in_maps = []
    for core in range(2 * Bn):
        b, h = core // 2, core % 2
        if h == 0:
            xin = np.concatenate([np.zeros((half, D), np.float32), x[b, :half]], axis=0)
        else:
            xin = x[b]
        in_maps.append(core_map(inputs, b, xin, float(h)))
    res = run_bass_kernel_spmd(nc, in_maps, core_ids=list(range(2 * Bn)))
    out = np.empty((Bn, Sq, D), np.float32)
    for core in range(2 * Bn):
        b, h = core // 2, core % 2
        out[b, h * half:(h + 1) * half] = res.results[core]["yout"]
    return out
```

```python
import math
from contextlib import ExitStack
import numpy as np
import concourse.bass as bass
import concourse.mybir as mybir
from concourse.bass_utils import run_bass_kernel_spmd

F32 = mybir.dt.float32
BF16 = mybir.dt.bfloat16
AF = mybir.ActivationFunctionType
ALU = mybir.AluOpType
AX = mybir.AxisListType

N_DMA_SEMS = 24
COMPUTE = ("pe", "act", "dve", "pool")
ENG = {"pe": "tensor", "act": "scalar", "dve": "vector", "pool": "gpsimd", "sp": "sync"}

D = 1024
NH = 4
DV = 256
DK = 128
S5W = 512
FH = 2816
INW = 4616
NKT = 8
ALPHA = 2.0 ** 0.25
LN_EPS = 1e-5
TB = 8
ARENA_F32 = 50688


class Prog:
    def __init__(self, nc, stack):
        self.nc = nc
        self.stack = stack
        self.ops = []
        self.arena = stack.enter_context(nc.sbuf_tensor("arena", [128, ARENA_F32], F32))
        self.top = 0
        self.peak = 0
        self.banks = [stack.enter_context(nc.psum_tensor(f"bank{i}", [128, 512], F32)) for i in range(8)]
        self.bank_i = 0
        self.uid = 0

    def alloc(self, shape, dtype=F32):
        n = 1
        for s in shape[1:]:
            n *= s
        words = (n + 1) // 2 if dtype == BF16 else n
        words = (words + 7) // 8 * 8
        a = self.top
        self.top += words
        self.peak = max(self.peak, self.top)
        assert self.top <= ARENA_F32, f"SBUF arena overflow {self.top}"
        v = self.arena[:, a:a + words]
        if dtype == BF16:
            v = v.bitcast(BF16)
        v = v[:, 0:n]
        if len(shape) == 3:
            v = v.rearrange("p (a b) -> p a b", b=shape[2])
        elif len(shape) == 4:
            v = v.rearrange("p (a b c) -> p a b c", b=shape[2], c=shape[3])
        elif len(shape) == 5:
            v = v.rearrange("p (a b c d) -> p a b c d", b=shape[2], c=shape[3], d=shape[4])
        return v[0:shape[0]] if shape[0] != 128 else v

    def key(self, prefix="k"):
        self.uid += 1
        return f"{prefix}{self.uid}"

    def bank(self):
        i = self.bank_i
        self.bank_i = (i + 1) % 8
        return self.banks[i], f"bank{i}"

    def op(self, eng, fn, reads=(), writes=()):
        self.ops.append(dict(eng=eng, fn=fn, reads=tuple(reads), writes=tuple(writes), dma=False, bar=False))

    def dma(self, out, in_, reads=(), writes=(), q="sp", **kw):
        def fn(e, out=out, in_=in_, kw=kw):
            return e.dma_start(out=out, in_=in_, **kw)
        rd, wr = list(reads), list(writes)
        for ap, lst in ((in_, rd), (out, wr)):
            if isinstance(ap, bass.AP) and ap.tensor.name == "arena":
                lst.append(ap)
        self.ops.append(dict(eng=q, fn=fn, reads=tuple(rd), writes=tuple(wr), dma=True, bar=False))

    def barrier(self):
        self.ops.append(dict(eng=None, fn=None, reads=(), writes=(), dma=False, bar=True))

    @staticmethod
    def _res(x):
        if isinstance(x, str):
            return ("key", x)
        name = x.tensor.name
        if name != "arena":
            return ("key", name)
        esz = 2 if x.dtype == BF16 else 4
        aps = x.ap
        pstride = ARENA_F32 * 4 // esz
        off = x.offset
        p0 = off // pstride
        lo = (off % pstride) * esz
        if aps[0][0] == 0:
            npart = 1
        else:
            npart = aps[0][1]
        span = 1
        for (st_, cnt) in aps[1:]:
            span += (cnt - 1) * abs(st_)
        return ("box", p0, p0 + npart, lo, lo + span * esz)

    def mmg(self, items, extra_reads=()):
        def f(e, items=items):
            for it in items:
                kw = it[5] if len(it) > 5 else {}
                ins = e.matmul(it[0], it[1], it[2], start=it[3], stop=it[4], **kw)
            return ins
        rd, wr = list(extra_reads), []
        for it in items:
            rd += [it[1], it[2]]
            wr.append(it[0])
        self.op("pe", f, rd, wr)

    def trg(self, items):
        def f(e, items=items):
            for (o, i_, idn) in items:
                ins = e.transpose(o, i_, idn)
            return ins
        self.op("pe", f, [x for it in items for x in (it[1], it[2])], [it[0] for it in items])

    def act(self, out, in_, func, bias=None, scale=1.0):
        rd = [in_] + [x for x in (bias, scale) if isinstance(x, bass.AP)]
        kw = {} if bias is None else {"bias": bias}
        self.op("act", lambda e: e.activation(out, in_, func, scale=scale, **kw), rd, [out])

    def tt(self, eng, out, a, b, op):
        self.op(eng, lambda e: e.tensor_tensor(out, a, b, op), [a, b], [out])

    def ts(self, eng, out, a, s1, s2, op0, op1=None):
        rd = [a] + [x for x in (s1, s2) if isinstance(x, bass.AP)]
        if op1 is None:
            self.op(eng, lambda e: e.tensor_scalar(out, a, s1, s2, op0), rd, [out])
        else:
            self.op(eng, lambda e: e.tensor_scalar(out, a, s1, s2, op0, op1), rd, [out])

    def stt(self, out, a, sc, b, op0, op1):
        rd = [a, b] + ([sc] if isinstance(sc, bass.AP) else [])
        self.op("dve", lambda e: e.scalar_tensor_tensor(out, a, sc, b, op0, op1), rd, [out])

    def cp(self, eng, out, in_):
        if eng == "act":
            self.op("act", lambda e: e.copy(out, in_), [in_], [out])
        else:
            self.op(eng, lambda e: e.tensor_copy(out, in_), [in_], [out])

    def view(self, a, shape, dtype=F32):
        top = self.top
        self.top = a
        v = self.alloc(shape, dtype)
        used = self.top
        self.top = max(top, used)
        return v

    def emit(self, final_wait_eng="sp"):
        nc = self.nc
        ops = self.ops
        n = len(ops)
        last_w, readers = {}, {}
        boxes = []
        deps = [set() for _ in ops]
        since_bar = []
        pending = {}
        for i, o in enumerate(ops):
            if o["bar"]:
                last_per_eng, dmas = {}, []
                for p in since_bar:
                    po = ops[p]
                    if po["dma"]:
                        dmas.append(p)
                    else:
                        last_per_eng[po["eng"]] = p
                pre = set(last_per_eng.values()) | set(dmas)
                for e in list(COMPUTE) + ["sp"]:
                    pending[e] = set(pre) | pending.get(e, set())
                since_bar = []
                last_w, readers, boxes = {}, {}, []
                continue
            d = set()
            rres = [self._res(x) for x in o["reads"]]
            wres = [self._res(x) for x in o["writes"]]
            for r in rres:
                if r[0] == "key":
                    if r[1] in last_w:
                        d.add(last_w[r[1]])
                    if r[1].startswith("bank"):
                        for r_ in readers.get(r[1], ()):
                            if ops[r_]["eng"] != o["eng"]:
                                d.add(r_)
                else:
                    _, p0, p1, lo, hi = r
                    for bx in boxes:
                        if bx[5] and bx[1] < p1 and p0 < bx[2] and bx[3] < hi and lo < bx[4]:
                            d.add(bx[0])
            for w in wres:
                if w[0] == "key":
                    if w[1] in last_w:
                        d.add(last_w[w[1]])
                    for r_ in readers.get(w[1], ()):
                        d.add(r_)
                else:
                    _, p0, p1, lo, hi = w
                    for bx in boxes:
                        if bx[1] < p1 and p0 < bx[2] and bx[3] < hi and lo < bx[4]:
                            d.add(bx[0])
            for p in d:
                if p == i:
                    continue
                po = ops[p]
                if not po["dma"] and not o["dma"] and po["eng"] == o["eng"] and o["eng"] == "pe":
                    continue
                deps[i].add(p)
            if pending.get(o["eng"]):
                for p in pending[o["eng"]]:
                    po = ops[p]
                    if (not po["dma"]) and (not o["dma"]) and po["eng"] == o["eng"]:
                        continue
                    deps[i].add(p)
                pending[o["eng"]] = set()
            ek = ("dma", i) if o["dma"] else o["eng"]
            for w in wres:
                if w[0] == "key":
                    last_w[w[1]] = i
                    readers[w[1]] = []
                else:
                    _, p0, p1, lo, hi = w
                    boxes = [bx for bx in boxes if not (p0 <= bx[1] and bx[2] <= p1 and lo <= bx[3] and bx[4] <= hi)]
                    boxes.append([i, p0, p1, lo, hi, True, ek])
            for r in rres:
                if r[0] == "key":
                    readers.setdefault(r[1], []).append(i)
                else:
                    _, p0, p1, lo, hi = r
                    boxes = [bx for bx in boxes if not (not bx[5] and bx[6] == ek and bx[1] == p0 and bx[2] == p1 and bx[3] == lo and bx[4] == hi)]
                    boxes.append([i, p0, p1, lo, hi, False, ek])
            since_bar.append(i)
        needed = set()
        for i in range(n):
            needed |= deps[i]
        sig_no, cnt = {}, {e: 0 for e in COMPUTE}
        dma_slot, dma_tot, ndma = {}, [0] * N_DMA_SEMS, 0
        nq = {"sp": 0, "pool": 0, "act": 0}
        NSP = N_DMA_SEMS - 8
        for i, o in enumerate(ops):
            if o["bar"]:
                continue
            if o["dma"]:
                if o["eng"] == "pool":
                    s = NSP + nq["pool"] % 8
                    nq["pool"] += 1
                else:
                    s = nq["sp"] % NSP
                    nq["sp"] += 1
                ndma += 1
                prev = dma_tot[s]
                dma_tot[s] += 16
                dma_slot[i] = (s, prev, dma_tot[s])
            elif i in needed:
                cnt[o["eng"]] += 1
                sig_no[i] = cnt[o["eng"]]
        st = self.stack
        csem = {e: st.enter_context(nc.semaphore(f"s_{e}")) for e in COMPUTE}
        dsem = [st.enter_context(nc.semaphore(f"s_dma{j}")) for j in range(N_DMA_SEMS)]
        used = sorted({o["eng"] for o in ops if not o["bar"]} | {final_wait_eng})
        with nc.Block() as block:
            for ename in used:
                def body(e, ename=ename):
                    seen = {c: 0 for c in csem}
                    seen_dma = [0] * N_DMA_SEMS
                    for i, o in enumerate(ops):
                        if o["bar"] or o["eng"] != ename:
                            continue
                        for p in sorted(deps[i]):
                            po = ops[p]
                            if po["dma"]:
                                s, _, tgt = dma_slot[p]
                                if seen_dma[s] < tgt:
                                    e.wait_ge(dsem[s], tgt)
                                    seen_dma[s] = tgt
                            else:
                                pe_ = po["eng"]
                                nn = sig_no[p]
                                if seen[pe_] < nn:
                                    e.wait_ge(csem[pe_], nn)
                                    seen[pe_] = nn
                        if o["dma"]:
                            s, prev, tgt = dma_slot[i]
                            if prev > 0 and seen_dma[s] < prev:
                                e.wait_ge(dsem[s], prev)
                                seen_dma[s] = prev
                            o["fn"](e).then_inc(dsem[s], 16)
                        else:
                            ins = o["fn"](e)
                            if i in sig_no:
                                ins.then_inc(csem[ename], 1)
                    if ename == final_wait_eng:
                        for s in range(N_DMA_SEMS):
                            if dma_tot[s] > seen_dma[s]:
                                e.wait_ge(dsem[s], dma_tot[s])
                getattr(block, ENG[ename])(body)
        return dict(n_ops=n, sig=cnt, ndma=ndma, peak_kb=self.peak * 4 / 1024)


WA_SRC = [0, 512, 1024, 1536, 2056, 2568, 3080, 3592, 4104]
BIN_GROUPS = [(0, 8), (1024, 8), (2056, 4), (2568, 8), (3592, 8)]
WF_GROUPS = [(0, 8), (8, 8), (16, 6)]


def dram_in(nc, name, shape, dtype=F32):
    return nc.dram_tensor(name, list(shape), dtype, kind="ExternalInput").ap()


class Builder:
    def __init__(self, state_ns, full_ns, skip_sub, dbg=()):
        self.state_ns = list(state_ns)
        self.full_ns = list(full_ns)
        self.skip_sub = skip_sub
        self.dbg = set(dbg)
        self.ntok = 128 * (sum(state_ns) + sum(full_ns))
        self.nout = 128 * (sum(full_ns) - skip_sub)
        self.nc = bass.Bass("TRN2", target_bir_lowering=False)
        self.dbg_out = {}

    def dbg_dump(self, name, ap, keys, dtype=F32):
        if name not in self.dbg:
            return
        shape = list(ap.shape)
        o = self.nc.dram_tensor("dbg_" + name, shape, dtype, kind="ExternalOutput").ap()
        self.P.dma(o, ap, reads=keys)

    def build(self):
        nc = self.nc
        I = {}
        def inp(name, shape):
            I[name] = dram_in(nc, name, shape)
        inp("xin", [self.ntok, D]); inp("cvec", [D]); inp("flagv", [128, 1])
        inp("w_ada", [D, 6 * D]); inp("b_ada", [6 * D]); inp("w_in", [D, INW]); inp("b_in", [INW])
        inp("w_mlstm_conv", [4, D]); inp("b_mlstm_conv", [D]); inp("w_mlstm_q", [NH, DV, DK]); inp("w_mlstm_k", [NH, DV, DK])
        inp("mlstm_norm_gain", [D]); inp("w_mlstm_down", [D, D])
        inp("s5_lam_re", [32, 64]); inp("s5_lam_im", [32, 64]); inp("s5_log_dt", [32])
        inp("s5_b_re", [32, 64, 16]); inp("s5_b_im", [32, 64, 16]); inp("s5_c_re", [32, 16, 64]); inp("s5_c_im", [32, 16, 64])
        inp("s5_d", [S5W]); inp("w_s5_glu", [S5W, 2 * D]); inp("w_mix_out", [D, D])
        inp("ln1_gain", [D]); inp("ln1_bias", [D]); inp("w_ffn_up", [D, 2 * FH]); inp("w_ffn_conv", [3, FH]); inp("b_ffn_conv", [FH])
        inp("w_ffn_down", [FH, D]); inp("ln2_gain", [D]); inp("ln2_bias", [D])
        inp("c_ident", [128, 128]); inp("c_tri", [128, 128]); inp("c_bmask", [128, 128])
        self.I = I
        self.yout = nc.dram_tensor("yout", [self.nout, D], F32, kind="ExternalOutput").ap()
        S = {}
        def scr(name, shape, dtype=BF16):
            S[name] = nc.dram_tensor("scr_" + name, list(shape), dtype, kind="Internal").ap()
        scr("wA", [9, 128, 8, 512]); scr("wD", [2, 128, 8, 512]); scr("wG", [4, 128, 4, 512]); scr("wM", [2, 128, 8, 512])
        scr("wU", [11, 128, 8, 512]); scr("wF", [6, 128, 8, 512]); scr("fdiag", [22, 128, 3, 128])
        scr("s5K", [128, 4, 8, 128]); scr("s5X", [2, 128, 2, 8, 2, 128]); scr("s5Y", [2, 128, 8, 8, 2, 32])
        self.S = S
        st = ExitStack()
        with st:
            self.P = P = Prog(nc, st)
            self.persistent()
            self.prologue()
            self.main()
            import os as _os
            if _os.environ.get("KMAX"):
                P.ops = P.ops[:int(_os.environ["KMAX"])]
            info = P.emit()
        self.info = info
        return nc

    def persistent(self):
        P, I = self.P, self.I
        A = P.alloc
        T = self.T = {}
        T["ident_f"] = A([128, 128]); T["tri_f"] = A([128, 128]); T["bmask_f"] = A([128, 128]); T["ones_f"] = A([128, 128])
        T["ident_b"] = A([128, 128], BF16); T["mask_b"] = A([128, 128], BF16)
        P.dma(T["ident_f"], I["c_ident"], writes=["ident_f"])
        P.dma(T["tri_f"], I["c_tri"], writes=["tri_f"])
        P.dma(T["bmask_f"], I["c_bmask"], writes=["bmask_f"])
        P.op("pool", lambda e: e.memset(T["ones_f"], 1.0), writes=["ones_f"])
        P.op("dve", lambda e: e.tensor_copy(T["ident_b"], T["ident_f"]), reads=["ident_f"], writes=["ident_b"])
        P.op("dve", lambda e: e.tensor_copy(T["mask_b"], T["tri_f"]), reads=["tri_f"], writes=["mask_b"])
        T["mhalf"] = A([128, 4]); T["flag"] = A([128, 1]); T["xmh"] = A([128, 8, 3], BF16)
        P.op("pool", lambda e: e.memset(T["mhalf"], -0.5), writes=["mhalf"])
        P.dma(T["flag"], I["flagv"], writes=["flag"])
        T["modT"] = A([128, 48])
        T["binT"] = A([128, 36]); T["bgate"] = A([128, 8])
        T["wgt"] = A([128, 8, 8], BF16); T["wq"] = A([128, 4, 2, 128], BF16); T["wk"] = A([128, 4, 2, 128], BF16)
        T["cdiag"] = A([128, 8, 4, 128], BF16); T["cb"] = A([128, 8]); T["gainT"] = A([128, 8])
        T["fcb"] = A([128, 22]); T["dT"] = A([128, 4])
        T["C32"] = A([128, 4, 260]); T["Cb"] = A([128, 4, 260], BF16)
        T["rotc"] = A([128, 16, 64]); T["rots"] = A([128, 16, 64]); T["rho8"] = A([128, 16]); T["u8"] = A([128, 2, 16])
        T["xcar"] = A([128, 2, 16])
        self.d0 = {}
        for nb in sorted({ns * 16 for ns in self.state_ns + self.full_ns}):
            self.d0[nb] = A([128, 16, nb])
        T["fhalo"] = A([128, 22, 2], BF16)

    def ap_cols(self, vec, off, ntile):
        return bass.AP(vec.tensor, off, [[1, 128], [128, ntile]])

    def ap_bcast(self, vec, off, n):
        return bass.AP(vec.tensor, off, [[0, 128], [1, n]])


def bc(ap, n):
    return bass.AP(ap.tensor, ap.offset, [list(x) for x in ap.ap] + [[0, n]])


def bc_mid(ap, n):
    a = [list(x) for x in ap.ap]
    return bass.AP(ap.tensor, ap.offset, [a[0], [0, n]] + a[1:])


SLOW = dict(allow_slow_non_contiguous=True)


def _prologue(self):
    P, I, T, S = self.P, self.I, self.T, self.S
    A = P.alloc
    mark = P.top
    TT = lambda eng, out, a, b, op, rd, wr: P.op(eng, lambda e: e.tensor_tensor(out, a, b, op), reads=rd, writes=wr)
    w_in_v = I["w_in"].rearrange("(kt p) c -> p kt c", p=128)
    for c in range(9):
        P.dma(S["wA"][c], w_in_v[:, :, WA_SRC[c]:WA_SRC[c] + 512], writes=[f"wA{c}"], q="pool")
    P.dma(T["wgt"], w_in_v[:, :, 2048:2056], writes=["wgt"], q="pool")
    P.dma(T["wq"], I["w_mlstm_q"].rearrange("h (kt p) d -> p h kt d", p=128), writes=["wq"], q="pool")
    P.dma(T["wk"], I["w_mlstm_k"].rearrange("h (kt p) d -> p h kt d", p=128), writes=["wk"], q="pool")
    wd_v = I["w_mlstm_down"].rearrange("(kt p) c -> p kt c", p=128)
    for c in range(2):
        P.dma(S["wD"][c], wd_v[:, :, 512 * c:512 * c + 512], writes=[f"wD{c}"], q="pool")
    wg_v = I["w_s5_glu"].rearrange("(kt p) c -> p kt c", p=128)
    for c in range(4):
        P.dma(S["wG"][c], wg_v[:, :, 512 * c:512 * c + 512], writes=[f"wG{c}"], q="pool")
    wu_v = I["w_ffn_up"].rearrange("(kt p) c -> p kt c", p=128)
    for c in range(11):
        P.dma(S["wU"][c], wu_v[:, :, 512 * c:512 * c + 512], writes=[f"wU{c}"], q="pool")
    colstg = [A([128, 128]) for _ in range(2)]
    mark2 = P.top
    cact = A([128, 8]); badaT = A([128, 48]); cw = A([128, 8, 4]); fcw = A([128, 22, 3])
    self.cl_i = 0
    for i_ in range(2):
        P.op("pool", lambda e, i_=i_: e.memset(colstg[i_], 0.0), writes=[f"colstg{i_}"])

    def load_cols(dst, vec, off, nt, dkey, dst_is_3d=None):
        sg = colstg[self.cl_i % 2]; sk = f"colstg{self.cl_i % 2}"; self.cl_i += 1
        P.dma(sg[0:nt, :], bass.AP(vec.tensor, off, [[128, nt], [1, 128]]), reads=[sk], writes=[sk])
        bk_, bkk = P.bank()
        P.op("pe", lambda e, sg=sg, bk_=bk_: e.transpose(bk_[:, 0:128], sg, T["ident_f"]), reads=[sk, "ident_f"], writes=[bkk])
        P.op("dve", lambda e, dst=dst, bk_=bk_, nt=nt: e.tensor_copy(dst, bk_[:, 0:nt] if dst_is_3d is None else dst_is_3d(bk_)), reads=[bkk], writes=[dkey])

    load_cols(cact, I["cvec"], 0, 8, "cact")
    load_cols(badaT, I["b_ada"], 0, 48, "badaT")
    c0 = 0
    for gi, (off, nt) in enumerate(BIN_GROUPS):
        load_cols(T["binT"][:, c0:c0 + nt], I["b_in"], off, nt, f"binT{gi}")
        c0 += nt
    P.dma(T["bgate"], self.ap_bcast(I["b_in"], 2048, 8), writes=["bgate"])
    load_cols(cw.rearrange("p c j -> p j c"), I["w_mlstm_conv"], 0, 32, "cw", dst_is_3d=lambda b_: b_[:, 0:32].rearrange("p (j c) -> p j c", c=8))
    load_cols(T["cb"], I["b_mlstm_conv"], 0, 8, "cb")
    load_cols(T["gainT"], I["mlstm_norm_gain"], 0, 8, "gainT")
    load_cols(fcw.rearrange("p t j -> p j t"), I["w_ffn_conv"], 0, 66, "fcw", dst_is_3d=lambda b_: b_[:, 0:66].rearrange("p (j t) -> p j t", t=22))
    load_cols(T["fcb"], I["b_ffn_conv"], 0, 22, "fcb")
    load_cols(T["dT"], I["s5_d"], 0, 4, "dT")
    self.load_cols = load_cols
    n = 0
    for ct in range(8):
        for j in range(4):
            eng = "dve" if n % 2 == 0 else "pool"; n += 1
            P.op(eng, lambda e, ct=ct, j=j: e.tensor_scalar(T["cdiag"][:, ct, j, :], T["ident_f"], cw[:, ct, j:j + 1], None, ALU.mult),
                 reads=["cw", "ident_f"], writes=[f"cdiag{ct}_{j}"])
    fd = A([128, 22, 3, 128], BF16)
    for t in range(22):
        for j in range(3):
            eng = "dve" if n % 2 == 0 else "pool"; n += 1
            P.op(eng, lambda e, t=t, j=j: e.tensor_scalar(fd[:, t, j, :], T["ident_f"], fcw[:, t, j:j + 1], None, ALU.mult),
                 reads=["fcw", "ident_f"], writes=[f"fd{t}_{j}"])
    P.dma(S["fdiag"].rearrange("t p j d -> p t j d"), fd, reads=[f"fd{t}_{j}" for t in range(22) for j in range(3)], writes=["fdiag"])
    P.op("act", lambda e: e.activation(cact, cact, AF.Silu), reads=["cact"], writes=["cact"])
    cact2 = A([128, 8, 2])
    P.op("dve", lambda e: e.tensor_copy(cact2, bc(cact, 2)), reads=["cact"], writes=["cact2"])
    cactB = A([128, 8, 128])
    P.op("dve", lambda e: e.tensor_copy(cactB, bc(cact, 128)), reads=["cact"], writes=["cactB"])
    g1b = A([128, D]); g2b = A([128, D])
    P.dma(g1b, self.ap_bcast(I["b_ada"], 2 * D, D), writes=["g1b0", "g1b1"])
    P.dma(g2b, self.ap_bcast(I["b_ada"], 5 * D, D), writes=["g2b0", "g2b1"])
    stg = [A([128, 8, 512]) for _ in range(2)]
    wada_v = I["w_ada"].rearrange("(kt p) c -> p kt c", p=128)
    mb, mbk = P.bank()
    for c in range(12):
        sg, sk = stg[c % 2], f"stg{c % 2}"
        P.dma(sg, wada_v[:, :, 512 * c:512 * c + 512], writes=[sk])
        kind, half = c // 2, c % 2
        if kind in (2, 5):
            gb_, gk = (g1b, f"g1b{half}") if kind == 2 else (g2b, f"g2b{half}")
            bk_, bkk = P.bank()
            def f(e, sg=sg, bk_=bk_):
                for kt in range(8):
                    r = e.matmul(bk_[:, 0:512], cactB[:, kt, :], sg[:, kt, :], start=(kt == 0), stop=(kt == 7))
                return r
            P.op("pe", f, reads=[sk, "cactB"], writes=[bkk])
            dst = gb_[:, 512 * half:512 * half + 512]
            P.op("dve", lambda e, dst=dst, bk_=bk_: e.scalar_tensor_tensor(dst, bk_[:, 0:512], 1.0, dst, ALU.add, ALU.add), reads=[bkk, gk], writes=[gk])
        else:
            def f(e, sg=sg, kind=kind, half=half):
                for j in range(4):
                    ct = kind * 8 + half * 4 + j
                    for kt in range(8):
                        r = e.matmul(mb[:, 2 * ct:2 * ct + 2], sg[:, kt, 128 * j:128 * j + 128], cact2[:, kt, :], start=(kt == 0), stop=(kt == 7))
                return r
            P.op("pe", f, reads=[sk, "cact2"], writes=[mbk])
    P.op("pool", lambda e: e.memset(T["modT"], 0.0), writes=["modT0", "modT24"])
    for (a, b) in ((0, 16), (24, 40)):
        P.op("dve", lambda e, a=a, b=b: e.tensor_tensor(T["modT"][:, a:b], mb[:, 2 * a:2 * b:2], badaT[:, a:b], ALU.add), reads=[mbk, "badaT"], writes=[f"modT{a}"])
    for a, kk in ((8, "modT0"), (32, "modT24")):
        P.op("dve", lambda e, a=a: e.tensor_scalar(T["modT"][:, a:a + 8], T["modT"][:, a:a + 8], 1.0, None, ALU.add), reads=[kk], writes=[kk])
    self.dbg_dump("modT", T["modT"], ["modT0", "modT24"])
    self.dbg_dump("g1b", g1b, ["g1b0", "g1b1"])
    ob = [A([128, 8, 512], BF16) for _ in range(2)]
    wm_v = I["w_mix_out"].rearrange("(kt p) c -> p kt c", p=128)
    wf_v = I["w_ffn_down"].rearrange("(kt p) c -> p kt c", p=128)
    jobs = [(wm_v, 0, 8, hf, g1b, f"g1b{hf}", S["wM"][hf], f"wM{hf}") for hf in range(2)]
    for gi, (k0, nk) in enumerate(WF_GROUPS):
        for hf in range(2):
            jobs.append((wf_v, k0, nk, hf, g2b, f"g2b{hf}", S["wF"][gi * 2 + hf], f"wF{gi * 2 + hf}"))
    for ji, (src, k0, nk, hf, gt, gk, dst, dk) in enumerate(jobs):
        sg, sk = stg[ji % 2], f"stg{ji % 2}"
        o_, ok_ = ob[ji % 2], f"ob{ji % 2}"
        P.dma(sg[:, 0:nk, :], src[:, k0:k0 + nk, 512 * hf:512 * hf + 512], writes=[sk])
        eng = "dve" if ji % 2 == 0 else "pool"
        P.op(eng, lambda e, o_=o_, sg=sg, nk=nk, gt=gt, hf=hf: e.tensor_tensor(o_[:, 0:nk, :], sg[:, 0:nk, :], bc_mid(gt[:, 512 * hf:512 * hf + 512], nk), ALU.mult),
             reads=[sk, gk], writes=[ok_])
        P.dma(dst[:, 0:nk, :], o_[:, 0:nk, :], reads=[ok_], writes=[dk])
    P.barrier()
    P.top = mark2
    self.s5_prologue()
    P.op("pool", lambda e: e.memset(T["C32"], 0.0), writes=["C32"])
    P.op("pool", lambda e: e.memset(T["Cb"], 0.0), writes=["Cb"])
    P.op("pool", lambda e: e.memset(T["xcar"], 0.0), writes=["xcar"])
    P.op("pool", lambda e: e.memset(T["fhalo"], 0.0), writes=["fhalo"])
    P.op("pool", lambda e: e.memset(T["xmh"], 0.0), writes=["xmh"])
    P.barrier()
    P.top = mark
    print("ops after prologue", len(P.ops))


Builder.prologue = _prologue


def _s5_prologue(self):
    P, I, T, S = self.P, self.I, self.T, self.S
    A = P.alloc
    V = "dve"

    def tk(t):
        return "tmp_" + t.tensor.name + str(t.offset)

    def tt(out, a, b, op, rd, wr, eng=V):
        P.op(eng, lambda e: e.tensor_tensor(out, a, b, op), reads=rd, writes=wr)

    def cmul(outr, outi, ar, ai, br, bi, t1, t2, rd, wr):
        k1, k2 = "tmp_" + t1.tensor.name + str(t1.offset), "tmp_" + t2.tensor.name + str(t2.offset)
        tt(t1, ar, br, ALU.mult, rd, [k1])
        tt(t2, ai, bi, ALU.mult, rd, [k2])
        tt(outr, t1, t2, ALU.subtract, [k1, k2], [wr + "r"])
        tt(t1, ar, bi, ALU.mult, rd + [wr + "r"], [k1])
        tt(t2, ai, br, ALU.mult, rd + [wr + "r"], [k2])
        tt(outi, t1, t2, ALU.add, [k1, k2], [wr + "i"])

    lamr = A([128, 16]); lami = A([128, 16]); ldt = A([128, 16]); dt = A([128, 16]); phi = A([128, 16]); aa = A([128, 16])
    cc = A([128, 16]); ss = A([128, 16]); t1 = A([128, 16]); t2 = A([128, 16])
    self.load_cols(lamr, I["s5_lam_re"], 0, 16, "lamr")
    self.load_cols(lami, I["s5_lam_im"], 0, 16, "lami")
    ldtb = A([128, 32])
    P.dma(ldtb, self.ap_bcast(I["s5_log_dt"], 0, 32), writes=["ldtb"])
    for h in range(2):
        P.op(V, lambda e, h=h: e.tensor_copy(ldt[64 * h:64 * h + 64, :], ldtb[64 * h:64 * h + 64, h:32:2]), reads=["ldtb"], writes=[f"ldt{h}"])
    def taylor_exp(out, x, deg, xk, ok, tmp):
        P.op(V, lambda e: e.tensor_scalar(out, x, 1.0 / deg, 1.0, ALU.mult, ALU.add), reads=xk, writes=[ok])
        for n_ in range(deg - 1, 0, -1):
            tt(tmp, x, out, ALU.mult, xk + [ok], [tk(tmp)])
            P.op(V, lambda e, n_=n_: e.tensor_scalar(out, tmp, 1.0 / n_, 1.0, ALU.mult, ALU.add), reads=[tk(tmp)], writes=[ok])

    P.op(V, lambda e: e.tensor_scalar(ldt, ldt, 1.0 / 16, None, ALU.mult), reads=["ldt0", "ldt1"], writes=["ldt0", "ldt1"])
    taylor_exp(dt, ldt, 10, ["ldt0", "ldt1"], "dt", t1)
    for _ in range(4):
        tt(t2, dt, dt, ALU.mult, ["dt"], [tk(t2)])
        P.op(V, lambda e: e.tensor_copy(dt, t2), reads=[tk(t2)], writes=["dt"])
    tt(phi, lami, dt, ALU.mult, ["lami", "dt"], ["phi"])
    tt(aa, lamr, dt, ALU.mult, ["lamr", "dt"], ["aa"])
    P.op("act", lambda e: e.activation(ss, phi, AF.Sin, scale=1.0 / 32), reads=["phi"], writes=["ss"])
    hp = A([128, 1])
    P.op("pool", lambda e: e.memset(hp, math.pi / 2), writes=["hp"])
    P.op("act", lambda e: e.activation(cc, phi, AF.Sin, scale=1.0 / 32, bias=hp), reads=["phi", "hp"], writes=["cc"])
    for it in range(5):
        tt(t1, cc, cc, ALU.mult, ["cc"], [tk(t1)])
        tt(t2, ss, ss, ALU.mult, ["ss"], [tk(t2)])
        P.op(V, lambda e: e.scalar_tensor_tensor(ss, cc, 2.0, ss, ALU.mult, ALU.mult), reads=["cc", "ss", tk(t2)], writes=["ss"])
        tt(cc, t1, t2, ALU.subtract, [tk(t1), tk(t2), "ss"], ["cc"])
    UPr = A([128, 9, 16]); UPi = A([128, 9, 16]); MG = A([128, 9, 16]); PWr = A([128, 9, 16]); PWi = A([128, 9, 16])
    P.op("pool", lambda e: e.memset(UPr[:, 0, :], 1.0), writes=["UP0r"])
    P.op("pool", lambda e: e.memset(UPi[:, 0, :], 0.0), writes=["UP0i"])
    P.op(V, lambda e: e.tensor_copy(UPr[:, 1, :], cc), reads=["cc"], writes=["UP1r"])
    P.op(V, lambda e: e.tensor_copy(UPi[:, 1, :], ss), reads=["ss"], writes=["UP1i"])
    for k in range(2, 9):
        cmul(UPr[:, k, :], UPi[:, k, :], UPr[:, k - 1, :], UPi[:, k - 1, :], UPr[:, 1, :], UPi[:, 1, :], t1, t2,
             [f"UP{k - 1}r", f"UP{k - 1}i", "UP1r", "UP1i"], f"UP{k}")
    P.op("pool", lambda e: e.memset(MG[:, 0, :], 1.0), writes=["MG0"])
    taylor_exp(MG[:, 1, :], aa, 7, ["aa"], "MG1", t1)
    for k in range(2, 9):
        tt(MG[:, k, :], MG[:, k - 1, :], MG[:, 1, :], ALU.mult, [f"MG{k - 1}", "MG1"], [f"MG{k}"])
    allup = [f"UP{k}{c}" for k in range(9) for c in "ri"] + [f"MG{k}" for k in range(9)]
    tt(PWr, MG, UPr, ALU.mult, allup, ["PWr"])
    tt(PWi, MG, UPi, ALU.mult, allup, ["PWi"])
    PW = ["PWr", "PWi"]
    den = A([128, 16]); am1 = A([128, 16]); zr = A([128, 16]); zi = A([128, 16])
    tt(t1, lamr, lamr, ALU.mult, ["lamr"] + PW, [tk(t1)])
    tt(t2, lami, lami, ALU.mult, ["lami"] + PW, [tk(t2)])
    tt(den, t1, t2, ALU.add, [tk(t1), tk(t2)], ["den"])
    P.op(V, lambda e: e.reciprocal(den, den), reads=["den"], writes=["den"])
    P.op(V, lambda e: e.tensor_scalar(am1, PWr[:, 1, :], -1.0, None, ALU.add), reads=PW, writes=["am1"])
    tt(t1, am1, lamr, ALU.mult, ["am1", "lamr", "den"], [tk(t1)])
    tt(t2, PWi[:, 1, :], lami, ALU.mult, PW + ["lami", "den"], [tk(t2)])
    tt(zr, t1, t2, ALU.add, [tk(t1), tk(t2)], ["zr"])
    tt(zr, zr, den, ALU.mult, ["zr", "den"], ["zr"])
    tt(t1, PWi[:, 1, :], lamr, ALU.mult, PW + ["lamr", "zr"], [tk(t1)])
    tt(t2, am1, lami, ALU.mult, ["am1", "lami", "zr"], [tk(t2)])
    tt(zi, t1, t2, ALU.subtract, [tk(t1), tk(t2)], ["zi"])
    tt(zi, zi, den, ALU.mult, ["zi", "den"], ["zi"])
    Bre = A([128, 16, 16]); Bim = A([128, 16, 16]); Cre = A([128, 16, 16]); Cim = A([128, 16, 16])
    b_ap = lambda v: bass.AP(v.tensor, 0, [[16, 128], [2048, 16], [1, 16]])
    P.dma(Bre, b_ap(I["s5_b_re"]), writes=["Bre"])
    P.dma(Bim, b_ap(I["s5_b_im"]), writes=["Bim"])
    Cl = A([128, 16, 128])
    P.op("pool", lambda e: e.memset(Cl, 0.0), writes=["Cl"])
    for nm, dstC in (("s5_c_re", Cre), ("s5_c_im", Cim)):
        for q in range(16):
            P.dma(Cl[0:16, q, :].rearrange("c (g p) -> c g p", p=64), bass.AP(I[nm].tensor, 2048 * q, [[64, 16], [1024, 2], [1, 64]]), reads=["Cl"], writes=[f"Cl{q}"])
        for b4 in range(4):
            bk_, bkk = P.bank()
            def f(e, bk_=bk_, b4=b4):
                for qq in range(4):
                    r = e.transpose(bk_[:, 128 * qq:128 * qq + 128], Cl[:, 4 * b4 + qq, :], T["ident_f"])
                return r
            P.op("pe", f, reads=[f"Cl{4 * b4 + qq}" for qq in range(4)] + ["ident_f"], writes=[bkk])
            P.op(V, lambda e, dstC=dstC, bk_=bk_, b4=b4: e.tensor_copy(dstC[:, 4 * b4:4 * b4 + 4, :], bk_[:, 0:512].rearrange("p (q x) -> p q x", x=128)[:, :, 0:16]),
                 reads=[bkk], writes=["C" + nm[-2:] + str(b4)])
    CK = [f"C{x}{b}" for x in ("re", "im") for b in range(4)]
    BBr = A([128, 16, 16]); BBi = A([128, 16, 16]); W1 = A([128, 16, 16]); W2 = A([128, 16, 16])
    cmul(BBr, BBi, bc(zr, 16), bc(zi, 16), Bre, Bim, W1, W2, ["zr", "zi", "Bre", "Bim"], "BB")
    PBr = A([128, 8, 16, 16]); PBi = A([128, 8, 16, 16])
    for k in range(8):
        cmul(PBr[:, k], PBi[:, k], bc(PWr[:, k, :], 16), bc(PWi[:, k, :], 16), BBr, BBi, W1, W2, PW + ["BBr", "BBi"], f"PB{k}")
    CBD = A([128, 2, 16, 32])
    P.op("pool", lambda e: e.memset(CBD, 0.0), writes=["CBD"])
    for h in range(2):
        sl = slice(64 * h, 64 * h + 64)
        P.op(V, lambda e, sl=sl, h=h: e.tensor_copy(CBD[sl, 0, :, 16 * h:16 * h + 16], Cre[sl]), reads=[f"Cre{b}" for b in range(4)] + ["CBD"], writes=["CBD"])
        P.op(V, lambda e, sl=sl, h=h: e.tensor_scalar(CBD[sl, 1, :, 16 * h:16 * h + 16], Cim[sl], -1.0, None, ALU.mult), reads=[f"Cim{b}" for b in range(4)] + ["CBD"], writes=["CBD"])
    Mb = [A([128, 4, 2, 16]) for _ in range(4)]
    for b in range(4):
        P.op("pool", lambda e, b=b: e.memset(Mb[b], 0.0), writes=[f"Mb{b}"])
    XwS = A([128, 4, 8, 2, 128], BF16)
    KcS = A([128, 4, 8, 128], BF16)
    dtmp = A([128, 128]); ktmp = A([128, 128])
    nb_ = 0
    for ct in range(4):
        for k in range(8):
            tau = 7 - k
            kb, kbk = P.bank()
            mfs = []
            for ri in range(2):
                m, mk = Mb[nb_ % 4], f"Mb{nb_ % 4}"; nb_ += 1
                src = (PBr if ri == 0 else PBi)
                for h in range(2):
                    sl = slice(64 * h, 64 * h + 64)
                    P.op(V if h == 0 else "pool", lambda e, m=m, sl=sl, h=h, src=src, k=k, ct=ct: e.tensor_copy(m[sl, :, h, :], src[sl, k, 4 * ct:4 * ct + 4, :]),
                         reads=[f"PB{k}r", f"PB{k}i", mk], writes=[mk])
                mf = m.rearrange("p a b c -> p (a b c)")
                mfs.append((mf, mk))
                P.op("pe", lambda e, mf=mf, kb=kb, ri=ri: e.transpose(kb[:, 128 + 128 * ri:256 + 128 * ri], mf, T["ident_f"]), reads=[mk, "ident_f"], writes=[kbk])
            for ri in range(2):
                mf, mk = mfs[ri]
                P.op("pe", lambda e, mf=mf, kb=kb, ri=ri, ct=ct: e.matmul(kb[:, 0:128], mf, CBD[:, ri, 4 * ct:4 * ct + 4, :].rearrange("p a b -> p (a b)"), start=(ri == 0), stop=(ri == 1)),
                     reads=[mk, "CBD"], writes=[kbk])
            P.op("act", lambda e, kb=kb, ct=ct, tau=tau: e.copy(XwS[:, ct, tau].rearrange("p a b -> p (a b)"), kb[:, 128:384]), reads=[kbk], writes=["XwS"])
            if k == 0:
                P.op(V, lambda e, ct=ct: e.tensor_scalar(dtmp, T["ident_f"], T["dT"][:, ct:ct + 1], None, ALU.mult), reads=["dT", "ident_f", "KcS"], writes=["dtmp"])
                P.op(V, lambda e, kb=kb: e.tensor_tensor(ktmp, kb[:, 0:128], T["bmask_f"], ALU.mult), reads=[kbk, "bmask_f", "KcS"], writes=["ktmp"])
                P.op(V, lambda e, ct=ct, k=k: e.tensor_tensor(KcS[:, ct, k, :], ktmp, dtmp, ALU.add), reads=["ktmp", "dtmp"], writes=["KcS"])
            else:
                P.op(V, lambda e, kb=kb, ct=ct, k=k: e.tensor_tensor(KcS[:, ct, k, :], kb[:, 0:128], T["bmask_f"], ALU.mult), reads=[kbk, "bmask_f"], writes=["KcS"])
    P.dma(S["s5K"], KcS, reads=["KcS"], writes=["s5K"])
    for c in range(2):
        P.dma(S["s5X"][c], XwS[:, 2 * c:2 * c + 2], reads=["XwS"], writes=[f"s5X{c}"])
    self.dbg_dump("KcS", KcS, ["KcS"], BF16)
    self.dbg_dump("XwS", XwS, ["XwS"], BF16)
    YwS = A([128, 16, 8, 2, 32], BF16)
    P.op("pool", lambda e: e.memset(YwS, 0.0), writes=["YwS"])
    YR = A([128, 16, 16]); YI = A([128, 16, 16])
    for tau in range(8):
        pr, pi = bc(PWr[:, tau + 1, :], 16), bc(PWi[:, tau + 1, :], 16)
        cmul(YR, YI, Cre, Cim, pr, pi, W1, W2, CK + PW + ["YwS"], "YY")
        for h in range(2):
            sl = slice(64 * h, 64 * h + 64)
            P.op(V, lambda e, sl=sl, h=h, tau=tau: e.tensor_copy(YwS[sl, :, tau, 0, 16 * h:16 * h + 16], YR[sl]), reads=["YYr", "YwS"], writes=["YwS"])
            P.op(V, lambda e, sl=sl, h=h, tau=tau: e.tensor_scalar(YwS[sl, :, tau, 1, 16 * h:16 * h + 16], YI[sl], -1.0, None, ALU.mult), reads=["YYi", "YwS"], writes=["YwS"])
    for c in range(2):
        P.dma(S["s5Y"][c], YwS[:, 8 * c:8 * c + 8], reads=["YwS"], writes=[f"s5Y{c}"])
    self.dbg_dump("YwS", YwS, ["YwS"], BF16)
    rc, rs = T["rotc"], T["rots"]
    P.op("pool", lambda e: e.memset(rc[:, :, 0], 1.0), writes=["rot"])
    P.op("pool", lambda e: e.memset(rs[:, :, 0], 0.0), reads=["rot"], writes=["rot"])
    P.op(V, lambda e: e.tensor_copy(T["u8"][:, 0, :], UPr[:, 8, :]), reads=["UP8r"], writes=["u8"])
    P.op(V, lambda e: e.tensor_copy(T["u8"][:, 1, :], UPi[:, 8, :]), reads=["UP8i", "u8"], writes=["u8"])
    r1r = A([128, 16]); r1i = A([128, 16]); wr_ = A([128, 16]); wi_ = A([128, 16])
    P.op(V, lambda e: e.tensor_copy(r1r, UPr[:, 8, :]), reads=["UP8r"], writes=["r1r"])
    P.op(V, lambda e: e.tensor_scalar(r1i, UPi[:, 8, :], -1.0, None, ALU.mult), reads=["UP8i"], writes=["r1i"])
    ln = 1
    R1 = A([128, 16, 32]); R2 = A([128, 16, 32])
    while ln < 64:
        cmul(wr_, wi_, rc[:, :, ln - 1], rs[:, :, ln - 1], r1r, r1i, t1, t2, ["rot", "r1r", "r1i"], "W")
        a_r, a_i = rc[:, :, 0:ln], rs[:, :, 0:ln]
        w_r, w_i = bc(wr_, ln), bc(wi_, ln)
        x1, x2 = R1[:, :, 0:ln], R2[:, :, 0:ln]
        tt(x1, a_r, w_r, ALU.mult, ["rot", "Wr", "Wi"], ["x1"])
        tt(x2, a_i, w_i, ALU.mult, ["rot", "Wr", "Wi"], ["x2"])
        tt(rc[:, :, ln:2 * ln], x1, x2, ALU.subtract, ["x1", "x2"], ["rot"])
        tt(x1, a_r, w_i, ALU.mult, ["rot", "Wr", "Wi"], ["x1"])
        tt(x2, a_i, w_r, ALU.mult, ["rot", "Wr", "Wi"], ["x2"])
        tt(rs[:, :, ln:2 * ln], x1, x2, ALU.add, ["x1", "x2", "rot"], ["rot"])
        ln *= 2
    P.op(V, lambda e: e.tensor_copy(T["rho8"], MG[:, 8, :]), reads=["MG8"], writes=["rho8"])
    for nb, d0 in self.d0.items():
        P.op(V, lambda e, d0=d0, nb=nb: e.tensor_copy(d0, bc(T["rho8"], nb)), reads=["rho8"], writes=[f"d0_{nb}"])
        P.op(V, lambda e, d0=d0: e.memset(d0[:, :, 0], 0.0), reads=[f"d0_{nb}"], writes=[f"d0_{nb}"])
    self.dbg_dump("rotc", rc, ["rot"])
    self.dbg_dump("rots", rs, ["rot"])


Builder.s5_prologue = _s5_prologue


def _main(self):
    P, I, T, S = self.P, self.I, self.T, self.S
    A = P.alloc
    NSM = max(self.state_ns + self.full_ns)
    NM = 128 * NSM
    NBM = 16 * NSM
    B = self.B = {}
    B["XR"] = [A([128, D]) for _ in range(NSM + 1)]
    B["xn"] = [A([128, D], BF16) for _ in range(2)]
    B["hT"] = A([128, 8, NM], BF16)
    B["hm"] = A([128, NSM, D], BF16)
    B["omS"] = A([128, 8, NM], BF16)
    B["prodT"] = A([128, 8, NM], BF16)
    B["yT"] = A([128, 8, NM], BF16)
    B["gtmp"] = A([128, 3, NM], BF16)
    B["uT"] = A([128, 4, NM], BF16)
    B["ysT"] = A([128, 4, NM], BF16)
    B["gsm"] = A([128, NSM, 48])
    B["lnt"] = A([128, 2, D])
    B["ring"] = [A([128, 8, 512], BF16) for _ in range(3)]
    B["smr"] = A([128, 512])
    self.sm_i = 0
    r0 = P.top
    B["xmT"] = A([128, 8, 3 + NM], BF16)
    B["xcT"] = A([128, 8, NM], BF16)
    B["QT"] = A([128, 4, NM], BF16)
    B["KT"] = A([128, 4, NM], BF16)
    B["Vx"] = A([128, NSM, 4, 258], BF16)
    B["KW"] = A([128, 2, 4, 128], BF16)
    B["SW"] = A([128, 2, 4, 128], BF16)
    r1 = P.top
    P.top = r0
    B["Xin"] = A([128, 2, 16, NBM])
    B["Zr"] = A([128, 16 * NBM]); B["Zi"] = A([128, 16 * NBM])
    B["Ta"] = A([128, 16 * NBM]); B["Tb"] = A([128, 16 * NBM])
    B["XS"] = A([128, 2, 16, NBM])
    B["xprev"] = A([128, 2, 16, NBM], BF16)
    r2 = P.top
    P.top = r0
    B["prodF"] = A([128, 22, NM], BF16)
    B["gpre"] = A([128, 2, 2 + NM], BF16)
    B["gact"] = A([128, 2, NM], BF16)
    r3 = P.top
    P.top = max(r1, r2, r3)
    self.ring_i = 0
    self.ring_tag = [None, None, None]
    self.ring_use = [0, 0, 0]
    self.ring_shape = [None, None, None]
    self.ring_clock = 0
    self.xr_i = 0
    tok = 0
    out_row = 0
    n_pre = len(self.state_ns) + (1 if self.skip_sub > 0 else 0)
    tiles = [("state", ns) for ns in self.state_ns] + [("full", ns) for ns in self.full_ns]
    skipped = 0
    for ti, (mode, ns) in enumerate(tiles):
        store = None
        if mode == "full":
            if skipped < self.skip_sub:
                assert ns <= self.skip_sub - skipped
                skipped += ns
            else:
                store = out_row
                out_row += 128 * ns
        self.tile(mode, tok, ns, store)
        tok += 128 * ns
        if ti == n_pre - 1:
            self.apply_flag()
    assert out_row == self.nout


def _sm(self, n):
    if self.sm_i + n > 512:
        self.sm_i = 0
    v = self.B["smr"][:, self.sm_i:self.sm_i + n]
    self.sm_i += n
    return v


def _wchunk(self, name, idx, shape=None, src=None):
    tag = (name, idx)
    self.ring_clock += 1
    for i in range(3):
        if self.ring_tag[i] == tag:
            self.ring_use[i] = self.ring_clock
            return self.chunk_view(i, shape if shape is not None else self.ring_shape[i])
    i = min(range(3), key=lambda k: self.ring_use[k])
    self.ring_tag[i] = tag
    self.ring_use[i] = self.ring_clock
    if src is None:
        src = self.S[name] if idx is None else self.S[name][idx]
    self.ring_shape[i] = list(src.shape)
    dst = self.chunk_view(i, list(src.shape))
    self.P.dma(dst, src)
    return self.chunk_view(i, shape if shape is not None else list(src.shape))


def _chunk_view(self, i, shape):
    slot = self.B["ring"][i]
    if shape is None or list(shape) == [128, 8, 512]:
        return slot
    flat = slot.rearrange("p a b -> p (a b)")
    n = 1
    for x in shape[1:]:
        n *= x
    v = flat[:, 0:n]
    if len(shape) == 3:
        return v.rearrange("p (a b) -> p a b", b=shape[2])
    if len(shape) == 4:
        return v.rearrange("p (a b c) -> p a b c", b=shape[2], c=shape[3])
    if len(shape) == 5:
        return v.rearrange("p (a b c d) -> p a b c d", b=shape[2], c=shape[3], d=shape[4])
    return v


def _apply_flag(self):
    P, T, B = self.P, self.T, self.B
    fl = T["flag"][:, 0:1]
    P.ts("dve", T["C32"], T["C32"], fl, None, ALU.mult)
    P.cp("pool", T["Cb"], T["C32"])
    P.ts("dve", T["xcar"], T["xcar"], fl, None, ALU.mult)
    P.ts("dve", T["xmh"], T["xmh"], fl, None, ALU.mult)
    P.ts("dve", T["fhalo"], T["fhalo"], fl, None, ALU.mult)


def _ln_stats(self, x):
    P, T = self.P, self.T
    st6 = self.sm(12).rearrange("p (a b) -> p a b", b=6)
    mv = self.sm(2); tmp = self.sm(1); rstd = self.sm(1); nmr = self.sm(1)
    P.op("dve", lambda e: e.bn_stats(st6[:, 0, :], x[:, 0:512]), [x[:, 0:512]], [st6[:, 0, :]])
    P.op("dve", lambda e: e.bn_stats(st6[:, 1, :], x[:, 512:1024]), [x[:, 512:1024]], [st6[:, 1, :]])
    P.op("dve", lambda e: e.bn_aggr(mv, st6.rearrange("p a b -> p (a b)")), [st6], [mv])
    P.ts("pool", tmp, mv[:, 1:2], LN_EPS, None, ALU.add)
    P.tt("pool", rstd, tmp, T["mhalf"][:, 0:1], ALU.pow)
    P.stt(nmr, mv[:, 0:1], -1.0, rstd, ALU.mult, ALU.mult)
    return rstd, nmr


def _ln_to_T(self, x, s, sc0, sh0):
    P, T, B = self.P, self.T, self.B
    rstd, nmr = self.ln_stats(x)
    xn = B["xn"][s % 2]
    P.act(xn, x, AF.Identity, bias=nmr, scale=rstd)
    bk, _ = P.bank()
    bb = bk[:, :].bitcast(BF16)
    P.trg([(bb[:, 128 * kt:128 * kt + 128], xn[:, 128 * kt:128 * kt + 128], T["ident_b"]) for kt in range(8)])
    for kt in range(8):
        P.act(B["hT"][:, kt, 128 * s:128 * s + 128], bb[:, 128 * kt:128 * kt + 128], AF.Identity,
              bias=T["modT"][:, sh0 + kt:sh0 + kt + 1], scale=T["modT"][:, sc0 + kt:sc0 + kt + 1])


def _inproj_tile(self, ptile, N):
    P, B = self.P, self.B
    w = self.wchunk("wA", ptile // 4)
    j = ptile % 4
    bk, _ = P.bank()
    P.mmg([(bk[:, 0:N], w[:, kt, 128 * j:128 * j + 128], B["hT"][:, kt, 0:N], kt == 0, kt == 7) for kt in range(8)])
    return bk


Builder.main = _main
Builder.sm = _sm
Builder.wchunk = _wchunk
Builder.chunk_view = _chunk_view
Builder.apply_flag = _apply_flag
Builder.ln_stats = _ln_stats
Builder.ln_to_T = _ln_to_T
Builder.inproj_tile = _inproj_tile


def _tile(self, mode, tok0, ns, store):
    P, I, T, S, B = self.P, self.I, self.T, self.S, self.B
    full = (mode == "full")
    N = 128 * ns
    nb = 16 * ns
    hT, xmT, xcT, QT, KT, Vx = B["hT"], B["xmT"], B["xcT"], B["QT"], B["KT"], B["Vx"]
    cs = lambda s: slice(128 * s, 128 * s + 128)
    xs = []
    for s in range(ns):
        x = B["XR"][self.xr_i]
        self.xr_i = (self.xr_i + 1) % len(B["XR"])
        xs.append(x)
        r0 = tok0 + 128 * s
        P.dma(x, I["xin"][r0:r0 + 128, :], q="pool")
        self.ln_to_T(x, s, 8, 0)
    P.cp("pool", xmT[:, :, 0:3], T["xmh"])
    P.op("pool", lambda e: e.memset(Vx[:, 0:ns, :, 256:257], 1.0), [], [Vx[:, 0:ns, :, 256:257]])
    for ct in range(8):
        bk = self.inproj_tile(ct, N)
        P.act(xmT[:, ct, 3:3 + N], bk[:, 0:N], AF.Identity, bias=T["binT"][:, ct:ct + 1])
    G = B["gsm"]
    for s in range(ns):
        g = G[:, s, :]
        bk, _ = P.bank()
        P.mmg([(bk[:, 0:8], hT[:, kt, cs(s)], T["wgt"][:, kt, :], kt == 0, kt == 7) for kt in range(8)])
        P.tt("dve", g[:, 0:8], bk[:, 0:8], T["bgate"], ALU.add)
        P.act(g[:, 8:12], g[:, 4:8], AF.Exp, scale=-1.0)
        P.act(g[:, 12:16], g[:, 8:12], AF.Ln, bias=T["ones_f"][:, 0:1])
        b2, _ = P.bank()
        P.mmg([(b2[:, 0:4], T["tri_f"], g[:, 12:16], True, True), (b2[:, 4:8], T["ones_f"], g[:, 12:16], True, True)])
        P.tt("dve", g[:, 32:36], g[:, 0:4], b2[:, 0:4], ALU.add)
        P.tt("dve", g[:, 36:40], g[:, 32:36], b2[:, 4:8], ALU.subtract)
        P.act(g[:, 20:24], g[:, 36:40], AF.Exp)
        P.act(g[:, 24:28], b2[:, 4:8], AF.Exp, scale=-1.0)
        if full:
            P.act(g[:, 16:20], g[:, 32:36], AF.Exp)
            P.act(g[:, 28:32], b2[:, 0:4], AF.Exp)
    for ct in range(8):
        bk, _ = P.bank()
        P.mmg([(bk[:, 0:N], T["cdiag"][:, ct, j, :], xmT[:, ct, j:j + N], j == 0, j == 3) for j in range(4)])
        P.act(xcT[:, ct, 0:N], bk[:, 0:N], AF.Silu, bias=T["cb"][:, ct:ct + 1])
    if full:
        for h in range(4):
            bk, _ = P.bank()
            P.mmg([(bk[:, 0:N], T["wq"][:, h, kt, :], xcT[:, 2 * h + kt, 0:N], kt == 0, kt == 1) for kt in range(2)])
            P.op("act", lambda e, h=h, bk=bk: e.mul(QT[:, h, 0:N], bk[:, 0:N], DK ** -0.5), [bk[:, 0:N]], [QT[:, h, 0:N]])
            bk2, _ = P.bank()
            P.mmg([(bk2[:, 0:N], T["wk"][:, h, kt, :], xcT[:, 2 * h + kt, 0:N], kt == 0, kt == 1) for kt in range(2)])
            P.cp("dve", KT[:, h, 0:N], bk2[:, 0:N])
    for s in range(ns):
        g = G[:, s, :]
        par = s % 2
        KW, SW = B["KW"][:, par], B["SW"][:, par]
        kb, _ = P.bank()
        items = []
        for h in range(4):
            for kt in range(2):
                items.append((kb[:, 128 * h:128 * h + 128], xcT[:, 2 * h + kt, cs(s)], T["wk"][:, h, kt, :], kt == 0, kt == 1))
        P.mmg(items)
        P.tt("dve", KW, kb[:, 0:512].rearrange("p (h d) -> p h d", d=128), bc(g[:, 20:24], 128), ALU.mult)
        vb, _ = P.bank()
        vbb = vb[:, :].bitcast(BF16)
        P.trg([(vbb[:, 128 * ct:128 * ct + 128], xmT[:, ct, 3 + 128 * s:3 + 128 * s + 128], T["ident_b"]) for ct in range(8)])
        P.cp("act", Vx[:, s, :, 0:256], vbb[:, 0:1024].rearrange("p (h v) -> p h v", v=256))
        if full:
            sb_, _ = P.bank()
            P.mmg([(sb_[:, 128 * h:128 * h + 128], KT[:, h, cs(s)], QT[:, h, cs(s)], True, True) for h in range(4)])
            for h in range(4):
                P.stt(SW[:, h, :], sb_[:, 128 * h:128 * h + 128], g[:, 16 + h:17 + h], T["mask_b"], ALU.mult, ALU.mult)
            nbs = []
            for h in range(4):
                nbk, _ = P.bank()
                nbs.append(nbk)
                P.mmg([(nbk[:, 0:257], SW[:, h, :], Vx[:, s, h, 0:257], True, False),
                       (nbk[:, 0:257], QT[:, h, cs(s)], T["Cb"][:, h, 0:257], False, True)])
            a1 = self.sm(4); rd = self.sm(4); st6 = self.sm(24).rearrange("p (h b) -> p h b", b=6); mv = self.sm(8).rearrange("p (h b) -> p h b", b=2)
            t1 = self.sm(4); aa = self.sm(4); nbv = self.sm(4)
            for h in range(4):
                P.act(a1[:, h:h + 1], nbs[h][:, 256:257], AF.Abs)
            P.tt("dve", a1, a1, g[:, 28:32], ALU.max)
            P.op("dve", lambda e, rd=rd, a1=a1: e.reciprocal(rd, a1), [a1], [rd])
            for h in range(4):
                P.op("dve", lambda e, h=h, st6=st6, nbs=nbs: e.bn_stats(st6[:, h, :], nbs[h][:, 0:256]), [nbs[h][:, 0:256]], [st6[:, h, :]])
            for h in range(4):
                P.op("dve", lambda e, h=h, st6=st6, mv=mv: e.bn_aggr(mv[:, h, :], st6[:, h, :]), [st6[:, h, :]], [mv[:, h, :]])
            P.tt("pool", t1, mv[:, :, 1], rd, ALU.mult)
            P.tt("pool", t1, t1, rd, ALU.mult)
            P.ts("pool", t1, t1, LN_EPS, None, ALU.add)
            P.tt("pool", t1, t1, T["mhalf"], ALU.pow)
            P.tt("pool", aa, t1, rd, ALU.mult)
            P.stt(nbv, mv[:, :, 0], -1.0, aa, ALU.mult, ALU.mult)
            for h in range(4):
                P.act(B["hm"][:, s, 256 * h:256 * h + 256], nbs[h][:, 0:256], AF.Identity, bias=nbv[:, h:h + 1], scale=aa[:, h:h + 1])
        for h in range(4):
            cbk, _ = P.bank()
            P.mmg([(cbk[:, 0:257], KW[:, h, :], Vx[:, s, h, 0:257], True, True)])
            P.stt(T["C32"][:, h, 0:257], T["C32"][:, h, 0:257], g[:, 24 + h:25 + h], cbk[:, 0:257], ALU.mult, ALU.add)
            P.cp("pool", T["Cb"][:, h, 0:257], T["C32"][:, h, 0:257])
    P.cp("pool", T["xmh"], xmT[:, :, N:N + 3])
    self.tile_s5(full, ns, part=1)
    if full:
        self.tile_mix_out(ns, xs)
        self.tile_s5(full, ns, part=2)
        self.tile_post(ns, xs, store)


Builder.tile = _tile


def _tile_mix_out(self, ns, xs):
    P, T, B = self.P, self.T, self.B
    N = 128 * ns
    hm, omS, prodT, yT, gtmp = B["hm"], B["omS"], B["prodT"], B["yT"], B["gtmp"]
    for ct in range(8):
        bk = self.inproj_tile(8 + ct, N)
        P.act(omS[:, ct, 0:N], bk[:, 0:N], AF.Sigmoid, bias=T["binT"][:, 8 + ct:9 + ct])
    for vp in range(4):
        bk, _ = P.bank()
        bb = bk[:, :].bitcast(BF16)
        items = []
        for j in range(2):
            vt = 2 * vp + j
            for s in range(ns):
                items.append((bb[:, 512 * j + 128 * s:512 * j + 128 * s + 128], hm[:, s, 128 * vt:128 * vt + 128], T["ident_b"]))
        P.trg(items)
        for j in range(2):
            vt = 2 * vp + j
            P.stt(prodT[:, vt, 0:N], bb[:, 512 * j:512 * j + N], T["gainT"][:, vt:vt + 1], omS[:, vt, 0:N], ALU.mult, ALU.mult)
    for dt_ in range(8):
        w = self.wchunk("wD", dt_ // 4)
        j = dt_ % 4
        bk, _ = P.bank()
        P.mmg([(bk[:, 0:N], w[:, kt, 128 * j:128 * j + 128], prodT[:, kt, 0:N], kt == 0, kt == 7) for kt in range(8)])
        b2 = self.inproj_tile(20 + dt_, N)
        P.act(gtmp[:, 0, 0:N], b2[:, 0:N], AF.Sigmoid, bias=T["binT"][:, 20 + dt_:21 + dt_])
        P.tt("dve", yT[:, dt_, 0:N], bk[:, 0:N], gtmp[:, 0, 0:N], ALU.mult)


def _tile_s5(self, full, ns, part=1):
    P, T, B = self.P, self.T, self.B
    N = 128 * ns
    nb = 16 * ns
    uT, ysT, yT, gtmp = B["uT"], B["ysT"], B["yT"], B["gtmp"]
    xp = B["xprev"]
    if part == 2:
        return self.tile_s5_out(ns)
    for ct in range(4):
        bk = self.inproj_tile(16 + ct, N)
        P.act(uT[:, ct, 0:N], bk[:, 0:N], AF.Identity, bias=T["binT"][:, 16 + ct:17 + ct])
    xb = [P.bank()[0] for _ in range(4)]
    for ct in range(4):
        w = self.wchunk("s5X", ct // 2)
        for q in range(4):
            items = []
            kw = dict(tile_position=(96, 0)) if q == 3 else {}
            for ri in range(2):
                c0 = (ct * 2 + ri) * nb
                for tau in range(8):
                    items.append((xb[q][:, c0:c0 + nb], w[32 * q:32 * q + 32, ct % 2, tau, ri, :], uT[32 * q:32 * q + 32, ct, tau:N:8], tau == 0, tau == 7, kw))
            P.mmg(items)
    Xin = B["Xin"]
    for q in range(4):
        src = xb[q][:, 0:8 * nb].rearrange("p (c r n) -> p r c n", c=4, r=2)
        P.cp("act" if q % 2 == 0 else "dve", Xin[:, :, q:16:4, 0:nb], src)
    cj, sj = T["rotc"][:, :, 0:nb], T["rots"][:, :, 0:nb]
    v3 = lambda t: t[:, 0:16 * nb].rearrange("p (q n) -> p q n", n=nb)
    Zr, Zi, Ta, Tb = v3(B["Zr"]), v3(B["Zi"]), v3(B["Ta"]), v3(B["Tb"])
    Xr, Xi = Xin[:, 0, :, 0:nb], Xin[:, 1, :, 0:nb]
    P.tt("pool", Ta, cj, Xr, ALU.mult); P.tt("pool", Tb, sj, Xi, ALU.mult); P.tt("dve", Zr, Ta, Tb, ALU.subtract)
    P.tt("pool", Ta, cj, Xi, ALU.mult); P.tt("pool", Tb, sj, Xr, ALU.mult); P.tt("dve", Zi, Ta, Tb, ALU.add)
    xc = T["xcar"]; u8 = T["u8"]
    i_r = self.sm(16); i_i = self.sm(16); ta = self.sm(16); tb = self.sm(16)
    P.tt("dve", ta, u8[:, 0, :], xc[:, 0, :], ALU.mult); P.tt("dve", tb, u8[:, 1, :], xc[:, 1, :], ALU.mult); P.tt("dve", i_r, ta, tb, ALU.subtract)
    P.tt("dve", ta, u8[:, 0, :], xc[:, 1, :], ALU.mult); P.tt("dve", tb, u8[:, 1, :], xc[:, 0, :], ALU.mult); P.tt("dve", i_i, ta, tb, ALU.add)
    P.tt("dve", i_r, i_r, T["rho8"], ALU.mult); P.tt("dve", i_i, i_i, T["rho8"], ALU.mult)
    P.tt("dve", Zr[:, :, 0], Zr[:, :, 0], i_r, ALU.add); P.tt("dve", Zi[:, :, 0], Zi[:, :, 0], i_i, ALU.add)
    d0 = self.d0[nb].rearrange("p q n -> p (q n)")
    fr, fi = B["Ta"][:, 0:16 * nb], B["Tb"][:, 0:16 * nb]
    P.op("dve", lambda e: e.tensor_tensor_scan(fr, d0, B["Zr"][:, 0:16 * nb], 0.0, ALU.mult, ALU.add), [d0, B["Zr"][:, 0:16 * nb]], [fr])
    P.op("dve", lambda e: e.tensor_tensor_scan(fi, d0, B["Zi"][:, 0:16 * nb], 0.0, ALU.mult, ALU.add), [d0, B["Zi"][:, 0:16 * nb]], [fi])
    XS = B["XS"]
    xr_o, xi_o = XS[:, 0, :, 0:nb], XS[:, 1, :, 0:nb]
    P.tt("pool", Zr, cj, Ta, ALU.mult); P.tt("pool", Zi, sj, Tb, ALU.mult); P.tt("dve", xr_o, Zr, Zi, ALU.add)
    P.tt("pool", Zr, cj, Tb, ALU.mult); P.tt("pool", Zi, sj, Ta, ALU.mult); P.tt("dve", xi_o, Zr, Zi, ALU.subtract)
    xp = B["xprev"]
    if full:
        P.cp("pool", xp[:, :, :, 0], xc)
        if nb > 1:
            P.cp("pool", xp[:, :, :, 1:nb], XS[:, :, :, 0:nb - 1])
    P.cp("dve", xc, XS[:, :, :, nb - 1])


def _tile_s5_out(self, ns):
    P, T, B = self.P, self.T, self.B
    N = 128 * ns
    nb = 16 * ns
    uT, ysT, yT, gtmp = B["uT"], B["ysT"], B["yT"], B["gtmp"]
    xp = B["xprev"]
    kc = None
    for ct in range(4):
        kc = self.wchunk("s5K", None)
        yb, _ = P.bank()
        items = []
        for tp in range(8):
            for tau in range(tp, 8):
                items.append((yb[:, tau:N:8], kc[:, ct, tp, :], uT[:, ct, tau - tp:N:8], (tp == 0 and tau == 0), False, dict(skip_group_check=True)))
        P.mmg(items)
        items = []
        yw = self.wchunk("s5Y", ct // 2)
        for q in range(4):
            for tau in range(8):
                for ri in range(2):
                    last = (q == 3 and tau == 7 and ri == 1)
                    items.append((yb[32 * q:32 * q + 32, tau:N:8], yw[:, (4 * ct + q) % 8, tau, ri, :], xp[:, ri, 4 * ct + q, 0:nb],
                                  False, last, dict(tile_position=(0, 32 * q), skip_group_check=True)))
        P.mmg(items)
        P.act(ysT[:, ct, 0:N], yb[:, 0:N], AF.Gelu_apprx_tanh)
    for dt_ in range(8):
        j = dt_ % 4
        wv = self.wchunk("wG", dt_ // 4)
        bv, _ = P.bank()
        P.mmg([(bv[:, 0:N], wv[:, kt, 128 * j:128 * j + 128], ysT[:, kt, 0:N], kt == 0, kt == 3) for kt in range(4)])
        wg = self.wchunk("wG", 2 + dt_ // 4)
        bg, _ = P.bank()
        P.mmg([(bg[:, 0:N], wg[:, kt, 128 * j:128 * j + 128], ysT[:, kt, 0:N], kt == 0, kt == 3) for kt in range(4)])
        P.act(gtmp[:, 1, 0:N], bg[:, 0:N], AF.Sigmoid)
        P.tt("dve", gtmp[:, 2, 0:N], bv[:, 0:N], gtmp[:, 1, 0:N], ALU.mult)
        b2 = self.inproj_tile(28 + dt_, N)
        P.act(gtmp[:, 0, 0:N], b2[:, 0:N], AF.Sigmoid, bias=T["binT"][:, 28 + dt_:29 + dt_])
        P.tt("dve", gtmp[:, 2, 0:N], gtmp[:, 2, 0:N], gtmp[:, 0, 0:N], ALU.mult)
        P.tt("dve", yT[:, dt_, 0:N], yT[:, dt_, 0:N], gtmp[:, 2, 0:N], ALU.add)


Builder.tile_mix_out = _tile_mix_out
Builder.tile_s5 = _tile_s5
Builder.tile_s5_out = _tile_s5_out


def _post_ln(self, x, gb_key):
    P, B = self.P, self.B
    rstd, nmr = self.ln_stats(x)
    P.act(x, x, AF.Identity, bias=nmr, scale=rstd)
    P.tt("dve", x, x, B["lnt"][:, 0, :], ALU.mult)
    P.tt("pool", x, x, B["lnt"][:, 1, :], ALU.add)


def _tile_post(self, ns, xs, store):
    P, I, T, S, B = self.P, self.I, self.T, self.S, self.B
    N = 128 * ns
    yT, hT, prodF = B["yT"], B["hT"], B["prodF"]
    cs = lambda s: slice(128 * s, 128 * s + 128)
    lnt = B["lnt"]
    P.dma(lnt[:, 0, :], self.ap_bcast(I["ln1_gain"], 0, D))
    P.dma(lnt[:, 1, :], self.ap_bcast(I["ln1_bias"], 0, D))
    for hf in range(2):
        w = self.wchunk("wM", hf)
        for s in range(ns):
            bk, _ = P.bank()
            P.mmg([(bk[:, 0:512], yT[:, kt, cs(s)], w[:, kt, :], kt == 0, kt == 7) for kt in range(8)])
            xh = xs[s][:, 512 * hf:512 * hf + 512]
            P.stt(xh, xh, ALPHA, bk[:, 0:512], ALU.mult, ALU.add)
    for s in range(ns):
        self.post_ln(xs[s], "ln1")
        self.ln_to_T(xs[s], s, 32, 24)
    fh = T["fhalo"]
    gpre, gact = B["gpre"], B["gact"]
    pend = None

    def conv_stage(t, par, bv):
        fd = self.wchunk("fdiag", t // 10, src=S["fdiag"][10 * (t // 10):min(22, 10 * (t // 10) + 10)].rearrange("t p j d -> p t j d"))
        tl = t % 10
        bc_, _ = P.bank()
        P.mmg([(bc_[:, 0:N], fd[:, tl, j, :], gpre[:, par, j:j + N], j == 0, j == 2) for j in range(3)])
        P.act(gact[:, par, 0:N], bc_[:, 0:N], AF.Gelu_apprx_tanh, bias=T["fcb"][:, t:t + 1])
        P.tt("dve", prodF[:, t, 0:N], bv[:, 0:N], gact[:, par, 0:N], ALU.mult)

    for t in range(22):
        par = t % 2
        wv = self.wchunk("wU", t // 4)
        bv, _ = P.bank()
        P.mmg([(bv[:, 0:N], wv[:, kt, 128 * (t % 4):128 * (t % 4) + 128], hT[:, kt, 0:N], kt == 0, kt == 7) for kt in range(8)])
        gt_ = 22 + t
        wg = self.wchunk("wU", gt_ // 4)
        bg, _ = P.bank()
        P.mmg([(bg[:, 0:N], wg[:, kt, 128 * (gt_ % 4):128 * (gt_ % 4) + 128], hT[:, kt, 0:N], kt == 0, kt == 7) for kt in range(8)])
        P.cp("pool", gpre[:, par, 0:2], fh[:, t, :])
        P.cp("act", gpre[:, par, 2:2 + N], bg[:, 0:N])
        P.cp("pool", fh[:, t, :], gpre[:, par, N:N + 2])
        if pend is not None:
            conv_stage(*pend)
        pend = (t, par, bv)
    conv_stage(*pend)
    P.dma(lnt[:, 0, :], self.ap_bcast(I["ln2_gain"], 0, D))
    P.dma(lnt[:, 1, :], self.ap_bcast(I["ln2_bias"], 0, D))
    for hf in range(2):
        acc = [P.bank()[0] for _ in range(ns)]
        for gi, (k0, nk) in enumerate(WF_GROUPS):
            w = self.wchunk("wF", gi * 2 + hf, src=S["wF"][gi * 2 + hf][:, 0:nk, :])
            for s in range(ns):
                P.mmg([(acc[s][:, 0:512], prodF[:, k0 + kk, cs(s)], w[:, kk, :], (gi == 0 and kk == 0), (gi == 2 and kk == nk - 1)) for kk in range(nk)])
        for s in range(ns):
            xh = xs[s][:, 512 * hf:512 * hf + 512]
            P.stt(xh, xh, ALPHA, acc[s][:, 0:512], ALU.mult, ALU.add)
    for s in range(ns):
        self.post_ln(xs[s], "ln2")
        if store is not None:
            P.dma(self.yout[store + 128 * s:store + 128 * s + 128, :], xs[s], q="pool")
    self.dbg_tile = True


Builder.post_ln = _post_ln
Builder.tile_post = _tile_post


_CACHE = {}


def _consts():
    bm = np.zeros((128, 128), np.float32)
    for q in range(4):
        bm[32 * q:32 * q + 32, 32 * q:32 * q + 32] = 1.0
    return np.eye(128, dtype=np.float32), np.triu(np.ones((128, 128), np.float32)), bm


def core_map(inputs, b, xin, flag):
    ident, tri, bm = _consts()
    m = {"xin": np.ascontiguousarray(xin, dtype=np.float32), "cvec": np.ascontiguousarray(inputs["c"][b], dtype=np.float32),
         "flagv": np.full((128, 1), flag, np.float32), "c_ident": ident, "c_tri": tri, "c_bmask": bm}
    for k, v in inputs.items():
        if k in ("x", "c"):
            continue
        m[k] = np.ascontiguousarray(np.asarray(v)[0], dtype=np.float32)
    return m


def kernel(**inputs):
    inputs = {k: np.asarray(v) for k, v in inputs.items()}
    x = inputs["x"]
    Bn, Sq, _ = x.shape
    half = Sq // 2
    n_state = (half - 128) // 128
    state_ns = [4] * (n_state // 4) + ([n_state % 4] if n_state % 4 else [])
    full_ns = [1] + [4] * (half // 512)
    key = (tuple(state_ns), tuple(full_ns))
    if key not in _CACHE:
        bld = Builder(state_ns, full_ns, 1)
        _CACHE[key] = bld.build()
    nc = _CACHE[key]
    in_maps = []
    for core in range(2 * Bn):
        b, h = core // 2, core % 2
        if h == 0:
            xin = np.concatenate([np.zeros((half, D), np.float32), x[b, :half]], axis=0)
        else:
            xin = x[b]
        in_maps.append(core_map(inputs, b, xin, float(h)))
    res = run_bass_kernel_spmd(nc, in_maps, core_ids=list(range(2 * Bn)))
    out = np.empty((Bn, Sq, D), np.float32)
    for core in range(2 * Bn):
        b, h = core // 2, core % 2
        out[b, h * half:(h + 1) * half] = res.results[core]["yout"]
    return out
```

```python
import math
from contextlib import ExitStack
import numpy as np
import concourse.bass as bass
import concourse.mybir as mybir
from concourse.bass_utils import run_bass_kernel_spmd

F32 = mybir.dt.float32
BF16 = mybir.dt.bfloat16
AF = mybir.ActivationFunctionType
ALU = mybir.AluOpType
AX = mybir.AxisListType

N_DMA_SEMS = 24
COMPUTE = ("pe", "act", "dve", "pool")
ENG = {"pe": "tensor", "act": "scalar", "dve": "vector", "pool": "gpsimd", "sp": "sync"}

D = 1024
NH = 4
DV = 256
DK = 128
S5W = 512
FH = 2816
INW = 4616
NKT = 8
ALPHA = 2.0 ** 0.25
LN_EPS = 1e-5
TB = 8
ARENA_F32 = 50688


class Prog:
    def __init__(self, nc, stack):
        self.nc = nc
        self.stack = stack
        self.ops = []
        self.arena = stack.enter_context(nc.sbuf_tensor("arena", [128, ARENA_F32], F32))
        self.top = 0
        self.peak = 0
        self.banks = [stack.enter_context(nc.psum_tensor(f"bank{i}", [128, 512], F32)) for i in range(8)]
        self.bank_i = 0
        self.uid = 0

    def alloc(self, shape, dtype=F32):
        n = 1
        for s in shape[1:]:
            n *= s
        words = (n + 1) // 2 if dtype == BF16 else n
        words = (words + 7) // 8 * 8
        a = self.top
        self.top += words
        self.peak = max(self.peak, self.top)
        assert self.top <= ARENA_F32, f"SBUF arena overflow {self.top}"
        v = self.arena[:, a:a + words]
        if dtype == BF16:
            v = v.bitcast(BF16)
        v = v[:, 0:n]
        if len(shape) == 3:
            v = v.rearrange("p (a b) -> p a b", b=shape[2])
        elif len(shape) == 4:
            v = v.rearrange("p (a b c) -> p a b c", b=shape[2], c=shape[3])
        elif len(shape) == 5:
            v = v.rearrange("p (a b c d) -> p a b c d", b=shape[2], c=shape[3], d=shape[4])
        return v[0:shape[0]] if shape[0] != 128 else v

    def key(self, prefix="k"):
        self.uid += 1
        return f"{prefix}{self.uid}"

    def bank(self):
        i = self.bank_i
        self.bank_i = (i + 1) % 8
        return self.banks[i], f"bank{i}"

    def op(self, eng, fn, reads=(), writes=()):
        self.ops.append(dict(eng=eng, fn=fn, reads=tuple(reads), writes=tuple(writes), dma=False, bar=False))

    def dma(self, out, in_, reads=(), writes=(), q="sp", **kw):
        def fn(e, out=out, in_=in_, kw=kw):
            return e.dma_start(out=out, in_=in_, **kw)
        rd, wr = list(reads), list(writes)
        for ap, lst in ((in_, rd), (out, wr)):
            if isinstance(ap, bass.AP) and ap.tensor.name == "arena":
                lst.append(ap)
        self.ops.append(dict(eng=q, fn=fn, reads=tuple(rd), writes=tuple(wr), dma=True, bar=False))

    def barrier(self):
        self.ops.append(dict(eng=None, fn=None, reads=(), writes=(), dma=False, bar=True))

    @staticmethod
    def _res(x):
        if isinstance(x, str):
            return ("key", x)
        name = x.tensor.name
        if name != "arena":
            return ("key", name)
        esz = 2 if x.dtype == BF16 else 4
        aps = x.ap
        pstride = ARENA_F32 * 4 // esz
        off = x.offset
        p0 = off // pstride
        lo = (off % pstride) * esz
        if aps[0][0] == 0:
            npart = 1
        else:
            npart = aps[0][1]
        span = 1
        for (st_, cnt) in aps[1:]:
            span += (cnt - 1) * abs(st_)
        return ("box", p0, p0 + npart, lo, lo + span * esz)

    def mmg(self, items, extra_reads=()):
        def f(e, items=items):
            for it in items:
                kw = it[5] if len(it) > 5 else {}
                ins = e.matmul(it[0], it[1], it[2], start=it[3], stop=it[4], **kw)
            return ins
        rd, wr = list(extra_reads), []
        for it in items:
            rd += [it[1], it[2]]
            wr.append(it[0])
        self.op("pe", f, rd, wr)

    def trg(self, items):
        def f(e, items=items):
            for (o, i_, idn) in items:
                ins = e.transpose(o, i_, idn)
            return ins
        self.op("pe", f, [x for it in items for x in (it[1], it[2])], [it[0] for it in items])

    def act(self, out, in_, func, bias=None, scale=1.0):
        rd = [in_] + [x for x in (bias, scale) if isinstance(x, bass.AP)]
        kw = {} if bias is None else {"bias": bias}
        self.op("act", lambda e: e.activation(out, in_, func, scale=scale, **kw), rd, [out])

    def tt(self, eng, out, a, b, op):
        self.op(eng, lambda e: e.tensor_tensor(out, a, b, op), [a, b], [out])

    def ts(self, eng, out, a, s1, s2, op0, op1=None):
        rd = [a] + [x for x in (s1, s2) if isinstance(x, bass.AP)]
        if op1 is None:
            self.op(eng, lambda e: e.tensor_scalar(out, a, s1, s2, op0), rd, [out])
        else:
            self.op(eng, lambda e: e.tensor_scalar(out, a, s1, s2, op0, op1), rd, [out])

    def stt(self, out, a, sc, b, op0, op1):
        rd = [a, b] + ([sc] if isinstance(sc, bass.AP) else [])
        self.op("dve", lambda e: e.scalar_tensor_tensor(out, a, sc, b, op0, op1), rd, [out])

    def cp(self, eng, out, in_):
        if eng == "act":
            self.op("act", lambda e: e.copy(out, in_), [in_], [out])
        else:
            self.op(eng, lambda e: e.tensor_copy(out, in_), [in_], [out])

    def view(self, a, shape, dtype=F32):
        top = self.top
        self.top = a
        v = self.alloc(shape, dtype)
        used = self.top
        self.top = max(top, used)
        return v

    def emit(self, final_wait_eng="sp"):
        nc = self.nc
        ops = self.ops
        n = len(ops)
        last_w, readers = {}, {}
        boxes = []
        deps = [set() for _ in ops]
        since_bar = []
        pending = {}
        for i, o in enumerate(ops):
            if o["bar"]:
                last_per_eng, dmas = {}, []
                for p in since_bar:
                    po = ops[p]
                    if po["dma"]:
                        dmas.append(p)
                    else:
                        last_per_eng[po["eng"]] = p
                pre = set(last_per_eng.values()) | set(dmas)
                for e in list(COMPUTE) + ["sp"]:
                    pending[e] = set(pre) | pending.get(e, set())
                since_bar = []
                last_w, readers, boxes = {}, {}, []
                continue
            d = set()
            rres = [self._res(x) for x in o["reads"]]
            wres = [self._res(x) for x in o["writes"]]
            for r in rres:
                if r[0] == "key":
                    if r[1] in last_w:
                        d.add(last_w[r[1]])
                    if r[1].startswith("bank"):
                        for r_ in readers.get(r[1], ()):
                            if ops[r_]["eng"] != o["eng"]:
                                d.add(r_)
                else:
                    _, p0, p1, lo, hi = r
                    for bx in boxes:
                        if bx[5] and bx[1] < p1 and p0 < bx[2] and bx[3] < hi and lo < bx[4]:
                            d.add(bx[0])
            for w in wres:
                if w[0] == "key":
                    if w[1] in last_w:
                        d.add(last_w[w[1]])
                    for r_ in readers.get(w[1], ()):
                        d.add(r_)
                else:
                    _, p0, p1, lo, hi = w
                    for bx in boxes:
                        if bx[1] < p1 and p0 < bx[2] and bx[3] < hi and lo < bx[4]:
                            d.add(bx[0])
            for p in d:
                if p == i:
                    continue
                po = ops[p]
                if not po["dma"] and not o["dma"] and po["eng"] == o["eng"] and o["eng"] == "pe":
                    continue
                deps[i].add(p)
            if pending.get(o["eng"]):
                for p in pending[o["eng"]]:
                    po = ops[p]
                    if (not po["dma"]) and (not o["dma"]) and po["eng"] == o["eng"]:
                        continue
                    deps[i].add(p)
                pending[o["eng"]] = set()
            ek = ("dma", i) if o["dma"] else o["eng"]
            for w in wres:
                if w[0] == "key":
                    last_w[w[1]] = i
                    readers[w[1]] = []
                else:
                    _, p0, p1, lo, hi = w
                    boxes = [bx for bx in boxes if not (p0 <= bx[1] and bx[2] <= p1 and lo <= bx[3] and bx[4] <= hi)]
                    boxes.append([i, p0, p1, lo, hi, True, ek])
            for r in rres:
                if r[0] == "key":
                    readers.setdefault(r[1], []).append(i)
                else:
                    _, p0, p1, lo, hi = r
                    boxes = [bx for bx in boxes if not (not bx[5] and bx[6] == ek and bx[1] == p0 and bx[2] == p1 and bx[3] == lo and bx[4] == hi)]
                    boxes.append([i, p0, p1, lo, hi, False, ek])
            since_bar.append(i)
        needed = set()
        for i in range(n):
            needed |= deps[i]
        sig_no, cnt = {}, {e: 0 for e in COMPUTE}
        dma_slot, dma_tot, ndma = {}, [0] * N_DMA_SEMS, 0
        nq = {"sp": 0, "pool": 0, "act": 0}
        NSP = N_DMA_SEMS - 8
        for i, o in enumerate(ops):
            if o["bar"]:
                continue
            if o["dma"]:
                if o["eng"] == "pool":
                    s = NSP + nq["pool"] % 8
                    nq["pool"] += 1
                else:
                    s = nq["sp"] % NSP
                    nq["sp"] += 1
                ndma += 1
                prev = dma_tot[s]
                dma_tot[s] += 16
                dma_slot[i] = (s, prev, dma_tot[s])
            elif i in needed:
                cnt[o["eng"]] += 1
                sig_no[i] = cnt[o["eng"]]
        st = self.stack
        csem = {e: st.enter_context(nc.semaphore(f"s_{e}")) for e in COMPUTE}
        dsem = [st.enter_context(nc.semaphore(f"s_dma{j}")) for j in range(N_DMA_SEMS)]
        used = sorted({o["eng"] for o in ops if not o["bar"]} | {final_wait_eng})
        with nc.Block() as block:
            for ename in used:
                def body(e, ename=ename):
                    seen = {c: 0 for c in csem}
                    seen_dma = [0] * N_DMA_SEMS
                    for i, o in enumerate(ops):
                        if o["bar"] or o["eng"] != ename:
                            continue
                        for p in sorted(deps[i]):
                            po = ops[p]
                            if po["dma"]:
                                s, _, tgt = dma_slot[p]
                                if seen_dma[s] < tgt:
                                    e.wait_ge(dsem[s], tgt)
                                    seen_dma[s] = tgt
                            else:
                                pe_ = po["eng"]
                                nn = sig_no[p]
                                if seen[pe_] < nn:
                                    e.wait_ge(csem[pe_], nn)
                                    seen[pe_] = nn
                        if o["dma"]:
                            s, prev, tgt = dma_slot[i]
                            if prev > 0 and seen_dma[s] < prev:
                                e.wait_ge(dsem[s], prev)
                                seen_dma[s] = prev
                            o["fn"](e).then_inc(dsem[s], 16)
                        else:
                            ins = o["fn"](e)
                            if i in sig_no:
                                ins.then_inc(csem[ename], 1)
                    if ename == final_wait_eng:
                        for s in range(N_DMA_SEMS):
                            if dma_tot[s] > seen_dma[s]:
                                e.wait_ge(dsem[s], dma_tot[s])
                getattr(block, ENG[ename])(body)
        return dict(n_ops=n, sig=cnt, ndma=ndma, peak_kb=self.peak * 4 / 1024)


WA_SRC = [0, 512, 1024, 1536, 2056, 2568, 3080, 3592, 4104]
BIN_GROUPS = [(0, 8), (1024, 8), (2056, 4), (2568, 8), (3592, 8)]
WF_GROUPS = [(0, 8), (8, 8), (16, 6)]


def dram_in(nc, name, shape, dtype=F32):
    return nc.dram_tensor(name, list(shape), dtype, kind="ExternalInput").ap()


class Builder:
    def __init__(self, state_ns, full_ns, skip_sub, dbg=()):
        self.state_ns = list(state_ns)
        self.full_ns = list(full_ns)
        self.skip_sub = skip_sub
        self.dbg = set(dbg)
        self.ntok = 128 * (sum(state_ns) + sum(full_ns))
        self.nout = 128 * (sum(full_ns) - skip_sub)
        self.nc = bass.Bass("TRN2", target_bir_lowering=False)
        self.dbg_out = {}

    def dbg_dump(self, name, ap, keys, dtype=F32):
        if name not in self.dbg:
            return
        shape = list(ap.shape)
        o = self.nc.dram_tensor("dbg_" + name, shape, dtype, kind="ExternalOutput").ap()
        self.P.dma(o, ap, reads=keys)

    def build(self):
        nc = self.nc
        I = {}
        def inp(name, shape):
            I[name] = dram_in(nc, name, shape)
        inp("xin", [self.ntok, D]); inp("cvec", [D]); inp("flagv", [128, 1])
        inp("w_ada", [D, 6 * D]); inp("b_ada", [6 * D]); inp("w_in", [D, INW]); inp("b_in", [INW])
        inp("w_mlstm_conv", [4, D]); inp("b_mlstm_conv", [D]); inp("w_mlstm_q", [NH, DV, DK]); inp("w_mlstm_k", [NH, DV, DK])
        inp("mlstm_norm_gain", [D]); inp("w_mlstm_down", [D, D])
        inp("s5_lam_re", [32, 64]); inp("s5_lam_im", [32, 64]); inp("s5_log_dt", [32])
        inp("s5_b_re", [32, 64, 16]); inp("s5_b_im", [32, 64, 16]); inp("s5_c_re", [32, 16, 64]); inp("s5_c_im", [32, 16, 64])
        inp("s5_d", [S5W]); inp("w_s5_glu", [S5W, 2 * D]); inp("w_mix_out", [D, D])
        inp("ln1_gain", [D]); inp("ln1_bias", [D]); inp("w_ffn_up", [D, 2 * FH]); inp("w_ffn_conv", [3, FH]); inp("b_ffn_conv", [FH])
        inp("w_ffn_down", [FH, D]); inp("ln2_gain", [D]); inp("ln2_bias", [D])
        inp("c_ident", [128, 128]); inp("c_tri", [128, 128]); inp("c_bmask", [128, 128])
        self.I = I
        self.yout = nc.dram_tensor("yout", [self.nout, D], F32, kind="ExternalOutput").ap()
        S = {}
        def scr(name, shape, dtype=BF16):
            S[name] = nc.dram_tensor("scr_" + name, list(shape), dtype, kind="Internal").ap()
        scr("wA", [9, 128, 8, 512]); scr("wD", [2, 128, 8, 512]); scr("wG", [4, 128, 4, 512]); scr("wM", [2, 128, 8, 512])
        scr("wU", [11, 128, 8, 512]); scr("wF", [6, 128, 8, 512]); scr("fdiag", [22, 128, 3, 128])
        scr("s5K", [128, 4, 8, 128]); scr("s5X", [2, 128, 2, 8, 2, 128]); scr("s5Y", [2, 128, 8, 8, 2, 32])
        self.S = S
        st = ExitStack()
        with st:
            self.P = P = Prog(nc, st)
            self.persistent()
            self.prologue()
            self.main()
            import os as _os
            if _os.environ.get("KMAX"):
                P.ops = P.ops[:int(_os.environ["KMAX"])]
            info = P.emit()
        self.info = info
        return nc

    def persistent(self):
        P, I = self.P, self.I
        A = P.alloc
        T = self.T = {}
        T["ident_f"] = A([128, 128]); T["tri_f"] = A([128, 128]); T["bmask_f"] = A([128, 128]); T["ones_f"] = A([128, 128])
        T["ident_b"] = A([128, 128], BF16); T["mask_b"] = A([128, 128], BF16)
        P.dma(T["ident_f"], I["c_ident"], writes=["ident_f"])
        P.dma(T["tri_f"], I["c_tri"], writes=["tri_f"])
        P.dma(T["bmask_f"], I["c_bmask"], writes=["bmask_f"])
        P.op("pool", lambda e: e.memset(T["ones_f"], 1.0), writes=["ones_f"])
        P.op("dve", lambda e: e.tensor_copy(T["ident_b"], T["ident_f"]), reads=["ident_f"], writes=["ident_b"])
        P.op("dve", lambda e: e.tensor_copy(T["mask_b"], T["tri_f"]), reads=["tri_f"], writes=["mask_b"])
        T["mhalf"] = A([128, 4]); T["flag"] = A([128, 1]); T["xmh"] = A([128, 8, 3], BF16)
        P.op("pool", lambda e: e.memset(T["mhalf"], -0.5), writes=["mhalf"])
        P.dma(T["flag"], I["flagv"], writes=["flag"])
        T["modT"] = A([128, 48])
        T["binT"] = A([128, 36]); T["bgate"] = A([128, 8])
        T["wgt"] = A([128, 8, 8], BF16); T["wq"] = A([128, 4, 2, 128], BF16); T["wk"] = A([128, 4, 2, 128], BF16)
        T["cdiag"] = A([128, 8, 4, 128], BF16); T["cb"] = A([128, 8]); T["gainT"] = A([128, 8])
        T["fcb"] = A([128, 22]); T["dT"] = A([128, 4])
        T["C32"] = A([128, 4, 260]); T["Cb"] = A([128, 4, 260], BF16)
        T["rotc"] = A([128, 16, 64]); T["rots"] = A([128, 16, 64]); T["rho8"] = A([128, 16]); T["u8"] = A([128, 2, 16])
        T["xcar"] = A([128, 2, 16])
        self.d0 = {}
        for nb in sorted({ns * 16 for ns in self.state_ns + self.full_ns}):
            self.d0[nb] = A([128, 16, nb])
        T["fhalo"] = A([128, 22, 2], BF16)

    def ap_cols(self, vec, off, ntile):
        return bass.AP(vec.tensor, off, [[1, 128], [128, ntile]])

    def ap_bcast(self, vec, off, n):
        return bass.AP(vec.tensor, off, [[0, 128], [1, n]])


def bc(ap, n):
    return bass.AP(ap.tensor, ap.offset, [list(x) for x in ap.ap] + [[0, n]])


def bc_mid(ap, n):
    a = [list(x) for x in ap.ap]
    return bass.AP(ap.tensor, ap.offset, [a[0], [0, n]] + a[1:])


SLOW = dict(allow_slow_non_contiguous=True)


def _prologue(self):
    P, I, T, S = self.P, self.I, self.T, self.S
    A = P.alloc
    mark = P.top
    TT = lambda eng, out, a, b, op, rd, wr: P.op(eng, lambda e: e.tensor_tensor(out, a, b, op), reads=rd, writes=wr)
    w_in_v = I["w_in"].rearrange("(kt p) c -> p kt c", p=128)
    for c in range(9):
        P.dma(S["wA"][c], w_in_v[:, :, WA_SRC[c]:WA_SRC[c] + 512], writes=[f"wA{c}"], q="pool")
    P.dma(T["wgt"], w_in_v[:, :, 2048:2056], writes=["wgt"], q="pool")
    P.dma(T["wq"], I["w_mlstm_q"].rearrange("h (kt p) d -> p h kt d", p=128), writes=["wq"], q="pool")
    P.dma(T["wk"], I["w_mlstm_k"].rearrange("h (kt p) d -> p h kt d", p=128), writes=["wk"], q="pool")
    wd_v = I["w_mlstm_down"].rearrange("(kt p) c -> p kt c", p=128)
    for c in range(2):
        P.dma(S["wD"][c], wd_v[:, :, 512 * c:512 * c + 512], writes=[f"wD{c}"], q="pool")
    wg_v = I["w_s5_glu"].rearrange("(kt p) c -> p kt c", p=128)
    for c in range(4):
        P.dma(S["wG"][c], wg_v[:, :, 512 * c:512 * c + 512], writes=[f"wG{c}"], q="pool")
    wu_v = I["w_ffn_up"].rearrange("(kt p) c -> p kt c", p=128)
    for c in range(11):
        P.dma(S["wU"][c], wu_v[:, :, 512 * c:512 * c + 512], writes=[f"wU{c}"], q="pool")
    colstg = [A([128, 128]) for _ in range(2)]
    mark2 = P.top
    cact = A([128, 8]); badaT = A([128, 48]); cw = A([128, 8, 4]); fcw = A([128, 22, 3])
    self.cl_i = 0
    for i_ in range(2):
        P.op("pool", lambda e, i_=i_: e.memset(colstg[i_], 0.0), writes=[f"colstg{i_}"])

    def load_cols(dst, vec, off, nt, dkey, dst_is_3d=None):
        sg = colstg[self.cl_i % 2]; sk = f"colstg{self.cl_i % 2}"; self.cl_i += 1
        P.dma(sg[0:nt, :], bass.AP(vec.tensor, off, [[128, nt], [1, 128]]), reads=[sk], writes=[sk])
        bk_, bkk = P.bank()
        P.op("pe", lambda e, sg=sg, bk_=bk_: e.transpose(bk_[:, 0:128], sg, T["ident_f"]), reads=[sk, "ident_f"], writes=[bkk])
        P.op("dve", lambda e, dst=dst, bk_=bk_, nt=nt: e.tensor_copy(dst, bk_[:, 0:nt] if dst_is_3d is None else dst_is_3d(bk_)), reads=[bkk], writes=[dkey])

    load_cols(cact, I["cvec"], 0, 8, "cact")
    load_cols(badaT, I["b_ada"], 0, 48, "badaT")
    c0 = 0
    for gi, (off, nt) in enumerate(BIN_GROUPS):
        load_cols(T["binT"][:, c0:c0 + nt], I["b_in"], off, nt, f"binT{gi}")
        c0 += nt
    P.dma(T["bgate"], self.ap_bcast(I["b_in"], 2048, 8), writes=["bgate"])
    load_cols(cw.rearrange("p c j -> p j c"), I["w_mlstm_conv"], 0, 32, "cw", dst_is_3d=lambda b_: b_[:, 0:32].rearrange("p (j c) -> p j c", c=8))
    load_cols(T["cb"], I["b_mlstm_conv"], 0, 8, "cb")
    load_cols(T["gainT"], I["mlstm_norm_gain"], 0, 8, "gainT")
    load_cols(fcw.rearrange("p t j -> p j t"), I["w_ffn_conv"], 0, 66, "fcw", dst_is_3d=lambda b_: b_[:, 0:66].rearrange("p (j t) -> p j t", t=22))
    load_cols(T["fcb"], I["b_ffn_conv"], 0, 22, "fcb")
    load_cols(T["dT"], I["s5_d"], 0, 4, "dT")
    self.load_cols = load_cols
    P.op("act", lambda e: e.activation(cact, cact, AF.Silu), reads=["cact"], writes=["cact"])
    cact2 = A([128, 8, 2])
    P.op("dve", lambda e: e.tensor_copy(cact2, bc(cact, 2)), reads=["cact"], writes=["cact2"])
    cactB = A([128, 8, 128])
    P.op("dve", lambda e: e.tensor_copy(cactB, bc(cact, 128)), reads=["cact"], writes=["cactB"])
    g1b = A([128, D]); g2b = A([128, D])
    P.dma(g1b, self.ap_bcast(I["b_ada"], 2 * D, D), writes=["g1b0", "g1b1"])
    P.dma(g2b, self.ap_bcast(I["b_ada"], 5 * D, D), writes=["g2b0", "g2b1"])
    stg = [A([128, 8, 512]) for _ in range(2)]
    wada_v = I["w_ada"].rearrange("(kt p) c -> p kt c", p=128)
    mb, mbk = P.bank()
    for c in range(12):
        sg, sk = stg[c % 2], f"stg{c % 2}"
        P.dma(sg, wada_v[:, :, 512 * c:512 * c + 512], writes=[sk])
        kind, half = c // 2, c % 2
        if kind in (2, 5):
            gb_, gk = (g1b, f"g1b{half}") if kind == 2 else (g2b, f"g2b{half}")
            bk_, bkk = P.bank()
            def f(e, sg=sg, bk_=bk_):
                for kt in range(8):
                    r = e.matmul(bk_[:, 0:512], cactB[:, kt, :], sg[:, kt, :], start=(kt == 0), stop=(kt == 7))
                return r
            P.op("pe", f, reads=[sk, "cactB"], writes=[bkk])
            dst = gb_[:, 512 * half:512 * half + 512]
            P.op("dve", lambda e, dst=dst, bk_=bk_: e.scalar_tensor_tensor(dst, bk_[:, 0:512], 1.0, dst, ALU.add, ALU.add), reads=[bkk, gk], writes=[gk])
        else:
            def f(e, sg=sg, kind=kind, half=half):
                for j in range(4):
                    ct = kind * 8 + half * 4 + j
                    for kt in range(8):
                        r = e.matmul(mb[:, 2 * ct:2 * ct + 2], sg[:, kt, 128 * j:128 * j + 128], cact2[:, kt, :], start=(kt == 0), stop=(kt == 7))
                return r
            P.op("pe", f, reads=[sk, "cact2"], writes=[mbk])
    P.op("pool", lambda e: e.memset(T["modT"], 0.0), writes=["modT0", "modT24"])
    for (a, b) in ((0, 16), (24, 40)):
        P.op("dve", lambda e, a=a, b=b: e.tensor_tensor(T["modT"][:, a:b], mb[:, 2 * a:2 * b:2], badaT[:, a:b], ALU.add), reads=[mbk, "badaT"], writes=[f"modT{a}"])
    for a, kk in ((8, "modT0"), (32, "modT24")):
        P.op("dve", lambda e, a=a: e.tensor_scalar(T["modT"][:, a:a + 8], T["modT"][:, a:a + 8], 1.0, None, ALU.add), reads=[kk], writes=[kk])
    self.dbg_dump("modT", T["modT"], ["modT0", "modT24"])
    self.dbg_dump("g1b", g1b, ["g1b0", "g1b1"])
    ob = [A([128, 8, 512], BF16) for _ in range(2)]
    wm_v = I["w_mix_out"].rearrange("(kt p) c -> p kt c", p=128)
    wf_v = I["w_ffn_down"].rearrange("(kt p) c -> p kt c", p=128)
    jobs = [(wm_v, 0, 8, hf, g1b, f"g1b{hf}", S["wM"][hf], f"wM{hf}") for hf in range(2)]
    for gi, (k0, nk) in enumerate(WF_GROUPS):
        for hf in range(2):
            jobs.append((wf_v, k0, nk, hf, g2b, f"g2b{hf}", S["wF"][gi * 2 + hf], f"wF{gi * 2 + hf}"))
    for ji, (src, k0, nk, hf, gt, gk, dst, dk) in enumerate(jobs):
        sg, sk = stg[ji % 2], f"stg{ji % 2}"
        o_, ok_ = ob[ji % 2], f"ob{ji % 2}"
        P.dma(sg[:, 0:nk, :], src[:, k0:k0 + nk, 512 * hf:512 * hf + 512], writes=[sk])
        eng = "dve" if ji % 2 == 0 else "pool"
        P.op(eng, lambda e, o_=o_, sg=sg, nk=nk, gt=gt, hf=hf: e.tensor_tensor(o_[:, 0:nk, :], sg[:, 0:nk, :], bc_mid(gt[:, 512 * hf:512 * hf + 512], nk), ALU.mult),
             reads=[sk, gk], writes=[ok_])
        P.dma(dst[:, 0:nk, :], o_[:, 0:nk, :], reads=[ok_], writes=[dk])
    n = 0
    for ct in range(8):
        for j in range(4):
            eng = "dve" if n % 2 == 0 else "pool"; n += 1
            P.op(eng, lambda e, ct=ct, j=j: e.tensor_scalar(T["cdiag"][:, ct, j, :], T["ident_f"], cw[:, ct, j:j + 1], None, ALU.mult),
                 reads=["cw", "ident_f"], writes=[f"cdiag{ct}_{j}"])
    fd = A([128, 22, 3, 128], BF16)
    for t in range(22):
        for j in range(3):
            eng = "dve" if n % 2 == 0 else "pool"; n += 1
            P.op(eng, lambda e, t=t, j=j: e.tensor_scalar(fd[:, t, j, :], T["ident_f"], fcw[:, t, j:j + 1], None, ALU.mult),
                 reads=["fcw", "ident_f"], writes=[f"fd{t}_{j}"])
    P.dma(S["fdiag"].rearrange("t p j d -> p t j d"), fd, reads=[f"fd{t}_{j}" for t in range(22) for j in range(3)], writes=["fdiag"])
    P.barrier()
    P.top = mark2
    self.s5_prologue()
    P.op("pool", lambda e: e.memset(T["C32"], 0.0), writes=["C32"])
    P.op("pool", lambda e: e.memset(T["Cb"], 0.0), writes=["Cb"])
    P.op("pool", lambda e: e.memset(T["xcar"], 0.0), writes=["xcar"])
    P.op("pool", lambda e: e.memset(T["fhalo"], 0.0), writes=["fhalo"])
    P.op("pool", lambda e: e.memset(T["xmh"], 0.0), writes=["xmh"])
    P.barrier()
    P.top = mark
    print("ops after prologue", len(P.ops))


Builder.prologue = _prologue


def _s5_prologue(self):
    P, I, T, S = self.P, self.I, self.T, self.S
    A = P.alloc
    V = "dve"

    def tk(t):
        return "tmp_" + t.tensor.name + str(t.offset)

    def tt(out, a, b, op, rd, wr, eng=V):
        P.op(eng, lambda e: e.tensor_tensor(out, a, b, op), reads=rd, writes=wr)

    def cmul(outr, outi, ar, ai, br, bi, t1, t2, rd, wr):
        k1, k2 = "tmp_" + t1.tensor.name + str(t1.offset), "tmp_" + t2.tensor.name + str(t2.offset)
        tt(t1, ar, br, ALU.mult, rd, [k1])
        tt(t2, ai, bi, ALU.mult, rd, [k2])
        tt(outr, t1, t2, ALU.subtract, [k1, k2], [wr + "r"])
        tt(t1, ar, bi, ALU.mult, rd + [wr + "r"], [k1])
        tt(t2, ai, br, ALU.mult, rd + [wr + "r"], [k2])
        tt(outi, t1, t2, ALU.add, [k1, k2], [wr + "i"])

    lamr = A([128, 16]); lami = A([128, 16]); ldt = A([128, 16]); dt = A([128, 16]); phi = A([128, 16]); aa = A([128, 16])
    cc = A([128, 16]); ss = A([128, 16]); t1 = A([128, 16]); t2 = A([128, 16])
    self.load_cols(lamr, I["s5_lam_re"], 0, 16, "lamr")
    self.load_cols(lami, I["s5_lam_im"], 0, 16, "lami")
    ldtb = A([128, 32])
    P.dma(ldtb, self.ap_bcast(I["s5_log_dt"], 0, 32), writes=["ldtb"])
    for h in range(2):
        P.op(V, lambda e, h=h: e.tensor_copy(ldt[64 * h:64 * h + 64, :], ldtb[64 * h:64 * h + 64, h:32:2]), reads=["ldtb"], writes=[f"ldt{h}"])
    def taylor_exp(out, x, deg, xk, ok, tmp):
        P.op(V, lambda e: e.tensor_scalar(out, x, 1.0 / deg, 1.0, ALU.mult, ALU.add), reads=xk, writes=[ok])
        for n_ in range(deg - 1, 0, -1):
            tt(tmp, x, out, ALU.mult, xk + [ok], [tk(tmp)])
            P.op(V, lambda e, n_=n_: e.tensor_scalar(out, tmp, 1.0 / n_, 1.0, ALU.mult, ALU.add), reads=[tk(tmp)], writes=[ok])

    P.op(V, lambda e: e.tensor_scalar(ldt, ldt, 1.0 / 16, None, ALU.mult), reads=["ldt0", "ldt1"], writes=["ldt0", "ldt1"])
    taylor_exp(dt, ldt, 10, ["ldt0", "ldt1"], "dt", t1)
    for _ in range(4):
        tt(t2, dt, dt, ALU.mult, ["dt"], [tk(t2)])
        P.op(V, lambda e: e.tensor_copy(dt, t2), reads=[tk(t2)], writes=["dt"])
    tt(phi, lami, dt, ALU.mult, ["lami", "dt"], ["phi"])
    tt(aa, lamr, dt, ALU.mult, ["lamr", "dt"], ["aa"])
    P.op("act", lambda e: e.activation(ss, phi, AF.Sin, scale=1.0 / 32), reads=["phi"], writes=["ss"])
    hp = A([128, 1])
    P.op("pool", lambda e: e.memset(hp, math.pi / 2), writes=["hp"])
    P.op("act", lambda e: e.activation(cc, phi, AF.Sin, scale=1.0 / 32, bias=hp), reads=["phi", "hp"], writes=["cc"])
    for it in range(5):
        tt(t1, cc, cc, ALU.mult, ["cc"], [tk(t1)])
        tt(t2, ss, ss, ALU.mult, ["ss"], [tk(t2)])
        P.op(V, lambda e: e.scalar_tensor_tensor(ss, cc, 2.0, ss, ALU.mult, ALU.mult), reads=["cc", "ss", tk(t2)], writes=["ss"])
        tt(cc, t1, t2, ALU.subtract, [tk(t1), tk(t2), "ss"], ["cc"])
    UPr = A([128, 9, 16]); UPi = A([128, 9, 16]); MG = A([128, 9, 16]); PWr = A([128, 9, 16]); PWi = A([128, 9, 16])
    P.op("pool", lambda e: e.memset(UPr[:, 0, :], 1.0), writes=["UP0r"])
    P.op("pool", lambda e: e.memset(UPi[:, 0, :], 0.0), writes=["UP0i"])
    P.op(V, lambda e: e.tensor_copy(UPr[:, 1, :], cc), reads=["cc"], writes=["UP1r"])
    P.op(V, lambda e: e.tensor_copy(UPi[:, 1, :], ss), reads=["ss"], writes=["UP1i"])
    for k in range(2, 9):
        cmul(UPr[:, k, :], UPi[:, k, :], UPr[:, k - 1, :], UPi[:, k - 1, :], UPr[:, 1, :], UPi[:, 1, :], t1, t2,
             [f"UP{k - 1}r", f"UP{k - 1}i", "UP1r", "UP1i"], f"UP{k}")
    P.op("pool", lambda e: e.memset(MG[:, 0, :], 1.0), writes=["MG0"])
    taylor_exp(MG[:, 1, :], aa, 7, ["aa"], "MG1", t1)
    for k in range(2, 9):
        tt(MG[:, k, :], MG[:, k - 1, :], MG[:, 1, :], ALU.mult, [f"MG{k - 1}", "MG1"], [f"MG{k}"])
    allup = [f"UP{k}{c}" for k in range(9) for c in "ri"] + [f"MG{k}" for k in range(9)]
    tt(PWr, MG, UPr, ALU.mult, allup, ["PWr"])
    tt(PWi, MG, UPi, ALU.mult, allup, ["PWi"])
    PW = ["PWr", "PWi"]
    den = A([128, 16]); am1 = A([128, 16]); zr = A([128, 16]); zi = A([128, 16])
    tt(t1, lamr, lamr, ALU.mult, ["lamr"] + PW, [tk(t1)])
    tt(t2, lami, lami, ALU.mult, ["lami"] + PW, [tk(t2)])
    tt(den, t1, t2, ALU.add, [tk(t1), tk(t2)], ["den"])
    P.op(V, lambda e: e.reciprocal(den, den), reads=["den"], writes=["den"])
    P.op(V, lambda e: e.tensor_scalar(am1, PWr[:, 1, :], -1.0, None, ALU.add), reads=PW, writes=["am1"])
    tt(t1, am1, lamr, ALU.mult, ["am1", "lamr", "den"], [tk(t1)])
    tt(t2, PWi[:, 1, :], lami, ALU.mult, PW + ["lami", "den"], [tk(t2)])
    tt(zr, t1, t2, ALU.add, [tk(t1), tk(t2)], ["zr"])
    tt(zr, zr, den, ALU.mult, ["zr", "den"], ["zr"])
    tt(t1, PWi[:, 1, :], lamr, ALU.mult, PW + ["lamr", "zr"], [tk(t1)])
    tt(t2, am1, lami, ALU.mult, ["am1", "lami", "zr"], [tk(t2)])
    tt(zi, t1, t2, ALU.subtract, [tk(t1), tk(t2)], ["zi"])
    tt(zi, zi, den, ALU.mult, ["zi", "den"], ["zi"])
    Bre = A([128, 16, 16]); Bim = A([128, 16, 16]); Cre = A([128, 16, 16]); Cim = A([128, 16, 16])
    b_ap = lambda v: bass.AP(v.tensor, 0, [[16, 128], [2048, 16], [1, 16]])
    P.dma(Bre, b_ap(I["s5_b_re"]), writes=["Bre"])
    P.dma(Bim, b_ap(I["s5_b_im"]), writes=["Bim"])
    Cl = A([128, 16, 128])
    P.op("pool", lambda e: e.memset(Cl, 0.0), writes=["Cl"])
    for nm, dstC in (("s5_c_re", Cre), ("s5_c_im", Cim)):
        for q in range(16):
            P.dma(Cl[0:16, q, :].rearrange("c (g p) -> c g p", p=64), bass.AP(I[nm].tensor, 2048 * q, [[64, 16], [1024, 2], [1, 64]]), reads=["Cl"], writes=[f"Cl{q}"])
        for b4 in range(4):
            bk_, bkk = P.bank()
            def f(e, bk_=bk_, b4=b4):
                for qq in range(4):
                    r = e.transpose(bk_[:, 128 * qq:128 * qq + 128], Cl[:, 4 * b4 + qq, :], T["ident_f"])
                return r
            P.op("pe", f, reads=[f"Cl{4 * b4 + qq}" for qq in range(4)] + ["ident_f"], writes=[bkk])
            P.op(V, lambda e, dstC=dstC, bk_=bk_, b4=b4: e.tensor_copy(dstC[:, 4 * b4:4 * b4 + 4, :], bk_[:, 0:512].rearrange("p (q x) -> p q x", x=128)[:, :, 0:16]),
                 reads=[bkk], writes=["C" + nm[-2:] + str(b4)])
    CK = [f"C{x}{b}" for x in ("re", "im") for b in range(4)]
    BBr = A([128, 16, 16]); BBi = A([128, 16, 16]); W1 = A([128, 16, 16]); W2 = A([128, 16, 16])
    cmul(BBr, BBi, bc(zr, 16), bc(zi, 16), Bre, Bim, W1, W2, ["zr", "zi", "Bre", "Bim"], "BB")
    PBr = A([128, 8, 16, 16]); PBi = A([128, 8, 16, 16])
    for k in range(8):
        cmul(PBr[:, k], PBi[:, k], bc(PWr[:, k, :], 16), bc(PWi[:, k, :], 16), BBr, BBi, W1, W2, PW + ["BBr", "BBi"], f"PB{k}")
    CBD = A([128, 2, 16, 32])
    P.op("pool", lambda e: e.memset(CBD, 0.0), writes=["CBD"])
    for h in range(2):
        sl = slice(64 * h, 64 * h + 64)
        P.op(V, lambda e, sl=sl, h=h: e.tensor_copy(CBD[sl, 0, :, 16 * h:16 * h + 16], Cre[sl]), reads=[f"Cre{b}" for b in range(4)] + ["CBD"], writes=["CBD"])
        P.op(V, lambda e, sl=sl, h=h: e.tensor_scalar(CBD[sl, 1, :, 16 * h:16 * h + 16], Cim[sl], -1.0, None, ALU.mult), reads=[f"Cim{b}" for b in range(4)] + ["CBD"], writes=["CBD"])
    Mb = [A([128, 4, 2, 16]) for _ in range(4)]
    for b in range(4):
        P.op("pool", lambda e, b=b: e.memset(Mb[b], 0.0), writes=[f"Mb{b}"])
    XwS = A([128, 4, 8, 2, 128], BF16)
    KcS = A([128, 4, 8, 128], BF16)
    dtmp = A([128, 128]); ktmp = A([128, 128])
    nb_ = 0
    for ct in range(4):
        for k in range(8):
            tau = 7 - k
            kb, kbk = P.bank()
            mfs = []
            for ri in range(2):
                m, mk = Mb[nb_ % 4], f"Mb{nb_ % 4}"; nb_ += 1
                src = (PBr if ri == 0 else PBi)
                for h in range(2):
                    sl = slice(64 * h, 64 * h + 64)
                    P.op(V if h == 0 else "pool", lambda e, m=m, sl=sl, h=h, src=src, k=k, ct=ct: e.tensor_copy(m[sl, :, h, :], src[sl, k, 4 * ct:4 * ct + 4, :]),
                         reads=[f"PB{k}r", f"PB{k}i", mk], writes=[mk])
                mf = m.rearrange("p a b c -> p (a b c)")
                mfs.append((mf, mk))
                P.op("pe", lambda e, mf=mf, kb=kb, ri=ri: e.transpose(kb[:, 128 + 128 * ri:256 + 128 * ri], mf, T["ident_f"]), reads=[mk, "ident_f"], writes=[kbk])
            for ri in range(2):
                mf, mk = mfs[ri]
                P.op("pe", lambda e, mf=mf, kb=kb, ri=ri, ct=ct: e.matmul(kb[:, 0:128], mf, CBD[:, ri, 4 * ct:4 * ct + 4, :].rearrange("p a b -> p (a b)"), start=(ri == 0), stop=(ri == 1)),
                     reads=[mk, "CBD"], writes=[kbk])
            P.op("act", lambda e, kb=kb, ct=ct, tau=tau: e.copy(XwS[:, ct, tau].rearrange("p a b -> p (a b)"), kb[:, 128:384]), reads=[kbk], writes=["XwS"])
            if k == 0:
                P.op(V, lambda e, ct=ct: e.tensor_scalar(dtmp, T["ident_f"], T["dT"][:, ct:ct + 1], None, ALU.mult), reads=["dT", "ident_f", "KcS"], writes=["dtmp"])
                P.op(V, lambda e, kb=kb: e.tensor_tensor(ktmp, kb[:, 0:128], T["bmask_f"], ALU.mult), reads=[kbk, "bmask_f", "KcS"], writes=["ktmp"])
                P.op(V, lambda e, ct=ct, k=k: e.tensor_tensor(KcS[:, ct, k, :], ktmp, dtmp, ALU.add), reads=["ktmp", "dtmp"], writes=["KcS"])
            else:
                P.op(V, lambda e, kb=kb, ct=ct, k=k: e.tensor_tensor(KcS[:, ct, k, :], kb[:, 0:128], T["bmask_f"], ALU.mult), reads=[kbk, "bmask_f"], writes=["KcS"])
    P.dma(S["s5K"], KcS, reads=["KcS"], writes=["s5K"])
    for c in range(2):
        P.dma(S["s5X"][c], XwS[:, 2 * c:2 * c + 2], reads=["XwS"], writes=[f"s5X{c}"])
    self.dbg_dump("KcS", KcS, ["KcS"], BF16)
    self.dbg_dump("XwS", XwS, ["XwS"], BF16)
    YwS = A([128, 16, 8, 2, 32], BF16)
    P.op("pool", lambda e: e.memset(YwS, 0.0), writes=["YwS"])
    YR = A([128, 16, 16]); YI = A([128, 16, 16])
    for tau in range(8):
        pr, pi = bc(PWr[:, tau + 1, :], 16), bc(PWi[:, tau + 1, :], 16)
        cmul(YR, YI, Cre, Cim, pr, pi, W1, W2, CK + PW + ["YwS"], "YY")
        for h in range(2):
            sl = slice(64 * h, 64 * h + 64)
            P.op(V, lambda e, sl=sl, h=h, tau=tau: e.tensor_copy(YwS[sl, :, tau, 0, 16 * h:16 * h + 16], YR[sl]), reads=["YYr", "YwS"], writes=["YwS"])
            P.op(V, lambda e, sl=sl, h=h, tau=tau: e.tensor_scalar(YwS[sl, :, tau, 1, 16 * h:16 * h + 16], YI[sl], -1.0, None, ALU.mult), reads=["YYi", "YwS"], writes=["YwS"])
    for c in range(2):
        P.dma(S["s5Y"][c], YwS[:, 8 * c:8 * c + 8], reads=["YwS"], writes=[f"s5Y{c}"])
    self.dbg_dump("YwS", YwS, ["YwS"], BF16)
    rc, rs = T["rotc"], T["rots"]
    P.op("pool", lambda e: e.memset(rc[:, :, 0], 1.0), writes=["rot"])
    P.op("pool", lambda e: e.memset(rs[:, :, 0], 0.0), reads=["rot"], writes=["rot"])
    P.op(V, lambda e: e.tensor_copy(T["u8"][:, 0, :], UPr[:, 8, :]), reads=["UP8r"], writes=["u8"])
    P.op(V, lambda e: e.tensor_copy(T["u8"][:, 1, :], UPi[:, 8, :]), reads=["UP8i", "u8"], writes=["u8"])
    r1r = A([128, 16]); r1i = A([128, 16]); wr_ = A([128, 16]); wi_ = A([128, 16])
    P.op(V, lambda e: e.tensor_copy(r1r, UPr[:, 8, :]), reads=["UP8r"], writes=["r1r"])
    P.op(V, lambda e: e.tensor_scalar(r1i, UPi[:, 8, :], -1.0, None, ALU.mult), reads=["UP8i"], writes=["r1i"])
    ln = 1
    R1 = A([128, 16, 32]); R2 = A([128, 16, 32])
    while ln < 64:
        cmul(wr_, wi_, rc[:, :, ln - 1], rs[:, :, ln - 1], r1r, r1i, t1, t2, ["rot", "r1r", "r1i"], "W")
        a_r, a_i = rc[:, :, 0:ln], rs[:, :, 0:ln]
        w_r, w_i = bc(wr_, ln), bc(wi_, ln)
        x1, x2 = R1[:, :, 0:ln], R2[:, :, 0:ln]
        tt(x1, a_r, w_r, ALU.mult, ["rot", "Wr", "Wi"], ["x1"])
        tt(x2, a_i, w_i, ALU.mult, ["rot", "Wr", "Wi"], ["x2"])
        tt(rc[:, :, ln:2 * ln], x1, x2, ALU.subtract, ["x1", "x2"], ["rot"])
        tt(x1, a_r, w_i, ALU.mult, ["rot", "Wr", "Wi"], ["x1"])
        tt(x2, a_i, w_r, ALU.mult, ["rot", "Wr", "Wi"], ["x2"])
        tt(rs[:, :, ln:2 * ln], x1, x2, ALU.add, ["x1", "x2", "rot"], ["rot"])
        ln *= 2
    P.op(V, lambda e: e.tensor_copy(T["rho8"], MG[:, 8, :]), reads=["MG8"], writes=["rho8"])
    for nb, d0 in self.d0.items():
        P.op(V, lambda e, d0=d0, nb=nb: e.tensor_copy(d0, bc(T["rho8"], nb)), reads=["rho8"], writes=[f"d0_{nb}"])
        P.op(V, lambda e, d0=d0: e.memset(d0[:, :, 0], 0.0), reads=[f"d0_{nb}"], writes=[f"d0_{nb}"])
    self.dbg_dump("rotc", rc, ["rot"])
    self.dbg_dump("rots", rs, ["rot"])


Builder.s5_prologue = _s5_prologue


def _main(self):
    P, I, T, S = self.P, self.I, self.T, self.S
    A = P.alloc
    NSM = max(self.state_ns + self.full_ns)
    NM = 128 * NSM
    NBM = 16 * NSM
    B = self.B = {}
    B["XR"] = [A([128, D]) for _ in range(NSM + 1)]
    B["xn"] = [A([128, D], BF16) for _ in range(2)]
    B["hT"] = A([128, 8, NM], BF16)
    B["hm"] = A([128, NSM, D], BF16)
    B["omS"] = A([128, 8, NM], BF16)
    B["prodT"] = A([128, 8, NM], BF16)
    B["yT"] = A([128, 8, NM], BF16)
    B["gtmp"] = A([128, 3, NM], BF16)
    B["uT"] = A([128, 4, NM], BF16)
    B["ysT"] = A([128, 4, NM], BF16)
    B["gsm"] = A([128, NSM, 48])
    B["lnt"] = A([128, 2, D])
    B["ring"] = [A([128, 8, 512], BF16) for _ in range(3)]
    B["smr"] = A([128, 512])
    self.sm_i = 0
    r0 = P.top
    B["xmT"] = A([128, 8, 3 + NM], BF16)
    B["xcT"] = A([128, 8, NM], BF16)
    B["QT"] = A([128, 4, NM], BF16)
    B["KT"] = A([128, 4, NM], BF16)
    B["Vx"] = A([128, NSM, 4, 258], BF16)
    B["KW"] = A([128, 2, 4, 128], BF16)
    B["SW"] = A([128, 2, 4, 128], BF16)
    r1 = P.top
    P.top = r0
    B["Xin"] = A([128, 2, 16, NBM])
    B["Zr"] = A([128, 16 * NBM]); B["Zi"] = A([128, 16 * NBM])
    B["Ta"] = A([128, 16 * NBM]); B["Tb"] = A([128, 16 * NBM])
    B["XS"] = A([128, 2, 16, NBM])
    B["xprev"] = A([128, 2, 16, NBM], BF16)
    r2 = P.top
    P.top = r0
    B["prodF"] = A([128, 22, NM], BF16)
    B["gpre"] = A([128, 2, 2 + NM], BF16)
    B["gact"] = A([128, 2, NM], BF16)
    r3 = P.top
    P.top = max(r1, r2, r3)
    self.ring_i = 0
    self.ring_tag = [None, None, None]
    self.ring_use = [0, 0, 0]
    self.ring_shape = [None, None, None]
    self.ring_clock = 0
    self.xr_i = 0
    tok = 0
    out_row = 0
    n_pre = len(self.state_ns) + (1 if self.skip_sub > 0 else 0)
    tiles = [("state", ns) for ns in self.state_ns] + [("full", ns) for ns in self.full_ns]
    skipped = 0
    for ti, (mode, ns) in enumerate(tiles):
        store = None
        if mode == "full":
            if skipped < self.skip_sub:
                assert ns <= self.skip_sub - skipped
                skipped += ns
            else:
                store = out_row
                out_row += 128 * ns
        self.tile(mode, tok, ns, store)
        tok += 128 * ns
        if ti == n_pre - 1:
            self.apply_flag()
    assert out_row == self.nout


def _sm(self, n):
    if self.sm_i + n > 512:
        self.sm_i = 0
    v = self.B["smr"][:, self.sm_i:self.sm_i + n]
    self.sm_i += n
    return v


def _wchunk(self, name, idx, shape=None, src=None):
    tag = (name, idx)
    self.ring_clock += 1
    for i in range(3):
        if self.ring_tag[i] == tag:
            self.ring_use[i] = self.ring_clock
            return self.chunk_view(i, shape if shape is not None else self.ring_shape[i])
    i = min(range(3), key=lambda k: self.ring_use[k])
    self.ring_tag[i] = tag
    self.ring_use[i] = self.ring_clock
    if src is None:
        src = self.S[name] if idx is None else self.S[name][idx]
    self.ring_shape[i] = list(src.shape)
    dst = self.chunk_view(i, list(src.shape))
    self.P.dma(dst, src)
    return self.chunk_view(i, shape if shape is not None else list(src.shape))


def _chunk_view(self, i, shape):
    slot = self.B["ring"][i]
    if shape is None or list(shape) == [128, 8, 512]:
        return slot
    flat = slot.rearrange("p a b -> p (a b)")
    n = 1
    for x in shape[1:]:
        n *= x
    v = flat[:, 0:n]
    if len(shape) == 3:
        return v.rearrange("p (a b) -> p a b", b=shape[2])
    if len(shape) == 4:
        return v.rearrange("p (a b c) -> p a b c", b=shape[2], c=shape[3])
    if len(shape) == 5:
        return v.rearrange("p (a b c d) -> p a b c d", b=shape[2], c=shape[3], d=shape[4])
    return v


def _apply_flag(self):
    P, T, B = self.P, self.T, self.B
    fl = T["flag"][:, 0:1]
    P.ts("dve", T["C32"], T["C32"], fl, None, ALU.mult)
    P.cp("pool", T["Cb"], T["C32"])
    P.ts("dve", T["xcar"], T["xcar"], fl, None, ALU.mult)
    P.ts("dve", T["xmh"], T["xmh"], fl, None, ALU.mult)
    P.ts("dve", T["fhalo"], T["fhalo"], fl, None, ALU.mult)


def _ln_stats(self, x):
    P, T = self.P, self.T
    st6 = self.sm(12).rearrange("p (a b) -> p a b", b=6)
    mv = self.sm(2); tmp = self.sm(1); rstd = self.sm(1); nmr = self.sm(1)
    P.op("dve", lambda e: e.bn_stats(st6[:, 0, :], x[:, 0:512]), [x[:, 0:512]], [st6[:, 0, :]])
    P.op("dve", lambda e: e.bn_stats(st6[:, 1, :], x[:, 512:1024]), [x[:, 512:1024]], [st6[:, 1, :]])
    P.op("dve", lambda e: e.bn_aggr(mv, st6.rearrange("p a b -> p (a b)")), [st6], [mv])
    P.ts("pool", tmp, mv[:, 1:2], LN_EPS, None, ALU.add)
    P.tt("pool", rstd, tmp, T["mhalf"][:, 0:1], ALU.pow)
    P.stt(nmr, mv[:, 0:1], -1.0, rstd, ALU.mult, ALU.mult)
    return rstd, nmr


def _ln_to_T(self, x, s, sc0, sh0, stats=None):
    P, T, B = self.P, self.T, self.B
    rstd, nmr = stats if stats is not None else self.ln_stats(x)
    xn = B["xn"][s % 2]
    P.act(xn, x, AF.Identity, bias=nmr, scale=rstd)
    bk, _ = P.bank()
    bb = bk[:, :].bitcast(BF16)
    P.trg([(bb[:, 128 * kt:128 * kt + 128], xn[:, 128 * kt:128 * kt + 128], T["ident_b"]) for kt in range(8)])
    for kt in range(8):
        P.act(B["hT"][:, kt, 128 * s:128 * s + 128], bb[:, 128 * kt:128 * kt + 128], AF.Identity,
              bias=T["modT"][:, sh0 + kt:sh0 + kt + 1], scale=T["modT"][:, sc0 + kt:sc0 + kt + 1])


def _inproj_tile(self, ptile, N):
    P, B = self.P, self.B
    w = self.wchunk("wA", ptile // 4)
    j = ptile % 4
    bk, _ = P.bank()
    P.mmg([(bk[:, 0:N], w[:, kt, 128 * j:128 * j + 128], B["hT"][:, kt, 0:N], kt == 0, kt == 7) for kt in range(8)])
    return bk


Builder.main = _main
Builder.sm = _sm
Builder.wchunk = _wchunk
Builder.chunk_view = _chunk_view
Builder.apply_flag = _apply_flag
Builder.ln_stats = _ln_stats
Builder.ln_to_T = _ln_to_T
Builder.inproj_tile = _inproj_tile


def _tile(self, mode, tok0, ns, store):
    P, I, T, S, B = self.P, self.I, self.T, self.S, self.B
    full = (mode == "full")
    N = 128 * ns
    nb = 16 * ns
    hT, xmT, xcT, QT, KT, Vx = B["hT"], B["xmT"], B["xcT"], B["QT"], B["KT"], B["Vx"]
    cs = lambda s: slice(128 * s, 128 * s + 128)
    xs = []
    for s in range(ns):
        x = B["XR"][self.xr_i]
        self.xr_i = (self.xr_i + 1) % len(B["XR"])
        xs.append(x)
        r0 = tok0 + 128 * s
        P.dma(x, I["xin"][r0:r0 + 128, :], q="pool")
    st_ = [self.ln_stats(xs[s]) for s in range(ns)]
    for s in range(ns):
        self.ln_to_T(xs[s], s, 8, 0, stats=st_[s])
    P.cp("pool", xmT[:, :, 0:3], T["xmh"])
    P.op("pool", lambda e: e.memset(Vx[:, 0:ns, :, 256:257], 1.0), [], [Vx[:, 0:ns, :, 256:257]])
    for ct in range(8):
        bk = self.inproj_tile(ct, N)
        P.act(xmT[:, ct, 3:3 + N], bk[:, 0:N], AF.Identity, bias=T["binT"][:, ct:ct + 1])
    G = B["gsm"]
    for s in range(ns):
        g = G[:, s, :]
        bk, _ = P.bank()
        P.mmg([(bk[:, 0:8], hT[:, kt, cs(s)], T["wgt"][:, kt, :], kt == 0, kt == 7) for kt in range(8)])
        P.tt("dve", g[:, 0:8], bk[:, 0:8], T["bgate"], ALU.add)
        P.act(g[:, 8:12], g[:, 4:8], AF.Exp, scale=-1.0)
        P.act(g[:, 12:16], g[:, 8:12], AF.Ln, bias=T["ones_f"][:, 0:1])
        b2, _ = P.bank()
        P.mmg([(b2[:, 0:4], T["tri_f"], g[:, 12:16], True, True), (b2[:, 4:8], T["ones_f"], g[:, 12:16], True, True)])
        P.tt("dve", g[:, 32:36], g[:, 0:4], b2[:, 0:4], ALU.add)
        P.tt("dve", g[:, 36:40], g[:, 32:36], b2[:, 4:8], ALU.subtract)
        P.act(g[:, 20:24], g[:, 36:40], AF.Exp)
        P.act(g[:, 24:28], b2[:, 4:8], AF.Exp, scale=-1.0)
        if full:
            P.act(g[:, 16:20], g[:, 32:36], AF.Exp)
            P.act(g[:, 28:32], b2[:, 0:4], AF.Exp)
    for ct in range(8):
        bk, _ = P.bank()
        P.mmg([(bk[:, 0:N], T["cdiag"][:, ct, j, :], xmT[:, ct, j:j + N], j == 0, j == 3) for j in range(4)])
        P.act(xcT[:, ct, 0:N], bk[:, 0:N], AF.Silu, bias=T["cb"][:, ct:ct + 1])
    if full:
        for h in range(4):
            bk, _ = P.bank()
            P.mmg([(bk[:, 0:N], T["wq"][:, h, kt, :], xcT[:, 2 * h + kt, 0:N], kt == 0, kt == 1) for kt in range(2)])
            P.op("act", lambda e, h=h, bk=bk: e.mul(QT[:, h, 0:N], bk[:, 0:N], DK ** -0.5), [bk[:, 0:N]], [QT[:, h, 0:N]])
            bk2, _ = P.bank()
            P.mmg([(bk2[:, 0:N], T["wk"][:, h, kt, :], xcT[:, 2 * h + kt, 0:N], kt == 0, kt == 1) for kt in range(2)])
            P.cp("dve", KT[:, h, 0:N], bk2[:, 0:N])
    for s in range(ns):
        g = G[:, s, :]
        par = s % 2
        KW, SW = B["KW"][:, par], B["SW"][:, par]
        kb, _ = P.bank()
        items = []
        for h in range(4):
            for kt in range(2):
                items.append((kb[:, 128 * h:128 * h + 128], xcT[:, 2 * h + kt, cs(s)], T["wk"][:, h, kt, :], kt == 0, kt == 1))
        P.mmg(items)
        P.tt("dve", KW, kb[:, 0:512].rearrange("p (h d) -> p h d", d=128), bc(g[:, 20:24], 128), ALU.mult)
        vb, _ = P.bank()
        vbb = vb[:, :].bitcast(BF16)
        P.trg([(vbb[:, 128 * ct:128 * ct + 128], xmT[:, ct, 3 + 128 * s:3 + 128 * s + 128], T["ident_b"]) for ct in range(8)])
        P.cp("act", Vx[:, s, :, 0:256], vbb[:, 0:1024].rearrange("p (h v) -> p h v", v=256))
        if full:
            sb_, _ = P.bank()
            P.mmg([(sb_[:, 128 * h:128 * h + 128], KT[:, h, cs(s)], QT[:, h, cs(s)], True, True) for h in range(4)])
            for h in range(4):
                P.stt(SW[:, h, :], sb_[:, 128 * h:128 * h + 128], g[:, 16 + h:17 + h], T["mask_b"], ALU.mult, ALU.mult)
            nbs = []
            for h in range(4):
                nbk, _ = P.bank()
                nbs.append(nbk)
                P.mmg([(nbk[:, 0:257], SW[:, h, :], Vx[:, s, h, 0:257], True, False),
                       (nbk[:, 0:257], QT[:, h, cs(s)], T["Cb"][:, h, 0:257], False, True)])
            a1 = self.sm(4); rd = self.sm(4); st6 = self.sm(24).rearrange("p (h b) -> p h b", b=6); mv = self.sm(8).rearrange("p (h b) -> p h b", b=2)
            t1 = self.sm(4); aa = self.sm(4); nbv = self.sm(4)
            for h in range(4):
                P.act(a1[:, h:h + 1], nbs[h][:, 256:257], AF.Abs)
            P.tt("dve", a1, a1, g[:, 28:32], ALU.max)
            P.op("dve", lambda e, rd=rd, a1=a1: e.reciprocal(rd, a1), [a1], [rd])
            for h in range(4):
                P.op("dve", lambda e, h=h, st6=st6, nbs=nbs: e.bn_stats(st6[:, h, :], nbs[h][:, 0:256]), [nbs[h][:, 0:256]], [st6[:, h, :]])
            for h in range(4):
                P.op("dve", lambda e, h=h, st6=st6, mv=mv: e.bn_aggr(mv[:, h, :], st6[:, h, :]), [st6[:, h, :]], [mv[:, h, :]])
            P.tt("pool", t1, mv[:, :, 1], rd, ALU.mult)
            P.tt("pool", t1, t1, rd, ALU.mult)
            P.ts("pool", t1, t1, LN_EPS, None, ALU.add)
            P.tt("pool", t1, t1, T["mhalf"], ALU.pow)
            P.tt("pool", aa, t1, rd, ALU.mult)
            P.stt(nbv, mv[:, :, 0], -1.0, aa, ALU.mult, ALU.mult)
            for h in range(4):
                P.act(B["hm"][:, s, 256 * h:256 * h + 256], nbs[h][:, 0:256], AF.Identity, bias=nbv[:, h:h + 1], scale=aa[:, h:h + 1])
        for h in range(4):
            cbk, _ = P.bank()
            P.mmg([(cbk[:, 0:257], KW[:, h, :], Vx[:, s, h, 0:257], True, True)])
            P.stt(T["C32"][:, h, 0:257], T["C32"][:, h, 0:257], g[:, 24 + h:25 + h], cbk[:, 0:257], ALU.mult, ALU.add)
            P.cp("pool", T["Cb"][:, h, 0:257], T["C32"][:, h, 0:257])
    P.cp("pool", T["xmh"], xmT[:, :, N:N + 3])
    self.tile_s5(full, ns, part=1)
    if full:
        self.tile_mix_out(ns, xs)
        self.tile_s5(full, ns, part=2)
        self.tile_post(ns, xs, store)


Builder.tile = _tile


def _tile_mix_out(self, ns, xs):
    P, T, B = self.P, self.T, self.B
    N = 128 * ns
    hm, omS, prodT, yT, gtmp = B["hm"], B["omS"], B["prodT"], B["yT"], B["gtmp"]
    for ct in range(8):
        bk = self.inproj_tile(8 + ct, N)
        P.act(omS[:, ct, 0:N], bk[:, 0:N], AF.Sigmoid, bias=T["binT"][:, 8 + ct:9 + ct])
    for vp in range(4):
        bk, _ = P.bank()
        bb = bk[:, :].bitcast(BF16)
        items = []
        for j in range(2):
            vt = 2 * vp + j
            for s in range(ns):
                items.append((bb[:, 512 * j + 128 * s:512 * j + 128 * s + 128], hm[:, s, 128 * vt:128 * vt + 128], T["ident_b"]))
        P.trg(items)
        for j in range(2):
            vt = 2 * vp + j
            P.stt(prodT[:, vt, 0:N], bb[:, 512 * j:512 * j + N], T["gainT"][:, vt:vt + 1], omS[:, vt, 0:N], ALU.mult, ALU.mult)
    for dt_ in range(8):
        w = self.wchunk("wD", dt_ // 4)
        j = dt_ % 4
        bk, _ = P.bank()
        P.mmg([(bk[:, 0:N], w[:, kt, 128 * j:128 * j + 128], prodT[:, kt, 0:N], kt == 0, kt == 7) for kt in range(8)])
        b2 = self.inproj_tile(20 + dt_, N)
        P.act(gtmp[:, 0, 0:N], b2[:, 0:N], AF.Sigmoid, bias=T["binT"][:, 20 + dt_:21 + dt_])
        P.tt("dve", yT[:, dt_, 0:N], bk[:, 0:N], gtmp[:, 0, 0:N], ALU.mult)


def _tile_s5(self, full, ns, part=1):
    P, T, B = self.P, self.T, self.B
    N = 128 * ns
    nb = 16 * ns
    uT, ysT, yT, gtmp = B["uT"], B["ysT"], B["yT"], B["gtmp"]
    xp = B["xprev"]
    if part == 2:
        return self.tile_s5_out(ns)
    for ct in range(4):
        bk = self.inproj_tile(16 + ct, N)
        P.act(uT[:, ct, 0:N], bk[:, 0:N], AF.Identity, bias=T["binT"][:, 16 + ct:17 + ct])
    xb = [P.bank()[0] for _ in range(4)]
    for ct in range(4):
        w = self.wchunk("s5X", ct // 2)
        for q in range(4):
            items = []
            kw = dict(tile_position=(96, 0)) if q == 3 else {}
            for ri in range(2):
                c0 = (ct * 2 + ri) * nb
                for tau in range(8):
                    items.append((xb[q][:, c0:c0 + nb], w[32 * q:32 * q + 32, ct % 2, tau, ri, :], uT[32 * q:32 * q + 32, ct, tau:N:8], tau == 0, tau == 7, kw))
            P.mmg(items)
    Xin = B["Xin"]
    for q in range(4):
        src = xb[q][:, 0:8 * nb].rearrange("p (c r n) -> p r c n", c=4, r=2)
        P.cp("act" if q % 2 == 0 else "dve", Xin[:, :, q:16:4, 0:nb], src)
    cj, sj = T["rotc"][:, :, 0:nb], T["rots"][:, :, 0:nb]
    v3 = lambda t: t[:, 0:16 * nb].rearrange("p (q n) -> p q n", n=nb)
    Zr, Zi, Ta, Tb = v3(B["Zr"]), v3(B["Zi"]), v3(B["Ta"]), v3(B["Tb"])
    Xr, Xi = Xin[:, 0, :, 0:nb], Xin[:, 1, :, 0:nb]
    P.tt("pool", Ta, cj, Xr, ALU.mult); P.tt("pool", Tb, sj, Xi, ALU.mult); P.tt("dve", Zr, Ta, Tb, ALU.subtract)
    P.tt("pool", Ta, cj, Xi, ALU.mult); P.tt("pool", Tb, sj, Xr, ALU.mult); P.tt("dve", Zi, Ta, Tb, ALU.add)
    xc = T["xcar"]; u8 = T["u8"]
    i_r = self.sm(16); i_i = self.sm(16); ta = self.sm(16); tb = self.sm(16)
    P.tt("dve", ta, u8[:, 0, :], xc[:, 0, :], ALU.mult); P.tt("dve", tb, u8[:, 1, :], xc[:, 1, :], ALU.mult); P.tt("dve", i_r, ta, tb, ALU.subtract)
    P.tt("dve", ta, u8[:, 0, :], xc[:, 1, :], ALU.mult); P.tt("dve", tb, u8[:, 1, :], xc[:, 0, :], ALU.mult); P.tt("dve", i_i, ta, tb, ALU.add)
    P.tt("dve", i_r, i_r, T["rho8"], ALU.mult); P.tt("dve", i_i, i_i, T["rho8"], ALU.mult)
    P.tt("dve", Zr[:, :, 0], Zr[:, :, 0], i_r, ALU.add); P.tt("dve", Zi[:, :, 0], Zi[:, :, 0], i_i, ALU.add)
    d0 = self.d0[nb].rearrange("p q n -> p (q n)")
    fr, fi = B["Ta"][:, 0:16 * nb], B["Tb"][:, 0:16 * nb]
    P.op("dve", lambda e: e.tensor_tensor_scan(fr, d0, B["Zr"][:, 0:16 * nb], 0.0, ALU.mult, ALU.add), [d0, B["Zr"][:, 0:16 * nb]], [fr])
    P.op("dve", lambda e: e.tensor_tensor_scan(fi, d0, B["Zi"][:, 0:16 * nb], 0.0, ALU.mult, ALU.add), [d0, B["Zi"][:, 0:16 * nb]], [fi])
    XS = B["XS"]
    xr_o, xi_o = XS[:, 0, :, 0:nb], XS[:, 1, :, 0:nb]
    P.tt("pool", Zr, cj, Ta, ALU.mult); P.tt("pool", Zi, sj, Tb, ALU.mult); P.tt("dve", xr_o, Zr, Zi, ALU.add)
    P.tt("pool", Zr, cj, Tb, ALU.mult); P.tt("pool", Zi, sj, Ta, ALU.mult); P.tt("dve", xi_o, Zr, Zi, ALU.subtract)
    xp = B["xprev"]
    if full:
        P.cp("pool", xp[:, :, :, 0], xc)
        if nb > 1:
            P.cp("pool", xp[:, :, :, 1:nb], XS[:, :, :, 0:nb - 1])
    P.cp("dve", xc, XS[:, :, :, nb - 1])


def _tile_s5_out(self, ns):
    P, T, B = self.P, self.T, self.B
    N = 128 * ns
    nb = 16 * ns
    uT, ysT, yT, gtmp = B["uT"], B["ysT"], B["yT"], B["gtmp"]
    xp = B["xprev"]
    kc = None
    for ct in range(4):
        kc = self.wchunk("s5K", None)
        yb, _ = P.bank()
        items = []
        for tp in range(8):
            for tau in range(tp, 8):
                items.append((yb[:, tau:N:8], kc[:, ct, tp, :], uT[:, ct, tau - tp:N:8], (tp == 0 and tau == 0), False, dict(skip_group_check=True)))
        P.mmg(items)
        items = []
        yw = self.wchunk("s5Y", ct // 2)
        for q in range(4):
            for tau in range(8):
                for ri in range(2):
                    last = (q == 3 and tau == 7 and ri == 1)
                    items.append((yb[32 * q:32 * q + 32, tau:N:8], yw[:, (4 * ct + q) % 8, tau, ri, :], xp[:, ri, 4 * ct + q, 0:nb],
                                  False, last, dict(tile_position=(0, 32 * q), skip_group_check=True)))
        P.mmg(items)
        P.act(ysT[:, ct, 0:N], yb[:, 0:N], AF.Gelu_apprx_tanh)
    for dt_ in range(8):
        j = dt_ % 4
        wv = self.wchunk("wG", dt_ // 4)
        bv, _ = P.bank()
        P.mmg([(bv[:, 0:N], wv[:, kt, 128 * j:128 * j + 128], ysT[:, kt, 0:N], kt == 0, kt == 3) for kt in range(4)])
        wg = self.wchunk("wG", 2 + dt_ // 4)
        bg, _ = P.bank()
        P.mmg([(bg[:, 0:N], wg[:, kt, 128 * j:128 * j + 128], ysT[:, kt, 0:N], kt == 0, kt == 3) for kt in range(4)])
        P.act(gtmp[:, 1, 0:N], bg[:, 0:N], AF.Sigmoid)
        P.tt("dve", gtmp[:, 2, 0:N], bv[:, 0:N], gtmp[:, 1, 0:N], ALU.mult)
        b2 = self.inproj_tile(28 + dt_, N)
        P.act(gtmp[:, 0, 0:N], b2[:, 0:N], AF.Sigmoid, bias=T["binT"][:, 28 + dt_:29 + dt_])
        P.tt("dve", gtmp[:, 2, 0:N], gtmp[:, 2, 0:N], gtmp[:, 0, 0:N], ALU.mult)
        P.tt("dve", yT[:, dt_, 0:N], yT[:, dt_, 0:N], gtmp[:, 2, 0:N], ALU.add)


Builder.tile_mix_out = _tile_mix_out
Builder.tile_s5 = _tile_s5
Builder.tile_s5_out = _tile_s5_out


def _post_ln(self, x, gb_key):
    P, B = self.P, self.B
    rstd, nmr = gb_key
    P.act(x, x, AF.Identity, bias=nmr, scale=rstd)
    P.tt("dve", x, x, B["lnt"][:, 0, :], ALU.mult)
    P.tt("pool", x, x, B["lnt"][:, 1, :], ALU.add)


def _tile_post(self, ns, xs, store):
    P, I, T, S, B = self.P, self.I, self.T, self.S, self.B
    N = 128 * ns
    yT, hT, prodF = B["yT"], B["hT"], B["prodF"]
    cs = lambda s: slice(128 * s, 128 * s + 128)
    lnt = B["lnt"]
    P.dma(lnt[:, 0, :], self.ap_bcast(I["ln1_gain"], 0, D))
    P.dma(lnt[:, 1, :], self.ap_bcast(I["ln1_bias"], 0, D))
    for hf in range(2):
        w = self.wchunk("wM", hf)
        for s in range(ns):
            bk, _ = P.bank()
            P.mmg([(bk[:, 0:512], yT[:, kt, cs(s)], w[:, kt, :], kt == 0, kt == 7) for kt in range(8)])
            xh = xs[s][:, 512 * hf:512 * hf + 512]
            P.stt(xh, xh, ALPHA, bk[:, 0:512], ALU.mult, ALU.add)
    st_ = [self.ln_stats(xs[s]) for s in range(ns)]
    for s in range(ns):
        self.post_ln(xs[s], st_[s])
    st_ = [self.ln_stats(xs[s]) for s in range(ns)]
    for s in range(ns):
        self.ln_to_T(xs[s], s, 32, 24, stats=st_[s])
    fh = T["fhalo"]
    gpre, gact = B["gpre"], B["gact"]
    pend = None

    def conv_stage(t, par, bv):
        fd = self.wchunk("fdiag", t // 10, src=S["fdiag"][10 * (t // 10):min(22, 10 * (t // 10) + 10)].rearrange("t p j d -> p t j d"))
        tl = t % 10
        bc_, _ = P.bank()
        P.mmg([(bc_[:, 0:N], fd[:, tl, j, :], gpre[:, par, j:j + N], j == 0, j == 2) for j in range(3)])
        P.act(gact[:, par, 0:N], bc_[:, 0:N], AF.Gelu_apprx_tanh, bias=T["fcb"][:, t:t + 1])
        P.tt("dve", prodF[:, t, 0:N], bv[:, 0:N], gact[:, par, 0:N], ALU.mult)

    for t in range(22):
        par = t % 2
        wv = self.wchunk("wU", t // 4)
        bv, _ = P.bank()
        P.mmg([(bv[:, 0:N], wv[:, kt, 128 * (t % 4):128 * (t % 4) + 128], hT[:, kt, 0:N], kt == 0, kt == 7) for kt in range(8)])
        gt_ = 22 + t
        wg = self.wchunk("wU", gt_ // 4)
        bg, _ = P.bank()
        P.mmg([(bg[:, 0:N], wg[:, kt, 128 * (gt_ % 4):128 * (gt_ % 4) + 128], hT[:, kt, 0:N], kt == 0, kt == 7) for kt in range(8)])
        P.cp("pool", gpre[:, par, 0:2], fh[:, t, :])
        P.cp("act", gpre[:, par, 2:2 + N], bg[:, 0:N])
        P.cp("pool", fh[:, t, :], gpre[:, par, N:N + 2])
        if pend is not None:
            conv_stage(*pend)
        pend = (t, par, bv)
    conv_stage(*pend)
    P.dma(lnt[:, 0, :], self.ap_bcast(I["ln2_gain"], 0, D))
    P.dma(lnt[:, 1, :], self.ap_bcast(I["ln2_bias"], 0, D))
    for hf in range(2):
        acc = [P.bank()[0] for _ in range(ns)]
        for gi, (k0, nk) in enumerate(WF_GROUPS):
            w = self.wchunk("wF", gi * 2 + hf, src=S["wF"][gi * 2 + hf][:, 0:nk, :])
            for s in range(ns):
                P.mmg([(acc[s][:, 0:512], prodF[:, k0 + kk, cs(s)], w[:, kk, :], (gi == 0 and kk == 0), (gi == 2 and kk == nk - 1)) for kk in range(nk)])
        for s in range(ns):
            xh = xs[s][:, 512 * hf:512 * hf + 512]
            P.stt(xh, xh, ALPHA, acc[s][:, 0:512], ALU.mult, ALU.add)
    st_ = [self.ln_stats(xs[s]) for s in range(ns)]
    for s in range(ns):
        self.post_ln(xs[s], st_[s])
        if store is not None:
            P.dma(self.yout[store + 128 * s:store + 128 * s + 128, :], xs[s], q="pool")
    self.dbg_tile = True


Builder.post_ln = _post_ln
Builder.tile_post = _tile_post


_CACHE = {}


def _consts():
    bm = np.zeros((128, 128), np.float32)
    for q in range(4):
        bm[32 * q:32 * q + 32, 32 * q:32 * q + 32] = 1.0
    return np.eye(128, dtype=np.float32), np.triu(np.ones((128, 128), np.float32)), bm


def core_map(inputs, b, xin, flag):
    ident, tri, bm = _consts()
    m = {"xin": np.ascontiguousarray(xin, dtype=np.float32), "cvec": np.ascontiguousarray(inputs["c"][b], dtype=np.float32),
         "flagv": np.full((128, 1), flag, np.float32), "c_ident": ident, "c_tri": tri, "c_bmask": bm}
    for k, v in inputs.items():
        if k in ("x", "c"):
            continue
        m[k] = np.ascontiguousarray(np.asarray(v)[0], dtype=np.float32)
    return m


def kernel(**inputs):
    inputs = {k: np.asarray(v) for k, v in inputs.items()}
    x = inputs["x"]
    Bn, Sq, _ = x.shape
    half = Sq // 2
    n_state = (half - 128) // 128
    state_ns = [4] * (n_state // 4) + ([n_state % 4] if n_state % 4 else [])
    full_ns = [1] + [4] * (half // 512)
    key = (tuple(state_ns), tuple(full_ns))
    if key not in _CACHE:
        bld = Builder(state_ns, full_ns, 1)
        _CACHE[key] = bld.build()
    nc = _CACHE[key]
    in_maps = []
    for core in range(2 * Bn):
        b, h = core // 2, core % 2
        if h == 0:
            xin = np.concatenate([np.zeros((half, D), np.float32), x[b, :half]], axis=0)
        else:
            xin = x[b]
        in_maps.append(core_map(inputs, b, xin, float(h)))
    res = run_bass_kernel_spmd(nc, in_maps, core_ids=list(range(2 * Bn)))
    out = np.empty((Bn, Sq, D), np.float32)
    for core in range(2 * Bn):
        b, h = core // 2, core % 2
        out[b, h * half:(h + 1) * half] = res.results[core]["yout"]
    return out
```

```python
import math
from contextlib import ExitStack
import numpy as np
import concourse.bass as bass
import concourse.mybir as mybir
from concourse.bass_utils import run_bass_kernel_spmd

F32 = mybir.dt.float32
BF16 = mybir.dt.bfloat16
AF = mybir.ActivationFunctionType
ALU = mybir.AluOpType
AX = mybir.AxisListType

N_DMA_SEMS = 24
COMPUTE = ("pe", "act", "dve", "pool")
ENG = {"pe": "tensor", "act": "scalar", "dve": "vector", "pool": "gpsimd", "sp": "sync"}

D = 1024
NH = 4
DV = 256
DK = 128
S5W = 512
FH = 2816
INW = 4616
NKT = 8
ALPHA = 2.0 ** 0.25
LN_EPS = 1e-5
TB = 8
ARENA_F32 = 50688


class Prog:
    def __init__(self, nc, stack):
        self.nc = nc
        self.stack = stack
        self.ops = []
        self.arena = stack.enter_context(nc.sbuf_tensor("arena", [128, ARENA_F32], F32))
        self.top = 0
        self.peak = 0
        self.banks = [stack.enter_context(nc.psum_tensor(f"bank{i}", [128, 512], F32)) for i in range(8)]
        self.bank_i = 0
        self.uid = 0

    def alloc(self, shape, dtype=F32):
        n = 1
        for s in shape[1:]:
            n *= s
        words = (n + 1) // 2 if dtype == BF16 else n
        words = (words + 7) // 8 * 8
        a = self.top
        self.top += words
        self.peak = max(self.peak, self.top)
        assert self.top <= ARENA_F32, f"SBUF arena overflow {self.top}"
        v = self.arena[:, a:a + words]
        if dtype == BF16:
            v = v.bitcast(BF16)
        v = v[:, 0:n]
        if len(shape) == 3:
            v = v.rearrange("p (a b) -> p a b", b=shape[2])
        elif len(shape) == 4:
            v = v.rearrange("p (a b c) -> p a b c", b=shape[2], c=shape[3])
        elif len(shape) == 5:
            v = v.rearrange("p (a b c d) -> p a b c d", b=shape[2], c=shape[3], d=shape[4])
        return v[0:shape[0]] if shape[0] != 128 else v

    def key(self, prefix="k"):
        self.uid += 1
        return f"{prefix}{self.uid}"

    def bank(self):
        i = self.bank_i
        self.bank_i = (i + 1) % 8
        return self.banks[i], f"bank{i}"

    def op(self, eng, fn, reads=(), writes=()):
        self.ops.append(dict(eng=eng, fn=fn, reads=tuple(reads), writes=tuple(writes), dma=False, bar=False))

    def dma(self, out, in_, reads=(), writes=(), q="sp", **kw):
        def fn(e, out=out, in_=in_, kw=kw):
            return e.dma_start(out=out, in_=in_, **kw)
        rd, wr = list(reads), list(writes)
        for ap, lst in ((in_, rd), (out, wr)):
            if isinstance(ap, bass.AP) and ap.tensor.name == "arena":
                lst.append(ap)
        self.ops.append(dict(eng=q, fn=fn, reads=tuple(rd), writes=tuple(wr), dma=True, bar=False))

    def barrier(self):
        self.ops.append(dict(eng=None, fn=None, reads=(), writes=(), dma=False, bar=True))

    @staticmethod
    def _res(x):
        if isinstance(x, str):
            return ("key", x)
        name = x.tensor.name
        if name != "arena":
            return ("key", name)
        esz = 2 if x.dtype == BF16 else 4
        aps = x.ap
        pstride = ARENA_F32 * 4 // esz
        off = x.offset
        p0 = off // pstride
        lo = (off % pstride) * esz
        if aps[0][0] == 0:
            npart = 1
        else:
            npart = aps[0][1]
        span = 1
        for (st_, cnt) in aps[1:]:
            span += (cnt - 1) * abs(st_)
        return ("box", p0, p0 + npart, lo, lo + span * esz)

    def mmg(self, items, extra_reads=()):
        def f(e, items=items):
            for it in items:
                kw = it[5] if len(it) > 5 else {}
                ins = e.matmul(it[0], it[1], it[2], start=it[3], stop=it[4], **kw)
            return ins
        rd, wr = list(extra_reads), []
        for it in items:
            rd += [it[1], it[2]]
            wr.append(it[0])
        self.op("pe", f, rd, wr)

    def trg(self, items):
        def f(e, items=items):
            for (o, i_, idn) in items:
                ins = e.transpose(o, i_, idn)
            return ins
        self.op("pe", f, [x for it in items for x in (it[1], it[2])], [it[0] for it in items])

    def act(self, out, in_, func, bias=None, scale=1.0):
        rd = [in_] + [x for x in (bias, scale) if isinstance(x, bass.AP)]
        kw = {} if bias is None else {"bias": bias}
        self.op("act", lambda e: e.activation(out, in_, func, scale=scale, **kw), rd, [out])

    def tt(self, eng, out, a, b, op):
        self.op(eng, lambda e: e.tensor_tensor(out, a, b, op), [a, b], [out])

    def ts(self, eng, out, a, s1, s2, op0, op1=None):
        rd = [a] + [x for x in (s1, s2) if isinstance(x, bass.AP)]
        if op1 is None:
            self.op(eng, lambda e: e.tensor_scalar(out, a, s1, s2, op0), rd, [out])
        else:
            self.op(eng, lambda e: e.tensor_scalar(out, a, s1, s2, op0, op1), rd, [out])

    def stt(self, out, a, sc, b, op0, op1):
        rd = [a, b] + ([sc] if isinstance(sc, bass.AP) else [])
        self.op("dve", lambda e: e.scalar_tensor_tensor(out, a, sc, b, op0, op1), rd, [out])

    def cp(self, eng, out, in_):
        if eng == "act":
            self.op("act", lambda e: e.copy(out, in_), [in_], [out])
        else:
            self.op(eng, lambda e: e.tensor_copy(out, in_), [in_], [out])

    def view(self, a, shape, dtype=F32):
        top = self.top
        self.top = a
        v = self.alloc(shape, dtype)
        used = self.top
        self.top = max(top, used)
        return v

    def emit(self, final_wait_eng="sp"):
        nc = self.nc
        ops = self.ops
        n = len(ops)
        last_w, readers = {}, {}
        boxes = []
        deps = [set() for _ in ops]
        since_bar = []
        pending = {}
        for i, o in enumerate(ops):
            if o["bar"]:
                last_per_eng, dmas = {}, []
                for p in since_bar:
                    po = ops[p]
                    if po["dma"]:
                        dmas.append(p)
                    else:
                        last_per_eng[po["eng"]] = p
                pre = set(last_per_eng.values()) | set(dmas)
                for e in list(COMPUTE) + ["sp"]:
                    pending[e] = set(pre) | pending.get(e, set())
                since_bar = []
                last_w, readers, boxes = {}, {}, []
                continue
            d = set()
            rres = [self._res(x) for x in o["reads"]]
            wres = [self._res(x) for x in o["writes"]]
            for r in rres:
                if r[0] == "key":
                    if r[1] in last_w:
                        d.add(last_w[r[1]])
                    if r[1].startswith("bank"):
                        for r_ in readers.get(r[1], ()):
                            if ops[r_]["eng"] != o["eng"]:
                                d.add(r_)
                else:
                    _, p0, p1, lo, hi = r
                    for bx in boxes:
                        if bx[5] and bx[1] < p1 and p0 < bx[2] and bx[3] < hi and lo < bx[4]:
                            d.add(bx[0])
            for w in wres:
                if w[0] == "key":
                    if w[1] in last_w:
                        d.add(last_w[w[1]])
                    for r_ in readers.get(w[1], ()):
                        d.add(r_)
                else:
                    _, p0, p1, lo, hi = w
                    for bx in boxes:
                        if bx[1] < p1 and p0 < bx[2] and bx[3] < hi and lo < bx[4]:
                            d.add(bx[0])
            for p in d:
                if p == i:
                    continue
                po = ops[p]
                if not po["dma"] and not o["dma"] and po["eng"] == o["eng"] and o["eng"] == "pe":
                    continue
                deps[i].add(p)
            if pending.get(o["eng"]):
                for p in pending[o["eng"]]:
                    po = ops[p]
                    if (not po["dma"]) and (not o["dma"]) and po["eng"] == o["eng"]:
                        continue
                    deps[i].add(p)
                pending[o["eng"]] = set()
            ek = ("dma", i) if o["dma"] else o["eng"]
            for w in wres:
                if w[0] == "key":
                    last_w[w[1]] = i
                    readers[w[1]] = []
                else:
                    _, p0, p1, lo, hi = w
                    boxes = [bx for bx in boxes if not (p0 <= bx[1] and bx[2] <= p1 and lo <= bx[3] and bx[4] <= hi)]
                    boxes.append([i, p0, p1, lo, hi, True, ek])
            for r in rres:
                if r[0] == "key":
                    readers.setdefault(r[1], []).append(i)
                else:
                    _, p0, p1, lo, hi = r
                    boxes = [bx for bx in boxes if not (not bx[5] and bx[6] == ek and bx[1] == p0 and bx[2] == p1 and bx[3] == lo and bx[4] == hi)]
                    boxes.append([i, p0, p1, lo, hi, False, ek])
            since_bar.append(i)
        needed = set()
        for i in range(n):
            needed |= deps[i]
        sig_no, cnt = {}, {e: 0 for e in COMPUTE}
        dma_slot, dma_tot, ndma = {}, [0] * N_DMA_SEMS, 0
        nq = {"sp": 0, "pool": 0, "act": 0}
        NSP = N_DMA_SEMS - 8
        for i, o in enumerate(ops):
            if o["bar"]:
                continue
            if o["dma"]:
                if o["eng"] == "pool":
                    s = NSP + nq["pool"] % 8
                    nq["pool"] += 1
                else:
                    s = nq["sp"] % NSP
                    nq["sp"] += 1
                ndma += 1
                prev = dma_tot[s]
                dma_tot[s] += 16
                dma_slot[i] = (s, prev, dma_tot[s])
            elif i in needed:
                cnt[o["eng"]] += 1
                sig_no[i] = cnt[o["eng"]]
        st = self.stack
        csem = {e: st.enter_context(nc.semaphore(f"s_{e}")) for e in COMPUTE}
        dsem = [st.enter_context(nc.semaphore(f"s_dma{j}")) for j in range(N_DMA_SEMS)]
        used = sorted({o["eng"] for o in ops if not o["bar"]} | {final_wait_eng})
        with nc.Block() as block:
            for ename in used:
                def body(e, ename=ename):
                    seen = {c: 0 for c in csem}
                    seen_dma = [0] * N_DMA_SEMS
                    for i, o in enumerate(ops):
                        if o["bar"] or o["eng"] != ename:
                            continue
                        for p in sorted(deps[i]):
                            po = ops[p]
                            if po["dma"]:
                                s, _, tgt = dma_slot[p]
                                if seen_dma[s] < tgt:
                                    e.wait_ge(dsem[s], tgt)
                                    seen_dma[s] = tgt
                            else:
                                pe_ = po["eng"]
                                nn = sig_no[p]
                                if seen[pe_] < nn:
                                    e.wait_ge(csem[pe_], nn)
                                    seen[pe_] = nn
                        if o["dma"]:
                            s, prev, tgt = dma_slot[i]
                            if prev > 0 and seen_dma[s] < prev:
                                e.wait_ge(dsem[s], prev)
                                seen_dma[s] = prev
                            o["fn"](e).then_inc(dsem[s], 16)
                        else:
                            ins = o["fn"](e)
                            if i in sig_no:
                                ins.then_inc(csem[ename], 1)
                    if ename == final_wait_eng:
                        for s in range(N_DMA_SEMS):
                            if dma_tot[s] > seen_dma[s]:
                                e.wait_ge(dsem[s], dma_tot[s])
                getattr(block, ENG[ename])(body)
        return dict(n_ops=n, sig=cnt, ndma=ndma, peak_kb=self.peak * 4 / 1024)


WA_SRC = [0, 512, 1024, 1536, 2056, 2568, 3080, 3592, 4104]
BIN_GROUPS = [(0, 8), (1024, 8), (2056, 4), (2568, 8), (3592, 8)]
WF_GROUPS = [(0, 8), (8, 8), (16, 6)]


def dram_in(nc, name, shape, dtype=F32):
    return nc.dram_tensor(name, list(shape), dtype, kind="ExternalInput").ap()


class Builder:
    def __init__(self, state_ns, full_ns, skip_sub, dbg=()):
        self.state_ns = list(state_ns)
        self.full_ns = list(full_ns)
        self.skip_sub = skip_sub
        self.dbg = set(dbg)
        self.ntok = 128 * (sum(state_ns) + sum(full_ns))
        self.nout = 128 * (sum(full_ns) - skip_sub)
        self.nc = bass.Bass("TRN2", target_bir_lowering=False)
        self.dbg_out = {}

    def dbg_dump(self, name, ap, keys, dtype=F32):
        if name not in self.dbg:
            return
        shape = list(ap.shape)
        o = self.nc.dram_tensor("dbg_" + name, shape, dtype, kind="ExternalOutput").ap()
        self.P.dma(o, ap, reads=keys)

    def build(self):
        nc = self.nc
        I = {}
        def inp(name, shape):
            I[name] = dram_in(nc, name, shape)
        inp("xin", [self.ntok, D]); inp("cvec", [D]); inp("flagv", [128, 1])
        inp("w_ada", [D, 6 * D]); inp("b_ada", [6 * D]); inp("w_in", [D, INW]); inp("b_in", [INW])
        inp("w_mlstm_conv", [4, D]); inp("b_mlstm_conv", [D]); inp("w_mlstm_q", [NH, DV, DK]); inp("w_mlstm_k", [NH, DV, DK])
        inp("mlstm_norm_gain", [D]); inp("w_mlstm_down", [D, D])
        inp("s5_lam_re", [32, 64]); inp("s5_lam_im", [32, 64]); inp("s5_log_dt", [32])
        inp("s5_b_re", [32, 64, 16]); inp("s5_b_im", [32, 64, 16]); inp("s5_c_re", [32, 16, 64]); inp("s5_c_im", [32, 16, 64])
        inp("s5_d", [S5W]); inp("w_s5_glu", [S5W, 2 * D]); inp("w_mix_out", [D, D])
        inp("ln1_gain", [D]); inp("ln1_bias", [D]); inp("w_ffn_up", [D, 2 * FH]); inp("w_ffn_conv", [3, FH]); inp("b_ffn_conv", [FH])
        inp("w_ffn_down", [FH, D]); inp("ln2_gain", [D]); inp("ln2_bias", [D])
        inp("c_ident", [128, 128]); inp("c_tri", [128, 128]); inp("c_bmask", [128, 128])
        self.I = I
        self.yout = nc.dram_tensor("yout", [self.nout, D], F32, kind="ExternalOutput").ap()
        S = {}
        def scr(name, shape, dtype=BF16):
            S[name] = nc.dram_tensor("scr_" + name, list(shape), dtype, kind="Internal").ap()
        scr("wA", [9, 128, 8, 512]); scr("wD", [2, 128, 8, 512]); scr("wG", [4, 128, 4, 512]); scr("wM", [2, 128, 8, 512])
        scr("wU", [11, 128, 8, 512]); scr("wF", [6, 128, 8, 512]); scr("fdiag", [22, 128, 3, 128])
        scr("s5K", [128, 4, 8, 128]); scr("s5X", [2, 128, 2, 8, 2, 128]); scr("s5Y", [2, 128, 8, 8, 2, 32])
        self.S = S
        st = ExitStack()
        with st:
            self.P = P = Prog(nc, st)
            self.persistent()
            self.prologue()
            self.main()
            import os as _os
            if _os.environ.get("KMAX"):
                P.ops = P.ops[:int(_os.environ["KMAX"])]
            info = P.emit()
        self.info = info
        return nc

    def persistent(self):
        P, I = self.P, self.I
        A = P.alloc
        T = self.T = {}
        T["ident_f"] = A([128, 128]); T["tri_f"] = A([128, 128]); T["bmask_f"] = A([128, 128]); T["ones_f"] = A([128, 128])
        T["ident_b"] = A([128, 128], BF16); T["mask_b"] = A([128, 128], BF16)
        P.dma(T["ident_f"], I["c_ident"], writes=["ident_f"])
        P.dma(T["tri_f"], I["c_tri"], writes=["tri_f"])
        P.dma(T["bmask_f"], I["c_bmask"], writes=["bmask_f"])
        P.op("pool", lambda e: e.memset(T["ones_f"], 1.0), writes=["ones_f"])
        P.op("dve", lambda e: e.tensor_copy(T["ident_b"], T["ident_f"]), reads=["ident_f"], writes=["ident_b"])
        P.op("dve", lambda e: e.tensor_copy(T["mask_b"], T["tri_f"]), reads=["tri_f"], writes=["mask_b"])
        T["mhalf"] = A([128, 4]); T["flag"] = A([128, 1]); T["xmh"] = A([128, 8, 3], BF16)
        P.op("pool", lambda e: e.memset(T["mhalf"], -0.5), writes=["mhalf"])
        P.dma(T["flag"], I["flagv"], writes=["flag"])
        T["modT"] = A([128, 48])
        T["binT"] = A([128, 36]); T["bgate"] = A([128, 8])
        T["wgt"] = A([128, 8, 8], BF16); T["wq"] = A([128, 4, 2, 128], BF16); T["wk"] = A([128, 4, 2, 128], BF16)
        T["cdiag"] = A([128, 8, 4, 128], BF16); T["cb"] = A([128, 8]); T["gainT"] = A([128, 8])
        T["fcb"] = A([128, 22]); T["dT"] = A([128, 4])
        T["C32"] = A([128, 4, 260]); T["Cb"] = A([128, 4, 260], BF16)
        T["rotc"] = A([128, 16, 64]); T["rots"] = A([128, 16, 64]); T["rho8"] = A([128, 16]); T["u8"] = A([128, 2, 16])
        T["xcar"] = A([128, 2, 16])
        self.d0 = {}
        for nb in sorted({ns * 16 for ns in self.state_ns + self.full_ns}):
            self.d0[nb] = A([128, 16, nb])
        T["fhalo"] = A([128, 22, 2], BF16)

    def ap_cols(self, vec, off, ntile):
        return bass.AP(vec.tensor, off, [[1, 128], [128, ntile]])

    def ap_bcast(self, vec, off, n):
        return bass.AP(vec.tensor, off, [[0, 128], [1, n]])


def bc(ap, n):
    return bass.AP(ap.tensor, ap.offset, [list(x) for x in ap.ap] + [[0, n]])


def bc_mid(ap, n):
    a = [list(x) for x in ap.ap]
    return bass.AP(ap.tensor, ap.offset, [a[0], [0, n]] + a[1:])


SLOW = dict(allow_slow_non_contiguous=True)


def _prologue(self):
    P, I, T, S = self.P, self.I, self.T, self.S
    A = P.alloc
    mark = P.top
    TT = lambda eng, out, a, b, op, rd, wr: P.op(eng, lambda e: e.tensor_tensor(out, a, b, op), reads=rd, writes=wr)
    w_in_v = I["w_in"].rearrange("(kt p) c -> p kt c", p=128)
    for c in range(9):
        P.dma(S["wA"][c], w_in_v[:, :, WA_SRC[c]:WA_SRC[c] + 512], writes=[f"wA{c}"], q="pool")
    P.dma(T["wgt"], w_in_v[:, :, 2048:2056], writes=["wgt"], q="pool")
    P.dma(T["wq"], I["w_mlstm_q"].rearrange("h (kt p) d -> p h kt d", p=128), writes=["wq"], q="pool")
    P.dma(T["wk"], I["w_mlstm_k"].rearrange("h (kt p) d -> p h kt d", p=128), writes=["wk"], q="pool")
    wd_v = I["w_mlstm_down"].rearrange("(kt p) c -> p kt c", p=128)
    for c in range(2):
        P.dma(S["wD"][c], wd_v[:, :, 512 * c:512 * c + 512], writes=[f"wD{c}"], q="pool")
    wg_v = I["w_s5_glu"].rearrange("(kt p) c -> p kt c", p=128)
    for c in range(4):
        P.dma(S["wG"][c], wg_v[:, :, 512 * c:512 * c + 512], writes=[f"wG{c}"], q="pool")
    wu_v = I["w_ffn_up"].rearrange("(kt p) c -> p kt c", p=128)
    for c in range(11):
        P.dma(S["wU"][c], wu_v[:, :, 512 * c:512 * c + 512], writes=[f"wU{c}"], q="pool")
    colstg = [A([128, 128]) for _ in range(2)]
    mark2 = P.top
    cact = A([128, 8]); badaT = A([128, 48]); cw = A([128, 8, 4]); fcw = A([128, 22, 3])
    self.cl_i = 0
    for i_ in range(2):
        P.op("pool", lambda e, i_=i_: e.memset(colstg[i_], 0.0), writes=[f"colstg{i_}"])

    def load_cols(dst, vec, off, nt, dkey, dst_is_3d=None):
        sg = colstg[self.cl_i % 2]; sk = f"colstg{self.cl_i % 2}"; self.cl_i += 1
        P.dma(sg[0:nt, :], bass.AP(vec.tensor, off, [[128, nt], [1, 128]]), reads=[sk], writes=[sk])
        bk_, bkk = P.bank()
        P.op("pe", lambda e, sg=sg, bk_=bk_: e.transpose(bk_[:, 0:128], sg, T["ident_f"]), reads=[sk, "ident_f"], writes=[bkk])
        P.op("dve", lambda e, dst=dst, bk_=bk_, nt=nt: e.tensor_copy(dst, bk_[:, 0:nt] if dst_is_3d is None else dst_is_3d(bk_)), reads=[bkk], writes=[dkey])

    load_cols(cact, I["cvec"], 0, 8, "cact")
    load_cols(badaT, I["b_ada"], 0, 48, "badaT")
    c0 = 0
    for gi, (off, nt) in enumerate(BIN_GROUPS):
        load_cols(T["binT"][:, c0:c0 + nt], I["b_in"], off, nt, f"binT{gi}")
        c0 += nt
    P.dma(T["bgate"], self.ap_bcast(I["b_in"], 2048, 8), writes=["bgate"])
    load_cols(cw.rearrange("p c j -> p j c"), I["w_mlstm_conv"], 0, 32, "cw", dst_is_3d=lambda b_: b_[:, 0:32].rearrange("p (j c) -> p j c", c=8))
    load_cols(T["cb"], I["b_mlstm_conv"], 0, 8, "cb")
    load_cols(T["gainT"], I["mlstm_norm_gain"], 0, 8, "gainT")
    load_cols(fcw.rearrange("p t j -> p j t"), I["w_ffn_conv"], 0, 66, "fcw", dst_is_3d=lambda b_: b_[:, 0:66].rearrange("p (j t) -> p j t", t=22))
    load_cols(T["fcb"], I["b_ffn_conv"], 0, 22, "fcb")
    load_cols(T["dT"], I["s5_d"], 0, 4, "dT")
    self.load_cols = load_cols
    P.op("act", lambda e: e.activation(cact, cact, AF.Silu), reads=["cact"], writes=["cact"])
    cact2 = A([128, 8, 2])
    P.op("dve", lambda e: e.tensor_copy(cact2, bc(cact, 2)), reads=["cact"], writes=["cact2"])
    cactB = A([128, 8, 128])
    P.op("dve", lambda e: e.tensor_copy(cactB, bc(cact, 128)), reads=["cact"], writes=["cactB"])
    g1b = A([128, D]); g2b = A([128, D])
    P.dma(g1b, self.ap_bcast(I["b_ada"], 2 * D, D), writes=["g1b0", "g1b1"])
    P.dma(g2b, self.ap_bcast(I["b_ada"], 5 * D, D), writes=["g2b0", "g2b1"])
    stg = [A([128, 8, 512]) for _ in range(2)]
    wada_v = I["w_ada"].rearrange("(kt p) c -> p kt c", p=128)
    mb, mbk = P.bank()
    for c in range(12):
        sg, sk = stg[c % 2], f"stg{c % 2}"
        P.dma(sg, wada_v[:, :, 512 * c:512 * c + 512], writes=[sk])
        kind, half = c // 2, c % 2
        if kind in (2, 5):
            gb_, gk = (g1b, f"g1b{half}") if kind == 2 else (g2b, f"g2b{half}")
            bk_, bkk = P.bank()
            def f(e, sg=sg, bk_=bk_):
                for kt in range(8):
                    r = e.matmul(bk_[:, 0:512], cactB[:, kt, :], sg[:, kt, :], start=(kt == 0), stop=(kt == 7))
                return r
            P.op("pe", f, reads=[sk, "cactB"], writes=[bkk])
            dst = gb_[:, 512 * half:512 * half + 512]
            P.op("dve", lambda e, dst=dst, bk_=bk_: e.scalar_tensor_tensor(dst, bk_[:, 0:512], 1.0, dst, ALU.add, ALU.add), reads=[bkk, gk], writes=[gk])
        else:
            def f(e, sg=sg, kind=kind, half=half):
                for j in range(4):
                    ct = kind * 8 + half * 4 + j
                    for kt in range(8):
                        r = e.matmul(mb[:, 2 * ct:2 * ct + 2], sg[:, kt, 128 * j:128 * j + 128], cact2[:, kt, :], start=(kt == 0), stop=(kt == 7))
                return r
            P.op("pe", f, reads=[sk, "cact2"], writes=[mbk])
    P.op("pool", lambda e: e.memset(T["modT"], 0.0), writes=["modT0", "modT24"])
    for (a, b) in ((0, 16), (24, 40)):
        P.op("dve", lambda e, a=a, b=b: e.tensor_tensor(T["modT"][:, a:b], mb[:, 2 * a:2 * b:2], badaT[:, a:b], ALU.add), reads=[mbk, "badaT"], writes=[f"modT{a}"])
    for a, kk in ((8, "modT0"), (32, "modT24")):
        P.op("dve", lambda e, a=a: e.tensor_scalar(T["modT"][:, a:a + 8], T["modT"][:, a:a + 8], 1.0, None, ALU.add), reads=[kk], writes=[kk])
    self.dbg_dump("modT", T["modT"], ["modT0", "modT24"])
    self.dbg_dump("g1b", g1b, ["g1b0", "g1b1"])
    ob = [A([128, 8, 512], BF16) for _ in range(2)]
    wm_v = I["w_mix_out"].rearrange("(kt p) c -> p kt c", p=128)
    wf_v = I["w_ffn_down"].rearrange("(kt p) c -> p kt c", p=128)
    jobs = [(wm_v, 0, 8, hf, g1b, f"g1b{hf}", S["wM"][hf], f"wM{hf}") for hf in range(2)]
    for gi, (k0, nk) in enumerate(WF_GROUPS):
        for hf in range(2):
            jobs.append((wf_v, k0, nk, hf, g2b, f"g2b{hf}", S["wF"][gi * 2 + hf], f"wF{gi * 2 + hf}"))
    for ji, (src, k0, nk, hf, gt, gk, dst, dk) in enumerate(jobs):
        sg, sk = stg[ji % 2], f"stg{ji % 2}"
        o_, ok_ = ob[ji % 2], f"ob{ji % 2}"
        P.dma(sg[:, 0:nk, :], src[:, k0:k0 + nk, 512 * hf:512 * hf + 512], writes=[sk])
        eng = "dve" if ji % 2 == 0 else "pool"
        P.op(eng, lambda e, o_=o_, sg=sg, nk=nk, gt=gt, hf=hf: e.tensor_tensor(o_[:, 0:nk, :], sg[:, 0:nk, :], bc_mid(gt[:, 512 * hf:512 * hf + 512], nk), ALU.mult),
             reads=[sk, gk], writes=[ok_])
        P.dma(dst[:, 0:nk, :], o_[:, 0:nk, :], reads=[ok_], writes=[dk])
    n = 0
    for ct in range(8):
        for j in range(4):
            eng = "dve" if n % 2 == 0 else "pool"; n += 1
            P.op(eng, lambda e, ct=ct, j=j: e.tensor_scalar(T["cdiag"][:, ct, j, :], T["ident_f"], cw[:, ct, j:j + 1], None, ALU.mult),
                 reads=["cw", "ident_f"], writes=[f"cdiag{ct}_{j}"])
    fd = A([128, 22, 3, 128], BF16)
    for t in range(22):
        for j in range(3):
            eng = "dve" if n % 2 == 0 else "pool"; n += 1
            P.op(eng, lambda e, t=t, j=j: e.tensor_scalar(fd[:, t, j, :], T["ident_f"], fcw[:, t, j:j + 1], None, ALU.mult),
                 reads=["fcw", "ident_f"], writes=[f"fd{t}_{j}"])
    P.dma(S["fdiag"].rearrange("t p j d -> p t j d"), fd, reads=[f"fd{t}_{j}" for t in range(22) for j in range(3)], writes=["fdiag"])
    P.barrier()
    P.top = mark2
    self.s5_prologue()
    P.op("pool", lambda e: e.memset(T["C32"], 0.0), writes=["C32"])
    P.op("pool", lambda e: e.memset(T["Cb"], 0.0), writes=["Cb"])
    P.op("pool", lambda e: e.memset(T["xcar"], 0.0), writes=["xcar"])
    P.op("pool", lambda e: e.memset(T["fhalo"], 0.0), writes=["fhalo"])
    P.op("pool", lambda e: e.memset(T["xmh"], 0.0), writes=["xmh"])
    P.barrier()
    P.top = mark
    print("ops after prologue", len(P.ops))


Builder.prologue = _prologue


def _s5_prologue(self):
    P, I, T, S = self.P, self.I, self.T, self.S
    A = P.alloc
    V = "dve"

    def tk(t):
        return "tmp_" + t.tensor.name + str(t.offset)

    def tt(out, a, b, op, rd, wr, eng=V):
        P.op(eng, lambda e: e.tensor_tensor(out, a, b, op), reads=rd, writes=wr)

    def cmul(outr, outi, ar, ai, br, bi, t1, t2, rd, wr):
        k1, k2 = "tmp_" + t1.tensor.name + str(t1.offset), "tmp_" + t2.tensor.name + str(t2.offset)
        tt(t1, ar, br, ALU.mult, rd, [k1])
        tt(t2, ai, bi, ALU.mult, rd, [k2])
        tt(outr, t1, t2, ALU.subtract, [k1, k2], [wr + "r"])
        tt(t1, ar, bi, ALU.mult, rd + [wr + "r"], [k1])
        tt(t2, ai, br, ALU.mult, rd + [wr + "r"], [k2])
        tt(outi, t1, t2, ALU.add, [k1, k2], [wr + "i"])

    lamr = A([128, 16]); lami = A([128, 16]); ldt = A([128, 16]); dt = A([128, 16]); phi = A([128, 16]); aa = A([128, 16])
    cc = A([128, 16]); ss = A([128, 16]); t1 = A([128, 16]); t2 = A([128, 16])
    self.load_cols(lamr, I["s5_lam_re"], 0, 16, "lamr")
    self.load_cols(lami, I["s5_lam_im"], 0, 16, "lami")
    ldtb = A([128, 32])
    P.dma(ldtb, self.ap_bcast(I["s5_log_dt"], 0, 32), writes=["ldtb"])
    for h in range(2):
        P.op(V, lambda e, h=h: e.tensor_copy(ldt[64 * h:64 * h + 64, :], ldtb[64 * h:64 * h + 64, h:32:2]), reads=["ldtb"], writes=[f"ldt{h}"])
    def taylor_exp(out, x, deg, xk, ok, tmp):
        P.op(V, lambda e: e.tensor_scalar(out, x, 1.0 / deg, 1.0, ALU.mult, ALU.add), reads=xk, writes=[ok])
        for n_ in range(deg - 1, 0, -1):
            tt(tmp, x, out, ALU.mult, xk + [ok], [tk(tmp)])
            P.op(V, lambda e, n_=n_: e.tensor_scalar(out, tmp, 1.0 / n_, 1.0, ALU.mult, ALU.add), reads=[tk(tmp)], writes=[ok])

    P.op(V, lambda e: e.tensor_scalar(ldt, ldt, 1.0 / 16, None, ALU.mult), reads=["ldt0", "ldt1"], writes=["ldt0", "ldt1"])
    taylor_exp(dt, ldt, 10, ["ldt0", "ldt1"], "dt", t1)
    for _ in range(4):
        tt(t2, dt, dt, ALU.mult, ["dt"], [tk(t2)])
        P.op(V, lambda e: e.tensor_copy(dt, t2), reads=[tk(t2)], writes=["dt"])
    tt(phi, lami, dt, ALU.mult, ["lami", "dt"], ["phi"])
    tt(aa, lamr, dt, ALU.mult, ["lamr", "dt"], ["aa"])
    P.op("act", lambda e: e.activation(ss, phi, AF.Sin, scale=1.0 / 32), reads=["phi"], writes=["ss"])
    hp = A([128, 1])
    P.op("pool", lambda e: e.memset(hp, math.pi / 2), writes=["hp"])
    P.op("act", lambda e: e.activation(cc, phi, AF.Sin, scale=1.0 / 32, bias=hp), reads=["phi", "hp"], writes=["cc"])
    for it in range(5):
        tt(t1, cc, cc, ALU.mult, ["cc"], [tk(t1)])
        tt(t2, ss, ss, ALU.mult, ["ss"], [tk(t2)])
        P.op(V, lambda e: e.scalar_tensor_tensor(ss, cc, 2.0, ss, ALU.mult, ALU.mult), reads=["cc", "ss", tk(t2)], writes=["ss"])
        tt(cc, t1, t2, ALU.subtract, [tk(t1), tk(t2), "ss"], ["cc"])
    UPr = A([128, 9, 16]); UPi = A([128, 9, 16]); MG = A([128, 9, 16]); PWr = A([128, 9, 16]); PWi = A([128, 9, 16])
    P.op("pool", lambda e: e.memset(UPr[:, 0, :], 1.0), writes=["UP0r"])
    P.op("pool", lambda e: e.memset(UPi[:, 0, :], 0.0), writes=["UP0i"])
    P.op(V, lambda e: e.tensor_copy(UPr[:, 1, :], cc), reads=["cc"], writes=["UP1r"])
    P.op(V, lambda e: e.tensor_copy(UPi[:, 1, :], ss), reads=["ss"], writes=["UP1i"])
    for k in range(2, 9):
        cmul(UPr[:, k, :], UPi[:, k, :], UPr[:, k - 1, :], UPi[:, k - 1, :], UPr[:, 1, :], UPi[:, 1, :], t1, t2,
             [f"UP{k - 1}r", f"UP{k - 1}i", "UP1r", "UP1i"], f"UP{k}")
    P.op("pool", lambda e: e.memset(MG[:, 0, :], 1.0), writes=["MG0"])
    taylor_exp(MG[:, 1, :], aa, 7, ["aa"], "MG1", t1)
    for k in range(2, 9):
        tt(MG[:, k, :], MG[:, k - 1, :], MG[:, 1, :], ALU.mult, [f"MG{k - 1}", "MG1"], [f"MG{k}"])
    allup = [f"UP{k}{c}" for k in range(9) for c in "ri"] + [f"MG{k}" for k in range(9)]
    tt(PWr, MG, UPr, ALU.mult, allup, ["PWr"])
    tt(PWi, MG, UPi, ALU.mult, allup, ["PWi"])
    PW = ["PWr", "PWi"]
    den = A([128, 16]); am1 = A([128, 16]); zr = A([128, 16]); zi = A([128, 16])
    tt(t1, lamr, lamr, ALU.mult, ["lamr"] + PW, [tk(t1)])
    tt(t2, lami, lami, ALU.mult, ["lami"] + PW, [tk(t2)])
    tt(den, t1, t2, ALU.add, [tk(t1), tk(t2)], ["den"])
    P.op(V, lambda e: e.reciprocal(den, den), reads=["den"], writes=["den"])
    P.op(V, lambda e: e.tensor_scalar(am1, PWr[:, 1, :], -1.0, None, ALU.add), reads=PW, writes=["am1"])
    tt(t1, am1, lamr, ALU.mult, ["am1", "lamr", "den"], [tk(t1)])
    tt(t2, PWi[:, 1, :], lami, ALU.mult, PW + ["lami", "den"], [tk(t2)])
    tt(zr, t1, t2, ALU.add, [tk(t1), tk(t2)], ["zr"])
    tt(zr, zr, den, ALU.mult, ["zr", "den"], ["zr"])
    tt(t1, PWi[:, 1, :], lamr, ALU.mult, PW + ["lamr", "zr"], [tk(t1)])
    tt(t2, am1, lami, ALU.mult, ["am1", "lami", "zr"], [tk(t2)])
    tt(zi, t1, t2, ALU.subtract, [tk(t1), tk(t2)], ["zi"])
    tt(zi, zi, den, ALU.mult, ["zi", "den"], ["zi"])
    Bre = A([128, 16, 16]); Bim = A([128, 16, 16]); Cre = A([128, 16, 16]); Cim = A([128, 16, 16])
    b_ap = lambda v: bass.AP(v.tensor, 0, [[16, 128], [2048, 16], [1, 16]])
    P.dma(Bre, b_ap(I["s5_b_re"]), writes=["Bre"])
    P.dma(Bim, b_ap(I["s5_b_im"]), writes=["Bim"])
    Cl = A([128, 16, 128])
    P.op("pool", lambda e: e.memset(Cl, 0.0), writes=["Cl"])
    for nm, dstC in (("s5_c_re", Cre), ("s5_c_im", Cim)):
        for q in range(16):
            P.dma(Cl[0:16, q, :].rearrange("c (g p) -> c g p", p=64), bass.AP(I[nm].tensor, 2048 * q, [[64, 16], [1024, 2], [1, 64]]), reads=["Cl"], writes=[f"Cl{q}"])
        for b4 in range(4):
            bk_, bkk = P.bank()
            def f(e, bk_=bk_, b4=b4):
                for qq in range(4):
                    r = e.transpose(bk_[:, 128 * qq:128 * qq + 128], Cl[:, 4 * b4 + qq, :], T["ident_f"])
                return r
            P.op("pe", f, reads=[f"Cl{4 * b4 + qq}" for qq in range(4)] + ["ident_f"], writes=[bkk])
            P.op(V, lambda e, dstC=dstC, bk_=bk_, b4=b4: e.tensor_copy(dstC[:, 4 * b4:4 * b4 + 4, :], bk_[:, 0:512].rearrange("p (q x) -> p q x", x=128)[:, :, 0:16]),
                 reads=[bkk], writes=["C" + nm[-2:] + str(b4)])
    CK = [f"C{x}{b}" for x in ("re", "im") for b in range(4)]
    BBr = A([128, 16, 16]); BBi = A([128, 16, 16]); W1 = A([128, 16, 16]); W2 = A([128, 16, 16])
    cmul(BBr, BBi, bc(zr, 16), bc(zi, 16), Bre, Bim, W1, W2, ["zr", "zi", "Bre", "Bim"], "BB")
    PBr = A([128, 8, 16, 16]); PBi = A([128, 8, 16, 16])
    for k in range(8):
        cmul(PBr[:, k], PBi[:, k], bc(PWr[:, k, :], 16), bc(PWi[:, k, :], 16), BBr, BBi, W1, W2, PW + ["BBr", "BBi"], f"PB{k}")
    CBD = A([128, 2, 16, 32])
    P.op("pool", lambda e: e.memset(CBD, 0.0), writes=["CBD"])
    for h in range(2):
        sl = slice(64 * h, 64 * h + 64)
        P.op(V, lambda e, sl=sl, h=h: e.tensor_copy(CBD[sl, 0, :, 16 * h:16 * h + 16], Cre[sl]), reads=[f"Cre{b}" for b in range(4)] + ["CBD"], writes=["CBD"])
        P.op(V, lambda e, sl=sl, h=h: e.tensor_scalar(CBD[sl, 1, :, 16 * h:16 * h + 16], Cim[sl], -1.0, None, ALU.mult), reads=[f"Cim{b}" for b in range(4)] + ["CBD"], writes=["CBD"])
    Mb = [A([128, 4, 2, 16]) for _ in range(4)]
    for b in range(4):
        P.op("pool", lambda e, b=b: e.memset(Mb[b], 0.0), writes=[f"Mb{b}"])
    XwS = A([128, 4, 8, 2, 128], BF16)
    KcS = A([128, 4, 8, 128], BF16)
    dtmp = A([128, 128]); ktmp = A([128, 128])
    nb_ = 0
    for ct in range(4):
        for k in range(8):
            tau = 7 - k
            kb, kbk = P.bank()
            mfs = []
            for ri in range(2):
                m, mk = Mb[nb_ % 4], f"Mb{nb_ % 4}"; nb_ += 1
                src = (PBr if ri == 0 else PBi)
                for h in range(2):
                    sl = slice(64 * h, 64 * h + 64)
                    P.op(V if h == 0 else "pool", lambda e, m=m, sl=sl, h=h, src=src, k=k, ct=ct: e.tensor_copy(m[sl, :, h, :], src[sl, k, 4 * ct:4 * ct + 4, :]),
                         reads=[f"PB{k}r", f"PB{k}i", mk], writes=[mk])
                mf = m.rearrange("p a b c -> p (a b c)")
                mfs.append((mf, mk))
                P.op("pe", lambda e, mf=mf, kb=kb, ri=ri: e.transpose(kb[:, 128 + 128 * ri:256 + 128 * ri], mf, T["ident_f"]), reads=[mk, "ident_f"], writes=[kbk])
            for ri in range(2):
                mf, mk = mfs[ri]
                P.op("pe", lambda e, mf=mf, kb=kb, ri=ri, ct=ct: e.matmul(kb[:, 0:128], mf, CBD[:, ri, 4 * ct:4 * ct + 4, :].rearrange("p a b -> p (a b)"), start=(ri == 0), stop=(ri == 1)),
                     reads=[mk, "CBD"], writes=[kbk])
            P.op("act", lambda e, kb=kb, ct=ct, tau=tau: e.copy(XwS[:, ct, tau].rearrange("p a b -> p (a b)"), kb[:, 128:384]), reads=[kbk], writes=["XwS"])
            if k == 0:
                P.op(V, lambda e, ct=ct: e.tensor_scalar(dtmp, T["ident_f"], T["dT"][:, ct:ct + 1], None, ALU.mult), reads=["dT", "ident_f", "KcS"], writes=["dtmp"])
                P.op(V, lambda e, kb=kb: e.tensor_tensor(ktmp, kb[:, 0:128], T["bmask_f"], ALU.mult), reads=[kbk, "bmask_f", "KcS"], writes=["ktmp"])
                P.op(V, lambda e, ct=ct, k=k: e.tensor_tensor(KcS[:, ct, k, :], ktmp, dtmp, ALU.add), reads=["ktmp", "dtmp"], writes=["KcS"])
            else:
                P.op(V, lambda e, kb=kb, ct=ct, k=k: e.tensor_tensor(KcS[:, ct, k, :], kb[:, 0:128], T["bmask_f"], ALU.mult), reads=[kbk, "bmask_f"], writes=["KcS"])
    P.dma(S["s5K"], KcS, reads=["KcS"], writes=["s5K"])
    for c in range(2):
        P.dma(S["s5X"][c], XwS[:, 2 * c:2 * c + 2], reads=["XwS"], writes=[f"s5X{c}"])
    self.dbg_dump("KcS", KcS, ["KcS"], BF16)
    self.dbg_dump("XwS", XwS, ["XwS"], BF16)
    YwS = A([128, 16, 8, 2, 32], BF16)
    P.op("pool", lambda e: e.memset(YwS, 0.0), writes=["YwS"])
    YR = A([128, 16, 16]); YI = A([128, 16, 16])
    for tau in range(8):
        pr, pi = bc(PWr[:, tau + 1, :], 16), bc(PWi[:, tau + 1, :], 16)
        cmul(YR, YI, Cre, Cim, pr, pi, W1, W2, CK + PW + ["YwS"], "YY")
        for h in range(2):
            sl = slice(64 * h, 64 * h + 64)
            P.op(V, lambda e, sl=sl, h=h, tau=tau: e.tensor_copy(YwS[sl, :, tau, 0, 16 * h:16 * h + 16], YR[sl]), reads=["YYr", "YwS"], writes=["YwS"])
            P.op(V, lambda e, sl=sl, h=h, tau=tau: e.tensor_scalar(YwS[sl, :, tau, 1, 16 * h:16 * h + 16], YI[sl], -1.0, None, ALU.mult), reads=["YYi", "YwS"], writes=["YwS"])
    for c in range(2):
        P.dma(S["s5Y"][c], YwS[:, 8 * c:8 * c + 8], reads=["YwS"], writes=[f"s5Y{c}"])
    self.dbg_dump("YwS", YwS, ["YwS"], BF16)
    rc, rs = T["rotc"], T["rots"]
    P.op("pool", lambda e: e.memset(rc[:, :, 0], 1.0), writes=["rot"])
    P.op("pool", lambda e: e.memset(rs[:, :, 0], 0.0), reads=["rot"], writes=["rot"])
    P.op(V, lambda e: e.tensor_copy(T["u8"][:, 0, :], UPr[:, 8, :]), reads=["UP8r"], writes=["u8"])
    P.op(V, lambda e: e.tensor_copy(T["u8"][:, 1, :], UPi[:, 8, :]), reads=["UP8i", "u8"], writes=["u8"])
    r1r = A([128, 16]); r1i = A([128, 16]); wr_ = A([128, 16]); wi_ = A([128, 16])
    P.op(V, lambda e: e.tensor_copy(r1r, UPr[:, 8, :]), reads=["UP8r"], writes=["r1r"])
    P.op(V, lambda e: e.tensor_scalar(r1i, UPi[:, 8, :], -1.0, None, ALU.mult), reads=["UP8i"], writes=["r1i"])
    ln = 1
    R1 = A([128, 16, 32]); R2 = A([128, 16, 32])
    while ln < 64:
        cmul(wr_, wi_, rc[:, :, ln - 1], rs[:, :, ln - 1], r1r, r1i, t1, t2, ["rot", "r1r", "r1i"], "W")
        a_r, a_i = rc[:, :, 0:ln], rs[:, :, 0:ln]
        w_r, w_i = bc(wr_, ln), bc(wi_, ln)
        x1, x2 = R1[:, :, 0:ln], R2[:, :, 0:ln]
        tt(x1, a_r, w_r, ALU.mult, ["rot", "Wr", "Wi"], ["x1"])
        tt(x2, a_i, w_i, ALU.mult, ["rot", "Wr", "Wi"], ["x2"])
        tt(rc[:, :, ln:2 * ln], x1, x2, ALU.subtract, ["x1", "x2"], ["rot"])
        tt(x1, a_r, w_i, ALU.mult, ["rot", "Wr", "Wi"], ["x1"])
        tt(x2, a_i, w_r, ALU.mult, ["rot", "Wr", "Wi"], ["x2"])
        tt(rs[:, :, ln:2 * ln], x1, x2, ALU.add, ["x1", "x2", "rot"], ["rot"])
        ln *= 2
    P.op(V, lambda e: e.tensor_copy(T["rho8"], MG[:, 8, :]), reads=["MG8"], writes=["rho8"])
    for nb, d0 in self.d0.items():
        P.op(V, lambda e, d0=d0, nb=nb: e.tensor_copy(d0, bc(T["rho8"], nb)), reads=["rho8"], writes=[f"d0_{nb}"])
        P.op(V, lambda e, d0=d0: e.memset(d0[:, :, 0], 0.0), reads=[f"d0_{nb}"], writes=[f"d0_{nb}"])
    self.dbg_dump("rotc", rc, ["rot"])
    self.dbg_dump("rots", rs, ["rot"])


Builder.s5_prologue = _s5_prologue


def _main(self):
    P, I, T, S = self.P, self.I, self.T, self.S
    A = P.alloc
    NSM = max(self.state_ns + self.full_ns)
    NM = 128 * NSM
    NBM = 16 * NSM
    B = self.B = {}
    B["XR"] = [A([128, D]) for _ in range(NSM + 1)]
    B["xn"] = [A([128, D], BF16) for _ in range(2)]
    B["hT"] = A([128, 8, NM], BF16)
    B["hm"] = A([128, NSM, D], BF16)
    B["omS"] = A([128, 8, NM], BF16)
    B["prodT"] = A([128, 8, NM], BF16)
    B["yT"] = A([128, 8, NM], BF16)
    B["gtmp"] = A([128, 3, NM], BF16)
    B["uT"] = A([128, 4, NM], BF16)
    B["ysT"] = A([128, 4, NM], BF16)
    B["gsm"] = A([128, NSM, 48])
    B["lnt"] = A([128, 2, D])
    B["ring"] = [A([128, 8, 512], BF16) for _ in range(3)]
    B["smr"] = A([128, 512])
    self.sm_i = 0
    r0 = P.top
    B["xmT"] = A([128, 8, 3 + NM], BF16)
    B["xcT"] = A([128, 8, NM], BF16)
    B["QT"] = A([128, 4, NM], BF16)
    B["KT"] = A([128, 4, NM], BF16)
    B["Vx"] = A([128, NSM, 4, 258], BF16)
    B["KW"] = A([128, 2, 4, 128], BF16)
    B["SW"] = A([128, 2, 4, 128], BF16)
    r1 = P.top
    P.top = r0
    B["Xin"] = A([128, 2, 16, NBM])
    B["Zr"] = A([128, 16 * NBM]); B["Zi"] = A([128, 16 * NBM])
    B["Ta"] = A([128, 16 * NBM]); B["Tb"] = A([128, 16 * NBM])
    B["XS"] = A([128, 2, 16, NBM])
    B["xprev"] = A([128, 2, 16, NBM], BF16)
    r2 = P.top
    P.top = r0
    B["prodF"] = A([128, 22, NM], BF16)
    B["gpre"] = A([128, 2, 2 + NM], BF16)
    B["gact"] = A([128, 2, NM], BF16)
    r3 = P.top
    P.top = max(r1, r2, r3)
    self.ring_i = 0
    self.ring_tag = [None, None, None]
    self.ring_use = [0, 0, 0]
    self.ring_shape = [None, None, None]
    self.ring_clock = 0
    self.xr_i = 0
    tok = 0
    out_row = 0
    n_pre = len(self.state_ns) + (1 if self.skip_sub > 0 else 0)
    tiles = [("state", ns) for ns in self.state_ns] + [("full", ns) for ns in self.full_ns]
    skipped = 0
    for ti, (mode, ns) in enumerate(tiles):
        store = None
        if mode == "full":
            if skipped < self.skip_sub:
                assert ns <= self.skip_sub - skipped
                skipped += ns
            else:
                store = out_row
                out_row += 128 * ns
        self.tile(mode, tok, ns, store)
        tok += 128 * ns
        if ti == n_pre - 1:
            self.apply_flag()
    assert out_row == self.nout


def _sm(self, n):
    if self.sm_i + n > 512:
        self.sm_i = 0
    v = self.B["smr"][:, self.sm_i:self.sm_i + n]
    self.sm_i += n
    return v


def _wchunk(self, name, idx, shape=None, src=None):
    tag = (name, idx)
    self.ring_clock += 1
    for i in range(3):
        if self.ring_tag[i] == tag:
            self.ring_use[i] = self.ring_clock
            return self.chunk_view(i, shape if shape is not None else self.ring_shape[i])
    i = min(range(3), key=lambda k: self.ring_use[k])
    self.ring_tag[i] = tag
    self.ring_use[i] = self.ring_clock
    if src is None:
        src = self.S[name] if idx is None else self.S[name][idx]
    self.ring_shape[i] = list(src.shape)
    dst = self.chunk_view(i, list(src.shape))
    self.P.dma(dst, src)
    return self.chunk_view(i, shape if shape is not None else list(src.shape))


def _chunk_view(self, i, shape):
    slot = self.B["ring"][i]
    if shape is None or list(shape) == [128, 8, 512]:
        return slot
    flat = slot.rearrange("p a b -> p (a b)")
    n = 1
    for x in shape[1:]:
        n *= x
    v = flat[:, 0:n]
    if len(shape) == 3:
        return v.rearrange("p (a b) -> p a b", b=shape[2])
    if len(shape) == 4:
        return v.rearrange("p (a b c) -> p a b c", b=shape[2], c=shape[3])
    if len(shape) == 5:
        return v.rearrange("p (a b c d) -> p a b c d", b=shape[2], c=shape[3], d=shape[4])
    return v


def _apply_flag(self):
    P, T, B = self.P, self.T, self.B
    fl = T["flag"][:, 0:1]
    P.ts("dve", T["C32"], T["C32"], fl, None, ALU.mult)
    P.cp("pool", T["Cb"], T["C32"])
    P.ts("dve", T["xcar"], T["xcar"], fl, None, ALU.mult)
    P.ts("dve", T["xmh"], T["xmh"], fl, None, ALU.mult)
    P.ts("dve", T["fhalo"], T["fhalo"], fl, None, ALU.mult)


def _ln_stats(self, x):
    P, T = self.P, self.T
    st6 = self.sm(12).rearrange("p (a b) -> p a b", b=6)
    mv = self.sm(2); tmp = self.sm(1); rstd = self.sm(1); nmr = self.sm(1)
    P.op("dve", lambda e: e.bn_stats(st6[:, 0, :], x[:, 0:512]), [x[:, 0:512]], [st6[:, 0, :]])
    P.op("dve", lambda e: e.bn_stats(st6[:, 1, :], x[:, 512:1024]), [x[:, 512:1024]], [st6[:, 1, :]])
    P.op("dve", lambda e: e.bn_aggr(mv, st6.rearrange("p a b -> p (a b)")), [st6], [mv])
    P.ts("pool", tmp, mv[:, 1:2], LN_EPS, None, ALU.add)
    P.tt("pool", rstd, tmp, T["mhalf"][:, 0:1], ALU.pow)
    P.stt(nmr, mv[:, 0:1], -1.0, rstd, ALU.mult, ALU.mult)
    return rstd, nmr


def _ln_to_T(self, x, s, sc0, sh0, stats=None):
    P, T, B = self.P, self.T, self.B
    rstd, nmr = stats if stats is not None else self.ln_stats(x)
    xn = B["xn"][s % 2]
    P.act(xn, x, AF.Identity, bias=nmr, scale=rstd)
    bk, _ = P.bank()
    bb = bk[:, :].bitcast(BF16)
    P.trg([(bb[:, 128 * kt:128 * kt + 128], xn[:, 128 * kt:128 * kt + 128], T["ident_b"]) for kt in range(8)])
    for kt in range(8):
        P.act(B["hT"][:, kt, 128 * s:128 * s + 128], bb[:, 128 * kt:128 * kt + 128], AF.Identity,
              bias=T["modT"][:, sh0 + kt:sh0 + kt + 1], scale=T["modT"][:, sc0 + kt:sc0 + kt + 1])


def _inproj_tile(self, ptile, N):
    P, B = self.P, self.B
    w = self.wchunk("wA", ptile // 4)
    j = ptile % 4
    bk, _ = P.bank()
    P.mmg([(bk[:, 0:N], w[:, kt, 128 * j:128 * j + 128], B["hT"][:, kt, 0:N], kt == 0, kt == 7) for kt in range(8)])
    return bk


Builder.main = _main
Builder.sm = _sm
Builder.wchunk = _wchunk
Builder.chunk_view = _chunk_view
Builder.apply_flag = _apply_flag
Builder.ln_stats = _ln_stats
Builder.ln_to_T = _ln_to_T
Builder.inproj_tile = _inproj_tile


def _tile(self, mode, tok0, ns, store):
    P, I, T, S, B = self.P, self.I, self.T, self.S, self.B
    full = (mode == "full")
    N = 128 * ns
    nb = 16 * ns
    hT, xmT, xcT, QT, KT, Vx = B["hT"], B["xmT"], B["xcT"], B["QT"], B["KT"], B["Vx"]
    cs = lambda s: slice(128 * s, 128 * s + 128)
    xs = []
    for s in range(ns):
        x = B["XR"][self.xr_i]
        self.xr_i = (self.xr_i + 1) % len(B["XR"])
        xs.append(x)
        r0 = tok0 + 128 * s
        P.dma(x, I["xin"][r0:r0 + 128, :], q="pool")
    st_ = [self.ln_stats(xs[s]) for s in range(ns)]
    for s in range(ns):
        self.ln_to_T(xs[s], s, 8, 0, stats=st_[s])
    P.cp("pool", xmT[:, :, 0:3], T["xmh"])
    P.op("pool", lambda e: e.memset(Vx[:, 0:ns, :, 256:257], 1.0), [], [Vx[:, 0:ns, :, 256:257]])
    for ct in range(8):
        bk = self.inproj_tile(ct, N)
        P.act(xmT[:, ct, 3:3 + N], bk[:, 0:N], AF.Identity, bias=T["binT"][:, ct:ct + 1])
    G = B["gsm"]
    for s in range(ns):
        g = G[:, s, :]
        bk, _ = P.bank()
        P.mmg([(bk[:, 0:8], hT[:, kt, cs(s)], T["wgt"][:, kt, :], kt == 0, kt == 7) for kt in range(8)])
        P.tt("dve", g[:, 0:8], bk[:, 0:8], T["bgate"], ALU.add)
        P.act(g[:, 8:12], g[:, 4:8], AF.Exp, scale=-1.0)
        P.act(g[:, 12:16], g[:, 8:12], AF.Ln, bias=T["ones_f"][:, 0:1])
        b2, _ = P.bank()
        P.mmg([(b2[:, 0:4], T["tri_f"], g[:, 12:16], True, True), (b2[:, 4:8], T["ones_f"], g[:, 12:16], True, True)])
        P.tt("dve", g[:, 32:36], g[:, 0:4], b2[:, 0:4], ALU.add)
        P.tt("dve", g[:, 36:40], g[:, 32:36], b2[:, 4:8], ALU.subtract)
        P.act(g[:, 20:24], g[:, 36:40], AF.Exp)
        P.act(g[:, 24:28], b2[:, 4:8], AF.Exp, scale=-1.0)
        if full:
            P.act(g[:, 16:20], g[:, 32:36], AF.Exp)
            P.act(g[:, 28:32], b2[:, 0:4], AF.Exp)
    for ct in range(8):
        bk, _ = P.bank()
        P.mmg([(bk[:, 0:N], T["cdiag"][:, ct, j, :], xmT[:, ct, j:j + N], j == 0, j == 3) for j in range(4)])
        P.act(xcT[:, ct, 0:N], bk[:, 0:N], AF.Silu, bias=T["cb"][:, ct:ct + 1])
    if full:
        for h in range(4):
            bk, _ = P.bank()
            P.mmg([(bk[:, 0:N], T["wq"][:, h, kt, :], xcT[:, 2 * h + kt, 0:N], kt == 0, kt == 1) for kt in range(2)])
            P.op("act", lambda e, h=h, bk=bk: e.mul(QT[:, h, 0:N], bk[:, 0:N], DK ** -0.5), [bk[:, 0:N]], [QT[:, h, 0:N]])
            bk2, _ = P.bank()
            P.mmg([(bk2[:, 0:N], T["wk"][:, h, kt, :], xcT[:, 2 * h + kt, 0:N], kt == 0, kt == 1) for kt in range(2)])
            P.cp("dve", KT[:, h, 0:N], bk2[:, 0:N])
    for s in range(ns):
        g = G[:, s, :]
        par = s % 2
        KW, SW = B["KW"][:, par], B["SW"][:, par]
        kb, _ = P.bank()
        items = []
        for h in range(4):
            for kt in range(2):
                items.append((kb[:, 128 * h:128 * h + 128], xcT[:, 2 * h + kt, cs(s)], T["wk"][:, h, kt, :], kt == 0, kt == 1))
        P.mmg(items)
        P.tt("dve", KW, kb[:, 0:512].rearrange("p (h d) -> p h d", d=128), bc(g[:, 20:24], 128), ALU.mult)
        vb, _ = P.bank()
        vbb = vb[:, :].bitcast(BF16)
        P.trg([(vbb[:, 128 * ct:128 * ct + 128], xmT[:, ct, 3 + 128 * s:3 + 128 * s + 128], T["ident_b"]) for ct in range(8)])
        P.cp("act", Vx[:, s, :, 0:256], vbb[:, 0:1024].rearrange("p (h v) -> p h v", v=256))
        if full:
            sb_, _ = P.bank()
            P.mmg([(sb_[:, 128 * h:128 * h + 128], KT[:, h, cs(s)], QT[:, h, cs(s)], True, True) for h in range(4)])
            for h in range(4):
                P.stt(SW[:, h, :], sb_[:, 128 * h:128 * h + 128], g[:, 16 + h:17 + h], T["mask_b"], ALU.mult, ALU.mult)
            nbs = []
            for h in range(4):
                nbk, _ = P.bank()
                nbs.append(nbk)
                P.mmg([(nbk[:, 0:257], SW[:, h, :], Vx[:, s, h, 0:257], True, False),
                       (nbk[:, 0:257], QT[:, h, cs(s)], T["Cb"][:, h, 0:257], False, True)])
            a1 = self.sm(4); rd = self.sm(4); st6 = self.sm(24).rearrange("p (h b) -> p h b", b=6); mv = self.sm(8).rearrange("p (h b) -> p h b", b=2)
            t1 = self.sm(4); aa = self.sm(4); nbv = self.sm(4)
            for h in range(4):
                P.act(a1[:, h:h + 1], nbs[h][:, 256:257], AF.Abs)
            P.tt("dve", a1, a1, g[:, 28:32], ALU.max)
            P.op("dve", lambda e, rd=rd, a1=a1: e.reciprocal(rd, a1), [a1], [rd])
            for h in range(4):
                P.op("dve", lambda e, h=h, st6=st6, nbs=nbs: e.bn_stats(st6[:, h, :], nbs[h][:, 0:256]), [nbs[h][:, 0:256]], [st6[:, h, :]])
            for h in range(4):
                P.op("dve", lambda e, h=h, st6=st6, mv=mv: e.bn_aggr(mv[:, h, :], st6[:, h, :]), [st6[:, h, :]], [mv[:, h, :]])
            P.tt("pool", t1, mv[:, :, 1], rd, ALU.mult)
            P.tt("pool", t1, t1, rd, ALU.mult)
            P.ts("pool", t1, t1, LN_EPS, None, ALU.add)
            P.tt("pool", t1, t1, T["mhalf"], ALU.pow)
            P.tt("pool", aa, t1, rd, ALU.mult)
            P.stt(nbv, mv[:, :, 0], -1.0, aa, ALU.mult, ALU.mult)
            for h in range(4):
                P.act(B["hm"][:, s, 256 * h:256 * h + 256], nbs[h][:, 0:256], AF.Identity, bias=nbv[:, h:h + 1], scale=aa[:, h:h + 1])
        for h in range(4):
            cbk, _ = P.bank()
            P.mmg([(cbk[:, 0:257], KW[:, h, :], Vx[:, s, h, 0:257], True, True)])
            P.stt(T["C32"][:, h, 0:257], T["C32"][:, h, 0:257], g[:, 24 + h:25 + h], cbk[:, 0:257], ALU.mult, ALU.add)
            P.cp("pool", T["Cb"][:, h, 0:257], T["C32"][:, h, 0:257])
    P.cp("pool", T["xmh"], xmT[:, :, N:N + 3])
    self.tile_s5(full, ns, part=1)
    if full:
        self.tile_mix_out(ns, xs)
        self.tile_s5(full, ns, part=2)
        self.tile_post(ns, xs, store)


Builder.tile = _tile


def _tile_mix_out(self, ns, xs):
    P, T, B = self.P, self.T, self.B
    N = 128 * ns
    hm, omS, prodT, yT, gtmp = B["hm"], B["omS"], B["prodT"], B["yT"], B["gtmp"]
    for ct in range(8):
        bk = self.inproj_tile(8 + ct, N)
        P.act(omS[:, ct, 0:N], bk[:, 0:N], AF.Sigmoid, bias=T["binT"][:, 8 + ct:9 + ct])
    for vp in range(4):
        bk, _ = P.bank()
        bb = bk[:, :].bitcast(BF16)
        items = []
        for j in range(2):
            vt = 2 * vp + j
            for s in range(ns):
                items.append((bb[:, 512 * j + 128 * s:512 * j + 128 * s + 128], hm[:, s, 128 * vt:128 * vt + 128], T["ident_b"]))
        P.trg(items)
        for j in range(2):
            vt = 2 * vp + j
            P.stt(prodT[:, vt, 0:N], bb[:, 512 * j:512 * j + N], T["gainT"][:, vt:vt + 1], omS[:, vt, 0:N], ALU.mult, ALU.mult)
    for dt_ in range(8):
        w = self.wchunk("wD", dt_ // 4)
        j = dt_ % 4
        bk, _ = P.bank()
        P.mmg([(bk[:, 0:N], w[:, kt, 128 * j:128 * j + 128], prodT[:, kt, 0:N], kt == 0, kt == 7) for kt in range(8)])
        b2 = self.inproj_tile(20 + dt_, N)
        P.act(gtmp[:, 0, 0:N], b2[:, 0:N], AF.Sigmoid, bias=T["binT"][:, 20 + dt_:21 + dt_])
        P.tt("dve", yT[:, dt_, 0:N], bk[:, 0:N], gtmp[:, 0, 0:N], ALU.mult)


def _tile_s5(self, full, ns, part=1):
    P, T, B = self.P, self.T, self.B
    N = 128 * ns
    nb = 16 * ns
    uT, ysT, yT, gtmp = B["uT"], B["ysT"], B["yT"], B["gtmp"]
    xp = B["xprev"]
    if part == 2:
        return self.tile_s5_out(ns)
    for ct in range(4):
        bk = self.inproj_tile(16 + ct, N)
        P.act(uT[:, ct, 0:N], bk[:, 0:N], AF.Identity, bias=T["binT"][:, 16 + ct:17 + ct])
    xb = [P.bank()[0] for _ in range(4)]
    for ct in range(4):
        w = self.wchunk("s5X", ct // 2)
        for q in range(4):
            items = []
            kw = dict(tile_position=(96, 0)) if q == 3 else {}
            for ri in range(2):
                c0 = (ct * 2 + ri) * nb
                for tau in range(8):
                    items.append((xb[q][:, c0:c0 + nb], w[32 * q:32 * q + 32, ct % 2, tau, ri, :], uT[32 * q:32 * q + 32, ct, tau:N:8], tau == 0, tau == 7, kw))
            P.mmg(items)
    Xin = B["Xin"]
    for q in range(4):
        src = xb[q][:, 0:8 * nb].rearrange("p (c r n) -> p r c n", c=4, r=2)
        P.cp("act" if q % 2 == 0 else "dve", Xin[:, :, q:16:4, 0:nb], src)
    cj, sj = T["rotc"][:, :, 0:nb], T["rots"][:, :, 0:nb]
    v3 = lambda t: t[:, 0:16 * nb].rearrange("p (q n) -> p q n", n=nb)
    Zr, Zi, Ta, Tb = v3(B["Zr"]), v3(B["Zi"]), v3(B["Ta"]), v3(B["Tb"])
    Xr, Xi = Xin[:, 0, :, 0:nb], Xin[:, 1, :, 0:nb]
    P.tt("pool", Ta, cj, Xr, ALU.mult); P.tt("pool", Tb, sj, Xi, ALU.mult); P.tt("dve", Zr, Ta, Tb, ALU.subtract)
    P.tt("pool", Ta, cj, Xi, ALU.mult); P.tt("pool", Tb, sj, Xr, ALU.mult); P.tt("dve", Zi, Ta, Tb, ALU.add)
    xc = T["xcar"]; u8 = T["u8"]
    i_r = self.sm(16); i_i = self.sm(16); ta = self.sm(16); tb = self.sm(16)
    P.tt("dve", ta, u8[:, 0, :], xc[:, 0, :], ALU.mult); P.tt("dve", tb, u8[:, 1, :], xc[:, 1, :], ALU.mult); P.tt("dve", i_r, ta, tb, ALU.subtract)
    P.tt("dve", ta, u8[:, 0, :], xc[:, 1, :], ALU.mult); P.tt("dve", tb, u8[:, 1, :], xc[:, 0, :], ALU.mult); P.tt("dve", i_i, ta, tb, ALU.add)
    P.tt("dve", i_r, i_r, T["rho8"], ALU.mult); P.tt("dve", i_i, i_i, T["rho8"], ALU.mult)
    P.tt("dve", Zr[:, :, 0], Zr[:, :, 0], i_r, ALU.add); P.tt("dve", Zi[:, :, 0], Zi[:, :, 0], i_i, ALU.add)
    d0 = self.d0[nb].rearrange("p q n -> p (q n)")
    fr, fi = B["Ta"][:, 0:16 * nb], B["Tb"][:, 0:16 * nb]
    P.op("dve", lambda e: e.tensor_tensor_scan(fr, d0, B["Zr"][:, 0:16 * nb], 0.0, ALU.mult, ALU.add), [d0, B["Zr"][:, 0:16 * nb]], [fr])
    P.op("dve", lambda e: e.tensor_tensor_scan(fi, d0, B["Zi"][:, 0:16 * nb], 0.0, ALU.mult, ALU.add), [d0, B["Zi"][:, 0:16 * nb]], [fi])
    XS = B["XS"]
    xr_o, xi_o = XS[:, 0, :, 0:nb], XS[:, 1, :, 0:nb]
    P.tt("pool", Zr, cj, Ta, ALU.mult); P.tt("pool", Zi, sj, Tb, ALU.mult); P.tt("dve", xr_o, Zr, Zi, ALU.add)
    P.tt("pool", Zr, cj, Tb, ALU.mult); P.tt("pool", Zi, sj, Ta, ALU.mult); P.tt("dve", xi_o, Zr, Zi, ALU.subtract)
    xp = B["xprev"]
    if full:
        P.cp("pool", xp[:, :, :, 0], xc)
        if nb > 1:
            P.cp("pool", xp[:, :, :, 1:nb], XS[:, :, :, 0:nb - 1])
    P.cp("dve", xc, XS[:, :, :, nb - 1])


def _tile_s5_out(self, ns):
    P, T, B = self.P, self.T, self.B
    N = 128 * ns
    nb = 16 * ns
    uT, ysT, yT, gtmp = B["uT"], B["ysT"], B["yT"], B["gtmp"]
    xp = B["xprev"]
    kc = None
    for ct in range(4):
        kc = self.wchunk("s5K", None)
        yb, _ = P.bank()
        items = []
        for tp in range(8):
            for tau in range(tp, 8):
                items.append((yb[:, tau:N:8], kc[:, ct, tp, :], uT[:, ct, tau - tp:N:8], (tp == 0 and tau == 0), False, dict(skip_group_check=True)))
        P.mmg(items)
        items = []
        yw = self.wchunk("s5Y", ct // 2)
        for q in range(4):
            for tau in range(8):
                for ri in range(2):
                    last = (q == 3 and tau == 7 and ri == 1)
                    items.append((yb[32 * q:32 * q + 32, tau:N:8], yw[:, (4 * ct + q) % 8, tau, ri, :], xp[:, ri, 4 * ct + q, 0:nb],
                                  False, last, dict(tile_position=(0, 32 * q), skip_group_check=True)))
        P.mmg(items)
        P.act(ysT[:, ct, 0:N], yb[:, 0:N], AF.Gelu_apprx_tanh)
    for dt_ in range(8):
        j = dt_ % 4
        wv = self.wchunk("wG", dt_ // 4)
        bv, _ = P.bank()
        P.mmg([(bv[:, 0:N], wv[:, kt, 128 * j:128 * j + 128], ysT[:, kt, 0:N], kt == 0, kt == 3) for kt in range(4)])
        wg = self.wchunk("wG", 2 + dt_ // 4)
        bg, _ = P.bank()
        P.mmg([(bg[:, 0:N], wg[:, kt, 128 * j:128 * j + 128], ysT[:, kt, 0:N], kt == 0, kt == 3) for kt in range(4)])
        P.act(gtmp[:, 1, 0:N], bg[:, 0:N], AF.Sigmoid)
        P.tt("dve", gtmp[:, 2, 0:N], bv[:, 0:N], gtmp[:, 1, 0:N], ALU.mult)
        b2 = self.inproj_tile(28 + dt_, N)
        P.act(gtmp[:, 0, 0:N], b2[:, 0:N], AF.Sigmoid, bias=T["binT"][:, 28 + dt_:29 + dt_])
        P.tt("dve", gtmp[:, 2, 0:N], gtmp[:, 2, 0:N], gtmp[:, 0, 0:N], ALU.mult)
        P.tt("dve", yT[:, dt_, 0:N], yT[:, dt_, 0:N], gtmp[:, 2, 0:N], ALU.add)


Builder.tile_mix_out = _tile_mix_out
Builder.tile_s5 = _tile_s5
Builder.tile_s5_out = _tile_s5_out


def _post_ln(self, x, gb_key):
    P, B = self.P, self.B
    rstd, nmr = gb_key
    P.act(x, x, AF.Identity, bias=nmr, scale=rstd)
    P.tt("dve", x, x, B["lnt"][:, 0, :], ALU.mult)
    P.tt("pool", x, x, B["lnt"][:, 1, :], ALU.add)


def _tile_post(self, ns, xs, store):
    P, I, T, S, B = self.P, self.I, self.T, self.S, self.B
    N = 128 * ns
    yT, hT, prodF = B["yT"], B["hT"], B["prodF"]
    cs = lambda s: slice(128 * s, 128 * s + 128)
    lnt = B["lnt"]
    P.dma(lnt[:, 0, :], self.ap_bcast(I["ln1_gain"], 0, D))
    P.dma(lnt[:, 1, :], self.ap_bcast(I["ln1_bias"], 0, D))
    for hf in range(2):
        w = self.wchunk("wM", hf)
        for s in range(ns):
            bk, _ = P.bank()
            P.mmg([(bk[:, 0:512], yT[:, kt, cs(s)], w[:, kt, :], kt == 0, kt == 7) for kt in range(8)])
            xh = xs[s][:, 512 * hf:512 * hf + 512]
            P.stt(xh, xh, ALPHA, bk[:, 0:512], ALU.mult, ALU.add)
    st_ = [self.ln_stats(xs[s]) for s in range(ns)]
    for s in range(ns):
        self.post_ln(xs[s], st_[s])
    st_ = [self.ln_stats(xs[s]) for s in range(ns)]
    for s in range(ns):
        self.ln_to_T(xs[s], s, 32, 24, stats=st_[s])
    fh = T["fhalo"]
    gpre, gact = B["gpre"], B["gact"]
    pend = None

    def conv_stage(t, par, bv, bc_):
        fd = self.wchunk("fdiag", t // 10, src=S["fdiag"][10 * (t // 10):min(22, 10 * (t // 10) + 10)].rearrange("t p j d -> p t j d"))
        tl = t % 10
        P.mmg([(bc_[:, 0:N], fd[:, tl, j, :], gpre[:, par, j:j + N], j == 0, j == 2) for j in range(3)])
        P.act(gact[:, par, 0:N], bc_[:, 0:N], AF.Gelu_apprx_tanh, bias=T["fcb"][:, t:t + 1])
        P.tt("dve", prodF[:, t, 0:N], bv[:, 0:N], gact[:, par, 0:N], ALU.mult)

    for t in range(22):
        par = t % 2
        wv = self.wchunk("wU", t // 4)
        bv, _ = P.bank()
        P.mmg([(bv[:, 0:N], wv[:, kt, 128 * (t % 4):128 * (t % 4) + 128], hT[:, kt, 0:N], kt == 0, kt == 7) for kt in range(8)])
        gt_ = 22 + t
        wg = self.wchunk("wU", gt_ // 4)
        bg, _ = P.bank()
        P.mmg([(bg[:, 0:N], wg[:, kt, 128 * (gt_ % 4):128 * (gt_ % 4) + 128], hT[:, kt, 0:N], kt == 0, kt == 7) for kt in range(8)])
        P.cp("pool", gpre[:, par, 0:2], fh[:, t, :])
        P.cp("act", gpre[:, par, 2:2 + N], bg[:, 0:N])
        P.cp("pool", fh[:, t, :], gpre[:, par, N:N + 2])
        if pend is not None:
            conv_stage(*pend)
        pend = (t, par, bv, bg)
    conv_stage(*pend)
    P.dma(lnt[:, 0, :], self.ap_bcast(I["ln2_gain"], 0, D))
    P.dma(lnt[:, 1, :], self.ap_bcast(I["ln2_bias"], 0, D))
    for hf in range(2):
        acc = [P.bank()[0] for _ in range(ns)]
        for gi, (k0, nk) in enumerate(WF_GROUPS):
            w = self.wchunk("wF", gi * 2 + hf, src=S["wF"][gi * 2 + hf][:, 0:nk, :])
            for s in range(ns):
                P.mmg([(acc[s][:, 0:512], prodF[:, k0 + kk, cs(s)], w[:, kk, :], (gi == 0 and kk == 0), (gi == 2 and kk == nk - 1)) for kk in range(nk)])
        for s in range(ns):
            xh = xs[s][:, 512 * hf:512 * hf + 512]
            P.stt(xh, xh, ALPHA, acc[s][:, 0:512], ALU.mult, ALU.add)
    st_ = [self.ln_stats(xs[s]) for s in range(ns)]
    for s in range(ns):
        self.post_ln(xs[s], st_[s])
        if store is not None:
            P.dma(self.yout[store + 128 * s:store + 128 * s + 128, :], xs[s], q="pool")
    self.dbg_tile = True


Builder.post_ln = _post_ln
Builder.tile_post = _tile_post


_CACHE = {}


def _consts():
    bm = np.zeros((128, 128), np.float32)
    for q in range(4):
        bm[32 * q:32 * q + 32, 32 * q:32 * q + 32] = 1.0
    return np.eye(128, dtype=np.float32), np.triu(np.ones((128, 128), np.float32)), bm


def core_map(inputs, b, xin, flag):
    ident, tri, bm = _consts()
    m = {"xin": np.ascontiguousarray(xin, dtype=np.float32), "cvec": np.ascontiguousarray(inputs["c"][b], dtype=np.float32),
         "flagv": np.full((128, 1), flag, np.float32), "c_ident": ident, "c_tri": tri, "c_bmask": bm}
    for k, v in inputs.items():
        if k in ("x", "c"):
            continue
        m[k] = np.ascontiguousarray(np.asarray(v)[0], dtype=np.float32)
    return m


def kernel(**inputs):
    inputs = {k: np.asarray(v) for k, v in inputs.items()}
    x = inputs["x"]
    Bn, Sq, _ = x.shape
    half = Sq // 2
    n_state = (half - 128) // 128
    state_ns = [4] * (n_state // 4) + ([n_state % 4] if n_state % 4 else [])
    full_ns = [1] + [4] * (half // 512)
    key = (tuple(state_ns), tuple(full_ns))
    if key not in _CACHE:
        bld = Builder(state_ns, full_ns, 1)
        _CACHE[key] = bld.build()
    nc = _CACHE[key]
    in_maps = []
    for core in range(2 * Bn):
        b, h = core // 2, core % 2
        if h == 0:
            xin = np.concatenate([np.zeros((half, D), np.float32), x[b, :half]], axis=0)
        else:
            xin = x[b]
        in_maps.append(core_map(inputs, b, xin, float(h)))
    res = run_bass_kernel_spmd(nc, in_maps, core_ids=list(range(2 * Bn)))
    out = np.empty((Bn, Sq, D), np.float32)
    for core in range(2 * Bn):
        b, h = core // 2, core % 2
        out[b, h * half:(h + 1) * half] = res.results[core]["yout"]
    return out
```

```python
import math
from contextlib import ExitStack
import numpy as np
import concourse.bass as bass
import concourse.mybir as mybir
from concourse.bass_utils import run_bass_kernel_spmd

F32 = mybir.dt.float32
BF16 = mybir.dt.bfloat16
AF = mybir.ActivationFunctionType
ALU = mybir.AluOpType
AX = mybir.AxisListType

N_DMA_SEMS = 24
COMPUTE = ("pe", "act", "dve", "pool")
ENG = {"pe": "tensor", "act": "scalar", "dve": "vector", "pool": "gpsimd", "sp": "sync"}

D = 1024
NH = 4
DV = 256
DK = 128
S5W = 512
FH = 2816
INW = 4616
NKT = 8
ALPHA = 2.0 ** 0.25
LN_EPS = 1e-5
TB = 8
ARENA_F32 = 50688


class Prog:
    def __init__(self, nc, stack):
        self.nc = nc
        self.stack = stack
        self.ops = []
        self.arena = stack.enter_context(nc.sbuf_tensor("arena", [128, ARENA_F32], F32))
        self.top = 0
        self.peak = 0
        self.banks = [stack.enter_context(nc.psum_tensor(f"bank{i}", [128, 512], F32)) for i in range(8)]
        self.bank_i = 0
        self.uid = 0

    def alloc(self, shape, dtype=F32):
        n = 1
        for s in shape[1:]:
            n *= s
        words = (n + 1) // 2 if dtype == BF16 else n
        words = (words + 7) // 8 * 8
        a = self.top
        self.top += words
        self.peak = max(self.peak, self.top)
        assert self.top <= ARENA_F32, f"SBUF arena overflow {self.top}"
        v = self.arena[:, a:a + words]
        if dtype == BF16:
            v = v.bitcast(BF16)
        v = v[:, 0:n]
        if len(shape) == 3:
            v = v.rearrange("p (a b) -> p a b", b=shape[2])
        elif len(shape) == 4:
            v = v.rearrange("p (a b c) -> p a b c", b=shape[2], c=shape[3])
        elif len(shape) == 5:
            v = v.rearrange("p (a b c d) -> p a b c d", b=shape[2], c=shape[3], d=shape[4])
        return v[0:shape[0]] if shape[0] != 128 else v

    def key(self, prefix="k"):
        self.uid += 1
        return f"{prefix}{self.uid}"

    def bank(self):
        i = self.bank_i
        self.bank_i = (i + 1) % 8
        return self.banks[i], f"bank{i}"

    def op(self, eng, fn, reads=(), writes=()):
        self.ops.append(dict(eng=eng, fn=fn, reads=tuple(reads), writes=tuple(writes), dma=False, bar=False))

    def dma(self, out, in_, reads=(), writes=(), q="sp", **kw):
        def fn(e, out=out, in_=in_, kw=kw):
            return e.dma_start(out=out, in_=in_, **kw)
        rd, wr = list(reads), list(writes)
        for ap, lst in ((in_, rd), (out, wr)):
            if isinstance(ap, bass.AP) and ap.tensor.name == "arena":
                lst.append(ap)
        self.ops.append(dict(eng=q, fn=fn, reads=tuple(rd), writes=tuple(wr), dma=True, bar=False))

    def barrier(self):
        self.ops.append(dict(eng=None, fn=None, reads=(), writes=(), dma=False, bar=True))

    @staticmethod
    def _res(x):
        if isinstance(x, str):
            return ("key", x)
        name = x.tensor.name
        if name != "arena":
            return ("key", name)
        esz = 2 if x.dtype == BF16 else 4
        aps = x.ap
        pstride = ARENA_F32 * 4 // esz
        off = x.offset
        p0 = off // pstride
        lo = (off % pstride) * esz
        if aps[0][0] == 0:
            npart = 1
        else:
            npart = aps[0][1]
        span = 1
        for (st_, cnt) in aps[1:]:
            span += (cnt - 1) * abs(st_)
        return ("box", p0, p0 + npart, lo, lo + span * esz)

    def mmg(self, items, extra_reads=()):
        def f(e, items=items):
            for it in items:
                kw = it[5] if len(it) > 5 else {}
                ins = e.matmul(it[0], it[1], it[2], start=it[3], stop=it[4], **kw)
            return ins
        rd, wr = list(extra_reads), []
        for it in items:
            rd += [it[1], it[2]]
            wr.append(it[0])
        self.op("pe", f, rd, wr)

    def trg(self, items):
        def f(e, items=items):
            for (o, i_, idn) in items:
                ins = e.transpose(o, i_, idn)
            return ins
        self.op("pe", f, [x for it in items for x in (it[1], it[2])], [it[0] for it in items])

    def act(self, out, in_, func, bias=None, scale=1.0):
        rd = [in_] + [x for x in (bias, scale) if isinstance(x, bass.AP)]
        kw = {} if bias is None else {"bias": bias}
        self.op("act", lambda e: e.activation(out, in_, func, scale=scale, **kw), rd, [out])

    def tt(self, eng, out, a, b, op):
        self.op(eng, lambda e: e.tensor_tensor(out, a, b, op), [a, b], [out])

    def ts(self, eng, out, a, s1, s2, op0, op1=None):
        rd = [a] + [x for x in (s1, s2) if isinstance(x, bass.AP)]
        if op1 is None:
            self.op(eng, lambda e: e.tensor_scalar(out, a, s1, s2, op0), rd, [out])
        else:
            self.op(eng, lambda e: e.tensor_scalar(out, a, s1, s2, op0, op1), rd, [out])

    def stt(self, out, a, sc, b, op0, op1):
        rd = [a, b] + ([sc] if isinstance(sc, bass.AP) else [])
        self.op("dve", lambda e: e.scalar_tensor_tensor(out, a, sc, b, op0, op1), rd, [out])

    def cp(self, eng, out, in_):
        if eng == "act":
            self.op("act", lambda e: e.copy(out, in_), [in_], [out])
        else:
            self.op(eng, lambda e: e.tensor_copy(out, in_), [in_], [out])

    def view(self, a, shape, dtype=F32):
        top = self.top
        self.top = a
        v = self.alloc(shape, dtype)
        used = self.top
        self.top = max(top, used)
        return v

    def emit(self, final_wait_eng="sp"):
        nc = self.nc
        ops = self.ops
        n = len(ops)
        last_w, readers = {}, {}
        boxes = []
        deps = [set() for _ in ops]
        since_bar = []
        pending = {}
        for i, o in enumerate(ops):
            if o["bar"]:
                last_per_eng, dmas = {}, []
                for p in since_bar:
                    po = ops[p]
                    if po["dma"]:
                        dmas.append(p)
                    else:
                        last_per_eng[po["eng"]] = p
                pre = set(last_per_eng.values()) | set(dmas)
                for e in list(COMPUTE) + ["sp"]:
                    pending[e] = set(pre) | pending.get(e, set())
                since_bar = []
                last_w, readers, boxes = {}, {}, []
                continue
            d = set()
            rres = [self._res(x) for x in o["reads"]]
            wres = [self._res(x) for x in o["writes"]]
            for r in rres:
                if r[0] == "key":
                    if r[1] in last_w:
                        d.add(last_w[r[1]])
                    if r[1].startswith("bank"):
                        for r_ in readers.get(r[1], ()):
                            if ops[r_]["eng"] != o["eng"]:
                                d.add(r_)
                else:
                    _, p0, p1, lo, hi = r
                    for bx in boxes:
                        if bx[5] and bx[1] < p1 and p0 < bx[2] and bx[3] < hi and lo < bx[4]:
                            d.add(bx[0])
            for w in wres:
                if w[0] == "key":
                    if w[1] in last_w:
                        d.add(last_w[w[1]])
                    for r_ in readers.get(w[1], ()):
                        d.add(r_)
                else:
                    _, p0, p1, lo, hi = w
                    for bx in boxes:
                        if bx[1] < p1 and p0 < bx[2] and bx[3] < hi and lo < bx[4]:
                            d.add(bx[0])
            for p in d:
                if p == i:
                    continue
                po = ops[p]
                if not po["dma"] and not o["dma"] and po["eng"] == o["eng"] and o["eng"] == "pe":
                    continue
                deps[i].add(p)
            if pending.get(o["eng"]):
                for p in pending[o["eng"]]:
                    po = ops[p]
                    if (not po["dma"]) and (not o["dma"]) and po["eng"] == o["eng"]:
                        continue
                    deps[i].add(p)
                pending[o["eng"]] = set()
            ek = ("dma", i) if o["dma"] else o["eng"]
            for w in wres:
                if w[0] == "key":
                    last_w[w[1]] = i
                    readers[w[1]] = []
                else:
                    _, p0, p1, lo, hi = w
                    boxes = [bx for bx in boxes if not (p0 <= bx[1] and bx[2] <= p1 and lo <= bx[3] and bx[4] <= hi)]
                    boxes.append([i, p0, p1, lo, hi, True, ek])
            for r in rres:
                if r[0] == "key":
                    readers.setdefault(r[1], []).append(i)
                else:
                    _, p0, p1, lo, hi = r
                    boxes = [bx for bx in boxes if not (not bx[5] and bx[6] == ek and bx[1] == p0 and bx[2] == p1 and bx[3] == lo and bx[4] == hi)]
                    boxes.append([i, p0, p1, lo, hi, False, ek])
            since_bar.append(i)
        needed = set()
        for i in range(n):
            needed |= deps[i]
        sig_no, cnt = {}, {e: 0 for e in COMPUTE}
        dma_slot, dma_tot, ndma = {}, [0] * N_DMA_SEMS, 0
        nq = {"sp": 0, "pool": 0, "act": 0}
        NSP = N_DMA_SEMS - 8
        for i, o in enumerate(ops):
            if o["bar"]:
                continue
            if o["dma"]:
                if o["eng"] == "pool":
                    s = NSP + nq["pool"] % 8
                    nq["pool"] += 1
                else:
                    s = nq["sp"] % NSP
                    nq["sp"] += 1
                ndma += 1
                prev = dma_tot[s]
                dma_tot[s] += 16
                dma_slot[i] = (s, prev, dma_tot[s])
            elif i in needed:
                cnt[o["eng"]] += 1
                sig_no[i] = cnt[o["eng"]]
        st = self.stack
        csem = {e: st.enter_context(nc.semaphore(f"s_{e}")) for e in COMPUTE}
        dsem = [st.enter_context(nc.semaphore(f"s_dma{j}")) for j in range(N_DMA_SEMS)]
        used = sorted({o["eng"] for o in ops if not o["bar"]} | {final_wait_eng})
        with nc.Block() as block:
            for ename in used:
                def body(e, ename=ename):
                    seen = {c: 0 for c in csem}
                    seen_dma = [0] * N_DMA_SEMS
                    for i, o in enumerate(ops):
                        if o["bar"] or o["eng"] != ename:
                            continue
                        for p in sorted(deps[i]):
                            po = ops[p]
                            if po["dma"]:
                                s, _, tgt = dma_slot[p]
                                if seen_dma[s] < tgt:
                                    e.wait_ge(dsem[s], tgt)
                                    seen_dma[s] = tgt
                            else:
                                pe_ = po["eng"]
                                nn = sig_no[p]
                                if seen[pe_] < nn:
                                    e.wait_ge(csem[pe_], nn)
                                    seen[pe_] = nn
                        if o["dma"]:
                            s, prev, tgt = dma_slot[i]
                            if prev > 0 and seen_dma[s] < prev:
                                e.wait_ge(dsem[s], prev)
                                seen_dma[s] = prev
                            o["fn"](e).then_inc(dsem[s], 16)
                        else:
                            ins = o["fn"](e)
                            if i in sig_no:
                                ins.then_inc(csem[ename], 1)
                    if ename == final_wait_eng:
                        for s in range(N_DMA_SEMS):
                            if dma_tot[s] > seen_dma[s]:
                                e.wait_ge(dsem[s], dma_tot[s])
                getattr(block, ENG[ename])(body)
        return dict(n_ops=n, sig=cnt, ndma=ndma, peak_kb=self.peak * 4 / 1024)


WA_SRC = [0, 512, 1024, 1536, 2056, 2568, 3080, 3592, 4104]
BIN_GROUPS = [(0, 8), (1024, 8), (2056, 4), (2568, 8), (3592, 8)]
WF_GROUPS = [(0, 8), (8, 8), (16, 6)]


def dram_in(nc, name, shape, dtype=F32):
    return nc.dram_tensor(name, list(shape), dtype, kind="ExternalInput").ap()


class Builder:
    def __init__(self, state_ns, full_ns, skip_sub, dbg=()):
        self.state_ns = list(state_ns)
        self.full_ns = list(full_ns)
        self.skip_sub = skip_sub
        self.dbg = set(dbg)
        self.ntok = 128 * (sum(state_ns) + sum(full_ns))
        self.nout = 128 * (sum(full_ns) - skip_sub)
        self.nc = bass.Bass("TRN2", target_bir_lowering=False)
        self.dbg_out = {}

    def dbg_dump(self, name, ap, keys, dtype=F32):
        if name not in self.dbg:
            return
        shape = list(ap.shape)
        o = self.nc.dram_tensor("dbg_" + name, shape, dtype, kind="ExternalOutput").ap()
        self.P.dma(o, ap, reads=keys)

    def build(self):
        nc = self.nc
        I = {}
        def inp(name, shape):
            I[name] = dram_in(nc, name, shape)
        inp("xin", [self.ntok, D]); inp("cvec", [D]); inp("flagv", [128, 1])
        inp("w_ada", [D, 6 * D]); inp("b_ada", [6 * D]); inp("w_in", [D, INW]); inp("b_in", [INW])
        inp("w_mlstm_conv", [4, D]); inp("b_mlstm_conv", [D]); inp("w_mlstm_q", [NH, DV, DK]); inp("w_mlstm_k", [NH, DV, DK])
        inp("mlstm_norm_gain", [D]); inp("w_mlstm_down", [D, D])
        inp("s5_lam_re", [32, 64]); inp("s5_lam_im", [32, 64]); inp("s5_log_dt", [32])
        inp("s5_b_re", [32, 64, 16]); inp("s5_b_im", [32, 64, 16]); inp("s5_c_re", [32, 16, 64]); inp("s5_c_im", [32, 16, 64])
        inp("s5_d", [S5W]); inp("w_s5_glu", [S5W, 2 * D]); inp("w_mix_out", [D, D])
        inp("ln1_gain", [D]); inp("ln1_bias", [D]); inp("w_ffn_up", [D, 2 * FH]); inp("w_ffn_conv", [3, FH]); inp("b_ffn_conv", [FH])
        inp("w_ffn_down", [FH, D]); inp("ln2_gain", [D]); inp("ln2_bias", [D])
        inp("c_ident", [128, 128]); inp("c_tri", [128, 128]); inp("c_bmask", [128, 128])
        self.I = I
        self.yout = nc.dram_tensor("yout", [self.nout, D], F32, kind="ExternalOutput").ap()
        S = {}
        def scr(name, shape, dtype=BF16):
            S[name] = nc.dram_tensor("scr_" + name, list(shape), dtype, kind="Internal").ap()
        scr("wA", [9, 128, 8, 512]); scr("wD", [2, 128, 8, 512]); scr("wG", [4, 128, 4, 512]); scr("wM", [2, 128, 8, 512])
        scr("wU", [11, 128, 8, 512]); scr("wF", [6, 128, 8, 512]); scr("fdiag", [22, 128, 3, 128])
        scr("s5K", [128, 4, 8, 128]); scr("s5X", [2, 128, 2, 8, 2, 128]); scr("s5Y", [2, 128, 8, 8, 2, 32])
        self.S = S
        st = ExitStack()
        with st:
            self.P = P = Prog(nc, st)
            self.persistent()
            self.prologue()
            self.main()
            import os as _os
            if _os.environ.get("KMAX"):
                P.ops = P.ops[:int(_os.environ["KMAX"])]
            info = P.emit()
        self.info = info
        return nc

    def persistent(self):
        P, I = self.P, self.I
        A = P.alloc
        T = self.T = {}
        T["ident_f"] = A([128, 128]); T["tri_f"] = A([128, 128]); T["bmask_f"] = A([128, 128]); T["ones_f"] = A([128, 128])
        T["ident_b"] = A([128, 128], BF16); T["mask_b"] = A([128, 128], BF16)
        P.dma(T["ident_f"], I["c_ident"], writes=["ident_f"])
        P.dma(T["tri_f"], I["c_tri"], writes=["tri_f"])
        P.dma(T["bmask_f"], I["c_bmask"], writes=["bmask_f"])
        P.op("pool", lambda e: e.memset(T["ones_f"], 1.0), writes=["ones_f"])
        P.op("dve", lambda e: e.tensor_copy(T["ident_b"], T["ident_f"]), reads=["ident_f"], writes=["ident_b"])
        P.op("dve", lambda e: e.tensor_copy(T["mask_b"], T["tri_f"]), reads=["tri_f"], writes=["mask_b"])
        T["mhalf"] = A([128, 4]); T["flag"] = A([128, 1]); T["xmh"] = A([128, 8, 3], BF16)
        P.op("pool", lambda e: e.memset(T["mhalf"], -0.5), writes=["mhalf"])
        P.dma(T["flag"], I["flagv"], writes=["flag"])
        T["modT"] = A([128, 48])
        T["binT"] = A([128, 36]); T["bgate"] = A([128, 8])
        T["wgt"] = A([128, 8, 8], BF16); T["wq"] = A([128, 4, 2, 128], BF16); T["wk"] = A([128, 4, 2, 128], BF16)
        T["cdiag"] = A([128, 8, 4, 128], BF16); T["cb"] = A([128, 8]); T["gainT"] = A([128, 8])
        T["fcb"] = A([128, 22]); T["dT"] = A([128, 4])
        T["C32"] = A([128, 4, 260]); T["Cb"] = A([128, 4, 260], BF16)
        T["rotc"] = A([128, 16, 64]); T["rots"] = A([128, 16, 64]); T["rho8"] = A([128, 16]); T["u8"] = A([128, 2, 16])
        T["xcar"] = A([128, 2, 16])
        self.d0 = {}
        for nb in sorted({ns * 16 for ns in self.state_ns + self.full_ns}):
            self.d0[nb] = A([128, 16, nb])
        T["fhalo"] = A([128, 22, 2], BF16)

    def ap_cols(self, vec, off, ntile):
        return bass.AP(vec.tensor, off, [[1, 128], [128, ntile]])

    def ap_bcast(self, vec, off, n):
        return bass.AP(vec.tensor, off, [[0, 128], [1, n]])


def bc(ap, n):
    return bass.AP(ap.tensor, ap.offset, [list(x) for x in ap.ap] + [[0, n]])


def bc_mid(ap, n):
    a = [list(x) for x in ap.ap]
    return bass.AP(ap.tensor, ap.offset, [a[0], [0, n]] + a[1:])


SLOW = dict(allow_slow_non_contiguous=True)


def _prologue(self):
    P, I, T, S = self.P, self.I, self.T, self.S
    A = P.alloc
    mark = P.top
    TT = lambda eng, out, a, b, op, rd, wr: P.op(eng, lambda e: e.tensor_tensor(out, a, b, op), reads=rd, writes=wr)
    w_in_v = I["w_in"].rearrange("(kt p) c -> p kt c", p=128)
    for c in range(9):
        P.dma(S["wA"][c], w_in_v[:, :, WA_SRC[c]:WA_SRC[c] + 512], writes=[f"wA{c}"], q="pool")
    P.dma(T["wgt"], w_in_v[:, :, 2048:2056], writes=["wgt"], q="pool")
    P.dma(T["wq"], I["w_mlstm_q"].rearrange("h (kt p) d -> p h kt d", p=128), writes=["wq"], q="pool")
    P.dma(T["wk"], I["w_mlstm_k"].rearrange("h (kt p) d -> p h kt d", p=128), writes=["wk"], q="pool")
    wd_v = I["w_mlstm_down"].rearrange("(kt p) c -> p kt c", p=128)
    for c in range(2):
        P.dma(S["wD"][c], wd_v[:, :, 512 * c:512 * c + 512], writes=[f"wD{c}"], q="pool")
    wg_v = I["w_s5_glu"].rearrange("(kt p) c -> p kt c", p=128)
    for c in range(4):
        P.dma(S["wG"][c], wg_v[:, :, 512 * c:512 * c + 512], writes=[f"wG{c}"], q="pool")
    wu_v = I["w_ffn_up"].rearrange("(kt p) c -> p kt c", p=128)
    for c in range(11):
        P.dma(S["wU"][c], wu_v[:, :, 512 * c:512 * c + 512], writes=[f"wU{c}"], q="pool")
    colstg = [A([128, 128]) for _ in range(2)]
    mark2 = P.top
    cact = A([128, 8]); badaT = A([128, 48]); cw = A([128, 8, 4]); fcw = A([128, 22, 3])
    self.cl_i = 0
    for i_ in range(2):
        P.op("dve", lambda e, i_=i_: e.memset(colstg[i_], 0.0), writes=[f"colstg{i_}"])

    def load_cols(dst, vec, off, nt, dkey, dst_is_3d=None):
        sg = colstg[self.cl_i % 2]; sk = f"colstg{self.cl_i % 2}"; self.cl_i += 1
        P.dma(sg[0:nt, :], bass.AP(vec.tensor, off, [[128, nt], [1, 128]]), reads=[sk], writes=[sk])
        bk_, bkk = P.bank()
        P.op("pe", lambda e, sg=sg, bk_=bk_: e.transpose(bk_[:, 0:128], sg, T["ident_f"]), reads=[sk, "ident_f"], writes=[bkk])
        P.op("dve", lambda e, dst=dst, bk_=bk_, nt=nt: e.tensor_copy(dst, bk_[:, 0:nt] if dst_is_3d is None else dst_is_3d(bk_)), reads=[bkk], writes=[dkey])

    load_cols(cact, I["cvec"], 0, 8, "cact")
    load_cols(badaT, I["b_ada"], 0, 48, "badaT")
    c0 = 0
    for gi, (off, nt) in enumerate(BIN_GROUPS):
        load_cols(T["binT"][:, c0:c0 + nt], I["b_in"], off, nt, f"binT{gi}")
        c0 += nt
    P.dma(T["bgate"], self.ap_bcast(I["b_in"], 2048, 8), writes=["bgate"])
    load_cols(cw.rearrange("p c j -> p j c"), I["w_mlstm_conv"], 0, 32, "cw", dst_is_3d=lambda b_: b_[:, 0:32].rearrange("p (j c) -> p j c", c=8))
    load_cols(T["cb"], I["b_mlstm_conv"], 0, 8, "cb")
    load_cols(T["gainT"], I["mlstm_norm_gain"], 0, 8, "gainT")
    load_cols(fcw.rearrange("p t j -> p j t"), I["w_ffn_conv"], 0, 66, "fcw", dst_is_3d=lambda b_: b_[:, 0:66].rearrange("p (j t) -> p j t", t=22))
    load_cols(T["fcb"], I["b_ffn_conv"], 0, 22, "fcb")
    load_cols(T["dT"], I["s5_d"], 0, 4, "dT")
    self.load_cols = load_cols
    P.op("act", lambda e: e.activation(cact, cact, AF.Silu), reads=["cact"], writes=["cact"])
    cact2 = A([128, 8, 2])
    P.op("dve", lambda e: e.tensor_copy(cact2, bc(cact, 2)), reads=["cact"], writes=["cact2"])
    cactB = A([128, 8, 128])
    P.op("dve", lambda e: e.tensor_copy(cactB, bc(cact, 128)), reads=["cact"], writes=["cactB"])
    g1b = A([128, D]); g2b = A([128, D])
    P.dma(g1b, self.ap_bcast(I["b_ada"], 2 * D, D), writes=["g1b0", "g1b1"])
    P.dma(g2b, self.ap_bcast(I["b_ada"], 5 * D, D), writes=["g2b0", "g2b1"])
    stg = [A([128, 8, 512]) for _ in range(2)]
    wada_v = I["w_ada"].rearrange("(kt p) c -> p kt c", p=128)
    mb, mbk = P.bank()
    for c in range(12):
        sg, sk = stg[c % 2], f"stg{c % 2}"
        P.dma(sg, wada_v[:, :, 512 * c:512 * c + 512], writes=[sk])
        kind, half = c // 2, c % 2
        if kind in (2, 5):
            gb_, gk = (g1b, f"g1b{half}") if kind == 2 else (g2b, f"g2b{half}")
            bk_, bkk = P.bank()
            def f(e, sg=sg, bk_=bk_):
                for kt in range(8):
                    r = e.matmul(bk_[:, 0:512], cactB[:, kt, :], sg[:, kt, :], start=(kt == 0), stop=(kt == 7))
                return r
            P.op("pe", f, reads=[sk, "cactB"], writes=[bkk])
            dst = gb_[:, 512 * half:512 * half + 512]
            P.op("dve", lambda e, dst=dst, bk_=bk_: e.scalar_tensor_tensor(dst, bk_[:, 0:512], 1.0, dst, ALU.add, ALU.add), reads=[bkk, gk], writes=[gk])
        else:
            def f(e, sg=sg, kind=kind, half=half):
                for j in range(4):
                    ct = kind * 8 + half * 4 + j
                    for kt in range(8):
                        r = e.matmul(mb[:, 2 * ct:2 * ct + 2], sg[:, kt, 128 * j:128 * j + 128], cact2[:, kt, :], start=(kt == 0), stop=(kt == 7))
                return r
            P.op("pe", f, reads=[sk, "cact2"], writes=[mbk])
    P.op("pool", lambda e: e.memset(T["modT"], 0.0), writes=["modT0", "modT24"])
    for (a, b) in ((0, 16), (24, 40)):
        P.op("dve", lambda e, a=a, b=b: e.tensor_tensor(T["modT"][:, a:b], mb[:, 2 * a:2 * b:2], badaT[:, a:b], ALU.add), reads=[mbk, "badaT"], writes=[f"modT{a}"])
    for a, kk in ((8, "modT0"), (32, "modT24")):
        P.op("dve", lambda e, a=a: e.tensor_scalar(T["modT"][:, a:a + 8], T["modT"][:, a:a + 8], 1.0, None, ALU.add), reads=[kk], writes=[kk])
    self.dbg_dump("modT", T["modT"], ["modT0", "modT24"])
    self.dbg_dump("g1b", g1b, ["g1b0", "g1b1"])
    ob = [A([128, 8, 512], BF16) for _ in range(2)]
    wm_v = I["w_mix_out"].rearrange("(kt p) c -> p kt c", p=128)
    wf_v = I["w_ffn_down"].rearrange("(kt p) c -> p kt c", p=128)
    jobs = [(wm_v, 0, 8, hf, g1b, f"g1b{hf}", S["wM"][hf], f"wM{hf}") for hf in range(2)]
    for gi, (k0, nk) in enumerate(WF_GROUPS):
        for hf in range(2):
            jobs.append((wf_v, k0, nk, hf, g2b, f"g2b{hf}", S["wF"][gi * 2 + hf], f"wF{gi * 2 + hf}"))
    for ji, (src, k0, nk, hf, gt, gk, dst, dk) in enumerate(jobs):
        sg, sk = stg[ji % 2], f"stg{ji % 2}"
        o_, ok_ = ob[ji % 2], f"ob{ji % 2}"
        P.dma(sg[:, 0:nk, :], src[:, k0:k0 + nk, 512 * hf:512 * hf + 512], writes=[sk])
        eng = "dve"
        P.op(eng, lambda e, o_=o_, sg=sg, nk=nk, gt=gt, hf=hf: e.tensor_tensor(o_[:, 0:nk, :], sg[:, 0:nk, :], bc_mid(gt[:, 512 * hf:512 * hf + 512], nk), ALU.mult),
             reads=[sk, gk], writes=[ok_])
        P.dma(dst[:, 0:nk, :], o_[:, 0:nk, :], reads=[ok_], writes=[dk])
    n = 0
    for ct in range(8):
        for j in range(4):
            eng = "dve"; n += 1
            P.op(eng, lambda e, ct=ct, j=j: e.tensor_scalar(T["cdiag"][:, ct, j, :], T["ident_f"], cw[:, ct, j:j + 1], None, ALU.mult),
                 reads=["cw", "ident_f"], writes=[f"cdiag{ct}_{j}"])
    fd = A([128, 22, 3, 128], BF16)
    for t in range(22):
        for j in range(3):
            eng = "dve"; n += 1
            P.op(eng, lambda e, t=t, j=j: e.tensor_scalar(fd[:, t, j, :], T["ident_f"], fcw[:, t, j:j + 1], None, ALU.mult),
                 reads=["fcw", "ident_f"], writes=[f"fd{t}_{j}"])
    P.dma(S["fdiag"].rearrange("t p j d -> p t j d"), fd, reads=[f"fd{t}_{j}" for t in range(22) for j in range(3)], writes=["fdiag"])
    P.barrier()
    P.top = mark2
    self.s5_prologue()
    P.op("pool", lambda e: e.memset(T["C32"], 0.0), writes=["C32"])
    P.op("pool", lambda e: e.memset(T["Cb"], 0.0), writes=["Cb"])
    P.op("pool", lambda e: e.memset(T["xcar"], 0.0), writes=["xcar"])
    P.op("pool", lambda e: e.memset(T["fhalo"], 0.0), writes=["fhalo"])
    P.op("pool", lambda e: e.memset(T["xmh"], 0.0), writes=["xmh"])
    P.barrier()
    P.top = mark
    print("ops after prologue", len(P.ops))


Builder.prologue = _prologue


def _s5_prologue(self):
    P, I, T, S = self.P, self.I, self.T, self.S
    A = P.alloc
    V = "dve"

    def tk(t):
        return "tmp_" + t.tensor.name + str(t.offset)

    def tt(out, a, b, op, rd, wr, eng=V):
        P.op(eng, lambda e: e.tensor_tensor(out, a, b, op), reads=rd, writes=wr)

    def cmul(outr, outi, ar, ai, br, bi, t1, t2, rd, wr):
        k1, k2 = "tmp_" + t1.tensor.name + str(t1.offset), "tmp_" + t2.tensor.name + str(t2.offset)
        tt(t1, ar, br, ALU.mult, rd, [k1])
        tt(t2, ai, bi, ALU.mult, rd, [k2])
        tt(outr, t1, t2, ALU.subtract, [k1, k2], [wr + "r"])
        tt(t1, ar, bi, ALU.mult, rd + [wr + "r"], [k1])
        tt(t2, ai, br, ALU.mult, rd + [wr + "r"], [k2])
        tt(outi, t1, t2, ALU.add, [k1, k2], [wr + "i"])

    lamr = A([128, 16]); lami = A([128, 16]); ldt = A([128, 16]); dt = A([128, 16]); phi = A([128, 16]); aa = A([128, 16])
    cc = A([128, 16]); ss = A([128, 16]); t1 = A([128, 16]); t2 = A([128, 16])
    self.load_cols(lamr, I["s5_lam_re"], 0, 16, "lamr")
    self.load_cols(lami, I["s5_lam_im"], 0, 16, "lami")
    ldtb = A([128, 32])
    P.dma(ldtb, self.ap_bcast(I["s5_log_dt"], 0, 32), writes=["ldtb"])
    for h in range(2):
        P.op(V, lambda e, h=h: e.tensor_copy(ldt[64 * h:64 * h + 64, :], ldtb[64 * h:64 * h + 64, h:32:2]), reads=["ldtb"], writes=[f"ldt{h}"])
    def taylor_exp(out, x, deg, xk, ok, tmp):
        P.op(V, lambda e: e.tensor_scalar(out, x, 1.0 / deg, 1.0, ALU.mult, ALU.add), reads=xk, writes=[ok])
        for n_ in range(deg - 1, 0, -1):
            tt(tmp, x, out, ALU.mult, xk + [ok], [tk(tmp)])
            P.op(V, lambda e, n_=n_: e.tensor_scalar(out, tmp, 1.0 / n_, 1.0, ALU.mult, ALU.add), reads=[tk(tmp)], writes=[ok])

    P.op(V, lambda e: e.tensor_scalar(ldt, ldt, 1.0 / 16, None, ALU.mult), reads=["ldt0", "ldt1"], writes=["ldt0", "ldt1"])
    taylor_exp(dt, ldt, 10, ["ldt0", "ldt1"], "dt", t1)
    for _ in range(4):
        tt(t2, dt, dt, ALU.mult, ["dt"], [tk(t2)])
        P.op(V, lambda e: e.tensor_copy(dt, t2), reads=[tk(t2)], writes=["dt"])
    tt(phi, lami, dt, ALU.mult, ["lami", "dt"], ["phi"])
    tt(aa, lamr, dt, ALU.mult, ["lamr", "dt"], ["aa"])
    P.op("act", lambda e: e.activation(ss, phi, AF.Sin, scale=1.0 / 32), reads=["phi"], writes=["ss"])
    hp = A([128, 1])
    P.op("pool", lambda e: e.memset(hp, math.pi / 2), writes=["hp"])
    P.op("act", lambda e: e.activation(cc, phi, AF.Sin, scale=1.0 / 32, bias=hp), reads=["phi", "hp"], writes=["cc"])
    for it in range(5):
        tt(t1, cc, cc, ALU.mult, ["cc"], [tk(t1)])
        tt(t2, ss, ss, ALU.mult, ["ss"], [tk(t2)])
        P.op(V, lambda e: e.scalar_tensor_tensor(ss, cc, 2.0, ss, ALU.mult, ALU.mult), reads=["cc", "ss", tk(t2)], writes=["ss"])
        tt(cc, t1, t2, ALU.subtract, [tk(t1), tk(t2), "ss"], ["cc"])
    UPr = A([128, 9, 16]); UPi = A([128, 9, 16]); MG = A([128, 9, 16]); PWr = A([128, 9, 16]); PWi = A([128, 9, 16])
    P.op("pool", lambda e: e.memset(UPr[:, 0, :], 1.0), writes=["UP0r"])
    P.op("pool", lambda e: e.memset(UPi[:, 0, :], 0.0), writes=["UP0i"])
    P.op(V, lambda e: e.tensor_copy(UPr[:, 1, :], cc), reads=["cc"], writes=["UP1r"])
    P.op(V, lambda e: e.tensor_copy(UPi[:, 1, :], ss), reads=["ss"], writes=["UP1i"])
    for k in range(2, 9):
        cmul(UPr[:, k, :], UPi[:, k, :], UPr[:, k - 1, :], UPi[:, k - 1, :], UPr[:, 1, :], UPi[:, 1, :], t1, t2,
             [f"UP{k - 1}r", f"UP{k - 1}i", "UP1r", "UP1i"], f"UP{k}")
    P.op("pool", lambda e: e.memset(MG[:, 0, :], 1.0), writes=["MG0"])
    taylor_exp(MG[:, 1, :], aa, 7, ["aa"], "MG1", t1)
    for k in range(2, 9):
        tt(MG[:, k, :], MG[:, k - 1, :], MG[:, 1, :], ALU.mult, [f"MG{k - 1}", "MG1"], [f"MG{k}"])
    allup = [f"UP{k}{c}" for k in range(9) for c in "ri"] + [f"MG{k}" for k in range(9)]
    tt(PWr, MG, UPr, ALU.mult, allup, ["PWr"])
    tt(PWi, MG, UPi, ALU.mult, allup, ["PWi"])
    PW = ["PWr", "PWi"]
    den = A([128, 16]); am1 = A([128, 16]); zr = A([128, 16]); zi = A([128, 16])
    tt(t1, lamr, lamr, ALU.mult, ["lamr"] + PW, [tk(t1)])
    tt(t2, lami, lami, ALU.mult, ["lami"] + PW, [tk(t2)])
    tt(den, t1, t2, ALU.add, [tk(t1), tk(t2)], ["den"])
    P.op(V, lambda e: e.reciprocal(den, den), reads=["den"], writes=["den"])
    P.op(V, lambda e: e.tensor_scalar(am1, PWr[:, 1, :], -1.0, None, ALU.add), reads=PW, writes=["am1"])
    tt(t1, am1, lamr, ALU.mult, ["am1", "lamr", "den"], [tk(t1)])
    tt(t2, PWi[:, 1, :], lami, ALU.mult, PW + ["lami", "den"], [tk(t2)])
    tt(zr, t1, t2, ALU.add, [tk(t1), tk(t2)], ["zr"])
    tt(zr, zr, den, ALU.mult, ["zr", "den"], ["zr"])
    tt(t1, PWi[:, 1, :], lamr, ALU.mult, PW + ["lamr", "zr"], [tk(t1)])
    tt(t2, am1, lami, ALU.mult, ["am1", "lami", "zr"], [tk(t2)])
    tt(zi, t1, t2, ALU.subtract, [tk(t1), tk(t2)], ["zi"])
    tt(zi, zi, den, ALU.mult, ["zi", "den"], ["zi"])
    Bre = A([128, 16, 16]); Bim = A([128, 16, 16]); Cre = A([128, 16, 16]); Cim = A([128, 16, 16])
    b_ap = lambda v: bass.AP(v.tensor, 0, [[16, 128], [2048, 16], [1, 16]])
    P.dma(Bre, b_ap(I["s5_b_re"]), writes=["Bre"])
    P.dma(Bim, b_ap(I["s5_b_im"]), writes=["Bim"])
    Cl = A([128, 16, 128])
    P.op("pool", lambda e: e.memset(Cl, 0.0), writes=["Cl"])
    for nm, dstC in (("s5_c_re", Cre), ("s5_c_im", Cim)):
        for q in range(16):
            P.dma(Cl[0:16, q, :].rearrange("c (g p) -> c g p", p=64), bass.AP(I[nm].tensor, 2048 * q, [[64, 16], [1024, 2], [1, 64]]), reads=["Cl"], writes=[f"Cl{q}"])
        for b4 in range(4):
            bk_, bkk = P.bank()
            def f(e, bk_=bk_, b4=b4):
                for qq in range(4):
                    r = e.transpose(bk_[:, 128 * qq:128 * qq + 128], Cl[:, 4 * b4 + qq, :], T["ident_f"])
                return r
            P.op("pe", f, reads=[f"Cl{4 * b4 + qq}" for qq in range(4)] + ["ident_f"], writes=[bkk])
            P.op(V, lambda e, dstC=dstC, bk_=bk_, b4=b4: e.tensor_copy(dstC[:, 4 * b4:4 * b4 + 4, :], bk_[:, 0:512].rearrange("p (q x) -> p q x", x=128)[:, :, 0:16]),
                 reads=[bkk], writes=["C" + nm[-2:] + str(b4)])
    CK = [f"C{x}{b}" for x in ("re", "im") for b in range(4)]
    BBr = A([128, 16, 16]); BBi = A([128, 16, 16]); W1 = A([128, 16, 16]); W2 = A([128, 16, 16])
    cmul(BBr, BBi, bc(zr, 16), bc(zi, 16), Bre, Bim, W1, W2, ["zr", "zi", "Bre", "Bim"], "BB")
    PBr = A([128, 8, 16, 16]); PBi = A([128, 8, 16, 16])
    for k in range(8):
        cmul(PBr[:, k], PBi[:, k], bc(PWr[:, k, :], 16), bc(PWi[:, k, :], 16), BBr, BBi, W1, W2, PW + ["BBr", "BBi"], f"PB{k}")
    CBD = A([128, 2, 16, 32])
    P.op("pool", lambda e: e.memset(CBD, 0.0), writes=["CBD"])
    for h in range(2):
        sl = slice(64 * h, 64 * h + 64)
        P.op(V, lambda e, sl=sl, h=h: e.tensor_copy(CBD[sl, 0, :, 16 * h:16 * h + 16], Cre[sl]), reads=[f"Cre{b}" for b in range(4)] + ["CBD"], writes=["CBD"])
        P.op(V, lambda e, sl=sl, h=h: e.tensor_scalar(CBD[sl, 1, :, 16 * h:16 * h + 16], Cim[sl], -1.0, None, ALU.mult), reads=[f"Cim{b}" for b in range(4)] + ["CBD"], writes=["CBD"])
    Mb = [A([128, 4, 2, 16]) for _ in range(4)]
    for b in range(4):
        P.op("pool", lambda e, b=b: e.memset(Mb[b], 0.0), writes=[f"Mb{b}"])
    XwS = A([128, 4, 8, 2, 128], BF16)
    KcS = A([128, 4, 8, 128], BF16)
    dtmp = A([128, 128]); ktmp = A([128, 128])
    nb_ = 0
    for ct in range(4):
        for k in range(8):
            tau = 7 - k
            kb, kbk = P.bank()
            mfs = []
            for ri in range(2):
                m, mk = Mb[nb_ % 4], f"Mb{nb_ % 4}"; nb_ += 1
                src = (PBr if ri == 0 else PBi)
                for h in range(2):
                    sl = slice(64 * h, 64 * h + 64)
                    P.op(V if h == 0 else "pool", lambda e, m=m, sl=sl, h=h, src=src, k=k, ct=ct: e.tensor_copy(m[sl, :, h, :], src[sl, k, 4 * ct:4 * ct + 4, :]),
                         reads=[f"PB{k}r", f"PB{k}i", mk], writes=[mk])
                mf = m.rearrange("p a b c -> p (a b c)")
                mfs.append((mf, mk))
                P.op("pe", lambda e, mf=mf, kb=kb, ri=ri: e.transpose(kb[:, 128 + 128 * ri:256 + 128 * ri], mf, T["ident_f"]), reads=[mk, "ident_f"], writes=[kbk])
            for ri in range(2):
                mf, mk = mfs[ri]
                P.op("pe", lambda e, mf=mf, kb=kb, ri=ri, ct=ct: e.matmul(kb[:, 0:128], mf, CBD[:, ri, 4 * ct:4 * ct + 4, :].rearrange("p a b -> p (a b)"), start=(ri == 0), stop=(ri == 1)),
                     reads=[mk, "CBD"], writes=[kbk])
            P.op("act", lambda e, kb=kb, ct=ct, tau=tau: e.copy(XwS[:, ct, tau].rearrange("p a b -> p (a b)"), kb[:, 128:384]), reads=[kbk], writes=["XwS"])
            if k == 0:
                P.op(V, lambda e, ct=ct: e.tensor_scalar(dtmp, T["ident_f"], T["dT"][:, ct:ct + 1], None, ALU.mult), reads=["dT", "ident_f", "KcS"], writes=["dtmp"])
                P.op(V, lambda e, kb=kb: e.tensor_tensor(ktmp, kb[:, 0:128], T["bmask_f"], ALU.mult), reads=[kbk, "bmask_f", "KcS"], writes=["ktmp"])
                P.op(V, lambda e, ct=ct, k=k: e.tensor_tensor(KcS[:, ct, k, :], ktmp, dtmp, ALU.add), reads=["ktmp", "dtmp"], writes=["KcS"])
            else:
                P.op(V, lambda e, kb=kb, ct=ct, k=k: e.tensor_tensor(KcS[:, ct, k, :], kb[:, 0:128], T["bmask_f"], ALU.mult), reads=[kbk, "bmask_f"], writes=["KcS"])
    P.dma(S["s5K"], KcS, reads=["KcS"], writes=["s5K"])
    for c in range(2):
        P.dma(S["s5X"][c], XwS[:, 2 * c:2 * c + 2], reads=["XwS"], writes=[f"s5X{c}"])
    self.dbg_dump("KcS", KcS, ["KcS"], BF16)
    self.dbg_dump("XwS", XwS, ["XwS"], BF16)
    YwS = A([128, 16, 8, 2, 32], BF16)
    P.op("pool", lambda e: e.memset(YwS, 0.0), writes=["YwS"])
    YR = A([128, 16, 16]); YI = A([128, 16, 16])
    for tau in range(8):
        pr, pi = bc(PWr[:, tau + 1, :], 16), bc(PWi[:, tau + 1, :], 16)
        cmul(YR, YI, Cre, Cim, pr, pi, W1, W2, CK + PW + ["YwS"], "YY")
        for h in range(2):
            sl = slice(64 * h, 64 * h + 64)
            P.op(V, lambda e, sl=sl, h=h, tau=tau: e.tensor_copy(YwS[sl, :, tau, 0, 16 * h:16 * h + 16], YR[sl]), reads=["YYr", "YwS"], writes=["YwS"])
            P.op(V, lambda e, sl=sl, h=h, tau=tau: e.tensor_scalar(YwS[sl, :, tau, 1, 16 * h:16 * h + 16], YI[sl], -1.0, None, ALU.mult), reads=["YYi", "YwS"], writes=["YwS"])
    for c in range(2):
        P.dma(S["s5Y"][c], YwS[:, 8 * c:8 * c + 8], reads=["YwS"], writes=[f"s5Y{c}"])
    self.dbg_dump("YwS", YwS, ["YwS"], BF16)
    rc, rs = T["rotc"], T["rots"]
    P.op("pool", lambda e: e.memset(rc[:, :, 0], 1.0), writes=["rot"])
    P.op("pool", lambda e: e.memset(rs[:, :, 0], 0.0), reads=["rot"], writes=["rot"])
    P.op(V, lambda e: e.tensor_copy(T["u8"][:, 0, :], UPr[:, 8, :]), reads=["UP8r"], writes=["u8"])
    P.op(V, lambda e: e.tensor_copy(T["u8"][:, 1, :], UPi[:, 8, :]), reads=["UP8i", "u8"], writes=["u8"])
    r1r = A([128, 16]); r1i = A([128, 16]); wr_ = A([128, 16]); wi_ = A([128, 16])
    P.op(V, lambda e: e.tensor_copy(r1r, UPr[:, 8, :]), reads=["UP8r"], writes=["r1r"])
    P.op(V, lambda e: e.tensor_scalar(r1i, UPi[:, 8, :], -1.0, None, ALU.mult), reads=["UP8i"], writes=["r1i"])
    ln = 1
    R1 = A([128, 16, 32]); R2 = A([128, 16, 32])
    while ln < 64:
        cmul(wr_, wi_, rc[:, :, ln - 1], rs[:, :, ln - 1], r1r, r1i, t1, t2, ["rot", "r1r", "r1i"], "W")
        a_r, a_i = rc[:, :, 0:ln], rs[:, :, 0:ln]
        w_r, w_i = bc(wr_, ln), bc(wi_, ln)
        x1, x2 = R1[:, :, 0:ln], R2[:, :, 0:ln]
        tt(x1, a_r, w_r, ALU.mult, ["rot", "Wr", "Wi"], ["x1"])
        tt(x2, a_i, w_i, ALU.mult, ["rot", "Wr", "Wi"], ["x2"])
        tt(rc[:, :, ln:2 * ln], x1, x2, ALU.subtract, ["x1", "x2"], ["rot"])
        tt(x1, a_r, w_i, ALU.mult, ["rot", "Wr", "Wi"], ["x1"])
        tt(x2, a_i, w_r, ALU.mult, ["rot", "Wr", "Wi"], ["x2"])
        tt(rs[:, :, ln:2 * ln], x1, x2, ALU.add, ["x1", "x2", "rot"], ["rot"])
        ln *= 2
    P.op(V, lambda e: e.tensor_copy(T["rho8"], MG[:, 8, :]), reads=["MG8"], writes=["rho8"])
    for nb, d0 in self.d0.items():
        P.op(V, lambda e, d0=d0, nb=nb: e.tensor_copy(d0, bc(T["rho8"], nb)), reads=["rho8"], writes=[f"d0_{nb}"])
        P.op(V, lambda e, d0=d0: e.memset(d0[:, :, 0], 0.0), reads=[f"d0_{nb}"], writes=[f"d0_{nb}"])
    self.dbg_dump("rotc", rc, ["rot"])
    self.dbg_dump("rots", rs, ["rot"])


Builder.s5_prologue = _s5_prologue


def _main(self):
    P, I, T, S = self.P, self.I, self.T, self.S
    A = P.alloc
    NSM = max(self.state_ns + self.full_ns)
    NM = 128 * NSM
    NBM = 16 * NSM
    B = self.B = {}
    B["XR"] = [A([128, D]) for _ in range(NSM + 1)]
    B["xn"] = [A([128, D], BF16) for _ in range(2)]
    B["hT"] = A([128, 8, NM], BF16)
    B["hm"] = A([128, NSM, D], BF16)
    B["omS"] = A([128, 8, NM], BF16)
    B["prodT"] = A([128, 8, NM], BF16)
    B["yT"] = A([128, 8, NM], BF16)
    B["gtmp"] = A([128, 3, NM], BF16)
    B["uT"] = A([128, 4, NM], BF16)
    B["ysT"] = A([128, 4, NM], BF16)
    B["gsm"] = A([128, NSM, 48])
    B["lnt"] = A([128, 2, D])
    B["ring"] = [A([128, 8, 512], BF16) for _ in range(3)]
    B["smr"] = A([128, 512])
    self.sm_i = 0
    r0 = P.top
    B["xmT"] = A([128, 8, 3 + NM], BF16)
    B["xcT"] = A([128, 8, NM], BF16)
    B["QT"] = A([128, 4, NM], BF16)
    B["KT"] = A([128, 4, NM], BF16)
    B["Vx"] = A([128, NSM, 4, 258], BF16)
    B["KW"] = A([128, 2, 4, 128], BF16)
    B["SW"] = A([128, 2, 4, 128], BF16)
    r1 = P.top
    P.top = r0
    B["Xin"] = A([128, 2, 16, NBM])
    B["Zr"] = A([128, 16 * NBM]); B["Zi"] = A([128, 16 * NBM])
    B["Ta"] = A([128, 16 * NBM]); B["Tb"] = A([128, 16 * NBM])
    B["XS"] = A([128, 2, 16, NBM])
    B["xprev"] = A([128, 2, 16, NBM], BF16)
    r2 = P.top
    P.top = r0
    B["prodF"] = A([128, 22, NM], BF16)
    B["gpre"] = A([128, 2, 2 + NM], BF16)
    B["gact"] = A([128, 2, NM], BF16)
    r3 = P.top
    P.top = max(r1, r2, r3)
    self.ring_i = 0
    self.ring_tag = [None, None, None]
    self.ring_use = [0, 0, 0]
    self.ring_shape = [None, None, None]
    self.ring_clock = 0
    self.xr_i = 0
    tok = 0
    out_row = 0
    n_pre = len(self.state_ns) + (1 if self.skip_sub > 0 else 0)
    tiles = [("state", ns) for ns in self.state_ns] + [("full", ns) for ns in self.full_ns]
    skipped = 0
    for ti, (mode, ns) in enumerate(tiles):
        store = None
        if mode == "full":
            if skipped < self.skip_sub:
                assert ns <= self.skip_sub - skipped
                skipped += ns
            else:
                store = out_row
                out_row += 128 * ns
        self.tile(mode, tok, ns, store)
        tok += 128 * ns
        if ti == n_pre - 1:
            self.apply_flag()
    assert out_row == self.nout


def _sm(self, n):
    if self.sm_i + n > 512:
        self.sm_i = 0
    v = self.B["smr"][:, self.sm_i:self.sm_i + n]
    self.sm_i += n
    return v


def _wchunk(self, name, idx, shape=None, src=None):
    tag = (name, idx)
    self.ring_clock += 1
    for i in range(3):
        if self.ring_tag[i] == tag:
            self.ring_use[i] = self.ring_clock
            return self.chunk_view(i, shape if shape is not None else self.ring_shape[i])
    i = min(range(3), key=lambda k: self.ring_use[k])
    self.ring_tag[i] = tag
    self.ring_use[i] = self.ring_clock
    if src is None:
        src = self.S[name] if idx is None else self.S[name][idx]
    self.ring_shape[i] = list(src.shape)
    dst = self.chunk_view(i, list(src.shape))
    self.P.dma(dst, src)
    return self.chunk_view(i, shape if shape is not None else list(src.shape))


def _chunk_view(self, i, shape):
    slot = self.B["ring"][i]
    if shape is None or list(shape) == [128, 8, 512]:
        return slot
    flat = slot.rearrange("p a b -> p (a b)")
    n = 1
    for x in shape[1:]:
        n *= x
    v = flat[:, 0:n]
    if len(shape) == 3:
        return v.rearrange("p (a b) -> p a b", b=shape[2])
    if len(shape) == 4:
        return v.rearrange("p (a b c) -> p a b c", b=shape[2], c=shape[3])
    if len(shape) == 5:
        return v.rearrange("p (a b c d) -> p a b c d", b=shape[2], c=shape[3], d=shape[4])
    return v


def _apply_flag(self):
    P, T, B = self.P, self.T, self.B
    fl = T["flag"][:, 0:1]
    P.ts("dve", T["C32"], T["C32"], fl, None, ALU.mult)
    P.cp("pool", T["Cb"], T["C32"])
    P.ts("dve", T["xcar"], T["xcar"], fl, None, ALU.mult)
    P.ts("dve", T["xmh"], T["xmh"], fl, None, ALU.mult)
    P.ts("dve", T["fhalo"], T["fhalo"], fl, None, ALU.mult)


def _ln_stats(self, x):
    P, T = self.P, self.T
    st6 = self.sm(12).rearrange("p (a b) -> p a b", b=6)
    mv = self.sm(2); tmp = self.sm(1); rstd = self.sm(1); nmr = self.sm(1)
    P.op("dve", lambda e: e.bn_stats(st6[:, 0, :], x[:, 0:512]), [x[:, 0:512]], [st6[:, 0, :]])
    P.op("dve", lambda e: e.bn_stats(st6[:, 1, :], x[:, 512:1024]), [x[:, 512:1024]], [st6[:, 1, :]])
    P.op("dve", lambda e: e.bn_aggr(mv, st6.rearrange("p a b -> p (a b)")), [st6], [mv])
    P.ts("pool", tmp, mv[:, 1:2], LN_EPS, None, ALU.add)
    P.tt("pool", rstd, tmp, T["mhalf"][:, 0:1], ALU.pow)
    P.stt(nmr, mv[:, 0:1], -1.0, rstd, ALU.mult, ALU.mult)
    return rstd, nmr


def _ln_to_T(self, x, s, sc0, sh0, stats=None):
    P, T, B = self.P, self.T, self.B
    rstd, nmr = stats if stats is not None else self.ln_stats(x)
    xn = B["xn"][s % 2]
    P.act(xn, x, AF.Identity, bias=nmr, scale=rstd)
    bk, _ = P.bank()
    bb = bk[:, :].bitcast(BF16)
    P.trg([(bb[:, 128 * kt:128 * kt + 128], xn[:, 128 * kt:128 * kt + 128], T["ident_b"]) for kt in range(8)])
    for kt in range(8):
        P.act(B["hT"][:, kt, 128 * s:128 * s + 128], bb[:, 128 * kt:128 * kt + 128], AF.Identity,
              bias=T["modT"][:, sh0 + kt:sh0 + kt + 1], scale=T["modT"][:, sc0 + kt:sc0 + kt + 1])


def _inproj_tile(self, ptile, N):
    P, B = self.P, self.B
    w = self.wchunk("wA", ptile // 4)
    j = ptile % 4
    bk, _ = P.bank()
    P.mmg([(bk[:, 0:N], w[:, kt, 128 * j:128 * j + 128], B["hT"][:, kt, 0:N], kt == 0, kt == 7) for kt in range(8)])
    return bk


Builder.main = _main
Builder.sm = _sm
Builder.wchunk = _wchunk
Builder.chunk_view = _chunk_view
Builder.apply_flag = _apply_flag
Builder.ln_stats = _ln_stats
Builder.ln_to_T = _ln_to_T
Builder.inproj_tile = _inproj_tile


def _tile(self, mode, tok0, ns, store):
    P, I, T, S, B = self.P, self.I, self.T, self.S, self.B
    full = (mode == "full")
    N = 128 * ns
    nb = 16 * ns
    hT, xmT, xcT, QT, KT, Vx = B["hT"], B["xmT"], B["xcT"], B["QT"], B["KT"], B["Vx"]
    cs = lambda s: slice(128 * s, 128 * s + 128)
    xs = []
    for s in range(ns):
        x = B["XR"][self.xr_i]
        self.xr_i = (self.xr_i + 1) % len(B["XR"])
        xs.append(x)
        r0 = tok0 + 128 * s
        P.dma(x, I["xin"][r0:r0 + 128, :], q="pool")
    st_ = [self.ln_stats(xs[s]) for s in range(ns)]
    for s in range(ns):
        self.ln_to_T(xs[s], s, 8, 0, stats=st_[s])
    P.cp("pool", xmT[:, :, 0:3], T["xmh"])
    P.op("pool", lambda e: e.memset(Vx[:, 0:ns, :, 256:257], 1.0), [], [Vx[:, 0:ns, :, 256:257]])
    for ct in range(8):
        bk = self.inproj_tile(ct, N)
        P.act(xmT[:, ct, 3:3 + N], bk[:, 0:N], AF.Identity, bias=T["binT"][:, ct:ct + 1])
    G = B["gsm"]
    for s in range(ns):
        g = G[:, s, :]
        bk, _ = P.bank()
        P.mmg([(bk[:, 0:8], hT[:, kt, cs(s)], T["wgt"][:, kt, :], kt == 0, kt == 7) for kt in range(8)])
        P.tt("dve", g[:, 0:8], bk[:, 0:8], T["bgate"], ALU.add)
        P.act(g[:, 8:12], g[:, 4:8], AF.Exp, scale=-1.0)
        P.act(g[:, 12:16], g[:, 8:12], AF.Ln, bias=T["ones_f"][:, 0:1])
        b2, _ = P.bank()
        P.mmg([(b2[:, 0:4], T["tri_f"], g[:, 12:16], True, True), (b2[:, 4:8], T["ones_f"], g[:, 12:16], True, True)])
        P.tt("dve", g[:, 32:36], g[:, 0:4], b2[:, 0:4], ALU.add)
        P.tt("dve", g[:, 36:40], g[:, 32:36], b2[:, 4:8], ALU.subtract)
        P.act(g[:, 20:24], g[:, 36:40], AF.Exp)
        P.act(g[:, 24:28], b2[:, 4:8], AF.Exp, scale=-1.0)
        if full:
            P.act(g[:, 16:20], g[:, 32:36], AF.Exp)
            P.act(g[:, 28:32], b2[:, 0:4], AF.Exp)
    for ct in range(8):
        bk, _ = P.bank()
        P.mmg([(bk[:, 0:N], T["cdiag"][:, ct, j, :], xmT[:, ct, j:j + N], j == 0, j == 3) for j in range(4)])
        P.act(xcT[:, ct, 0:N], bk[:, 0:N], AF.Silu, bias=T["cb"][:, ct:ct + 1])
    if full:
        for h in range(4):
            bk, _ = P.bank()
            P.mmg([(bk[:, 0:N], T["wq"][:, h, kt, :], xcT[:, 2 * h + kt, 0:N], kt == 0, kt == 1) for kt in range(2)])
            P.op("act", lambda e, h=h, bk=bk: e.mul(QT[:, h, 0:N], bk[:, 0:N], DK ** -0.5), [bk[:, 0:N]], [QT[:, h, 0:N]])
            bk2, _ = P.bank()
            P.mmg([(bk2[:, 0:N], T["wk"][:, h, kt, :], xcT[:, 2 * h + kt, 0:N], kt == 0, kt == 1) for kt in range(2)])
            P.cp("dve", KT[:, h, 0:N], bk2[:, 0:N])
    for s in range(ns):
        g = G[:, s, :]
        par = s % 2
        KW, SW = B["KW"][:, par], B["SW"][:, par]
        kb, _ = P.bank()
        items = []
        for h in range(4):
            for kt in range(2):
                items.append((kb[:, 128 * h:128 * h + 128], xcT[:, 2 * h + kt, cs(s)], T["wk"][:, h, kt, :], kt == 0, kt == 1))
        P.mmg(items)
        P.tt("dve", KW, kb[:, 0:512].rearrange("p (h d) -> p h d", d=128), bc(g[:, 20:24], 128), ALU.mult)
        vb, _ = P.bank()
        vbb = vb[:, :].bitcast(BF16)
        P.trg([(vbb[:, 128 * ct:128 * ct + 128], xmT[:, ct, 3 + 128 * s:3 + 128 * s + 128], T["ident_b"]) for ct in range(8)])
        P.cp("act", Vx[:, s, :, 0:256], vbb[:, 0:1024].rearrange("p (h v) -> p h v", v=256))
        if full:
            sb_, _ = P.bank()
            P.mmg([(sb_[:, 128 * h:128 * h + 128], KT[:, h, cs(s)], QT[:, h, cs(s)], True, True) for h in range(4)])
            for h in range(4):
                P.stt(SW[:, h, :], sb_[:, 128 * h:128 * h + 128], g[:, 16 + h:17 + h], T["mask_b"], ALU.mult, ALU.mult)
            nbs = []
            for h in range(4):
                nbk, _ = P.bank()
                nbs.append(nbk)
                P.mmg([(nbk[:, 0:257], SW[:, h, :], Vx[:, s, h, 0:257], True, False),
                       (nbk[:, 0:257], QT[:, h, cs(s)], T["Cb"][:, h, 0:257], False, True)])
            a1 = self.sm(4); rd = self.sm(4); st6 = self.sm(24).rearrange("p (h b) -> p h b", b=6); mv = self.sm(8).rearrange("p (h b) -> p h b", b=2)
            t1 = self.sm(4); aa = self.sm(4); nbv = self.sm(4)
            for h in range(4):
                P.act(a1[:, h:h + 1], nbs[h][:, 256:257], AF.Abs)
            P.tt("dve", a1, a1, g[:, 28:32], ALU.max)
            P.op("dve", lambda e, rd=rd, a1=a1: e.reciprocal(rd, a1), [a1], [rd])
            for h in range(4):
                P.op("dve", lambda e, h=h, st6=st6, nbs=nbs: e.bn_stats(st6[:, h, :], nbs[h][:, 0:256]), [nbs[h][:, 0:256]], [st6[:, h, :]])
            for h in range(4):
                P.op("dve", lambda e, h=h, st6=st6, mv=mv: e.bn_aggr(mv[:, h, :], st6[:, h, :]), [st6[:, h, :]], [mv[:, h, :]])
            P.tt("pool", t1, mv[:, :, 1], rd, ALU.mult)
            P.tt("pool", t1, t1, rd, ALU.mult)
            P.ts("pool", t1, t1, LN_EPS, None, ALU.add)
            P.tt("pool", t1, t1, T["mhalf"], ALU.pow)
            P.tt("pool", aa, t1, rd, ALU.mult)
            P.stt(nbv, mv[:, :, 0], -1.0, aa, ALU.mult, ALU.mult)
            for h in range(4):
                P.act(B["hm"][:, s, 256 * h:256 * h + 256], nbs[h][:, 0:256], AF.Identity, bias=nbv[:, h:h + 1], scale=aa[:, h:h + 1])
        for h in range(4):
            cbk, _ = P.bank()
            P.mmg([(cbk[:, 0:257], KW[:, h, :], Vx[:, s, h, 0:257], True, True)])
            P.stt(T["C32"][:, h, 0:257], T["C32"][:, h, 0:257], g[:, 24 + h:25 + h], cbk[:, 0:257], ALU.mult, ALU.add)
            P.cp("act", T["Cb"][:, h, 0:257], T["C32"][:, h, 0:257])
    P.cp("pool", T["xmh"], xmT[:, :, N:N + 3])
    self.tile_s5(full, ns, part=1)
    if full:
        self.tile_mix_out(ns, xs)
        self.tile_s5(full, ns, part=2)
        self.tile_post(ns, xs, store)


Builder.tile = _tile


def _tile_mix_out(self, ns, xs):
    P, T, B = self.P, self.T, self.B
    N = 128 * ns
    hm, omS, prodT, yT, gtmp = B["hm"], B["omS"], B["prodT"], B["yT"], B["gtmp"]
    for ct in range(8):
        bk = self.inproj_tile(8 + ct, N)
        P.act(omS[:, ct, 0:N], bk[:, 0:N], AF.Sigmoid, bias=T["binT"][:, 8 + ct:9 + ct])
    for vp in range(4):
        bk, _ = P.bank()
        bb = bk[:, :].bitcast(BF16)
        items = []
        for j in range(2):
            vt = 2 * vp + j
            for s in range(ns):
                items.append((bb[:, 512 * j + 128 * s:512 * j + 128 * s + 128], hm[:, s, 128 * vt:128 * vt + 128], T["ident_b"]))
        P.trg(items)
        for j in range(2):
            vt = 2 * vp + j
            P.stt(prodT[:, vt, 0:N], bb[:, 512 * j:512 * j + N], T["gainT"][:, vt:vt + 1], omS[:, vt, 0:N], ALU.mult, ALU.mult)
    for dt_ in range(8):
        w = self.wchunk("wD", dt_ // 4)
        j = dt_ % 4
        bk, _ = P.bank()
        P.mmg([(bk[:, 0:N], w[:, kt, 128 * j:128 * j + 128], prodT[:, kt, 0:N], kt == 0, kt == 7) for kt in range(8)])
        b2 = self.inproj_tile(20 + dt_, N)
        P.act(gtmp[:, 0, 0:N], b2[:, 0:N], AF.Sigmoid, bias=T["binT"][:, 20 + dt_:21 + dt_])
        P.tt("dve", yT[:, dt_, 0:N], bk[:, 0:N], gtmp[:, 0, 0:N], ALU.mult)


def _tile_s5(self, full, ns, part=1):
    P, T, B = self.P, self.T, self.B
    N = 128 * ns
    nb = 16 * ns
    uT, ysT, yT, gtmp = B["uT"], B["ysT"], B["yT"], B["gtmp"]
    xp = B["xprev"]
    if part == 2:
        return self.tile_s5_out(ns)
    for ct in range(4):
        bk = self.inproj_tile(16 + ct, N)
        P.act(uT[:, ct, 0:N], bk[:, 0:N], AF.Identity, bias=T["binT"][:, 16 + ct:17 + ct])
    xb = [P.bank()[0] for _ in range(4)]
    for ct in range(4):
        w = self.wchunk("s5X", ct // 2)
        for q in range(4):
            items = []
            kw = dict(tile_position=(96, 0)) if q == 3 else {}
            for ri in range(2):
                c0 = (ct * 2 + ri) * nb
                for tau in range(8):
                    items.append((xb[q][:, c0:c0 + nb], w[32 * q:32 * q + 32, ct % 2, tau, ri, :], uT[32 * q:32 * q + 32, ct, tau:N:8], tau == 0, tau == 7, kw))
            P.mmg(items)
    Xin = B["Xin"]
    for q in range(4):
        src = xb[q][:, 0:8 * nb].rearrange("p (c r n) -> p r c n", c=4, r=2)
        P.cp("act" if q % 2 == 0 else "dve", Xin[:, :, q:16:4, 0:nb], src)
    cj, sj = T["rotc"][:, :, 0:nb], T["rots"][:, :, 0:nb]
    v3 = lambda t: t[:, 0:16 * nb].rearrange("p (q n) -> p q n", n=nb)
    Zr, Zi, Ta, Tb = v3(B["Zr"]), v3(B["Zi"]), v3(B["Ta"]), v3(B["Tb"])
    Xr, Xi = Xin[:, 0, :, 0:nb], Xin[:, 1, :, 0:nb]
    P.tt("dve", Ta, cj, Xr, ALU.mult); P.tt("dve", Tb, sj, Xi, ALU.mult); P.tt("dve", Zr, Ta, Tb, ALU.subtract)
    P.tt("dve", Ta, cj, Xi, ALU.mult); P.tt("dve", Tb, sj, Xr, ALU.mult); P.tt("dve", Zi, Ta, Tb, ALU.add)
    xc = T["xcar"]; u8 = T["u8"]
    i_r = self.sm(16); i_i = self.sm(16); ta = self.sm(16); tb = self.sm(16)
    P.tt("dve", ta, u8[:, 0, :], xc[:, 0, :], ALU.mult); P.tt("dve", tb, u8[:, 1, :], xc[:, 1, :], ALU.mult); P.tt("dve", i_r, ta, tb, ALU.subtract)
    P.tt("dve", ta, u8[:, 0, :], xc[:, 1, :], ALU.mult); P.tt("dve", tb, u8[:, 1, :], xc[:, 0, :], ALU.mult); P.tt("dve", i_i, ta, tb, ALU.add)
    P.tt("dve", i_r, i_r, T["rho8"], ALU.mult); P.tt("dve", i_i, i_i, T["rho8"], ALU.mult)
    P.tt("dve", Zr[:, :, 0], Zr[:, :, 0], i_r, ALU.add); P.tt("dve", Zi[:, :, 0], Zi[:, :, 0], i_i, ALU.add)
    d0 = self.d0[nb].rearrange("p q n -> p (q n)")
    fr, fi = B["Ta"][:, 0:16 * nb], B["Tb"][:, 0:16 * nb]
    P.op("dve", lambda e: e.tensor_tensor_scan(fr, d0, B["Zr"][:, 0:16 * nb], 0.0, ALU.mult, ALU.add), [d0, B["Zr"][:, 0:16 * nb]], [fr])
    P.op("dve", lambda e: e.tensor_tensor_scan(fi, d0, B["Zi"][:, 0:16 * nb], 0.0, ALU.mult, ALU.add), [d0, B["Zi"][:, 0:16 * nb]], [fi])
    XS = B["XS"]
    xr_o, xi_o = XS[:, 0, :, 0:nb], XS[:, 1, :, 0:nb]
    P.tt("dve", Zr, cj, Ta, ALU.mult); P.tt("dve", Zi, sj, Tb, ALU.mult); P.tt("dve", xr_o, Zr, Zi, ALU.add)
    P.tt("dve", Zr, cj, Tb, ALU.mult); P.tt("dve", Zi, sj, Ta, ALU.mult); P.tt("dve", xi_o, Zr, Zi, ALU.subtract)
    xp = B["xprev"]
    if full:
        P.cp("act", xp[:, :, :, 0], xc)
        if nb > 1:
            P.cp("act", xp[:, :, :, 1:nb], XS[:, :, :, 0:nb - 1])
    P.cp("dve", xc, XS[:, :, :, nb - 1])


def _tile_s5_out(self, ns):
    P, T, B = self.P, self.T, self.B
    N = 128 * ns
    nb = 16 * ns
    uT, ysT, yT, gtmp = B["uT"], B["ysT"], B["yT"], B["gtmp"]
    xp = B["xprev"]
    kc = None
    for ct in range(4):
        kc = self.wchunk("s5K", None)
        yb, _ = P.bank()
        items = []
        for tp in range(8):
            for tau in range(tp, 8):
                items.append((yb[:, tau:N:8], kc[:, ct, tp, :], uT[:, ct, tau - tp:N:8], (tp == 0 and tau == 0), False, dict(skip_group_check=True)))
        P.mmg(items)
        items = []
        yw = self.wchunk("s5Y", ct // 2)
        for q in range(4):
            for tau in range(8):
                for ri in range(2):
                    last = (q == 3 and tau == 7 and ri == 1)
                    items.append((yb[32 * q:32 * q + 32, tau:N:8], yw[:, (4 * ct + q) % 8, tau, ri, :], xp[:, ri, 4 * ct + q, 0:nb],
                                  False, last, dict(tile_position=(0, 32 * q), skip_group_check=True)))
        P.mmg(items)
        P.act(ysT[:, ct, 0:N], yb[:, 0:N], AF.Gelu_apprx_tanh)
    for dt_ in range(8):
        j = dt_ % 4
        wv = self.wchunk("wG", dt_ // 4)
        bv, _ = P.bank()
        P.mmg([(bv[:, 0:N], wv[:, kt, 128 * j:128 * j + 128], ysT[:, kt, 0:N], kt == 0, kt == 3) for kt in range(4)])
        wg = self.wchunk("wG", 2 + dt_ // 4)
        bg, _ = P.bank()
        P.mmg([(bg[:, 0:N], wg[:, kt, 128 * j:128 * j + 128], ysT[:, kt, 0:N], kt == 0, kt == 3) for kt in range(4)])
        P.act(gtmp[:, 1, 0:N], bg[:, 0:N], AF.Sigmoid)
        P.tt("dve", gtmp[:, 2, 0:N], bv[:, 0:N], gtmp[:, 1, 0:N], ALU.mult)
        b2 = self.inproj_tile(28 + dt_, N)
        P.act(gtmp[:, 0, 0:N], b2[:, 0:N], AF.Sigmoid, bias=T["binT"][:, 28 + dt_:29 + dt_])
        P.tt("dve", gtmp[:, 2, 0:N], gtmp[:, 2, 0:N], gtmp[:, 0, 0:N], ALU.mult)
        P.tt("dve", yT[:, dt_, 0:N], yT[:, dt_, 0:N], gtmp[:, 2, 0:N], ALU.add)


Builder.tile_mix_out = _tile_mix_out
Builder.tile_s5 = _tile_s5
Builder.tile_s5_out = _tile_s5_out


def _post_ln(self, x, gb_key):
    P, B = self.P, self.B
    rstd, nmr = gb_key
    P.act(x, x, AF.Identity, bias=nmr, scale=rstd)
    P.tt("dve", x, x, B["lnt"][:, 0, :], ALU.mult)
    P.tt("dve", x, x, B["lnt"][:, 1, :], ALU.add)


def _tile_post(self, ns, xs, store):
    P, I, T, S, B = self.P, self.I, self.T, self.S, self.B
    N = 128 * ns
    yT, hT, prodF = B["yT"], B["hT"], B["prodF"]
    cs = lambda s: slice(128 * s, 128 * s + 128)
    lnt = B["lnt"]
    P.dma(lnt[:, 0, :], self.ap_bcast(I["ln1_gain"], 0, D))
    P.dma(lnt[:, 1, :], self.ap_bcast(I["ln1_bias"], 0, D))
    for hf in range(2):
        w = self.wchunk("wM", hf)
        for s in range(ns):
            bk, _ = P.bank()
            P.mmg([(bk[:, 0:512], yT[:, kt, cs(s)], w[:, kt, :], kt == 0, kt == 7) for kt in range(8)])
            xh = xs[s][:, 512 * hf:512 * hf + 512]
            P.stt(xh, xh, ALPHA, bk[:, 0:512], ALU.mult, ALU.add)
    st_ = [self.ln_stats(xs[s]) for s in range(ns)]
    for s in range(ns):
        self.post_ln(xs[s], st_[s])
    st_ = [self.ln_stats(xs[s]) for s in range(ns)]
    for s in range(ns):
        self.ln_to_T(xs[s], s, 32, 24, stats=st_[s])
    fh = T["fhalo"]
    gpre, gact = B["gpre"], B["gact"]
    pend = None

    def conv_stage(t, par, bv, bc_):
        fd = self.wchunk("fdiag", t // 10, src=S["fdiag"][10 * (t // 10):min(22, 10 * (t // 10) + 10)].rearrange("t p j d -> p t j d"))
        tl = t % 10
        P.mmg([(bc_[:, 0:N], fd[:, tl, j, :], gpre[:, par, j:j + N], j == 0, j == 2) for j in range(3)])
        P.act(gact[:, par, 0:N], bc_[:, 0:N], AF.Gelu_apprx_tanh, bias=T["fcb"][:, t:t + 1])
        P.tt("dve", prodF[:, t, 0:N], bv[:, 0:N], gact[:, par, 0:N], ALU.mult)

    for t in range(22):
        par = t % 2
        wv = self.wchunk("wU", t // 4)
        bv, _ = P.bank()
        P.mmg([(bv[:, 0:N], wv[:, kt, 128 * (t % 4):128 * (t % 4) + 128], hT[:, kt, 0:N], kt == 0, kt == 7) for kt in range(8)])
        gt_ = 22 + t
        wg = self.wchunk("wU", gt_ // 4)
        bg, _ = P.bank()
        P.mmg([(bg[:, 0:N], wg[:, kt, 128 * (gt_ % 4):128 * (gt_ % 4) + 128], hT[:, kt, 0:N], kt == 0, kt == 7) for kt in range(8)])
        P.cp("pool", gpre[:, par, 0:2], fh[:, t, :])
        P.cp("act", gpre[:, par, 2:2 + N], bg[:, 0:N])
        P.cp("pool", fh[:, t, :], gpre[:, par, N:N + 2])
        if pend is not None:
            conv_stage(*pend)
        pend = (t, par, bv, bg)
    conv_stage(*pend)
    P.dma(lnt[:, 0, :], self.ap_bcast(I["ln2_gain"], 0, D))
    P.dma(lnt[:, 1, :], self.ap_bcast(I["ln2_bias"], 0, D))
    for hf in range(2):
        acc = [P.bank()[0] for _ in range(ns)]
        for gi, (k0, nk) in enumerate(WF_GROUPS):
            w = self.wchunk("wF", gi * 2 + hf, src=S["wF"][gi * 2 + hf][:, 0:nk, :])
            for s in range(ns):
                P.mmg([(acc[s][:, 0:512], prodF[:, k0 + kk, cs(s)], w[:, kk, :], (gi == 0 and kk == 0), (gi == 2 and kk == nk - 1)) for kk in range(nk)])
        for s in range(ns):
            xh = xs[s][:, 512 * hf:512 * hf + 512]
            P.stt(xh, xh, ALPHA, acc[s][:, 0:512], ALU.mult, ALU.add)
    st_ = [self.ln_stats(xs[s]) for s in range(ns)]
    for s in range(ns):
        self.post_ln(xs[s], st_[s])
        if store is not None:
            P.dma(self.yout[store + 128 * s:store + 128 * s + 128, :], xs[s], q="pool")
    self.dbg_tile = True


Builder.post_ln = _post_ln
Builder.tile_post = _tile_post


_CACHE = {}


def _consts():
    bm = np.zeros((128, 128), np.float32)
    for q in range(4):
        bm[32 * q:32 * q + 32, 32 * q:32 * q + 32] = 1.0
    return np.eye(128, dtype=np.float32), np.triu(np.ones((128, 128), np.float32)), bm


def core_map(inputs, b, xin, flag):
    ident, tri, bm = _consts()
    m = {"xin": np.ascontiguousarray(xin, dtype=np.float32), "cvec": np.ascontiguousarray(inputs["c"][b], dtype=np.float32),
         "flagv": np.full((128, 1), flag, np.float32), "c_ident": ident, "c_tri": tri, "c_bmask": bm}
    for k, v in inputs.items():
        if k in ("x", "c"):
            continue
        m[k] = np.ascontiguousarray(np.asarray(v)[0], dtype=np.float32)
    return m


def kernel(**inputs):
    inputs = {k: np.asarray(v) for k, v in inputs.items()}
    x = inputs["x"]
    Bn, Sq, _ = x.shape
    half = Sq // 2
    n_state = (half - 128) // 128
    state_ns = [4] * (n_state // 4) + ([n_state % 4] if n_state % 4 else [])
    full_ns = [1] + [4] * (half // 512)
    key = (tuple(state_ns), tuple(full_ns))
    if key not in _CACHE:
        bld = Builder(state_ns, full_ns, 1)
        _CACHE[key] = bld.build()
    nc = _CACHE[key]
    in_maps = []
    for core in range(2 * Bn):
        b, h = core // 2, core % 2
        if h == 0:
            xin = np.concatenate([np.zeros((half, D), np.float32), x[b, :half]], axis=0)
        else:
            xin = x[b]
        in_maps.append(core_map(inputs, b, xin, float(h)))
    res = run_bass_kernel_spmd(nc, in_maps, core_ids=list(range(2 * Bn)))
    out = np.empty((Bn, Sq, D), np.float32)
    for core in range(2 * Bn):
        b, h = core // 2, core % 2
        out[b, h * half:(h + 1) * half] = res.results[core]["yout"]
    return out
```

```python
import math
from contextlib import ExitStack
import numpy as np
import concourse.bass as bass
import concourse.mybir as mybir
from concourse.bass_utils import run_bass_kernel_spmd

F32 = mybir.dt.float32
BF16 = mybir.dt.bfloat16
AF = mybir.ActivationFunctionType
ALU = mybir.AluOpType
AX = mybir.AxisListType

N_DMA_SEMS = 24
COMPUTE = ("pe", "act", "dve", "pool")
ENG = {"pe": "tensor", "act": "scalar", "dve": "vector", "pool": "gpsimd", "sp": "sync"}

D = 1024
NH = 4
DV = 256
DK = 128
S5W = 512
FH = 2816
INW = 4616
NKT = 8
ALPHA = 2.0 ** 0.25
LN_EPS = 1e-5
TB = 8
ARENA_F32 = 50688


class Prog:
    def __init__(self, nc, stack):
        self.nc = nc
        self.stack = stack
        self.ops = []
        self.arena = stack.enter_context(nc.sbuf_tensor("arena", [128, ARENA_F32], F32))
        self.top = 0
        self.peak = 0
        self.banks = [stack.enter_context(nc.psum_tensor(f"bank{i}", [128, 512], F32)) for i in range(8)]
        self.bank_i = 0
        self.uid = 0

    def alloc(self, shape, dtype=F32):
        n = 1
        for s in shape[1:]:
            n *= s
        words = (n + 1) // 2 if dtype == BF16 else n
        words = (words + 7) // 8 * 8
        a = self.top
        self.top += words
        self.peak = max(self.peak, self.top)
        assert self.top <= ARENA_F32, f"SBUF arena overflow {self.top}"
        v = self.arena[:, a:a + words]
        if dtype == BF16:
            v = v.bitcast(BF16)
        v = v[:, 0:n]
        if len(shape) == 3:
            v = v.rearrange("p (a b) -> p a b", b=shape[2])
        elif len(shape) == 4:
            v = v.rearrange("p (a b c) -> p a b c", b=shape[2], c=shape[3])
        elif len(shape) == 5:
            v = v.rearrange("p (a b c d) -> p a b c d", b=shape[2], c=shape[3], d=shape[4])
        return v[0:shape[0]] if shape[0] != 128 else v

    def key(self, prefix="k"):
        self.uid += 1
        return f"{prefix}{self.uid}"

    def bank(self):
        i = self.bank_i
        self.bank_i = (i + 1) % 8
        return self.banks[i], f"bank{i}"

    def op(self, eng, fn, reads=(), writes=()):
        self.ops.append(dict(eng=eng, fn=fn, reads=tuple(reads), writes=tuple(writes), dma=False, bar=False))

    def dma(self, out, in_, reads=(), writes=(), q="sp", **kw):
        def fn(e, out=out, in_=in_, kw=kw):
            return e.dma_start(out=out, in_=in_, **kw)
        rd, wr = list(reads), list(writes)
        for ap, lst in ((in_, rd), (out, wr)):
            if isinstance(ap, bass.AP) and ap.tensor.name == "arena":
                lst.append(ap)
        self.ops.append(dict(eng=q, fn=fn, reads=tuple(rd), writes=tuple(wr), dma=True, bar=False))

    def barrier(self):
        self.ops.append(dict(eng=None, fn=None, reads=(), writes=(), dma=False, bar=True))

    @staticmethod
    def _res(x):
        if isinstance(x, str):
            return ("key", x)
        name = x.tensor.name
        if name != "arena":
            return ("key", name)
        esz = 2 if x.dtype == BF16 else 4
        aps = x.ap
        pstride = ARENA_F32 * 4 // esz
        off = x.offset
        p0 = off // pstride
        lo = (off % pstride) * esz
        if aps[0][0] == 0:
            npart = 1
        else:
            npart = aps[0][1]
        span = 1
        for (st_, cnt) in aps[1:]:
            span += (cnt - 1) * abs(st_)
        return ("box", p0, p0 + npart, lo, lo + span * esz)

    def mmg(self, items, extra_reads=()):
        def f(e, items=items):
            for it in items:
                kw = it[5] if len(it) > 5 else {}
                ins = e.matmul(it[0], it[1], it[2], start=it[3], stop=it[4], **kw)
            return ins
        rd, wr = list(extra_reads), []
        for it in items:
            rd += [it[1], it[2]]
            wr.append(it[0])
        self.op("pe", f, rd, wr)

    def trg(self, items):
        def f(e, items=items):
            for (o, i_, idn) in items:
                ins = e.transpose(o, i_, idn)
            return ins
        self.op("pe", f, [x for it in items for x in (it[1], it[2])], [it[0] for it in items])

    def act(self, out, in_, func, bias=None, scale=1.0):
        rd = [in_] + [x for x in (bias, scale) if isinstance(x, bass.AP)]
        kw = {} if bias is None else {"bias": bias}
        self.op("act", lambda e: e.activation(out, in_, func, scale=scale, **kw), rd, [out])

    def tt(self, eng, out, a, b, op):
        self.op(eng, lambda e: e.tensor_tensor(out, a, b, op), [a, b], [out])

    def ts(self, eng, out, a, s1, s2, op0, op1=None):
        rd = [a] + [x for x in (s1, s2) if isinstance(x, bass.AP)]
        if op1 is None:
            self.op(eng, lambda e: e.tensor_scalar(out, a, s1, s2, op0), rd, [out])
        else:
            self.op(eng, lambda e: e.tensor_scalar(out, a, s1, s2, op0, op1), rd, [out])

    def stt(self, out, a, sc, b, op0, op1):
        rd = [a, b] + ([sc] if isinstance(sc, bass.AP) else [])
        self.op("dve", lambda e: e.scalar_tensor_tensor(out, a, sc, b, op0, op1), rd, [out])

    def cp(self, eng, out, in_):
        if eng == "act":
            self.op("act", lambda e: e.copy(out, in_), [in_], [out])
        else:
            self.op(eng, lambda e: e.tensor_copy(out, in_), [in_], [out])

    def view(self, a, shape, dtype=F32):
        top = self.top
        self.top = a
        v = self.alloc(shape, dtype)
        used = self.top
        self.top = max(top, used)
        return v

    def emit(self, final_wait_eng="sp"):
        nc = self.nc
        ops = self.ops
        n = len(ops)
        last_w, readers = {}, {}
        boxes = []
        deps = [set() for _ in ops]
        since_bar = []
        pending = {}
        for i, o in enumerate(ops):
            if o["bar"]:
                last_per_eng, dmas = {}, []
                for p in since_bar:
                    po = ops[p]
                    if po["dma"]:
                        dmas.append(p)
                    else:
                        last_per_eng[po["eng"]] = p
                pre = set(last_per_eng.values()) | set(dmas)
                for e in list(COMPUTE) + ["sp"]:
                    pending[e] = set(pre) | pending.get(e, set())
                since_bar = []
                last_w, readers, boxes = {}, {}, []
                continue
            d = set()
            rres = [self._res(x) for x in o["reads"]]
            wres = [self._res(x) for x in o["writes"]]
            for r in rres:
                if r[0] == "key":
                    if r[1] in last_w:
                        d.add(last_w[r[1]])
                    if r[1].startswith("bank"):
                        for r_ in readers.get(r[1], ()):
                            if ops[r_]["eng"] != o["eng"]:
                                d.add(r_)
                else:
                    _, p0, p1, lo, hi = r
                    for bx in boxes:
                        if bx[5] and bx[1] < p1 and p0 < bx[2] and bx[3] < hi and lo < bx[4]:
                            d.add(bx[0])
            for w in wres:
                if w[0] == "key":
                    if w[1] in last_w:
                        d.add(last_w[w[1]])
                    for r_ in readers.get(w[1], ()):
                        d.add(r_)
                else:
                    _, p0, p1, lo, hi = w
                    for bx in boxes:
                        if bx[1] < p1 and p0 < bx[2] and bx[3] < hi and lo < bx[4]:
                            d.add(bx[0])
            for p in d:
                if p == i:
                    continue
                po = ops[p]
                if not po["dma"] and not o["dma"] and po["eng"] == o["eng"] and o["eng"] == "pe":
                    continue
                deps[i].add(p)
            if pending.get(o["eng"]):
                for p in pending[o["eng"]]:
                    po = ops[p]
                    if (not po["dma"]) and (not o["dma"]) and po["eng"] == o["eng"]:
                        continue
                    deps[i].add(p)
                pending[o["eng"]] = set()
            ek = ("dma", i) if o["dma"] else o["eng"]
            for w in wres:
                if w[0] == "key":
                    last_w[w[1]] = i
                    readers[w[1]] = []
                else:
                    _, p0, p1, lo, hi = w
                    boxes = [bx for bx in boxes if not (p0 <= bx[1] and bx[2] <= p1 and lo <= bx[3] and bx[4] <= hi)]
                    boxes.append([i, p0, p1, lo, hi, True, ek])
            for r in rres:
                if r[0] == "key":
                    readers.setdefault(r[1], []).append(i)
                else:
                    _, p0, p1, lo, hi = r
                    boxes = [bx for bx in boxes if not (not bx[5] and bx[6] == ek and bx[1] == p0 and bx[2] == p1 and bx[3] == lo and bx[4] == hi)]
                    boxes.append([i, p0, p1, lo, hi, False, ek])
            since_bar.append(i)
        needed = set()
        for i in range(n):
            needed |= deps[i]
        sig_no, cnt = {}, {e: 0 for e in COMPUTE}
        dma_slot, dma_tot, ndma = {}, [0] * N_DMA_SEMS, 0
        nq = {"sp": 0, "pool": 0, "act": 0}
        NSP = N_DMA_SEMS - 8
        for i, o in enumerate(ops):
            if o["bar"]:
                continue
            if o["dma"]:
                if o["eng"] == "pool":
                    s = NSP + nq["pool"] % 8
                    nq["pool"] += 1
                else:
                    s = nq["sp"] % NSP
                    nq["sp"] += 1
                ndma += 1
                prev = dma_tot[s]
                dma_tot[s] += 16
                dma_slot[i] = (s, prev, dma_tot[s])
            elif i in needed:
                cnt[o["eng"]] += 1
                sig_no[i] = cnt[o["eng"]]
        st = self.stack
        csem = {e: st.enter_context(nc.semaphore(f"s_{e}")) for e in COMPUTE}
        dsem = [st.enter_context(nc.semaphore(f"s_dma{j}")) for j in range(N_DMA_SEMS)]
        used = sorted({o["eng"] for o in ops if not o["bar"]} | {final_wait_eng})
        with nc.Block() as block:
            for ename in used:
                def body(e, ename=ename):
                    seen = {c: 0 for c in csem}
                    seen_dma = [0] * N_DMA_SEMS
                    for i, o in enumerate(ops):
                        if o["bar"] or o["eng"] != ename:
                            continue
                        for p in sorted(deps[i]):
                            po = ops[p]
                            if po["dma"]:
                                s, _, tgt = dma_slot[p]
                                if seen_dma[s] < tgt:
                                    e.wait_ge(dsem[s], tgt)
                                    seen_dma[s] = tgt
                            else:
                                pe_ = po["eng"]
                                nn = sig_no[p]
                                if seen[pe_] < nn:
                                    e.wait_ge(csem[pe_], nn)
                                    seen[pe_] = nn
                        if o["dma"]:
                            s, prev, tgt = dma_slot[i]
                            if prev > 0 and seen_dma[s] < prev:
                                e.wait_ge(dsem[s], prev)
                                seen_dma[s] = prev
                            o["fn"](e).then_inc(dsem[s], 16)
                        else:
                            ins = o["fn"](e)
                            if i in sig_no:
                                ins.then_inc(csem[ename], 1)
                    if ename == final_wait_eng:
                        for s in range(N_DMA_SEMS):
                            if dma_tot[s] > seen_dma[s]:
                                e.wait_ge(dsem[s], dma_tot[s])
                getattr(block, ENG[ename])(body)
        return dict(n_ops=n, sig=cnt, ndma=ndma, peak_kb=self.peak * 4 / 1024)


WA_SRC = [0, 512, 1024, 1536, 2056, 2568, 3080, 3592, 4104]
BIN_GROUPS = [(0, 8), (1024, 8), (2056, 4), (2568, 8), (3592, 8)]
WF_GROUPS = [(0, 8), (8, 8), (16, 6)]


def dram_in(nc, name, shape, dtype=F32):
    return nc.dram_tensor(name, list(shape), dtype, kind="ExternalInput").ap()


class Builder:
    def __init__(self, state_ns, full_ns, skip_sub, dbg=()):
        self.state_ns = list(state_ns)
        self.full_ns = list(full_ns)
        self.skip_sub = skip_sub
        self.dbg = set(dbg)
        self.ntok = 128 * (sum(state_ns) + sum(full_ns))
        self.nout = 128 * (sum(full_ns) - skip_sub)
        self.nc = bass.Bass("TRN2", target_bir_lowering=False)
        self.dbg_out = {}

    def dbg_dump(self, name, ap, keys, dtype=F32):
        if name not in self.dbg:
            return
        shape = list(ap.shape)
        o = self.nc.dram_tensor("dbg_" + name, shape, dtype, kind="ExternalOutput").ap()
        self.P.dma(o, ap, reads=keys)

    def build(self):
        nc = self.nc
        I = {}
        def inp(name, shape):
            I[name] = dram_in(nc, name, shape)
        inp("xin", [self.ntok, D]); inp("cvec", [D]); inp("flagv", [128, 1])
        inp("w_ada", [D, 6 * D]); inp("b_ada", [6 * D]); inp("w_in", [D, INW]); inp("b_in", [INW])
        inp("w_mlstm_conv", [4, D]); inp("b_mlstm_conv", [D]); inp("w_mlstm_q", [NH, DV, DK]); inp("w_mlstm_k", [NH, DV, DK])
        inp("mlstm_norm_gain", [D]); inp("w_mlstm_down", [D, D])
        inp("s5_lam_re", [32, 64]); inp("s5_lam_im", [32, 64]); inp("s5_log_dt", [32])
        inp("s5_b_re", [32, 64, 16]); inp("s5_b_im", [32, 64, 16]); inp("s5_c_re", [32, 16, 64]); inp("s5_c_im", [32, 16, 64])
        inp("s5_d", [S5W]); inp("w_s5_glu", [S5W, 2 * D]); inp("w_mix_out", [D, D])
        inp("ln1_gain", [D]); inp("ln1_bias", [D]); inp("w_ffn_up", [D, 2 * FH]); inp("w_ffn_conv", [3, FH]); inp("b_ffn_conv", [FH])
        inp("w_ffn_down", [FH, D]); inp("ln2_gain", [D]); inp("ln2_bias", [D])
        inp("c_ident", [128, 128]); inp("c_tri", [128, 128]); inp("c_bmask", [128, 128])
        self.I = I
        self.yout = nc.dram_tensor("yout", [self.nout, D], F32, kind="ExternalOutput").ap()
        S = {}
        def scr(name, shape, dtype=BF16):
            S[name] = nc.dram_tensor("scr_" + name, list(shape), dtype, kind="Internal").ap()
        scr("wA", [9, 128, 8, 512]); scr("wD", [2, 128, 8, 512]); scr("wG", [4, 128, 4, 512]); scr("wM", [2, 128, 8, 512])
        scr("wU", [11, 128, 8, 512]); scr("wF", [6, 128, 8, 512]); scr("fdiag", [22, 128, 3, 128])
        scr("s5K", [128, 4, 8, 128]); scr("s5X", [2, 128, 2, 8, 2, 128]); scr("s5Y", [2, 128, 8, 8, 2, 32])
        self.S = S
        st = ExitStack()
        with st:
            self.P = P = Prog(nc, st)
            self.persistent()
            self.prologue()
            self.main()
            import os as _os
            if _os.environ.get("KMAX"):
                P.ops = P.ops[:int(_os.environ["KMAX"])]
            info = P.emit()
        self.info = info
        return nc

    def persistent(self):
        P, I = self.P, self.I
        A = P.alloc
        T = self.T = {}
        T["ident_f"] = A([128, 128]); T["tri_f"] = A([128, 128]); T["bmask_f"] = A([128, 128]); T["ones_f"] = A([128, 128])
        T["ident_b"] = A([128, 128], BF16); T["mask_b"] = A([128, 128], BF16)
        P.dma(T["ident_f"], I["c_ident"], writes=["ident_f"])
        P.dma(T["tri_f"], I["c_tri"], writes=["tri_f"])
        P.dma(T["bmask_f"], I["c_bmask"], writes=["bmask_f"])
        P.op("pool", lambda e: e.memset(T["ones_f"], 1.0), writes=["ones_f"])
        P.op("dve", lambda e: e.tensor_copy(T["ident_b"], T["ident_f"]), reads=["ident_f"], writes=["ident_b"])
        P.op("dve", lambda e: e.tensor_copy(T["mask_b"], T["tri_f"]), reads=["tri_f"], writes=["mask_b"])
        T["mhalf"] = A([128, 4]); T["flag"] = A([128, 1]); T["xmh"] = A([128, 8, 3], BF16)
        P.op("pool", lambda e: e.memset(T["mhalf"], -0.5), writes=["mhalf"])
        P.dma(T["flag"], I["flagv"], writes=["flag"])
        T["modT"] = A([128, 48])
        T["binT"] = A([128, 36]); T["bgate"] = A([128, 8])
        T["wgt"] = A([128, 8, 8], BF16); T["wq"] = A([128, 4, 2, 128], BF16); T["wk"] = A([128, 4, 2, 128], BF16)
        T["cdiag"] = A([128, 8, 4, 128], BF16); T["cb"] = A([128, 8]); T["gainT"] = A([128, 8])
        T["fcb"] = A([128, 22]); T["dT"] = A([128, 4])
        T["C32"] = A([128, 4, 260]); T["Cb"] = A([128, 4, 260], BF16)
        T["rotc"] = A([128, 16, 64]); T["rots"] = A([128, 16, 64]); T["rho8"] = A([128, 16]); T["u8"] = A([128, 2, 16])
        T["xcar"] = A([128, 2, 16])
        self.d0 = {}
        for nb in sorted({ns * 16 for ns in self.state_ns + self.full_ns}):
            self.d0[nb] = A([128, 16, nb])
        T["fhalo"] = A([128, 22, 2], BF16)

    def ap_cols(self, vec, off, ntile):
        return bass.AP(vec.tensor, off, [[1, 128], [128, ntile]])

    def ap_bcast(self, vec, off, n):
        return bass.AP(vec.tensor, off, [[0, 128], [1, n]])


def bc(ap, n):
    return bass.AP(ap.tensor, ap.offset, [list(x) for x in ap.ap] + [[0, n]])


def bc_mid(ap, n):
    a = [list(x) for x in ap.ap]
    return bass.AP(ap.tensor, ap.offset, [a[0], [0, n]] + a[1:])


SLOW = dict(allow_slow_non_contiguous=True)


def _prologue(self):
    P, I, T, S = self.P, self.I, self.T, self.S
    A = P.alloc
    mark = P.top
    TT = lambda eng, out, a, b, op, rd, wr: P.op(eng, lambda e: e.tensor_tensor(out, a, b, op), reads=rd, writes=wr)
    w_in_v = I["w_in"].rearrange("(kt p) c -> p kt c", p=128)
    for c in range(9):
        P.dma(S["wA"][c], w_in_v[:, :, WA_SRC[c]:WA_SRC[c] + 512], writes=[f"wA{c}"], q="pool")
    P.dma(T["wgt"], w_in_v[:, :, 2048:2056], writes=["wgt"], q="pool")
    P.dma(T["wq"], I["w_mlstm_q"].rearrange("h (kt p) d -> p h kt d", p=128), writes=["wq"], q="pool")
    P.dma(T["wk"], I["w_mlstm_k"].rearrange("h (kt p) d -> p h kt d", p=128), writes=["wk"], q="pool")
    wd_v = I["w_mlstm_down"].rearrange("(kt p) c -> p kt c", p=128)
    for c in range(2):
        P.dma(S["wD"][c], wd_v[:, :, 512 * c:512 * c + 512], writes=[f"wD{c}"], q="pool")
    wg_v = I["w_s5_glu"].rearrange("(kt p) c -> p kt c", p=128)
    for c in range(4):
        P.dma(S["wG"][c], wg_v[:, :, 512 * c:512 * c + 512], writes=[f"wG{c}"], q="pool")
    wu_v = I["w_ffn_up"].rearrange("(kt p) c -> p kt c", p=128)
    for c in range(11):
        P.dma(S["wU"][c], wu_v[:, :, 512 * c:512 * c + 512], writes=[f"wU{c}"], q="pool")
    colstg = [A([128, 128]) for _ in range(2)]
    mark2 = P.top
    cact = A([128, 8]); badaT = A([128, 48]); cw = A([128, 8, 4]); fcw = A([128, 22, 3])
    self.cl_i = 0
    for i_ in range(2):
        P.op("dve", lambda e, i_=i_: e.memset(colstg[i_], 0.0), writes=[f"colstg{i_}"])

    def load_cols(dst, vec, off, nt, dkey, dst_is_3d=None):
        sg = colstg[self.cl_i % 2]; sk = f"colstg{self.cl_i % 2}"; self.cl_i += 1
        P.dma(sg[0:nt, :], bass.AP(vec.tensor, off, [[128, nt], [1, 128]]), reads=[sk], writes=[sk])
        bk_, bkk = P.bank()
        P.op("pe", lambda e, sg=sg, bk_=bk_: e.transpose(bk_[:, 0:128], sg, T["ident_f"]), reads=[sk, "ident_f"], writes=[bkk])
        P.op("dve", lambda e, dst=dst, bk_=bk_, nt=nt: e.tensor_copy(dst, bk_[:, 0:nt] if dst_is_3d is None else dst_is_3d(bk_)), reads=[bkk], writes=[dkey])

    load_cols(cact, I["cvec"], 0, 8, "cact")
    load_cols(badaT, I["b_ada"], 0, 48, "badaT")
    c0 = 0
    for gi, (off, nt) in enumerate(BIN_GROUPS):
        load_cols(T["binT"][:, c0:c0 + nt], I["b_in"], off, nt, f"binT{gi}")
        c0 += nt
    P.dma(T["bgate"], self.ap_bcast(I["b_in"], 2048, 8), writes=["bgate"])
    load_cols(cw.rearrange("p c j -> p j c"), I["w_mlstm_conv"], 0, 32, "cw", dst_is_3d=lambda b_: b_[:, 0:32].rearrange("p (j c) -> p j c", c=8))
    load_cols(T["cb"], I["b_mlstm_conv"], 0, 8, "cb")
    load_cols(T["gainT"], I["mlstm_norm_gain"], 0, 8, "gainT")
    load_cols(fcw.rearrange("p t j -> p j t"), I["w_ffn_conv"], 0, 66, "fcw", dst_is_3d=lambda b_: b_[:, 0:66].rearrange("p (j t) -> p j t", t=22))
    load_cols(T["fcb"], I["b_ffn_conv"], 0, 22, "fcb")
    load_cols(T["dT"], I["s5_d"], 0, 4, "dT")
    self.load_cols = load_cols
    P.op("act", lambda e: e.activation(cact, cact, AF.Silu), reads=["cact"], writes=["cact"])
    cact2 = A([128, 8, 2])
    P.op("dve", lambda e: e.tensor_copy(cact2, bc(cact, 2)), reads=["cact"], writes=["cact2"])
    cactB = A([128, 8, 128])
    P.op("dve", lambda e: e.tensor_copy(cactB, bc(cact, 128)), reads=["cact"], writes=["cactB"])
    g1b = A([128, D]); g2b = A([128, D])
    P.dma(g1b, self.ap_bcast(I["b_ada"], 2 * D, D), writes=["g1b0", "g1b1"])
    P.dma(g2b, self.ap_bcast(I["b_ada"], 5 * D, D), writes=["g2b0", "g2b1"])
    stg = [A([128, 8, 512]) for _ in range(2)]
    wada_v = I["w_ada"].rearrange("(kt p) c -> p kt c", p=128)
    mb, mbk = P.bank()
    for c in range(12):
        sg, sk = stg[c % 2], f"stg{c % 2}"
        P.dma(sg, wada_v[:, :, 512 * c:512 * c + 512], writes=[sk])
        kind, half = c // 2, c % 2
        if kind in (2, 5):
            gb_, gk = (g1b, f"g1b{half}") if kind == 2 else (g2b, f"g2b{half}")
            bk_, bkk = P.bank()
            def f(e, sg=sg, bk_=bk_):
                for kt in range(8):
                    r = e.matmul(bk_[:, 0:512], cactB[:, kt, :], sg[:, kt, :], start=(kt == 0), stop=(kt == 7))
                return r
            P.op("pe", f, reads=[sk, "cactB"], writes=[bkk])
            dst = gb_[:, 512 * half:512 * half + 512]
            P.op("dve", lambda e, dst=dst, bk_=bk_: e.scalar_tensor_tensor(dst, bk_[:, 0:512], 1.0, dst, ALU.add, ALU.add), reads=[bkk, gk], writes=[gk])
        else:
            def f(e, sg=sg, kind=kind, half=half):
                for j in range(4):
                    ct = kind * 8 + half * 4 + j
                    for kt in range(8):
                        r = e.matmul(mb[:, 2 * ct:2 * ct + 2], sg[:, kt, 128 * j:128 * j + 128], cact2[:, kt, :], start=(kt == 0), stop=(kt == 7))
                return r
            P.op("pe", f, reads=[sk, "cact2"], writes=[mbk])
    P.op("pool", lambda e: e.memset(T["modT"], 0.0), writes=["modT0", "modT24"])
    for (a, b) in ((0, 16), (24, 40)):
        P.op("dve", lambda e, a=a, b=b: e.tensor_tensor(T["modT"][:, a:b], mb[:, 2 * a:2 * b:2], badaT[:, a:b], ALU.add), reads=[mbk, "badaT"], writes=[f"modT{a}"])
    for a, kk in ((8, "modT0"), (32, "modT24")):
        P.op("dve", lambda e, a=a: e.tensor_scalar(T["modT"][:, a:a + 8], T["modT"][:, a:a + 8], 1.0, None, ALU.add), reads=[kk], writes=[kk])
    self.dbg_dump("modT", T["modT"], ["modT0", "modT24"])
    self.dbg_dump("g1b", g1b, ["g1b0", "g1b1"])
    ob = [A([128, 8, 512], BF16) for _ in range(2)]
    wm_v = I["w_mix_out"].rearrange("(kt p) c -> p kt c", p=128)
    wf_v = I["w_ffn_down"].rearrange("(kt p) c -> p kt c", p=128)
    jobs = [(wm_v, 0, 8, hf, g1b, f"g1b{hf}", S["wM"][hf], f"wM{hf}") for hf in range(2)]
    for gi, (k0, nk) in enumerate(WF_GROUPS):
        for hf in range(2):
            jobs.append((wf_v, k0, nk, hf, g2b, f"g2b{hf}", S["wF"][gi * 2 + hf], f"wF{gi * 2 + hf}"))
    for ji, (src, k0, nk, hf, gt, gk, dst, dk) in enumerate(jobs):
        sg, sk = stg[ji % 2], f"stg{ji % 2}"
        o_, ok_ = ob[ji % 2], f"ob{ji % 2}"
        P.dma(sg[:, 0:nk, :], src[:, k0:k0 + nk, 512 * hf:512 * hf + 512], writes=[sk])
        eng = "dve"
        P.op(eng, lambda e, o_=o_, sg=sg, nk=nk, gt=gt, hf=hf: e.tensor_tensor(o_[:, 0:nk, :], sg[:, 0:nk, :], bc_mid(gt[:, 512 * hf:512 * hf + 512], nk), ALU.mult),
             reads=[sk, gk], writes=[ok_])
        P.dma(dst[:, 0:nk, :], o_[:, 0:nk, :], reads=[ok_], writes=[dk])
    n = 0
    for ct in range(8):
        for j in range(4):
            eng = "dve"; n += 1
            P.op(eng, lambda e, ct=ct, j=j: e.tensor_scalar(T["cdiag"][:, ct, j, :], T["ident_f"], cw[:, ct, j:j + 1], None, ALU.mult),
                 reads=["cw", "ident_f"], writes=[f"cdiag{ct}_{j}"])
    fd = A([128, 22, 3, 128], BF16)
    for t in range(22):
        for j in range(3):
            eng = "dve"; n += 1
            P.op(eng, lambda e, t=t, j=j: e.tensor_scalar(fd[:, t, j, :], T["ident_f"], fcw[:, t, j:j + 1], None, ALU.mult),
                 reads=["fcw", "ident_f"], writes=[f"fd{t}_{j}"])
    P.dma(S["fdiag"].rearrange("t p j d -> p t j d"), fd, reads=[f"fd{t}_{j}" for t in range(22) for j in range(3)], writes=["fdiag"])
    P.barrier()
    P.top = mark2
    self.s5_prologue()
    P.op("pool", lambda e: e.memset(T["C32"], 0.0), writes=["C32"])
    P.op("pool", lambda e: e.memset(T["Cb"], 0.0), writes=["Cb"])
    P.op("pool", lambda e: e.memset(T["xcar"], 0.0), writes=["xcar"])
    P.op("pool", lambda e: e.memset(T["fhalo"], 0.0), writes=["fhalo"])
    P.op("pool", lambda e: e.memset(T["xmh"], 0.0), writes=["xmh"])
    P.barrier()
    P.top = mark
    print("ops after prologue", len(P.ops))


Builder.prologue = _prologue


def _s5_prologue(self):
    P, I, T, S = self.P, self.I, self.T, self.S
    A = P.alloc
    V = "dve"

    def tk(t):
        return "tmp_" + t.tensor.name + str(t.offset)

    def tt(out, a, b, op, rd, wr, eng=V):
        P.op(eng, lambda e: e.tensor_tensor(out, a, b, op), reads=rd, writes=wr)

    def cmul(outr, outi, ar, ai, br, bi, t1, t2, rd, wr):
        k1, k2 = "tmp_" + t1.tensor.name + str(t1.offset), "tmp_" + t2.tensor.name + str(t2.offset)
        tt(t1, ar, br, ALU.mult, rd, [k1])
        tt(t2, ai, bi, ALU.mult, rd, [k2])
        tt(outr, t1, t2, ALU.subtract, [k1, k2], [wr + "r"])
        tt(t1, ar, bi, ALU.mult, rd + [wr + "r"], [k1])
        tt(t2, ai, br, ALU.mult, rd + [wr + "r"], [k2])
        tt(outi, t1, t2, ALU.add, [k1, k2], [wr + "i"])

    lamr = A([128, 16]); lami = A([128, 16]); ldt = A([128, 16]); dt = A([128, 16]); phi = A([128, 16]); aa = A([128, 16])
    cc = A([128, 16]); ss = A([128, 16]); t1 = A([128, 16]); t2 = A([128, 16])
    self.load_cols(lamr, I["s5_lam_re"], 0, 16, "lamr")
    self.load_cols(lami, I["s5_lam_im"], 0, 16, "lami")
    ldtb = A([128, 32])
    P.dma(ldtb, self.ap_bcast(I["s5_log_dt"], 0, 32), writes=["ldtb"])
    for h in range(2):
        P.op(V, lambda e, h=h: e.tensor_copy(ldt[64 * h:64 * h + 64, :], ldtb[64 * h:64 * h + 64, h:32:2]), reads=["ldtb"], writes=[f"ldt{h}"])
    def taylor_exp(out, x, deg, xk, ok, tmp):
        P.op(V, lambda e: e.tensor_scalar(out, x, 1.0 / deg, 1.0, ALU.mult, ALU.add), reads=xk, writes=[ok])
        for n_ in range(deg - 1, 0, -1):
            tt(tmp, x, out, ALU.mult, xk + [ok], [tk(tmp)])
            P.op(V, lambda e, n_=n_: e.tensor_scalar(out, tmp, 1.0 / n_, 1.0, ALU.mult, ALU.add), reads=[tk(tmp)], writes=[ok])

    P.op(V, lambda e: e.tensor_scalar(ldt, ldt, 1.0 / 16, None, ALU.mult), reads=["ldt0", "ldt1"], writes=["ldt0", "ldt1"])
    taylor_exp(dt, ldt, 10, ["ldt0", "ldt1"], "dt", t1)
    for _ in range(4):
        tt(t2, dt, dt, ALU.mult, ["dt"], [tk(t2)])
        P.op(V, lambda e: e.tensor_copy(dt, t2), reads=[tk(t2)], writes=["dt"])
    tt(phi, lami, dt, ALU.mult, ["lami", "dt"], ["phi"])
    tt(aa, lamr, dt, ALU.mult, ["lamr", "dt"], ["aa"])
    P.op("act", lambda e: e.activation(ss, phi, AF.Sin, scale=1.0 / 32), reads=["phi"], writes=["ss"])
    hp = A([128, 1])
    P.op("pool", lambda e: e.memset(hp, math.pi / 2), writes=["hp"])
    P.op("act", lambda e: e.activation(cc, phi, AF.Sin, scale=1.0 / 32, bias=hp), reads=["phi", "hp"], writes=["cc"])
    for it in range(5):
        tt(t1, cc, cc, ALU.mult, ["cc"], [tk(t1)])
        tt(t2, ss, ss, ALU.mult, ["ss"], [tk(t2)])
        P.op(V, lambda e: e.scalar_tensor_tensor(ss, cc, 2.0, ss, ALU.mult, ALU.mult), reads=["cc", "ss", tk(t2)], writes=["ss"])
        tt(cc, t1, t2, ALU.subtract, [tk(t1), tk(t2), "ss"], ["cc"])
    UPr = A([128, 9, 16]); UPi = A([128, 9, 16]); MG = A([128, 9, 16]); PWr = A([128, 9, 16]); PWi = A([128, 9, 16])
    P.op("pool", lambda e: e.memset(UPr[:, 0, :], 1.0), writes=["UP0r"])
    P.op("pool", lambda e: e.memset(UPi[:, 0, :], 0.0), writes=["UP0i"])
    P.op(V, lambda e: e.tensor_copy(UPr[:, 1, :], cc), reads=["cc"], writes=["UP1r"])
    P.op(V, lambda e: e.tensor_copy(UPi[:, 1, :], ss), reads=["ss"], writes=["UP1i"])
    for k in range(2, 9):
        cmul(UPr[:, k, :], UPi[:, k, :], UPr[:, k - 1, :], UPi[:, k - 1, :], UPr[:, 1, :], UPi[:, 1, :], t1, t2,
             [f"UP{k - 1}r", f"UP{k - 1}i", "UP1r", "UP1i"], f"UP{k}")
    P.op("pool", lambda e: e.memset(MG[:, 0, :], 1.0), writes=["MG0"])
    taylor_exp(MG[:, 1, :], aa, 7, ["aa"], "MG1", t1)
    for k in range(2, 9):
        tt(MG[:, k, :], MG[:, k - 1, :], MG[:, 1, :], ALU.mult, [f"MG{k - 1}", "MG1"], [f"MG{k}"])
    allup = [f"UP{k}{c}" for k in range(9) for c in "ri"] + [f"MG{k}" for k in range(9)]
    tt(PWr, MG, UPr, ALU.mult, allup, ["PWr"])
    tt(PWi, MG, UPi, ALU.mult, allup, ["PWi"])
    PW = ["PWr", "PWi"]
    den = A([128, 16]); am1 = A([128, 16]); zr = A([128, 16]); zi = A([128, 16])
    tt(t1, lamr, lamr, ALU.mult, ["lamr"] + PW, [tk(t1)])
    tt(t2, lami, lami, ALU.mult, ["lami"] + PW, [tk(t2)])
    tt(den, t1, t2, ALU.add, [tk(t1), tk(t2)], ["den"])
    P.op(V, lambda e: e.reciprocal(den, den), reads=["den"], writes=["den"])
    P.op(V, lambda e: e.tensor_scalar(am1, PWr[:, 1, :], -1.0, None, ALU.add), reads=PW, writes=["am1"])
    tt(t1, am1, lamr, ALU.mult, ["am1", "lamr", "den"], [tk(t1)])
    tt(t2, PWi[:, 1, :], lami, ALU.mult, PW + ["lami", "den"], [tk(t2)])
    tt(zr, t1, t2, ALU.add, [tk(t1), tk(t2)], ["zr"])
    tt(zr, zr, den, ALU.mult, ["zr", "den"], ["zr"])
    tt(t1, PWi[:, 1, :], lamr, ALU.mult, PW + ["lamr", "zr"], [tk(t1)])
    tt(t2, am1, lami, ALU.mult, ["am1", "lami", "zr"], [tk(t2)])
    tt(zi, t1, t2, ALU.subtract, [tk(t1), tk(t2)], ["zi"])
    tt(zi, zi, den, ALU.mult, ["zi", "den"], ["zi"])
    Bre = A([128, 16, 16]); Bim = A([128, 16, 16]); Cre = A([128, 16, 16]); Cim = A([128, 16, 16])
    b_ap = lambda v: bass.AP(v.tensor, 0, [[16, 128], [2048, 16], [1, 16]])
    P.dma(Bre, b_ap(I["s5_b_re"]), writes=["Bre"])
    P.dma(Bim, b_ap(I["s5_b_im"]), writes=["Bim"])
    Cl = A([128, 16, 128])
    P.op("pool", lambda e: e.memset(Cl, 0.0), writes=["Cl"])
    for nm, dstC in (("s5_c_re", Cre), ("s5_c_im", Cim)):
        for q in range(16):
            P.dma(Cl[0:16, q, :].rearrange("c (g p) -> c g p", p=64), bass.AP(I[nm].tensor, 2048 * q, [[64, 16], [1024, 2], [1, 64]]), reads=["Cl"], writes=[f"Cl{q}"])
        for b4 in range(4):
            bk_, bkk = P.bank()
            def f(e, bk_=bk_, b4=b4):
                for qq in range(4):
                    r = e.transpose(bk_[:, 128 * qq:128 * qq + 128], Cl[:, 4 * b4 + qq, :], T["ident_f"])
                return r
            P.op("pe", f, reads=[f"Cl{4 * b4 + qq}" for qq in range(4)] + ["ident_f"], writes=[bkk])
            P.op(V, lambda e, dstC=dstC, bk_=bk_, b4=b4: e.tensor_copy(dstC[:, 4 * b4:4 * b4 + 4, :], bk_[:, 0:512].rearrange("p (q x) -> p q x", x=128)[:, :, 0:16]),
                 reads=[bkk], writes=["C" + nm[-2:] + str(b4)])
    CK = [f"C{x}{b}" for x in ("re", "im") for b in range(4)]
    BBr = A([128, 16, 16]); BBi = A([128, 16, 16]); W1 = A([128, 16, 16]); W2 = A([128, 16, 16])
    cmul(BBr, BBi, bc(zr, 16), bc(zi, 16), Bre, Bim, W1, W2, ["zr", "zi", "Bre", "Bim"], "BB")
    PBr = A([128, 8, 16, 16]); PBi = A([128, 8, 16, 16])
    for k in range(8):
        cmul(PBr[:, k], PBi[:, k], bc(PWr[:, k, :], 16), bc(PWi[:, k, :], 16), BBr, BBi, W1, W2, PW + ["BBr", "BBi"], f"PB{k}")
    CBD = A([128, 2, 16, 32])
    P.op("pool", lambda e: e.memset(CBD, 0.0), writes=["CBD"])
    for h in range(2):
        sl = slice(64 * h, 64 * h + 64)
        P.op(V, lambda e, sl=sl, h=h: e.tensor_copy(CBD[sl, 0, :, 16 * h:16 * h + 16], Cre[sl]), reads=[f"Cre{b}" for b in range(4)] + ["CBD"], writes=["CBD"])
        P.op(V, lambda e, sl=sl, h=h: e.tensor_scalar(CBD[sl, 1, :, 16 * h:16 * h + 16], Cim[sl], -1.0, None, ALU.mult), reads=[f"Cim{b}" for b in range(4)] + ["CBD"], writes=["CBD"])
    Mb = [A([128, 4, 2, 16]) for _ in range(4)]
    for b in range(4):
        P.op("pool", lambda e, b=b: e.memset(Mb[b], 0.0), writes=[f"Mb{b}"])
    XwS = A([128, 4, 8, 2, 128], BF16)
    KcS = A([128, 4, 8, 128], BF16)
    dtmp = A([128, 128]); ktmp = A([128, 128])
    nb_ = 0
    for ct in range(4):
        for k in range(8):
            tau = 7 - k
            kb, kbk = P.bank()
            mfs = []
            for ri in range(2):
                m, mk = Mb[nb_ % 4], f"Mb{nb_ % 4}"; nb_ += 1
                src = (PBr if ri == 0 else PBi)
                for h in range(2):
                    sl = slice(64 * h, 64 * h + 64)
                    P.op(V if h == 0 else "pool", lambda e, m=m, sl=sl, h=h, src=src, k=k, ct=ct: e.tensor_copy(m[sl, :, h, :], src[sl, k, 4 * ct:4 * ct + 4, :]),
                         reads=[f"PB{k}r", f"PB{k}i", mk], writes=[mk])
                mf = m.rearrange("p a b c -> p (a b c)")
                mfs.append((mf, mk))
                P.op("pe", lambda e, mf=mf, kb=kb, ri=ri: e.transpose(kb[:, 128 + 128 * ri:256 + 128 * ri], mf, T["ident_f"]), reads=[mk, "ident_f"], writes=[kbk])
            for ri in range(2):
                mf, mk = mfs[ri]
                P.op("pe", lambda e, mf=mf, kb=kb, ri=ri, ct=ct: e.matmul(kb[:, 0:128], mf, CBD[:, ri, 4 * ct:4 * ct + 4, :].rearrange("p a b -> p (a b)"), start=(ri == 0), stop=(ri == 1)),
                     reads=[mk, "CBD"], writes=[kbk])
            P.op("act", lambda e, kb=kb, ct=ct, tau=tau: e.copy(XwS[:, ct, tau].rearrange("p a b -> p (a b)"), kb[:, 128:384]), reads=[kbk], writes=["XwS"])
            if k == 0:
                P.op(V, lambda e, ct=ct: e.tensor_scalar(dtmp, T["ident_f"], T["dT"][:, ct:ct + 1], None, ALU.mult), reads=["dT", "ident_f", "KcS"], writes=["dtmp"])
                P.op(V, lambda e, kb=kb: e.tensor_tensor(ktmp, kb[:, 0:128], T["bmask_f"], ALU.mult), reads=[kbk, "bmask_f", "KcS"], writes=["ktmp"])
                P.op(V, lambda e, ct=ct, k=k: e.tensor_tensor(KcS[:, ct, k, :], ktmp, dtmp, ALU.add), reads=["ktmp", "dtmp"], writes=["KcS"])
            else:
                P.op(V, lambda e, kb=kb, ct=ct, k=k: e.tensor_tensor(KcS[:, ct, k, :], kb[:, 0:128], T["bmask_f"], ALU.mult), reads=[kbk, "bmask_f"], writes=["KcS"])
    P.dma(S["s5K"], KcS, reads=["KcS"], writes=["s5K"])
    for c in range(2):
        P.dma(S["s5X"][c], XwS[:, 2 * c:2 * c + 2], reads=["XwS"], writes=[f"s5X{c}"])
    self.dbg_dump("KcS", KcS, ["KcS"], BF16)
    self.dbg_dump("XwS", XwS, ["XwS"], BF16)
    YwS = A([128, 16, 8, 2, 32], BF16)
    P.op("pool", lambda e: e.memset(YwS, 0.0), writes=["YwS"])
    YR = A([128, 16, 16]); YI = A([128, 16, 16])
    for tau in range(8):
        pr, pi = bc(PWr[:, tau + 1, :], 16), bc(PWi[:, tau + 1, :], 16)
        cmul(YR, YI, Cre, Cim, pr, pi, W1, W2, CK + PW + ["YwS"], "YY")
        for h in range(2):
            sl = slice(64 * h, 64 * h + 64)
            P.op(V, lambda e, sl=sl, h=h, tau=tau: e.tensor_copy(YwS[sl, :, tau, 0, 16 * h:16 * h + 16], YR[sl]), reads=["YYr", "YwS"], writes=["YwS"])
            P.op(V, lambda e, sl=sl, h=h, tau=tau: e.tensor_scalar(YwS[sl, :, tau, 1, 16 * h:16 * h + 16], YI[sl], -1.0, None, ALU.mult), reads=["YYi", "YwS"], writes=["YwS"])
    for c in range(2):
        P.dma(S["s5Y"][c], YwS[:, 8 * c:8 * c + 8], reads=["YwS"], writes=[f"s5Y{c}"])
    self.dbg_dump("YwS", YwS, ["YwS"], BF16)
    rc, rs = T["rotc"], T["rots"]
    P.op("pool", lambda e: e.memset(rc[:, :, 0], 1.0), writes=["rot"])
    P.op("pool", lambda e: e.memset(rs[:, :, 0], 0.0), reads=["rot"], writes=["rot"])
    P.op(V, lambda e: e.tensor_copy(T["u8"][:, 0, :], UPr[:, 8, :]), reads=["UP8r"], writes=["u8"])
    P.op(V, lambda e: e.tensor_copy(T["u8"][:, 1, :], UPi[:, 8, :]), reads=["UP8i", "u8"], writes=["u8"])
    r1r = A([128, 16]); r1i = A([128, 16]); wr_ = A([128, 16]); wi_ = A([128, 16])
    P.op(V, lambda e: e.tensor_copy(r1r, UPr[:, 8, :]), reads=["UP8r"], writes=["r1r"])
    P.op(V, lambda e: e.tensor_scalar(r1i, UPi[:, 8, :], -1.0, None, ALU.mult), reads=["UP8i"], writes=["r1i"])
    ln = 1
    R1 = A([128, 16, 32]); R2 = A([128, 16, 32])
    while ln < 64:
        cmul(wr_, wi_, rc[:, :, ln - 1], rs[:, :, ln - 1], r1r, r1i, t1, t2, ["rot", "r1r", "r1i"], "W")
        a_r, a_i = rc[:, :, 0:ln], rs[:, :, 0:ln]
        w_r, w_i = bc(wr_, ln), bc(wi_, ln)
        x1, x2 = R1[:, :, 0:ln], R2[:, :, 0:ln]
        tt(x1, a_r, w_r, ALU.mult, ["rot", "Wr", "Wi"], ["x1"])
        tt(x2, a_i, w_i, ALU.mult, ["rot", "Wr", "Wi"], ["x2"])
        tt(rc[:, :, ln:2 * ln], x1, x2, ALU.subtract, ["x1", "x2"], ["rot"])
        tt(x1, a_r, w_i, ALU.mult, ["rot", "Wr", "Wi"], ["x1"])
        tt(x2, a_i, w_r, ALU.mult, ["rot", "Wr", "Wi"], ["x2"])
        tt(rs[:, :, ln:2 * ln], x1, x2, ALU.add, ["x1", "x2", "rot"], ["rot"])
        ln *= 2
    P.op(V, lambda e: e.tensor_copy(T["rho8"], MG[:, 8, :]), reads=["MG8"], writes=["rho8"])
    for nb, d0 in self.d0.items():
        P.op(V, lambda e, d0=d0, nb=nb: e.tensor_copy(d0, bc(T["rho8"], nb)), reads=["rho8"], writes=[f"d0_{nb}"])
        P.op(V, lambda e, d0=d0: e.memset(d0[:, :, 0], 0.0), reads=[f"d0_{nb}"], writes=[f"d0_{nb}"])
    self.dbg_dump("rotc", rc, ["rot"])
    self.dbg_dump("rots", rs, ["rot"])


Builder.s5_prologue = _s5_prologue


def _main(self):
    P, I, T, S = self.P, self.I, self.T, self.S
    A = P.alloc
    NSM = max(self.state_ns + self.full_ns)
    NM = 128 * NSM
    NBM = 16 * NSM
    B = self.B = {}
    B["XR"] = [A([128, D]) for _ in range(NSM + 1)]
    B["xn"] = [A([128, D], BF16) for _ in range(2)]
    B["hT"] = A([128, 8, NM], BF16)
    B["hm"] = A([128, NSM, D], BF16)
    B["omS"] = A([128, 8, NM], BF16)
    B["prodT"] = A([128, 8, NM], BF16)
    B["yT"] = A([128, 8, NM], BF16)
    B["gtmp"] = A([128, 3, NM], BF16)
    B["uT"] = A([128, 4, NM], BF16)
    B["ysT"] = A([128, 4, NM], BF16)
    B["gsm"] = A([128, NSM, 48])
    B["lnt"] = A([128, 2, D])
    B["ring"] = [A([128, 8, 512], BF16) for _ in range(3)]
    B["smr"] = A([128, 512])
    self.sm_i = 0
    r0 = P.top
    B["xmT"] = A([128, 8, 3 + NM], BF16)
    B["xcT"] = A([128, 8, NM], BF16)
    B["QT"] = A([128, 4, NM], BF16)
    B["KT"] = A([128, 4, NM], BF16)
    B["Vx"] = A([128, NSM, 4, 258], BF16)
    B["KW"] = A([128, 2, 4, 128], BF16)
    B["SW"] = A([128, 2, 4, 128], BF16)
    r1 = P.top
    P.top = r0
    B["Xin"] = A([128, 2, 16, NBM])
    B["Zr"] = A([128, 16 * NBM]); B["Zi"] = A([128, 16 * NBM])
    B["Ta"] = A([128, 16 * NBM]); B["Tb"] = A([128, 16 * NBM])
    B["XS"] = A([128, 2, 16, NBM])
    B["xprev"] = A([128, 2, 16, NBM], BF16)
    r2 = P.top
    P.top = r0
    B["prodF"] = A([128, 22, NM], BF16)
    B["gpre"] = A([128, 2, 2 + NM], BF16)
    B["gact"] = A([128, 2, NM], BF16)
    r3 = P.top
    P.top = max(r1, r2, r3)
    self.ring_i = 0
    self.ring_tag = [None, None, None]
    self.ring_use = [0, 0, 0]
    self.ring_shape = [None, None, None]
    self.ring_clock = 0
    self.xr_i = 0
    tok = 0
    out_row = 0
    n_pre = len(self.state_ns) + (1 if self.skip_sub > 0 else 0)
    tiles = [("state", ns) for ns in self.state_ns] + [("full", ns) for ns in self.full_ns]
    skipped = 0
    for ti, (mode, ns) in enumerate(tiles):
        store = None
        if mode == "full":
            if skipped < self.skip_sub:
                assert ns <= self.skip_sub - skipped
                skipped += ns
            else:
                store = out_row
                out_row += 128 * ns
        self.tile(mode, tok, ns, store)
        tok += 128 * ns
        if ti == n_pre - 1:
            self.apply_flag()
    assert out_row == self.nout


def _sm(self, n):
    if self.sm_i + n > 512:
        self.sm_i = 0
    v = self.B["smr"][:, self.sm_i:self.sm_i + n]
    self.sm_i += n
    return v


def _wchunk(self, name, idx, shape=None, src=None):
    tag = (name, idx)
    self.ring_clock += 1
    for i in range(3):
        if self.ring_tag[i] == tag:
            self.ring_use[i] = self.ring_clock
            return self.chunk_view(i, shape if shape is not None else self.ring_shape[i])
    i = min(range(3), key=lambda k: self.ring_use[k])
    self.ring_tag[i] = tag
    self.ring_use[i] = self.ring_clock
    if src is None:
        src = self.S[name] if idx is None else self.S[name][idx]
    self.ring_shape[i] = list(src.shape)
    dst = self.chunk_view(i, list(src.shape))
    self.P.dma(dst, src)
    return self.chunk_view(i, shape if shape is not None else list(src.shape))


def _chunk_view(self, i, shape):
    slot = self.B["ring"][i]
    if shape is None or list(shape) == [128, 8, 512]:
        return slot
    flat = slot.rearrange("p a b -> p (a b)")
    n = 1
    for x in shape[1:]:
        n *= x
    v = flat[:, 0:n]
    if len(shape) == 3:
        return v.rearrange("p (a b) -> p a b", b=shape[2])
    if len(shape) == 4:
        return v.rearrange("p (a b c) -> p a b c", b=shape[2], c=shape[3])
    if len(shape) == 5:
        return v.rearrange("p (a b c d) -> p a b c d", b=shape[2], c=shape[3], d=shape[4])
    return v


def _apply_flag(self):
    P, T, B = self.P, self.T, self.B
    fl = T["flag"][:, 0:1]
    P.ts("dve", T["C32"], T["C32"], fl, None, ALU.mult)
    P.cp("pool", T["Cb"], T["C32"])
    P.ts("dve", T["xcar"], T["xcar"], fl, None, ALU.mult)
    P.ts("dve", T["xmh"], T["xmh"], fl, None, ALU.mult)
    P.ts("dve", T["fhalo"], T["fhalo"], fl, None, ALU.mult)


def _ln_stats(self, x):
    P, T = self.P, self.T
    st6 = self.sm(12).rearrange("p (a b) -> p a b", b=6)
    mv = self.sm(2); tmp = self.sm(1); rstd = self.sm(1); nmr = self.sm(1)
    P.op("dve", lambda e: e.bn_stats(st6[:, 0, :], x[:, 0:512]), [x[:, 0:512]], [st6[:, 0, :]])
    P.op("dve", lambda e: e.bn_stats(st6[:, 1, :], x[:, 512:1024]), [x[:, 512:1024]], [st6[:, 1, :]])
    P.op("dve", lambda e: e.bn_aggr(mv, st6.rearrange("p a b -> p (a b)")), [st6], [mv])
    P.ts("dve", tmp, mv[:, 1:2], LN_EPS, None, ALU.add)
    P.tt("pool", rstd, tmp, T["mhalf"][:, 0:1], ALU.pow)
    P.stt(nmr, mv[:, 0:1], -1.0, rstd, ALU.mult, ALU.mult)
    return rstd, nmr


def _ln_to_T(self, x, s, sc0, sh0, stats=None):
    P, T, B = self.P, self.T, self.B
    rstd, nmr = stats if stats is not None else self.ln_stats(x)
    xn = B["xn"][s % 2]
    P.act(xn, x, AF.Identity, bias=nmr, scale=rstd)
    bk, _ = P.bank()
    bb = bk[:, :].bitcast(BF16)
    P.trg([(bb[:, 128 * kt:128 * kt + 128], xn[:, 128 * kt:128 * kt + 128], T["ident_b"]) for kt in range(8)])
    for kt in range(8):
        P.act(B["hT"][:, kt, 128 * s:128 * s + 128], bb[:, 128 * kt:128 * kt + 128], AF.Identity,
              bias=T["modT"][:, sh0 + kt:sh0 + kt + 1], scale=T["modT"][:, sc0 + kt:sc0 + kt + 1])


def _inproj_tile(self, ptile, N):
    P, B = self.P, self.B
    w = self.wchunk("wA", ptile // 4)
    j = ptile % 4
    bk, _ = P.bank()
    P.mmg([(bk[:, 0:N], w[:, kt, 128 * j:128 * j + 128], B["hT"][:, kt, 0:N], kt == 0, kt == 7) for kt in range(8)])
    return bk


Builder.main = _main
Builder.sm = _sm
Builder.wchunk = _wchunk
Builder.chunk_view = _chunk_view
Builder.apply_flag = _apply_flag
Builder.ln_stats = _ln_stats
Builder.ln_to_T = _ln_to_T
Builder.inproj_tile = _inproj_tile


def _tile(self, mode, tok0, ns, store):
    P, I, T, S, B = self.P, self.I, self.T, self.S, self.B
    full = (mode == "full")
    N = 128 * ns
    nb = 16 * ns
    hT, xmT, xcT, QT, KT, Vx = B["hT"], B["xmT"], B["xcT"], B["QT"], B["KT"], B["Vx"]
    cs = lambda s: slice(128 * s, 128 * s + 128)
    xs = []
    for s in range(ns):
        x = B["XR"][self.xr_i]
        self.xr_i = (self.xr_i + 1) % len(B["XR"])
        xs.append(x)
        r0 = tok0 + 128 * s
        P.dma(x, I["xin"][r0:r0 + 128, :], q="pool")
    st_ = [self.ln_stats(xs[s]) for s in range(ns)]
    for s in range(ns):
        self.ln_to_T(xs[s], s, 8, 0, stats=st_[s])
    P.cp("pool", xmT[:, :, 0:3], T["xmh"])
    P.op("pool", lambda e: e.memset(Vx[:, 0:ns, :, 256:257], 1.0), [], [Vx[:, 0:ns, :, 256:257]])
    for ct in range(8):
        bk = self.inproj_tile(ct, N)
        P.act(xmT[:, ct, 3:3 + N], bk[:, 0:N], AF.Identity, bias=T["binT"][:, ct:ct + 1])
    G = B["gsm"]
    for s in range(ns):
        g = G[:, s, :]
        bk, _ = P.bank()
        P.mmg([(bk[:, 0:8], hT[:, kt, cs(s)], T["wgt"][:, kt, :], kt == 0, kt == 7) for kt in range(8)])
        P.tt("dve", g[:, 0:8], bk[:, 0:8], T["bgate"], ALU.add)
        P.act(g[:, 8:12], g[:, 4:8], AF.Exp, scale=-1.0)
        P.act(g[:, 12:16], g[:, 8:12], AF.Ln, bias=T["ones_f"][:, 0:1])
        b2, _ = P.bank()
        P.mmg([(b2[:, 0:4], T["tri_f"], g[:, 12:16], True, True), (b2[:, 4:8], T["ones_f"], g[:, 12:16], True, True)])
        P.tt("dve", g[:, 32:36], g[:, 0:4], b2[:, 0:4], ALU.add)
        P.tt("dve", g[:, 36:40], g[:, 32:36], b2[:, 4:8], ALU.subtract)
        P.act(g[:, 20:24], g[:, 36:40], AF.Exp)
        P.act(g[:, 24:28], b2[:, 4:8], AF.Exp, scale=-1.0)
        if full:
            P.act(g[:, 16:20], g[:, 32:36], AF.Exp)
            P.act(g[:, 28:32], b2[:, 0:4], AF.Exp)
    for ct in range(8):
        bk, _ = P.bank()
        P.mmg([(bk[:, 0:N], T["cdiag"][:, ct, j, :], xmT[:, ct, j:j + N], j == 0, j == 3) for j in range(4)])
        P.act(xcT[:, ct, 0:N], bk[:, 0:N], AF.Silu, bias=T["cb"][:, ct:ct + 1])
    if full:
        for h in range(4):
            bk, _ = P.bank()
            P.mmg([(bk[:, 0:N], T["wq"][:, h, kt, :], xcT[:, 2 * h + kt, 0:N], kt == 0, kt == 1) for kt in range(2)])
            P.op("act", lambda e, h=h, bk=bk: e.mul(QT[:, h, 0:N], bk[:, 0:N], DK ** -0.5), [bk[:, 0:N]], [QT[:, h, 0:N]])
            bk2, _ = P.bank()
            P.mmg([(bk2[:, 0:N], T["wk"][:, h, kt, :], xcT[:, 2 * h + kt, 0:N], kt == 0, kt == 1) for kt in range(2)])
            P.cp("dve", KT[:, h, 0:N], bk2[:, 0:N])
    for s in range(ns):
        g = G[:, s, :]
        par = s % 2
        KW, SW = B["KW"][:, par], B["SW"][:, par]
        kb, _ = P.bank()
        items = []
        for h in range(4):
            for kt in range(2):
                items.append((kb[:, 128 * h:128 * h + 128], xcT[:, 2 * h + kt, cs(s)], T["wk"][:, h, kt, :], kt == 0, kt == 1))
        P.mmg(items)
        P.tt("dve", KW, kb[:, 0:512].rearrange("p (h d) -> p h d", d=128), bc(g[:, 20:24], 128), ALU.mult)
        vb, _ = P.bank()
        vbb = vb[:, :].bitcast(BF16)
        P.trg([(vbb[:, 128 * ct:128 * ct + 128], xmT[:, ct, 3 + 128 * s:3 + 128 * s + 128], T["ident_b"]) for ct in range(8)])
        P.cp("act", Vx[:, s, :, 0:256], vbb[:, 0:1024].rearrange("p (h v) -> p h v", v=256))
        if full:
            sb_, _ = P.bank()
            P.mmg([(sb_[:, 128 * h:128 * h + 128], KT[:, h, cs(s)], QT[:, h, cs(s)], True, True) for h in range(4)])
            for h in range(4):
                P.stt(SW[:, h, :], sb_[:, 128 * h:128 * h + 128], g[:, 16 + h:17 + h], T["mask_b"], ALU.mult, ALU.mult)
            nbs = []
            for h in range(4):
                nbk, _ = P.bank()
                nbs.append(nbk)
                P.mmg([(nbk[:, 0:257], SW[:, h, :], Vx[:, s, h, 0:257], True, False),
                       (nbk[:, 0:257], QT[:, h, cs(s)], T["Cb"][:, h, 0:257], False, True)])
            a1 = self.sm(4); rd = self.sm(4); st6 = self.sm(24).rearrange("p (h b) -> p h b", b=6); mv = self.sm(8).rearrange("p (h b) -> p h b", b=2)
            t1 = self.sm(4); aa = self.sm(4); nbv = self.sm(4)
            for h in range(4):
                P.act(a1[:, h:h + 1], nbs[h][:, 256:257], AF.Abs)
            P.tt("dve", a1, a1, g[:, 28:32], ALU.max)
            P.op("dve", lambda e, rd=rd, a1=a1: e.reciprocal(rd, a1), [a1], [rd])
            for h in range(4):
                P.op("dve", lambda e, h=h, st6=st6, nbs=nbs: e.bn_stats(st6[:, h, :], nbs[h][:, 0:256]), [nbs[h][:, 0:256]], [st6[:, h, :]])
            for h in range(4):
                P.op("dve", lambda e, h=h, st6=st6, mv=mv: e.bn_aggr(mv[:, h, :], st6[:, h, :]), [st6[:, h, :]], [mv[:, h, :]])
            P.tt("dve", t1, mv[:, :, 1], rd, ALU.mult)
            P.tt("dve", t1, t1, rd, ALU.mult)
            P.ts("dve", t1, t1, LN_EPS, None, ALU.add)
            P.tt("pool", t1, t1, T["mhalf"], ALU.pow)
            P.tt("dve", aa, t1, rd, ALU.mult)
            P.stt(nbv, mv[:, :, 0], -1.0, aa, ALU.mult, ALU.mult)
            for h in range(4):
                P.act(B["hm"][:, s, 256 * h:256 * h + 256], nbs[h][:, 0:256], AF.Identity, bias=nbv[:, h:h + 1], scale=aa[:, h:h + 1])
        for h in range(4):
            cbk, _ = P.bank()
            P.mmg([(cbk[:, 0:257], KW[:, h, :], Vx[:, s, h, 0:257], True, True)])
            P.stt(T["C32"][:, h, 0:257], T["C32"][:, h, 0:257], g[:, 24 + h:25 + h], cbk[:, 0:257], ALU.mult, ALU.add)
            P.cp("act", T["Cb"][:, h, 0:257], T["C32"][:, h, 0:257])
    P.cp("pool", T["xmh"], xmT[:, :, N:N + 3])
    self.tile_s5(full, ns, part=1)
    if full:
        self.tile_mix_out(ns, xs)
        self.tile_s5(full, ns, part=2)
        self.tile_post(ns, xs, store)


Builder.tile = _tile


def _tile_mix_out(self, ns, xs):
    P, T, B = self.P, self.T, self.B
    N = 128 * ns
    hm, omS, prodT, yT, gtmp = B["hm"], B["omS"], B["prodT"], B["yT"], B["gtmp"]
    for ct in range(8):
        bk = self.inproj_tile(8 + ct, N)
        P.act(omS[:, ct, 0:N], bk[:, 0:N], AF.Sigmoid, bias=T["binT"][:, 8 + ct:9 + ct])
    for vp in range(4):
        bk, _ = P.bank()
        bb = bk[:, :].bitcast(BF16)
        items = []
        for j in range(2):
            vt = 2 * vp + j
            for s in range(ns):
                items.append((bb[:, 512 * j + 128 * s:512 * j + 128 * s + 128], hm[:, s, 128 * vt:128 * vt + 128], T["ident_b"]))
        P.trg(items)
        for j in range(2):
            vt = 2 * vp + j
            P.stt(prodT[:, vt, 0:N], bb[:, 512 * j:512 * j + N], T["gainT"][:, vt:vt + 1], omS[:, vt, 0:N], ALU.mult, ALU.mult)
    for dt_ in range(8):
        w = self.wchunk("wD", dt_ // 4)
        j = dt_ % 4
        bk, _ = P.bank()
        P.mmg([(bk[:, 0:N], w[:, kt, 128 * j:128 * j + 128], prodT[:, kt, 0:N], kt == 0, kt == 7) for kt in range(8)])
        b2 = self.inproj_tile(20 + dt_, N)
        P.act(gtmp[:, 0, 0:N], b2[:, 0:N], AF.Sigmoid, bias=T["binT"][:, 20 + dt_:21 + dt_])
        P.tt("dve", yT[:, dt_, 0:N], bk[:, 0:N], gtmp[:, 0, 0:N], ALU.mult)


def _tile_s5(self, full, ns, part=1):
    P, T, B = self.P, self.T, self.B
    N = 128 * ns
    nb = 16 * ns
    uT, ysT, yT, gtmp = B["uT"], B["ysT"], B["yT"], B["gtmp"]
    xp = B["xprev"]
    if part == 2:
        return self.tile_s5_out(ns)
    for ct in range(4):
        bk = self.inproj_tile(16 + ct, N)
        P.act(uT[:, ct, 0:N], bk[:, 0:N], AF.Identity, bias=T["binT"][:, 16 + ct:17 + ct])
    xb = [P.bank()[0] for _ in range(4)]
    for ct in range(4):
        w = self.wchunk("s5X", ct // 2)
        for q in range(4):
            items = []
            kw = dict(tile_position=(96, 0)) if q == 3 else {}
            for ri in range(2):
                c0 = (ct * 2 + ri) * nb
                for tau in range(8):
                    items.append((xb[q][:, c0:c0 + nb], w[32 * q:32 * q + 32, ct % 2, tau, ri, :], uT[32 * q:32 * q + 32, ct, tau:N:8], tau == 0, tau == 7, kw))
            P.mmg(items)
    Xin = B["Xin"]
    for q in range(4):
        src = xb[q][:, 0:8 * nb].rearrange("p (c r n) -> p r c n", c=4, r=2)
        P.cp("act" if q % 2 == 0 else "dve", Xin[:, :, q:16:4, 0:nb], src)
    cj, sj = T["rotc"][:, :, 0:nb], T["rots"][:, :, 0:nb]
    v3 = lambda t: t[:, 0:16 * nb].rearrange("p (q n) -> p q n", n=nb)
    Zr, Zi, Ta, Tb = v3(B["Zr"]), v3(B["Zi"]), v3(B["Ta"]), v3(B["Tb"])
    Xr, Xi = Xin[:, 0, :, 0:nb], Xin[:, 1, :, 0:nb]
    P.tt("dve", Ta, cj, Xr, ALU.mult); P.tt("dve", Tb, sj, Xi, ALU.mult); P.tt("dve", Zr, Ta, Tb, ALU.subtract)
    P.tt("dve", Ta, cj, Xi, ALU.mult); P.tt("dve", Tb, sj, Xr, ALU.mult); P.tt("dve", Zi, Ta, Tb, ALU.add)
    xc = T["xcar"]; u8 = T["u8"]
    i_r = self.sm(16); i_i = self.sm(16); ta = self.sm(16); tb = self.sm(16)
    P.tt("dve", ta, u8[:, 0, :], xc[:, 0, :], ALU.mult); P.tt("dve", tb, u8[:, 1, :], xc[:, 1, :], ALU.mult); P.tt("dve", i_r, ta, tb, ALU.subtract)
    P.tt("dve", ta, u8[:, 0, :], xc[:, 1, :], ALU.mult); P.tt("dve", tb, u8[:, 1, :], xc[:, 0, :], ALU.mult); P.tt("dve", i_i, ta, tb, ALU.add)
    P.tt("dve", i_r, i_r, T["rho8"], ALU.mult); P.tt("dve", i_i, i_i, T["rho8"], ALU.mult)
    P.tt("dve", Zr[:, :, 0], Zr[:, :, 0], i_r, ALU.add); P.tt("dve", Zi[:, :, 0], Zi[:, :, 0], i_i, ALU.add)
    d0 = self.d0[nb].rearrange("p q n -> p (q n)")
    fr, fi = B["Ta"][:, 0:16 * nb], B["Tb"][:, 0:16 * nb]
    P.op("dve", lambda e: e.tensor_tensor_scan(fr, d0, B["Zr"][:, 0:16 * nb], 0.0, ALU.mult, ALU.add), [d0, B["Zr"][:, 0:16 * nb]], [fr])
    P.op("dve", lambda e: e.tensor_tensor_scan(fi, d0, B["Zi"][:, 0:16 * nb], 0.0, ALU.mult, ALU.add), [d0, B["Zi"][:, 0:16 * nb]], [fi])
    XS = B["XS"]
    xr_o, xi_o = XS[:, 0, :, 0:nb], XS[:, 1, :, 0:nb]
    P.tt("dve", Zr, cj, Ta, ALU.mult); P.tt("dve", Zi, sj, Tb, ALU.mult); P.tt("dve", xr_o, Zr, Zi, ALU.add)
    P.tt("dve", Zr, cj, Tb, ALU.mult); P.tt("dve", Zi, sj, Ta, ALU.mult); P.tt("dve", xi_o, Zr, Zi, ALU.subtract)
    xp = B["xprev"]
    if full:
        P.cp("act", xp[:, :, :, 0], xc)
        if nb > 1:
            P.cp("act", xp[:, :, :, 1:nb], XS[:, :, :, 0:nb - 1])
    P.cp("dve", xc, XS[:, :, :, nb - 1])


def _tile_s5_out(self, ns):
    P, T, B = self.P, self.T, self.B
    N = 128 * ns
    nb = 16 * ns
    uT, ysT, yT, gtmp = B["uT"], B["ysT"], B["yT"], B["gtmp"]
    xp = B["xprev"]
    kc = None
    for ct in range(4):
        kc = self.wchunk("s5K", None)
        yb, _ = P.bank()
        items = []
        for tp in range(8):
            for tau in range(tp, 8):
                items.append((yb[:, tau:N:8], kc[:, ct, tp, :], uT[:, ct, tau - tp:N:8], (tp == 0 and tau == 0), False, dict(skip_group_check=True)))
        P.mmg(items)
        items = []
        yw = self.wchunk("s5Y", ct // 2)
        for q in range(4):
            for tau in range(8):
                for ri in range(2):
                    last = (q == 3 and tau == 7 and ri == 1)
                    items.append((yb[32 * q:32 * q + 32, tau:N:8], yw[:, (4 * ct + q) % 8, tau, ri, :], xp[:, ri, 4 * ct + q, 0:nb],
                                  False, last, dict(tile_position=(0, 32 * q), skip_group_check=True)))
        P.mmg(items)
        P.act(ysT[:, ct, 0:N], yb[:, 0:N], AF.Gelu_apprx_tanh)
    for dt_ in range(8):
        j = dt_ % 4
        wv = self.wchunk("wG", dt_ // 4)
        bv, _ = P.bank()
        P.mmg([(bv[:, 0:N], wv[:, kt, 128 * j:128 * j + 128], ysT[:, kt, 0:N], kt == 0, kt == 3) for kt in range(4)])
        wg = self.wchunk("wG", 2 + dt_ // 4)
        bg, _ = P.bank()
        P.mmg([(bg[:, 0:N], wg[:, kt, 128 * j:128 * j + 128], ysT[:, kt, 0:N], kt == 0, kt == 3) for kt in range(4)])
        P.act(gtmp[:, 1, 0:N], bg[:, 0:N], AF.Sigmoid)
        P.tt("dve", gtmp[:, 2, 0:N], bv[:, 0:N], gtmp[:, 1, 0:N], ALU.mult)
        b2 = self.inproj_tile(28 + dt_, N)
        P.act(gtmp[:, 0, 0:N], b2[:, 0:N], AF.Sigmoid, bias=T["binT"][:, 28 + dt_:29 + dt_])
        P.tt("dve", gtmp[:, 2, 0:N], gtmp[:, 2, 0:N], gtmp[:, 0, 0:N], ALU.mult)
        P.tt("dve", yT[:, dt_, 0:N], yT[:, dt_, 0:N], gtmp[:, 2, 0:N], ALU.add)


Builder.tile_mix_out = _tile_mix_out
Builder.tile_s5 = _tile_s5
Builder.tile_s5_out = _tile_s5_out


def _post_ln(self, x, gb_key):
    P, B = self.P, self.B
    rstd, nmr = gb_key
    P.act(x, x, AF.Identity, bias=nmr, scale=rstd)
    P.tt("dve", x, x, B["lnt"][:, 0, :], ALU.mult)
    P.tt("dve", x, x, B["lnt"][:, 1, :], ALU.add)


def _tile_post(self, ns, xs, store):
    P, I, T, S, B = self.P, self.I, self.T, self.S, self.B
    N = 128 * ns
    yT, hT, prodF = B["yT"], B["hT"], B["prodF"]
    cs = lambda s: slice(128 * s, 128 * s + 128)
    lnt = B["lnt"]
    P.dma(lnt[:, 0, :], self.ap_bcast(I["ln1_gain"], 0, D))
    P.dma(lnt[:, 1, :], self.ap_bcast(I["ln1_bias"], 0, D))
    for hf in range(2):
        w = self.wchunk("wM", hf)
        for s in range(ns):
            bk, _ = P.bank()
            P.mmg([(bk[:, 0:512], yT[:, kt, cs(s)], w[:, kt, :], kt == 0, kt == 7) for kt in range(8)])
            xh = xs[s][:, 512 * hf:512 * hf + 512]
            P.stt(xh, xh, ALPHA, bk[:, 0:512], ALU.mult, ALU.add)
    st_ = [self.ln_stats(xs[s]) for s in range(ns)]
    for s in range(ns):
        self.post_ln(xs[s], st_[s])
    st_ = [self.ln_stats(xs[s]) for s in range(ns)]
    for s in range(ns):
        self.ln_to_T(xs[s], s, 32, 24, stats=st_[s])
    fh = T["fhalo"]
    gpre, gact = B["gpre"], B["gact"]
    pend = None

    def conv_stage(t, par, bv, bc_):
        fd = self.wchunk("fdiag", t // 10, src=S["fdiag"][10 * (t // 10):min(22, 10 * (t // 10) + 10)].rearrange("t p j d -> p t j d"))
        tl = t % 10
        P.mmg([(bc_[:, 0:N], fd[:, tl, j, :], gpre[:, par, j:j + N], j == 0, j == 2) for j in range(3)])
        P.act(gact[:, par, 0:N], bc_[:, 0:N], AF.Gelu_apprx_tanh, bias=T["fcb"][:, t:t + 1])
        P.tt("dve", prodF[:, t, 0:N], bv[:, 0:N], gact[:, par, 0:N], ALU.mult)

    for t in range(22):
        par = t % 2
        wv = self.wchunk("wU", t // 4)
        bv, _ = P.bank()
        P.mmg([(bv[:, 0:N], wv[:, kt, 128 * (t % 4):128 * (t % 4) + 128], hT[:, kt, 0:N], kt == 0, kt == 7) for kt in range(8)])
        gt_ = 22 + t
        wg = self.wchunk("wU", gt_ // 4)
        bg, _ = P.bank()
        P.mmg([(bg[:, 0:N], wg[:, kt, 128 * (gt_ % 4):128 * (gt_ % 4) + 128], hT[:, kt, 0:N], kt == 0, kt == 7) for kt in range(8)])
        P.cp("dve", gpre[:, par, 0:2], fh[:, t, :])
        P.cp("act", gpre[:, par, 2:2 + N], bg[:, 0:N])
        P.cp("dve", fh[:, t, :], gpre[:, par, N:N + 2])
        if pend is not None:
            conv_stage(*pend)
        pend = (t, par, bv, bg)
    conv_stage(*pend)
    P.dma(lnt[:, 0, :], self.ap_bcast(I["ln2_gain"], 0, D))
    P.dma(lnt[:, 1, :], self.ap_bcast(I["ln2_bias"], 0, D))
    for hf in range(2):
        acc = [P.bank()[0] for _ in range(ns)]
        for gi, (k0, nk) in enumerate(WF_GROUPS):
            w = self.wchunk("wF", gi * 2 + hf, src=S["wF"][gi * 2 + hf][:, 0:nk, :])
            for s in range(ns):
                P.mmg([(acc[s][:, 0:512], prodF[:, k0 + kk, cs(s)], w[:, kk, :], (gi == 0 and kk == 0), (gi == 2 and kk == nk - 1)) for kk in range(nk)])
        for s in range(ns):
            xh = xs[s][:, 512 * hf:512 * hf + 512]
            P.stt(xh, xh, ALPHA, acc[s][:, 0:512], ALU.mult, ALU.add)
    st_ = [self.ln_stats(xs[s]) for s in range(ns)]
    for s in range(ns):
        self.post_ln(xs[s], st_[s])
        if store is not None:
            P.dma(self.yout[store + 128 * s:store + 128 * s + 128, :], xs[s], q="pool")
    self.dbg_tile = True


Builder.post_ln = _post_ln
Builder.tile_post = _tile_post


_CACHE = {}


def _consts():
    bm = np.zeros((128, 128), np.float32)
    for q in range(4):
        bm[32 * q:32 * q + 32, 32 * q:32 * q + 32] = 1.0
    return np.eye(128, dtype=np.float32), np.triu(np.ones((128, 128), np.float32)), bm


def core_map(inputs, b, xin, flag):
    ident, tri, bm = _consts()
    m = {"xin": np.ascontiguousarray(xin, dtype=np.float32), "cvec": np.ascontiguousarray(inputs["c"][b], dtype=np.float32),
         "flagv": np.full((128, 1), flag, np.float32), "c_ident": ident, "c_tri": tri, "c_bmask": bm}
    for k, v in inputs.items():
        if k in ("x", "c"):
            continue
        m[k] = np.ascontiguousarray(np.asarray(v)[0], dtype=np.float32)
    return m


def kernel(**inputs):
    inputs = {k: np.asarray(v) for k, v in inputs.items()}
    x = inputs["x"]
    Bn, Sq, _ = x.shape
    half = Sq // 2
    n_state = (half - 128) // 128
    state_ns = [4] * (n_state // 4) + ([n_state % 4] if n_state % 4 else [])
    full_ns = [1] + [4] * (half // 512)
    key = (tuple(state_ns), tuple(full_ns))
    if key not in _CACHE:
        bld = Builder(state_ns, full_ns, 1)
        _CACHE[key] = bld.build()
    nc = _CACHE[key]
    in_maps = []
    for core in range(2 * Bn):
        b, h = core // 2, core % 2
        if h == 0:
            xin = np.concatenate([np.zeros((half, D), np.float32), x[b, :half]], axis=0)
        else:
            xin = x[b]
        in_maps.append(core_map(inputs, b, xin, float(h)))
    res = run_bass_kernel_spmd(nc, in_maps, core_ids=list(range(2 * Bn)))
    out = np.empty((Bn, Sq, D), np.float32)
    for core in range(2 * Bn):
        b, h = core // 2, core % 2
        out[b, h * half:(h + 1) * half] = res.results[core]["yout"]
    return out
```

```python
import math
from contextlib import ExitStack
import numpy as np
import concourse.bass as bass
import concourse.mybir as mybir
from concourse.bass_utils import run_bass_kernel_spmd

F32 = mybir.dt.float32
BF16 = mybir.dt.bfloat16
AF = mybir.ActivationFunctionType
ALU = mybir.AluOpType
AX = mybir.AxisListType

N_DMA_SEMS = 24
COMPUTE = ("pe", "act", "dve", "pool")
ENG = {"pe": "tensor", "act": "scalar", "dve": "vector", "pool": "gpsimd", "sp": "sync"}

D = 1024
NH = 4
DV = 256
DK = 128
S5W = 512
FH = 2816
INW = 4616
NKT = 8
ALPHA = 2.0 ** 0.25
LN_EPS = 1e-5
TB = 8
ARENA_F32 = 50688


class Prog:
    def __init__(self, nc, stack):
        self.nc = nc
        self.stack = stack
        self.ops = []
        self.arena = stack.enter_context(nc.sbuf_tensor("arena", [128, ARENA_F32], F32))
        self.top = 0
        self.peak = 0
        self.banks = [stack.enter_context(nc.psum_tensor(f"bank{i}", [128, 512], F32)) for i in range(8)]
        self.bank_i = 0
        self.uid = 0

    def alloc(self, shape, dtype=F32):
        n = 1
        for s in shape[1:]:
            n *= s
        words = (n + 1) // 2 if dtype == BF16 else n
        words = (words + 7) // 8 * 8
        a = self.top
        self.top += words
        self.peak = max(self.peak, self.top)
        assert self.top <= ARENA_F32, f"SBUF arena overflow {self.top}"
        v = self.arena[:, a:a + words]
        if dtype == BF16:
            v = v.bitcast(BF16)
        v = v[:, 0:n]
        if len(shape) == 3:
            v = v.rearrange("p (a b) -> p a b", b=shape[2])
        elif len(shape) == 4:
            v = v.rearrange("p (a b c) -> p a b c", b=shape[2], c=shape[3])
        elif len(shape) == 5:
            v = v.rearrange("p (a b c d) -> p a b c d", b=shape[2], c=shape[3], d=shape[4])
        return v[0:shape[0]] if shape[0] != 128 else v

    def key(self, prefix="k"):
        self.uid += 1
        return f"{prefix}{self.uid}"

    def bank(self):
        i = self.bank_i
        self.bank_i = (i + 1) % 8
        return self.banks[i], f"bank{i}"

    def op(self, eng, fn, reads=(), writes=()):
        self.ops.append(dict(eng=eng, fn=fn, reads=tuple(reads), writes=tuple(writes), dma=False, bar=False))

    def dma(self, out, in_, reads=(), writes=(), q="sp", **kw):
        def fn(e, out=out, in_=in_, kw=kw):
            return e.dma_start(out=out, in_=in_, **kw)
        rd, wr = list(reads), list(writes)
        for ap, lst in ((in_, rd), (out, wr)):
            if isinstance(ap, bass.AP) and ap.tensor.name == "arena":
                lst.append(ap)
        self.ops.append(dict(eng=q, fn=fn, reads=tuple(rd), writes=tuple(wr), dma=True, bar=False))

    def barrier(self):
        self.ops.append(dict(eng=None, fn=None, reads=(), writes=(), dma=False, bar=True))

    @staticmethod
    def _res(x):
        if isinstance(x, str):
            return ("key", x)
        name = x.tensor.name
        if name != "arena":
            return ("key", name)
        esz = 2 if x.dtype == BF16 else 4
        aps = x.ap
        pstride = ARENA_F32 * 4 // esz
        off = x.offset
        p0 = off // pstride
        lo = (off % pstride) * esz
        if aps[0][0] == 0:
            npart = 1
        else:
            npart = aps[0][1]
        span = 1
        for (st_, cnt) in aps[1:]:
            span += (cnt - 1) * abs(st_)
        return ("box", p0, p0 + npart, lo, lo + span * esz)

    def mmg(self, items, extra_reads=()):
        def f(e, items=items):
            for it in items:
                kw = it[5] if len(it) > 5 else {}
                ins = e.matmul(it[0], it[1], it[2], start=it[3], stop=it[4], **kw)
            return ins
        rd, wr = list(extra_reads), []
        for it in items:
            rd += [it[1], it[2]]
            wr.append(it[0])
        self.op("pe", f, rd, wr)

    def trg(self, items):
        def f(e, items=items):
            for (o, i_, idn) in items:
                ins = e.transpose(o, i_, idn)
            return ins
        self.op("pe", f, [x for it in items for x in (it[1], it[2])], [it[0] for it in items])

    def act(self, out, in_, func, bias=None, scale=1.0):
        rd = [in_] + [x for x in (bias, scale) if isinstance(x, bass.AP)]
        kw = {} if bias is None else {"bias": bias}
        self.op("act", lambda e: e.activation(out, in_, func, scale=scale, **kw), rd, [out])

    def tt(self, eng, out, a, b, op):
        self.op(eng, lambda e: e.tensor_tensor(out, a, b, op), [a, b], [out])

    def ts(self, eng, out, a, s1, s2, op0, op1=None):
        rd = [a] + [x for x in (s1, s2) if isinstance(x, bass.AP)]
        if op1 is None:
            self.op(eng, lambda e: e.tensor_scalar(out, a, s1, s2, op0), rd, [out])
        else:
            self.op(eng, lambda e: e.tensor_scalar(out, a, s1, s2, op0, op1), rd, [out])

    def stt(self, out, a, sc, b, op0, op1):
        rd = [a, b] + ([sc] if isinstance(sc, bass.AP) else [])
        self.op("dve", lambda e: e.scalar_tensor_tensor(out, a, sc, b, op0, op1), rd, [out])

    def cp(self, eng, out, in_):
        if eng == "act":
            self.op("act", lambda e: e.copy(out, in_), [in_], [out])
        else:
            self.op(eng, lambda e: e.tensor_copy(out, in_), [in_], [out])

    def view(self, a, shape, dtype=F32):
        top = self.top
        self.top = a
        v = self.alloc(shape, dtype)
        used = self.top
        self.top = max(top, used)
        return v

    def emit(self, final_wait_eng="sp"):
        nc = self.nc
        ops = self.ops
        n = len(ops)
        last_w, readers = {}, {}
        boxes = []
        deps = [set() for _ in ops]
        since_bar = []
        pending = {}
        for i, o in enumerate(ops):
            if o["bar"]:
                last_per_eng, dmas = {}, []
                for p in since_bar:
                    po = ops[p]
                    if po["dma"]:
                        dmas.append(p)
                    else:
                        last_per_eng[po["eng"]] = p
                pre = set(last_per_eng.values()) | set(dmas)
                for e in list(COMPUTE) + ["sp"]:
                    pending[e] = set(pre) | pending.get(e, set())
                since_bar = []
                last_w, readers, boxes = {}, {}, []
                continue
            d = set()
            rres = [self._res(x) for x in o["reads"]]
            wres = [self._res(x) for x in o["writes"]]
            for r in rres:
                if r[0] == "key":
                    if r[1] in last_w:
                        d.add(last_w[r[1]])
                    if r[1].startswith("bank"):
                        for r_ in readers.get(r[1], ()):
                            if ops[r_]["eng"] != o["eng"]:
                                d.add(r_)
                else:
                    _, p0, p1, lo, hi = r
                    for bx in boxes:
                        if bx[5] and bx[1] < p1 and p0 < bx[2] and bx[3] < hi and lo < bx[4]:
                            d.add(bx[0])
            for w in wres:
                if w[0] == "key":
                    if w[1] in last_w:
                        d.add(last_w[w[1]])
                    for r_ in readers.get(w[1], ()):
                        d.add(r_)
                else:
                    _, p0, p1, lo, hi = w
                    for bx in boxes:
                        if bx[1] < p1 and p0 < bx[2] and bx[3] < hi and lo < bx[4]:
                            d.add(bx[0])
            for p in d:
                if p == i:
                    continue
                po = ops[p]
                if not po["dma"] and not o["dma"] and po["eng"] == o["eng"] and o["eng"] == "pe":
                    continue
                deps[i].add(p)
            if pending.get(o["eng"]):
                for p in pending[o["eng"]]:
                    po = ops[p]
                    if (not po["dma"]) and (not o["dma"]) and po["eng"] == o["eng"]:
                        continue
                    deps[i].add(p)
                pending[o["eng"]] = set()
            ek = ("dma", i) if o["dma"] else o["eng"]
            for w in wres:
                if w[0] == "key":
                    last_w[w[1]] = i
                    readers[w[1]] = []
                else:
                    _, p0, p1, lo, hi = w
                    boxes = [bx for bx in boxes if not (p0 <= bx[1] and bx[2] <= p1 and lo <= bx[3] and bx[4] <= hi)]
                    boxes.append([i, p0, p1, lo, hi, True, ek])
            for r in rres:
                if r[0] == "key":
                    readers.setdefault(r[1], []).append(i)
                else:
                    _, p0, p1, lo, hi = r
                    boxes = [bx for bx in boxes if not (not bx[5] and bx[6] == ek and bx[1] == p0 and bx[2] == p1 and bx[3] == lo and bx[4] == hi)]
                    boxes.append([i, p0, p1, lo, hi, False, ek])
            since_bar.append(i)
        needed = set()
        for i in range(n):
            needed |= deps[i]
        sig_no, cnt = {}, {e: 0 for e in COMPUTE}
        dma_slot, dma_tot, ndma = {}, [0] * N_DMA_SEMS, 0
        nq = {"sp": 0, "pool": 0, "act": 0}
        NSP = N_DMA_SEMS - 8
        for i, o in enumerate(ops):
            if o["bar"]:
                continue
            if o["dma"]:
                if o["eng"] == "pool":
                    s = NSP + nq["pool"] % 8
                    nq["pool"] += 1
                else:
                    s = nq["sp"] % NSP
                    nq["sp"] += 1
                ndma += 1
                prev = dma_tot[s]
                dma_tot[s] += 16
                dma_slot[i] = (s, prev, dma_tot[s])
            elif i in needed:
                cnt[o["eng"]] += 1
                sig_no[i] = cnt[o["eng"]]
        st = self.stack
        csem = {e: st.enter_context(nc.semaphore(f"s_{e}")) for e in COMPUTE}
        dsem = [st.enter_context(nc.semaphore(f"s_dma{j}")) for j in range(N_DMA_SEMS)]
        used = sorted({o["eng"] for o in ops if not o["bar"]} | {final_wait_eng})
        with nc.Block() as block:
            for ename in used:
                def body(e, ename=ename):
                    seen = {c: 0 for c in csem}
                    seen_dma = [0] * N_DMA_SEMS
                    for i, o in enumerate(ops):
                        if o["bar"] or o["eng"] != ename:
                            continue
                        for p in sorted(deps[i]):
                            po = ops[p]
                            if po["dma"]:
                                s, _, tgt = dma_slot[p]
                                if seen_dma[s] < tgt:
                                    e.wait_ge(dsem[s], tgt)
                                    seen_dma[s] = tgt
                            else:
                                pe_ = po["eng"]
                                nn = sig_no[p]
                                if seen[pe_] < nn:
                                    e.wait_ge(csem[pe_], nn)
                                    seen[pe_] = nn
                        if o["dma"]:
                            s, prev, tgt = dma_slot[i]
                            if prev > 0 and seen_dma[s] < prev:
                                e.wait_ge(dsem[s], prev)
                                seen_dma[s] = prev
                            o["fn"](e).then_inc(dsem[s], 16)
                        else:
                            ins = o["fn"](e)
                            if i in sig_no:
                                ins.then_inc(csem[ename], 1)
                    if ename == final_wait_eng:
                        for s in range(N_DMA_SEMS):
                            if dma_tot[s] > seen_dma[s]:
                                e.wait_ge(dsem[s], dma_tot[s])
                getattr(block, ENG[ename])(body)
        return dict(n_ops=n, sig=cnt, ndma=ndma, peak_kb=self.peak * 4 / 1024)


WA_SRC = [0, 512, 1024, 1536, 2056, 2568, 3080, 3592, 4104]
BIN_GROUPS = [(0, 8), (1024, 8), (2056, 4), (2568, 8), (3592, 8)]
WF_GROUPS = [(0, 8), (8, 8), (16, 6)]


def dram_in(nc, name, shape, dtype=F32):
    return nc.dram_tensor(name, list(shape), dtype, kind="ExternalInput").ap()


class Builder:
    def __init__(self, state_ns, full_ns, skip_sub, dbg=()):
        self.state_ns = list(state_ns)
        self.full_ns = list(full_ns)
        self.skip_sub = skip_sub
        self.dbg = set(dbg)
        self.ntok = 128 * (sum(state_ns) + sum(full_ns))
        self.nout = 128 * (sum(full_ns) - skip_sub)
        self.nc = bass.Bass("TRN2", target_bir_lowering=False)
        self.dbg_out = {}

    def dbg_dump(self, name, ap, keys, dtype=F32):
        if name not in self.dbg:
            return
        shape = list(ap.shape)
        o = self.nc.dram_tensor("dbg_" + name, shape, dtype, kind="ExternalOutput").ap()
        self.P.dma(o, ap, reads=keys)

    def build(self):
        nc = self.nc
        I = {}
        def inp(name, shape):
            I[name] = dram_in(nc, name, shape)
        inp("xin", [self.ntok, D]); inp("cvec", [D]); inp("flagv", [128, 1])
        inp("w_ada", [D, 6 * D]); inp("b_ada", [6 * D]); inp("w_in", [D, INW]); inp("b_in", [INW])
        inp("w_mlstm_conv", [4, D]); inp("b_mlstm_conv", [D]); inp("w_mlstm_q", [NH, DV, DK]); inp("w_mlstm_k", [NH, DV, DK])
        inp("mlstm_norm_gain", [D]); inp("w_mlstm_down", [D, D])
        inp("s5_lam_re", [32, 64]); inp("s5_lam_im", [32, 64]); inp("s5_log_dt", [32])
        inp("s5_b_re", [32, 64, 16]); inp("s5_b_im", [32, 64, 16]); inp("s5_c_re", [32, 16, 64]); inp("s5_c_im", [32, 16, 64])
        inp("s5_d", [S5W]); inp("w_s5_glu", [S5W, 2 * D]); inp("w_mix_out", [D, D])
        inp("ln1_gain", [D]); inp("ln1_bias", [D]); inp("w_ffn_up", [D, 2 * FH]); inp("w_ffn_conv", [3, FH]); inp("b_ffn_conv", [FH])
        inp("w_ffn_down", [FH, D]); inp("ln2_gain", [D]); inp("ln2_bias", [D])
        inp("c_ident", [128, 128]); inp("c_tri", [128, 128]); inp("c_bmask", [128, 128])
        self.I = I
        self.yout = nc.dram_tensor("yout", [self.nout, D], F32, kind="ExternalOutput").ap()
        S = {}
        def scr(name, shape, dtype=BF16):
            S[name] = nc.dram_tensor("scr_" + name, list(shape), dtype, kind="Internal").ap()
        scr("wA", [9, 128, 8, 512]); scr("wD", [2, 128, 8, 512]); scr("wG", [4, 128, 4, 512]); scr("wM", [2, 128, 8, 512])
        scr("wU", [11, 128, 8, 512]); scr("wF", [6, 128, 8, 512]); scr("fdiag", [22, 128, 3, 128])
        scr("s5K", [128, 4, 8, 128]); scr("s5X", [2, 128, 2, 8, 2, 128]); scr("s5Y", [2, 128, 8, 8, 2, 32])
        self.S = S
        st = ExitStack()
        with st:
            self.P = P = Prog(nc, st)
            self.persistent()
            self.prologue()
            self.main()
            import os as _os
            if _os.environ.get("KMAX"):
                P.ops = P.ops[:int(_os.environ["KMAX"])]
            info = P.emit()
        self.info = info
        return nc

    def persistent(self):
        P, I = self.P, self.I
        A = P.alloc
        T = self.T = {}
        T["ident_f"] = A([128, 128]); T["tri_f"] = A([128, 128]); T["bmask_f"] = A([128, 128]); T["ones_f"] = A([128, 128])
        T["ident_b"] = A([128, 128], BF16); T["mask_b"] = A([128, 128], BF16)
        P.dma(T["ident_f"], I["c_ident"], writes=["ident_f"])
        P.dma(T["tri_f"], I["c_tri"], writes=["tri_f"])
        P.dma(T["bmask_f"], I["c_bmask"], writes=["bmask_f"])
        P.op("pool", lambda e: e.memset(T["ones_f"], 1.0), writes=["ones_f"])
        P.op("dve", lambda e: e.tensor_copy(T["ident_b"], T["ident_f"]), reads=["ident_f"], writes=["ident_b"])
        P.op("dve", lambda e: e.tensor_copy(T["mask_b"], T["tri_f"]), reads=["tri_f"], writes=["mask_b"])
        T["mhalf"] = A([128, 4]); T["flag"] = A([128, 1]); T["xmh"] = A([128, 8, 3], BF16)
        P.op("pool", lambda e: e.memset(T["mhalf"], -0.5), writes=["mhalf"])
        P.dma(T["flag"], I["flagv"], writes=["flag"])
        T["modT"] = A([128, 48])
        T["binT"] = A([128, 36]); T["bgate"] = A([128, 8])
        T["wgt"] = A([128, 8, 8], BF16); T["wq"] = A([128, 4, 2, 128], BF16); T["wk"] = A([128, 4, 2, 128], BF16)
        T["cdiag"] = A([128, 8, 4, 128], BF16); T["cb"] = A([128, 8]); T["gainT"] = A([128, 8])
        T["fcb"] = A([128, 22]); T["dT"] = A([128, 4])
        T["C32"] = A([128, 4, 260]); T["Cb"] = A([128, 4, 260], BF16)
        T["rotc"] = A([128, 16, 64]); T["rots"] = A([128, 16, 64]); T["rho8"] = A([128, 16]); T["u8"] = A([128, 2, 16])
        T["xcar"] = A([128, 2, 16])
        self.d0 = {}
        for nb in sorted({ns * 16 for ns in self.state_ns + self.full_ns}):
            self.d0[nb] = A([128, 16, nb])
        T["fhalo"] = A([128, 22, 2], BF16)

    def ap_cols(self, vec, off, ntile):
        return bass.AP(vec.tensor, off, [[1, 128], [128, ntile]])

    def ap_bcast(self, vec, off, n):
        return bass.AP(vec.tensor, off, [[0, 128], [1, n]])


def bc(ap, n):
    return bass.AP(ap.tensor, ap.offset, [list(x) for x in ap.ap] + [[0, n]])


def bc_mid(ap, n):
    a = [list(x) for x in ap.ap]
    return bass.AP(ap.tensor, ap.offset, [a[0], [0, n]] + a[1:])


SLOW = dict(allow_slow_non_contiguous=True)


def _prologue(self):
    P, I, T, S = self.P, self.I, self.T, self.S
    A = P.alloc
    mark = P.top
    TT = lambda eng, out, a, b, op, rd, wr: P.op(eng, lambda e: e.tensor_tensor(out, a, b, op), reads=rd, writes=wr)
    w_in_v = I["w_in"].rearrange("(kt p) c -> p kt c", p=128)
    for c in range(9):
        P.dma(S["wA"][c], w_in_v[:, :, WA_SRC[c]:WA_SRC[c] + 512], writes=[f"wA{c}"], q="pool")
    P.dma(T["wgt"], w_in_v[:, :, 2048:2056], writes=["wgt"], q="pool")
    P.dma(T["wq"], I["w_mlstm_q"].rearrange("h (kt p) d -> p h kt d", p=128), writes=["wq"], q="pool")
    P.dma(T["wk"], I["w_mlstm_k"].rearrange("h (kt p) d -> p h kt d", p=128), writes=["wk"], q="pool")
    wd_v = I["w_mlstm_down"].rearrange("(kt p) c -> p kt c", p=128)
    for c in range(2):
        P.dma(S["wD"][c], wd_v[:, :, 512 * c:512 * c + 512], writes=[f"wD{c}"], q="pool")
    wg_v = I["w_s5_glu"].rearrange("(kt p) c -> p kt c", p=128)
    for c in range(4):
        P.dma(S["wG"][c], wg_v[:, :, 512 * c:512 * c + 512], writes=[f"wG{c}"], q="pool")
    wu_v = I["w_ffn_up"].rearrange("(kt p) c -> p kt c", p=128)
    for c in range(11):
        P.dma(S["wU"][c], wu_v[:, :, 512 * c:512 * c + 512], writes=[f"wU{c}"], q="pool")
    colstg = [A([128, 128]) for _ in range(2)]
    mark2 = P.top
    cact = A([128, 8]); badaT = A([128, 48]); cw = A([128, 8, 4]); fcw = A([128, 22, 3])
    self.cl_i = 0
    for i_ in range(2):
        P.op("dve", lambda e, i_=i_: e.memset(colstg[i_], 0.0), writes=[f"colstg{i_}"])

    def load_cols(dst, vec, off, nt, dkey, dst_is_3d=None):
        sg = colstg[self.cl_i % 2]; sk = f"colstg{self.cl_i % 2}"; self.cl_i += 1
        P.dma(sg[0:nt, :], bass.AP(vec.tensor, off, [[128, nt], [1, 128]]), reads=[sk], writes=[sk])
        bk_, bkk = P.bank()
        P.op("pe", lambda e, sg=sg, bk_=bk_: e.transpose(bk_[:, 0:128], sg, T["ident_f"]), reads=[sk, "ident_f"], writes=[bkk])
        P.op("dve", lambda e, dst=dst, bk_=bk_, nt=nt: e.tensor_copy(dst, bk_[:, 0:nt] if dst_is_3d is None else dst_is_3d(bk_)), reads=[bkk], writes=[dkey])

    load_cols(cact, I["cvec"], 0, 8, "cact")
    load_cols(badaT, I["b_ada"], 0, 48, "badaT")
    c0 = 0
    for gi, (off, nt) in enumerate(BIN_GROUPS):
        load_cols(T["binT"][:, c0:c0 + nt], I["b_in"], off, nt, f"binT{gi}")
        c0 += nt
    P.dma(T["bgate"], self.ap_bcast(I["b_in"], 2048, 8), writes=["bgate"])
    load_cols(cw.rearrange("p c j -> p j c"), I["w_mlstm_conv"], 0, 32, "cw", dst_is_3d=lambda b_: b_[:, 0:32].rearrange("p (j c) -> p j c", c=8))
    load_cols(T["cb"], I["b_mlstm_conv"], 0, 8, "cb")
    load_cols(T["gainT"], I["mlstm_norm_gain"], 0, 8, "gainT")
    load_cols(fcw.rearrange("p t j -> p j t"), I["w_ffn_conv"], 0, 66, "fcw", dst_is_3d=lambda b_: b_[:, 0:66].rearrange("p (j t) -> p j t", t=22))
    load_cols(T["fcb"], I["b_ffn_conv"], 0, 22, "fcb")
    load_cols(T["dT"], I["s5_d"], 0, 4, "dT")
    self.load_cols = load_cols
    P.op("act", lambda e: e.activation(cact, cact, AF.Silu), reads=["cact"], writes=["cact"])
    cact2 = A([128, 8, 2])
    P.op("dve", lambda e: e.tensor_copy(cact2, bc(cact, 2)), reads=["cact"], writes=["cact2"])
    cactB = A([128, 8, 128])
    P.op("dve", lambda e: e.tensor_copy(cactB, bc(cact, 128)), reads=["cact"], writes=["cactB"])
    g1b = A([128, D]); g2b = A([128, D])
    P.dma(g1b, self.ap_bcast(I["b_ada"], 2 * D, D), writes=["g1b0", "g1b1"])
    P.dma(g2b, self.ap_bcast(I["b_ada"], 5 * D, D), writes=["g2b0", "g2b1"])
    stg = [A([128, 8, 512]) for _ in range(2)]
    wada_v = I["w_ada"].rearrange("(kt p) c -> p kt c", p=128)
    mb, mbk = P.bank()
    for c in range(12):
        sg, sk = stg[c % 2], f"stg{c % 2}"
        P.dma(sg, wada_v[:, :, 512 * c:512 * c + 512], writes=[sk])
        kind, half = c // 2, c % 2
        if kind in (2, 5):
            gb_, gk = (g1b, f"g1b{half}") if kind == 2 else (g2b, f"g2b{half}")
            bk_, bkk = P.bank()
            def f(e, sg=sg, bk_=bk_):
                for kt in range(8):
                    r = e.matmul(bk_[:, 0:512], cactB[:, kt, :], sg[:, kt, :], start=(kt == 0), stop=(kt == 7))
                return r
            P.op("pe", f, reads=[sk, "cactB"], writes=[bkk])
            dst = gb_[:, 512 * half:512 * half + 512]
            P.op("dve", lambda e, dst=dst, bk_=bk_: e.scalar_tensor_tensor(dst, bk_[:, 0:512], 1.0, dst, ALU.add, ALU.add), reads=[bkk, gk], writes=[gk])
        else:
            def f(e, sg=sg, kind=kind, half=half):
                for j in range(4):
                    ct = kind * 8 + half * 4 + j
                    for kt in range(8):
                        r = e.matmul(mb[:, 2 * ct:2 * ct + 2], sg[:, kt, 128 * j:128 * j + 128], cact2[:, kt, :], start=(kt == 0), stop=(kt == 7))
                return r
            P.op("pe", f, reads=[sk, "cact2"], writes=[mbk])
    P.op("pool", lambda e: e.memset(T["modT"], 0.0), writes=["modT0", "modT24"])
    for (a, b) in ((0, 16), (24, 40)):
        P.op("dve", lambda e, a=a, b=b: e.tensor_tensor(T["modT"][:, a:b], mb[:, 2 * a:2 * b:2], badaT[:, a:b], ALU.add), reads=[mbk, "badaT"], writes=[f"modT{a}"])
    for a, kk in ((8, "modT0"), (32, "modT24")):
        P.op("dve", lambda e, a=a: e.tensor_scalar(T["modT"][:, a:a + 8], T["modT"][:, a:a + 8], 1.0, None, ALU.add), reads=[kk], writes=[kk])
    self.dbg_dump("modT", T["modT"], ["modT0", "modT24"])
    self.dbg_dump("g1b", g1b, ["g1b0", "g1b1"])
    ob = [A([128, 8, 512], BF16) for _ in range(2)]
    wm_v = I["w_mix_out"].rearrange("(kt p) c -> p kt c", p=128)
    wf_v = I["w_ffn_down"].rearrange("(kt p) c -> p kt c", p=128)
    jobs = [(wm_v, 0, 8, hf, g1b, f"g1b{hf}", S["wM"][hf], f"wM{hf}") for hf in range(2)]
    for gi, (k0, nk) in enumerate(WF_GROUPS):
        for hf in range(2):
            jobs.append((wf_v, k0, nk, hf, g2b, f"g2b{hf}", S["wF"][gi * 2 + hf], f"wF{gi * 2 + hf}"))
    for ji, (src, k0, nk, hf, gt, gk, dst, dk) in enumerate(jobs):
        sg, sk = stg[ji % 2], f"stg{ji % 2}"
        o_, ok_ = ob[ji % 2], f"ob{ji % 2}"
        P.dma(sg[:, 0:nk, :], src[:, k0:k0 + nk, 512 * hf:512 * hf + 512], writes=[sk])
        eng = "dve"
        P.op(eng, lambda e, o_=o_, sg=sg, nk=nk, gt=gt, hf=hf: e.tensor_tensor(o_[:, 0:nk, :], sg[:, 0:nk, :], bc_mid(gt[:, 512 * hf:512 * hf + 512], nk), ALU.mult),
             reads=[sk, gk], writes=[ok_])
        P.dma(dst[:, 0:nk, :], o_[:, 0:nk, :], reads=[ok_], writes=[dk])
    n = 0
    for ct in range(8):
        for j in range(4):
            eng = "dve"; n += 1
            P.op(eng, lambda e, ct=ct, j=j: e.tensor_scalar(T["cdiag"][:, ct, j, :], T["ident_f"], cw[:, ct, j:j + 1], None, ALU.mult),
                 reads=["cw", "ident_f"], writes=[f"cdiag{ct}_{j}"])
    fd = A([128, 22, 3, 128], BF16)
    for t in range(22):
        for j in range(3):
            eng = "dve"; n += 1
            P.op(eng, lambda e, t=t, j=j: e.tensor_scalar(fd[:, t, j, :], T["ident_f"], fcw[:, t, j:j + 1], None, ALU.mult),
                 reads=["fcw", "ident_f"], writes=[f"fd{t}_{j}"])
    P.dma(S["fdiag"].rearrange("t p j d -> p t j d"), fd, reads=[f"fd{t}_{j}" for t in range(22) for j in range(3)], writes=["fdiag"])
    P.barrier()
    P.top = mark2
    self.s5_prologue()
    P.op("pool", lambda e: e.memset(T["C32"], 0.0), writes=["C32"])
    P.op("pool", lambda e: e.memset(T["Cb"], 0.0), writes=["Cb"])
    P.op("pool", lambda e: e.memset(T["xcar"], 0.0), writes=["xcar"])
    P.op("pool", lambda e: e.memset(T["fhalo"], 0.0), writes=["fhalo"])
    P.op("pool", lambda e: e.memset(T["xmh"], 0.0), writes=["xmh"])
    P.barrier()
    P.top = mark
    print("ops after prologue", len(P.ops))


Builder.prologue = _prologue


def _s5_prologue(self):
    P, I, T, S = self.P, self.I, self.T, self.S
    A = P.alloc
    V = "dve"

    def tk(t):
        return "tmp_" + t.tensor.name + str(t.offset)

    def tt(out, a, b, op, rd, wr, eng=V):
        P.op(eng, lambda e: e.tensor_tensor(out, a, b, op), reads=rd, writes=wr)

    def cmul(outr, outi, ar, ai, br, bi, t1, t2, rd, wr):
        k1, k2 = "tmp_" + t1.tensor.name + str(t1.offset), "tmp_" + t2.tensor.name + str(t2.offset)
        tt(t1, ar, br, ALU.mult, rd, [k1])
        tt(t2, ai, bi, ALU.mult, rd, [k2])
        tt(outr, t1, t2, ALU.subtract, [k1, k2], [wr + "r"])
        tt(t1, ar, bi, ALU.mult, rd + [wr + "r"], [k1])
        tt(t2, ai, br, ALU.mult, rd + [wr + "r"], [k2])
        tt(outi, t1, t2, ALU.add, [k1, k2], [wr + "i"])

    lamr = A([128, 16]); lami = A([128, 16]); ldt = A([128, 16]); dt = A([128, 16]); phi = A([128, 16]); aa = A([128, 16])
    cc = A([128, 16]); ss = A([128, 16]); t1 = A([128, 16]); t2 = A([128, 16])
    self.load_cols(lamr, I["s5_lam_re"], 0, 16, "lamr")
    self.load_cols(lami, I["s5_lam_im"], 0, 16, "lami")
    ldtb = A([128, 32])
    P.dma(ldtb, self.ap_bcast(I["s5_log_dt"], 0, 32), writes=["ldtb"])
    for h in range(2):
        P.op(V, lambda e, h=h: e.tensor_copy(ldt[64 * h:64 * h + 64, :], ldtb[64 * h:64 * h + 64, h:32:2]), reads=["ldtb"], writes=[f"ldt{h}"])
    def taylor_exp(out, x, deg, xk, ok, tmp):
        P.op(V, lambda e: e.tensor_scalar(out, x, 1.0 / deg, 1.0, ALU.mult, ALU.add), reads=xk, writes=[ok])
        for n_ in range(deg - 1, 0, -1):
            tt(tmp, x, out, ALU.mult, xk + [ok], [tk(tmp)])
            P.op(V, lambda e, n_=n_: e.tensor_scalar(out, tmp, 1.0 / n_, 1.0, ALU.mult, ALU.add), reads=[tk(tmp)], writes=[ok])

    P.op(V, lambda e: e.tensor_scalar(ldt, ldt, 1.0 / 16, None, ALU.mult), reads=["ldt0", "ldt1"], writes=["ldt0", "ldt1"])
    taylor_exp(dt, ldt, 10, ["ldt0", "ldt1"], "dt", t1)
    for _ in range(4):
        tt(t2, dt, dt, ALU.mult, ["dt"], [tk(t2)])
        P.op(V, lambda e: e.tensor_copy(dt, t2), reads=[tk(t2)], writes=["dt"])
    tt(phi, lami, dt, ALU.mult, ["lami", "dt"], ["phi"])
    tt(aa, lamr, dt, ALU.mult, ["lamr", "dt"], ["aa"])
    P.op("act", lambda e: e.activation(ss, phi, AF.Sin, scale=1.0 / 32), reads=["phi"], writes=["ss"])
    hp = A([128, 1])
    P.op("pool", lambda e: e.memset(hp, math.pi / 2), writes=["hp"])
    P.op("act", lambda e: e.activation(cc, phi, AF.Sin, scale=1.0 / 32, bias=hp), reads=["phi", "hp"], writes=["cc"])
    for it in range(5):
        tt(t1, cc, cc, ALU.mult, ["cc"], [tk(t1)])
        tt(t2, ss, ss, ALU.mult, ["ss"], [tk(t2)])
        P.op(V, lambda e: e.scalar_tensor_tensor(ss, cc, 2.0, ss, ALU.mult, ALU.mult), reads=["cc", "ss", tk(t2)], writes=["ss"])
        tt(cc, t1, t2, ALU.subtract, [tk(t1), tk(t2), "ss"], ["cc"])
    UPr = A([128, 9, 16]); UPi = A([128, 9, 16]); MG = A([128, 9, 16]); PWr = A([128, 9, 16]); PWi = A([128, 9, 16])
    P.op("pool", lambda e: e.memset(UPr[:, 0, :], 1.0), writes=["UP0r"])
    P.op("pool", lambda e: e.memset(UPi[:, 0, :], 0.0), writes=["UP0i"])
    P.op(V, lambda e: e.tensor_copy(UPr[:, 1, :], cc), reads=["cc"], writes=["UP1r"])
    P.op(V, lambda e: e.tensor_copy(UPi[:, 1, :], ss), reads=["ss"], writes=["UP1i"])
    for k in range(2, 9):
        cmul(UPr[:, k, :], UPi[:, k, :], UPr[:, k - 1, :], UPi[:, k - 1, :], UPr[:, 1, :], UPi[:, 1, :], t1, t2,
             [f"UP{k - 1}r", f"UP{k - 1}i", "UP1r", "UP1i"], f"UP{k}")
    P.op("pool", lambda e: e.memset(MG[:, 0, :], 1.0), writes=["MG0"])
    taylor_exp(MG[:, 1, :], aa, 7, ["aa"], "MG1", t1)
    for k in range(2, 9):
        tt(MG[:, k, :], MG[:, k - 1, :], MG[:, 1, :], ALU.mult, [f"MG{k - 1}", "MG1"], [f"MG{k}"])
    allup = [f"UP{k}{c}" for k in range(9) for c in "ri"] + [f"MG{k}" for k in range(9)]
    tt(PWr, MG, UPr, ALU.mult, allup, ["PWr"])
    tt(PWi, MG, UPi, ALU.mult, allup, ["PWi"])
    PW = ["PWr", "PWi"]
    den = A([128, 16]); am1 = A([128, 16]); zr = A([128, 16]); zi = A([128, 16])
    tt(t1, lamr, lamr, ALU.mult, ["lamr"] + PW, [tk(t1)])
    tt(t2, lami, lami, ALU.mult, ["lami"] + PW, [tk(t2)])
    tt(den, t1, t2, ALU.add, [tk(t1), tk(t2)], ["den"])
    P.op(V, lambda e: e.reciprocal(den, den), reads=["den"], writes=["den"])
    P.op(V, lambda e: e.tensor_scalar(am1, PWr[:, 1, :], -1.0, None, ALU.add), reads=PW, writes=["am1"])
    tt(t1, am1, lamr, ALU.mult, ["am1", "lamr", "den"], [tk(t1)])
    tt(t2, PWi[:, 1, :], lami, ALU.mult, PW + ["lami", "den"], [tk(t2)])
    tt(zr, t1, t2, ALU.add, [tk(t1), tk(t2)], ["zr"])
    tt(zr, zr, den, ALU.mult, ["zr", "den"], ["zr"])
    tt(t1, PWi[:, 1, :], lamr, ALU.mult, PW + ["lamr", "zr"], [tk(t1)])
    tt(t2, am1, lami, ALU.mult, ["am1", "lami", "zr"], [tk(t2)])
    tt(zi, t1, t2, ALU.subtract, [tk(t1), tk(t2)], ["zi"])
    tt(zi, zi, den, ALU.mult, ["zi", "den"], ["zi"])
    Bre = A([128, 16, 16]); Bim = A([128, 16, 16]); Cre = A([128, 16, 16]); Cim = A([128, 16, 16])
    b_ap = lambda v: bass.AP(v.tensor, 0, [[16, 128], [2048, 16], [1, 16]])
    P.dma(Bre, b_ap(I["s5_b_re"]), writes=["Bre"])
    P.dma(Bim, b_ap(I["s5_b_im"]), writes=["Bim"])
    Cl = A([128, 16, 128])
    P.op("pool", lambda e: e.memset(Cl, 0.0), writes=["Cl"])
    for nm, dstC in (("s5_c_re", Cre), ("s5_c_im", Cim)):
        for q in range(16):
            P.dma(Cl[0:16, q, :].rearrange("c (g p) -> c g p", p=64), bass.AP(I[nm].tensor, 2048 * q, [[64, 16], [1024, 2], [1, 64]]), reads=["Cl"], writes=[f"Cl{q}"])
        for b4 in range(4):
            bk_, bkk = P.bank()
            def f(e, bk_=bk_, b4=b4):
                for qq in range(4):
                    r = e.transpose(bk_[:, 128 * qq:128 * qq + 128], Cl[:, 4 * b4 + qq, :], T["ident_f"])
                return r
            P.op("pe", f, reads=[f"Cl{4 * b4 + qq}" for qq in range(4)] + ["ident_f"], writes=[bkk])
            P.op(V, lambda e, dstC=dstC, bk_=bk_, b4=b4: e.tensor_copy(dstC[:, 4 * b4:4 * b4 + 4, :], bk_[:, 0:512].rearrange("p (q x) -> p q x", x=128)[:, :, 0:16]),
                 reads=[bkk], writes=["C" + nm[-2:] + str(b4)])
    CK = [f"C{x}{b}" for x in ("re", "im") for b in range(4)]
    BBr = A([128, 16, 16]); BBi = A([128, 16, 16]); W1 = A([128, 16, 16]); W2 = A([128, 16, 16])
    cmul(BBr, BBi, bc(zr, 16), bc(zi, 16), Bre, Bim, W1, W2, ["zr", "zi", "Bre", "Bim"], "BB")
    PBr = A([128, 8, 16, 16]); PBi = A([128, 8, 16, 16])
    for k in range(8):
        cmul(PBr[:, k], PBi[:, k], bc(PWr[:, k, :], 16), bc(PWi[:, k, :], 16), BBr, BBi, W1, W2, PW + ["BBr", "BBi"], f"PB{k}")
    CBD = A([128, 2, 16, 32])
    P.op("pool", lambda e: e.memset(CBD, 0.0), writes=["CBD"])
    for h in range(2):
        sl = slice(64 * h, 64 * h + 64)
        P.op(V, lambda e, sl=sl, h=h: e.tensor_copy(CBD[sl, 0, :, 16 * h:16 * h + 16], Cre[sl]), reads=[f"Cre{b}" for b in range(4)] + ["CBD"], writes=["CBD"])
        P.op(V, lambda e, sl=sl, h=h: e.tensor_scalar(CBD[sl, 1, :, 16 * h:16 * h + 16], Cim[sl], -1.0, None, ALU.mult), reads=[f"Cim{b}" for b in range(4)] + ["CBD"], writes=["CBD"])
    Mb = [A([128, 4, 2, 16]) for _ in range(4)]
    for b in range(4):
        P.op("pool", lambda e, b=b: e.memset(Mb[b], 0.0), writes=[f"Mb{b}"])
    XwS = A([128, 4, 8, 2, 128], BF16)
    KcS = A([128, 4, 8, 128], BF16)
    dtmp = A([128, 128]); ktmp = A([128, 128])
    nb_ = 0
    for ct in range(4):
        for k in range(8):
            tau = 7 - k
            kb, kbk = P.bank()
            mfs = []
            for ri in range(2):
                m, mk = Mb[nb_ % 4], f"Mb{nb_ % 4}"; nb_ += 1
                src = (PBr if ri == 0 else PBi)
                for h in range(2):
                    sl = slice(64 * h, 64 * h + 64)
                    P.op(V if h == 0 else "pool", lambda e, m=m, sl=sl, h=h, src=src, k=k, ct=ct: e.tensor_copy(m[sl, :, h, :], src[sl, k, 4 * ct:4 * ct + 4, :]),
                         reads=[f"PB{k}r", f"PB{k}i", mk], writes=[mk])
                mf = m.rearrange("p a b c -> p (a b c)")
                mfs.append((mf, mk))
                P.op("pe", lambda e, mf=mf, kb=kb, ri=ri: e.transpose(kb[:, 128 + 128 * ri:256 + 128 * ri], mf, T["ident_f"]), reads=[mk, "ident_f"], writes=[kbk])
            for ri in range(2):
                mf, mk = mfs[ri]
                P.op("pe", lambda e, mf=mf, kb=kb, ri=ri, ct=ct: e.matmul(kb[:, 0:128], mf, CBD[:, ri, 4 * ct:4 * ct + 4, :].rearrange("p a b -> p (a b)"), start=(ri == 0), stop=(ri == 1)),
                     reads=[mk, "CBD"], writes=[kbk])
            P.op("act", lambda e, kb=kb, ct=ct, tau=tau: e.copy(XwS[:, ct, tau].rearrange("p a b -> p (a b)"), kb[:, 128:384]), reads=[kbk], writes=["XwS"])
            if k == 0:
                P.op(V, lambda e, ct=ct: e.tensor_scalar(dtmp, T["ident_f"], T["dT"][:, ct:ct + 1], None, ALU.mult), reads=["dT", "ident_f", "KcS"], writes=["dtmp"])
                P.op(V, lambda e, kb=kb: e.tensor_tensor(ktmp, kb[:, 0:128], T["bmask_f"], ALU.mult), reads=[kbk, "bmask_f", "KcS"], writes=["ktmp"])
                P.op(V, lambda e, ct=ct, k=k: e.tensor_tensor(KcS[:, ct, k, :], ktmp, dtmp, ALU.add), reads=["ktmp", "dtmp"], writes=["KcS"])
            else:
                P.op(V, lambda e, kb=kb, ct=ct, k=k: e.tensor_tensor(KcS[:, ct, k, :], kb[:, 0:128], T["bmask_f"], ALU.mult), reads=[kbk, "bmask_f"], writes=["KcS"])
    P.dma(S["s5K"], KcS, reads=["KcS"], writes=["s5K"])
    for c in range(2):
        P.dma(S["s5X"][c], XwS[:, 2 * c:2 * c + 2], reads=["XwS"], writes=[f"s5X{c}"])
    self.dbg_dump("KcS", KcS, ["KcS"], BF16)
    self.dbg_dump("XwS", XwS, ["XwS"], BF16)
    YwS = A([128, 16, 8, 2, 32], BF16)
    P.op("pool", lambda e: e.memset(YwS, 0.0), writes=["YwS"])
    YR = A([128, 16, 16]); YI = A([128, 16, 16])
    for tau in range(8):
        pr, pi = bc(PWr[:, tau + 1, :], 16), bc(PWi[:, tau + 1, :], 16)
        cmul(YR, YI, Cre, Cim, pr, pi, W1, W2, CK + PW + ["YwS"], "YY")
        for h in range(2):
            sl = slice(64 * h, 64 * h + 64)
            P.op(V, lambda e, sl=sl, h=h, tau=tau: e.tensor_copy(YwS[sl, :, tau, 0, 16 * h:16 * h + 16], YR[sl]), reads=["YYr", "YwS"], writes=["YwS"])
            P.op(V, lambda e, sl=sl, h=h, tau=tau: e.tensor_scalar(YwS[sl, :, tau, 1, 16 * h:16 * h + 16], YI[sl], -1.0, None, ALU.mult), reads=["YYi", "YwS"], writes=["YwS"])
    for c in range(2):
        P.dma(S["s5Y"][c], YwS[:, 8 * c:8 * c + 8], reads=["YwS"], writes=[f"s5Y{c}"])
    self.dbg_dump("YwS", YwS, ["YwS"], BF16)
    rc, rs = T["rotc"], T["rots"]
    P.op("pool", lambda e: e.memset(rc[:, :, 0], 1.0), writes=["rot"])
    P.op("pool", lambda e: e.memset(rs[:, :, 0], 0.0), reads=["rot"], writes=["rot"])
    P.op(V, lambda e: e.tensor_copy(T["u8"][:, 0, :], UPr[:, 8, :]), reads=["UP8r"], writes=["u8"])
    P.op(V, lambda e: e.tensor_copy(T["u8"][:, 1, :], UPi[:, 8, :]), reads=["UP8i", "u8"], writes=["u8"])
    r1r = A([128, 16]); r1i = A([128, 16]); wr_ = A([128, 16]); wi_ = A([128, 16])
    P.op(V, lambda e: e.tensor_copy(r1r, UPr[:, 8, :]), reads=["UP8r"], writes=["r1r"])
    P.op(V, lambda e: e.tensor_scalar(r1i, UPi[:, 8, :], -1.0, None, ALU.mult), reads=["UP8i"], writes=["r1i"])
    ln = 1
    R1 = A([128, 16, 32]); R2 = A([128, 16, 32])
    while ln < 64:
        cmul(wr_, wi_, rc[:, :, ln - 1], rs[:, :, ln - 1], r1r, r1i, t1, t2, ["rot", "r1r", "r1i"], "W")
        a_r, a_i = rc[:, :, 0:ln], rs[:, :, 0:ln]
        w_r, w_i = bc(wr_, ln), bc(wi_, ln)
        x1, x2 = R1[:, :, 0:ln], R2[:, :, 0:ln]
        tt(x1, a_r, w_r, ALU.mult, ["rot", "Wr", "Wi"], ["x1"])
        tt(x2, a_i, w_i, ALU.mult, ["rot", "Wr", "Wi"], ["x2"])
        tt(rc[:, :, ln:2 * ln], x1, x2, ALU.subtract, ["x1", "x2"], ["rot"])
        tt(x1, a_r, w_i, ALU.mult, ["rot", "Wr", "Wi"], ["x1"])
        tt(x2, a_i, w_r, ALU.mult, ["rot", "Wr", "Wi"], ["x2"])
        tt(rs[:, :, ln:2 * ln], x1, x2, ALU.add, ["x1", "x2", "rot"], ["rot"])
        ln *= 2
    P.op(V, lambda e: e.tensor_copy(T["rho8"], MG[:, 8, :]), reads=["MG8"], writes=["rho8"])
    for nb, d0 in self.d0.items():
        P.op(V, lambda e, d0=d0, nb=nb: e.tensor_copy(d0, bc(T["rho8"], nb)), reads=["rho8"], writes=[f"d0_{nb}"])
        P.op(V, lambda e, d0=d0: e.memset(d0[:, :, 0], 0.0), reads=[f"d0_{nb}"], writes=[f"d0_{nb}"])
    self.dbg_dump("rotc", rc, ["rot"])
    self.dbg_dump("rots", rs, ["rot"])


Builder.s5_prologue = _s5_prologue


def _main(self):
    P, I, T, S = self.P, self.I, self.T, self.S
    A = P.alloc
    NSM = max(self.state_ns + self.full_ns)
    NM = 128 * NSM
    NBM = 16 * NSM
    B = self.B = {}
    B["XR"] = [A([128, D]) for _ in range(NSM + 1)]
    B["xn"] = [A([128, D], BF16) for _ in range(2)]
    B["hT"] = A([128, 8, NM], BF16)
    B["hm"] = A([128, NSM, D], BF16)
    B["omS"] = A([128, 8, NM], BF16)
    B["prodT"] = A([128, 8, NM], BF16)
    B["yT"] = A([128, 8, NM], BF16)
    B["gtmp"] = A([128, 3, NM], BF16)
    B["uT"] = A([128, 4, NM], BF16)
    B["ysT"] = A([128, 4, NM], BF16)
    B["gsm"] = A([128, NSM, 48])
    B["lnt"] = A([128, 2, D])
    B["ring"] = [A([128, 8, 512], BF16) for _ in range(3)]
    B["smr"] = A([128, 512])
    self.sm_i = 0
    r0 = P.top
    B["xmT"] = A([128, 8, 3 + NM], BF16)
    B["xcT"] = A([128, 8, NM], BF16)
    B["QT"] = A([128, 4, NM], BF16)
    B["KT"] = A([128, 4, NM], BF16)
    B["Vx"] = A([128, NSM, 4, 258], BF16)
    B["KW"] = A([128, 2, 4, 128], BF16)
    B["SW"] = A([128, 2, 4, 128], BF16)
    r1 = P.top
    P.top = r0
    B["Xin"] = A([128, 2, 16, NBM])
    B["Zr"] = A([128, 16 * NBM]); B["Zi"] = A([128, 16 * NBM])
    B["Ta"] = A([128, 16 * NBM]); B["Tb"] = A([128, 16 * NBM])
    B["XS"] = A([128, 2, 16, NBM])
    B["xprev"] = A([128, 2, 16, NBM], BF16)
    r2 = P.top
    P.top = r0
    B["prodF"] = A([128, 22, NM], BF16)
    B["gpre"] = A([128, 2, 2 + NM], BF16)
    B["gact"] = A([128, 2, NM], BF16)
    r3 = P.top
    P.top = max(r1, r2, r3)
    self.ring_i = 0
    self.ring_tag = [None, None, None]
    self.ring_use = [0, 0, 0]
    self.ring_shape = [None, None, None]
    self.ring_clock = 0
    self.xr_i = 0
    tok = 0
    out_row = 0
    n_pre = len(self.state_ns) + (1 if self.skip_sub > 0 else 0)
    tiles = [("state", ns) for ns in self.state_ns] + [("full", ns) for ns in self.full_ns]
    skipped = 0
    for ti, (mode, ns) in enumerate(tiles):
        store = None
        if mode == "full":
            if skipped < self.skip_sub:
                assert ns <= self.skip_sub - skipped
                skipped += ns
            else:
                store = out_row
                out_row += 128 * ns
        self.tile(mode, tok, ns, store)
        tok += 128 * ns
        if ti == n_pre - 1:
            self.apply_flag()
    assert out_row == self.nout


def _sm(self, n):
    if self.sm_i + n > 512:
        self.sm_i = 0
    v = self.B["smr"][:, self.sm_i:self.sm_i + n]
    self.sm_i += n
    return v


def _wchunk(self, name, idx, shape=None, src=None):
    tag = (name, idx)
    self.ring_clock += 1
    for i in range(3):
        if self.ring_tag[i] == tag:
            self.ring_use[i] = self.ring_clock
            return self.chunk_view(i, shape if shape is not None else self.ring_shape[i])
    i = min(range(3), key=lambda k: self.ring_use[k])
    self.ring_tag[i] = tag
    self.ring_use[i] = self.ring_clock
    if src is None:
        src = self.S[name] if idx is None else self.S[name][idx]
    self.ring_shape[i] = list(src.shape)
    dst = self.chunk_view(i, list(src.shape))
    self.P.dma(dst, src)
    return self.chunk_view(i, shape if shape is not None else list(src.shape))


def _chunk_view(self, i, shape):
    slot = self.B["ring"][i]
    if shape is None or list(shape) == [128, 8, 512]:
        return slot
    flat = slot.rearrange("p a b -> p (a b)")
    n = 1
    for x in shape[1:]:
        n *= x
    v = flat[:, 0:n]
    if len(shape) == 3:
        return v.rearrange("p (a b) -> p a b", b=shape[2])
    if len(shape) == 4:
        return v.rearrange("p (a b c) -> p a b c", b=shape[2], c=shape[3])
    if len(shape) == 5:
        return v.rearrange("p (a b c d) -> p a b c d", b=shape[2], c=shape[3], d=shape[4])
    return v


def _apply_flag(self):
    P, T, B = self.P, self.T, self.B
    fl = T["flag"][:, 0:1]
    P.ts("dve", T["C32"], T["C32"], fl, None, ALU.mult)
    P.cp("pool", T["Cb"], T["C32"])
    P.ts("dve", T["xcar"], T["xcar"], fl, None, ALU.mult)
    P.ts("dve", T["xmh"], T["xmh"], fl, None, ALU.mult)
    P.ts("dve", T["fhalo"], T["fhalo"], fl, None, ALU.mult)


def _ln_stats(self, x):
    P, T = self.P, self.T
    st6 = self.sm(12).rearrange("p (a b) -> p a b", b=6)
    mv = self.sm(2); tmp = self.sm(1); rstd = self.sm(1); nmr = self.sm(1)
    P.op("dve", lambda e: e.bn_stats(st6[:, 0, :], x[:, 0:512]), [x[:, 0:512]], [st6[:, 0, :]])
    P.op("dve", lambda e: e.bn_stats(st6[:, 1, :], x[:, 512:1024]), [x[:, 512:1024]], [st6[:, 1, :]])
    P.op("dve", lambda e: e.bn_aggr(mv, st6.rearrange("p a b -> p (a b)")), [st6], [mv])
    P.ts("dve", tmp, mv[:, 1:2], LN_EPS, None, ALU.add)
    P.tt("pool", rstd, tmp, T["mhalf"][:, 0:1], ALU.pow)
    P.stt(nmr, mv[:, 0:1], -1.0, rstd, ALU.mult, ALU.mult)
    return rstd, nmr


def _ln_to_T(self, x, s, sc0, sh0, stats=None):
    P, T, B = self.P, self.T, self.B
    rstd, nmr = stats if stats is not None else self.ln_stats(x)
    xn = B["xn"][s % 2]
    P.act(xn, x, AF.Identity, bias=nmr, scale=rstd)
    bk, _ = P.bank()
    bb = bk[:, :].bitcast(BF16)
    P.trg([(bb[:, 128 * kt:128 * kt + 128], xn[:, 128 * kt:128 * kt + 128], T["ident_b"]) for kt in range(8)])
    for kt in range(8):
        P.act(B["hT"][:, kt, 128 * s:128 * s + 128], bb[:, 128 * kt:128 * kt + 128], AF.Identity,
              bias=T["modT"][:, sh0 + kt:sh0 + kt + 1], scale=T["modT"][:, sc0 + kt:sc0 + kt + 1])


def _inproj_tile(self, ptile, N):
    P, B = self.P, self.B
    w = self.wchunk("wA", ptile // 4)
    j = ptile % 4
    bk, _ = P.bank()
    P.mmg([(bk[:, 0:N], w[:, kt, 128 * j:128 * j + 128], B["hT"][:, kt, 0:N], kt == 0, kt == 7) for kt in range(8)])
    return bk


Builder.main = _main
Builder.sm = _sm
Builder.wchunk = _wchunk
Builder.chunk_view = _chunk_view
Builder.apply_flag = _apply_flag
Builder.ln_stats = _ln_stats
Builder.ln_to_T = _ln_to_T
Builder.inproj_tile = _inproj_tile


def _tile(self, mode, tok0, ns, store):
    P, I, T, S, B = self.P, self.I, self.T, self.S, self.B
    full = (mode == "full")
    N = 128 * ns
    nb = 16 * ns
    hT, xmT, xcT, QT, KT, Vx = B["hT"], B["xmT"], B["xcT"], B["QT"], B["KT"], B["Vx"]
    cs = lambda s: slice(128 * s, 128 * s + 128)
    xs = []
    for s in range(ns):
        x = B["XR"][self.xr_i]
        self.xr_i = (self.xr_i + 1) % len(B["XR"])
        xs.append(x)
        r0 = tok0 + 128 * s
        P.dma(x, I["xin"][r0:r0 + 128, :], q="pool")
    st_ = [self.ln_stats(xs[s]) for s in range(ns)]
    for s in range(ns):
        self.ln_to_T(xs[s], s, 8, 0, stats=st_[s])
    P.cp("dve", xmT[:, :, 0:3], T["xmh"])
    P.op("dve", lambda e: e.memset(Vx[:, 0:ns, :, 256:257], 1.0), [], [Vx[:, 0:ns, :, 256:257]])
    for ct in range(8):
        bk = self.inproj_tile(ct, N)
        P.act(xmT[:, ct, 3:3 + N], bk[:, 0:N], AF.Identity, bias=T["binT"][:, ct:ct + 1])
    G = B["gsm"]
    for s in range(ns):
        g = G[:, s, :]
        bk, _ = P.bank()
        P.mmg([(bk[:, 0:8], hT[:, kt, cs(s)], T["wgt"][:, kt, :], kt == 0, kt == 7) for kt in range(8)])
        P.tt("dve", g[:, 0:8], bk[:, 0:8], T["bgate"], ALU.add)
        P.act(g[:, 8:12], g[:, 4:8], AF.Exp, scale=-1.0)
        P.act(g[:, 12:16], g[:, 8:12], AF.Ln, bias=T["ones_f"][:, 0:1])
        b2, _ = P.bank()
        P.mmg([(b2[:, 0:4], T["tri_f"], g[:, 12:16], True, True), (b2[:, 4:8], T["ones_f"], g[:, 12:16], True, True)])
        P.tt("dve", g[:, 32:36], g[:, 0:4], b2[:, 0:4], ALU.add)
        P.tt("dve", g[:, 36:40], g[:, 32:36], b2[:, 4:8], ALU.subtract)
        P.act(g[:, 20:24], g[:, 36:40], AF.Exp)
        P.act(g[:, 24:28], b2[:, 4:8], AF.Exp, scale=-1.0)
        if full:
            P.act(g[:, 16:20], g[:, 32:36], AF.Exp)
            P.act(g[:, 28:32], b2[:, 0:4], AF.Exp)
    for ct in range(8):
        bk, _ = P.bank()
        P.mmg([(bk[:, 0:N], T["cdiag"][:, ct, j, :], xmT[:, ct, j:j + N], j == 0, j == 3) for j in range(4)])
        P.act(xcT[:, ct, 0:N], bk[:, 0:N], AF.Silu, bias=T["cb"][:, ct:ct + 1])
    if full:
        for h in range(4):
            bk, _ = P.bank()
            P.mmg([(bk[:, 0:N], T["wq"][:, h, kt, :], xcT[:, 2 * h + kt, 0:N], kt == 0, kt == 1) for kt in range(2)])
            P.op("act", lambda e, h=h, bk=bk: e.mul(QT[:, h, 0:N], bk[:, 0:N], DK ** -0.5), [bk[:, 0:N]], [QT[:, h, 0:N]])
            bk2, _ = P.bank()
            P.mmg([(bk2[:, 0:N], T["wk"][:, h, kt, :], xcT[:, 2 * h + kt, 0:N], kt == 0, kt == 1) for kt in range(2)])
            P.cp("dve", KT[:, h, 0:N], bk2[:, 0:N])
    for s in range(ns):
        g = G[:, s, :]
        par = s % 2
        KW, SW = B["KW"][:, par], B["SW"][:, par]
        kb, _ = P.bank()
        items = []
        for h in range(4):
            for kt in range(2):
                items.append((kb[:, 128 * h:128 * h + 128], xcT[:, 2 * h + kt, cs(s)], T["wk"][:, h, kt, :], kt == 0, kt == 1))
        P.mmg(items)
        P.tt("dve", KW, kb[:, 0:512].rearrange("p (h d) -> p h d", d=128), bc(g[:, 20:24], 128), ALU.mult)
        vb, _ = P.bank()
        vbb = vb[:, :].bitcast(BF16)
        P.trg([(vbb[:, 128 * ct:128 * ct + 128], xmT[:, ct, 3 + 128 * s:3 + 128 * s + 128], T["ident_b"]) for ct in range(8)])
        P.cp("act", Vx[:, s, :, 0:256], vbb[:, 0:1024].rearrange("p (h v) -> p h v", v=256))
        if full:
            sb_, _ = P.bank()
            P.mmg([(sb_[:, 128 * h:128 * h + 128], KT[:, h, cs(s)], QT[:, h, cs(s)], True, True) for h in range(4)])
            for h in range(4):
                P.stt(SW[:, h, :], sb_[:, 128 * h:128 * h + 128], g[:, 16 + h:17 + h], T["mask_b"], ALU.mult, ALU.mult)
            nbs = []
            for h in range(4):
                nbk, _ = P.bank()
                nbs.append(nbk)
                P.mmg([(nbk[:, 0:257], SW[:, h, :], Vx[:, s, h, 0:257], True, False),
                       (nbk[:, 0:257], QT[:, h, cs(s)], T["Cb"][:, h, 0:257], False, True)])
            a1 = self.sm(4); rd = self.sm(4); st6 = self.sm(24).rearrange("p (h b) -> p h b", b=6); mv = self.sm(8).rearrange("p (h b) -> p h b", b=2)
            t1 = self.sm(4); aa = self.sm(4); nbv = self.sm(4)
            for h in range(4):
                P.act(a1[:, h:h + 1], nbs[h][:, 256:257], AF.Abs)
            P.tt("dve", a1, a1, g[:, 28:32], ALU.max)
            P.op("dve", lambda e, rd=rd, a1=a1: e.reciprocal(rd, a1), [a1], [rd])
            for h in range(4):
                P.op("dve", lambda e, h=h, st6=st6, nbs=nbs: e.bn_stats(st6[:, h, :], nbs[h][:, 0:256]), [nbs[h][:, 0:256]], [st6[:, h, :]])
            for h in range(4):
                P.op("dve", lambda e, h=h, st6=st6, mv=mv: e.bn_aggr(mv[:, h, :], st6[:, h, :]), [st6[:, h, :]], [mv[:, h, :]])
            P.tt("dve", t1, mv[:, :, 1], rd, ALU.mult)
            P.tt("dve", t1, t1, rd, ALU.mult)
            P.ts("dve", t1, t1, LN_EPS, None, ALU.add)
            P.tt("pool", t1, t1, T["mhalf"], ALU.pow)
            P.tt("dve", aa, t1, rd, ALU.mult)
            P.stt(nbv, mv[:, :, 0], -1.0, aa, ALU.mult, ALU.mult)
            for h in range(4):
                P.act(B["hm"][:, s, 256 * h:256 * h + 256], nbs[h][:, 0:256], AF.Identity, bias=nbv[:, h:h + 1], scale=aa[:, h:h + 1])
        for h in range(4):
            cbk, _ = P.bank()
            P.mmg([(cbk[:, 0:257], KW[:, h, :], Vx[:, s, h, 0:257], True, True)])
            P.stt(T["C32"][:, h, 0:257], T["C32"][:, h, 0:257], g[:, 24 + h:25 + h], cbk[:, 0:257], ALU.mult, ALU.add)
            P.cp("act", T["Cb"][:, h, 0:257], T["C32"][:, h, 0:257])
    P.cp("dve", T["xmh"], xmT[:, :, N:N + 3])
    self.tile_s5(full, ns, part=1)
    if full:
        self.tile_mix_out(ns, xs)
        self.tile_s5(full, ns, part=2)
        self.tile_post(ns, xs, store)


Builder.tile = _tile


def _tile_mix_out(self, ns, xs):
    P, T, B = self.P, self.T, self.B
    N = 128 * ns
    hm, omS, prodT, yT, gtmp = B["hm"], B["omS"], B["prodT"], B["yT"], B["gtmp"]
    for ct in range(8):
        bk = self.inproj_tile(8 + ct, N)
        P.act(omS[:, ct, 0:N], bk[:, 0:N], AF.Sigmoid, bias=T["binT"][:, 8 + ct:9 + ct])
    for vp in range(4):
        bk, _ = P.bank()
        bb = bk[:, :].bitcast(BF16)
        items = []
        for j in range(2):
            vt = 2 * vp + j
            for s in range(ns):
                items.append((bb[:, 512 * j + 128 * s:512 * j + 128 * s + 128], hm[:, s, 128 * vt:128 * vt + 128], T["ident_b"]))
        P.trg(items)
        for j in range(2):
            vt = 2 * vp + j
            P.stt(prodT[:, vt, 0:N], bb[:, 512 * j:512 * j + N], T["gainT"][:, vt:vt + 1], omS[:, vt, 0:N], ALU.mult, ALU.mult)
    for dt_ in range(8):
        w = self.wchunk("wD", dt_ // 4)
        j = dt_ % 4
        bk, _ = P.bank()
        P.mmg([(bk[:, 0:N], w[:, kt, 128 * j:128 * j + 128], prodT[:, kt, 0:N], kt == 0, kt == 7) for kt in range(8)])
        b2 = self.inproj_tile(20 + dt_, N)
        P.act(gtmp[:, 0, 0:N], b2[:, 0:N], AF.Sigmoid, bias=T["binT"][:, 20 + dt_:21 + dt_])
        P.tt("dve", yT[:, dt_, 0:N], bk[:, 0:N], gtmp[:, 0, 0:N], ALU.mult)


def _tile_s5(self, full, ns, part=1):
    P, T, B = self.P, self.T, self.B
    N = 128 * ns
    nb = 16 * ns
    uT, ysT, yT, gtmp = B["uT"], B["ysT"], B["yT"], B["gtmp"]
    xp = B["xprev"]
    if part == 2:
        return self.tile_s5_out(ns)
    for ct in range(4):
        bk = self.inproj_tile(16 + ct, N)
        P.act(uT[:, ct, 0:N], bk[:, 0:N], AF.Identity, bias=T["binT"][:, 16 + ct:17 + ct])
    xb = [P.bank()[0] for _ in range(4)]
    for ct in range(4):
        w = self.wchunk("s5X", ct // 2)
        for q in range(4):
            items = []
            kw = dict(tile_position=(96, 0)) if q == 3 else {}
            for ri in range(2):
                c0 = (ct * 2 + ri) * nb
                for tau in range(8):
                    items.append((xb[q][:, c0:c0 + nb], w[32 * q:32 * q + 32, ct % 2, tau, ri, :], uT[32 * q:32 * q + 32, ct, tau:N:8], tau == 0, tau == 7, kw))
            P.mmg(items)
    Xin = B["Xin"]
    for q in range(4):
        src = xb[q][:, 0:8 * nb].rearrange("p (c r n) -> p r c n", c=4, r=2)
        P.cp("act" if q % 2 == 0 else "dve", Xin[:, :, q:16:4, 0:nb], src)
    cj, sj = T["rotc"][:, :, 0:nb], T["rots"][:, :, 0:nb]
    v3 = lambda t: t[:, 0:16 * nb].rearrange("p (q n) -> p q n", n=nb)
    Zr, Zi, Ta, Tb = v3(B["Zr"]), v3(B["Zi"]), v3(B["Ta"]), v3(B["Tb"])
    Xr, Xi = Xin[:, 0, :, 0:nb], Xin[:, 1, :, 0:nb]
    P.tt("dve", Ta, cj, Xr, ALU.mult); P.tt("dve", Tb, sj, Xi, ALU.mult); P.tt("dve", Zr, Ta, Tb, ALU.subtract)
    P.tt("dve", Ta, cj, Xi, ALU.mult); P.tt("dve", Tb, sj, Xr, ALU.mult); P.tt("dve", Zi, Ta, Tb, ALU.add)
    xc = T["xcar"]; u8 = T["u8"]
    i_r = self.sm(16); i_i = self.sm(16); ta = self.sm(16); tb = self.sm(16)
    P.tt("dve", ta, u8[:, 0, :], xc[:, 0, :], ALU.mult); P.tt("dve", tb, u8[:, 1, :], xc[:, 1, :], ALU.mult); P.tt("dve", i_r, ta, tb, ALU.subtract)
    P.tt("dve", ta, u8[:, 0, :], xc[:, 1, :], ALU.mult); P.tt("dve", tb, u8[:, 1, :], xc[:, 0, :], ALU.mult); P.tt("dve", i_i, ta, tb, ALU.add)
    P.tt("dve", i_r, i_r, T["rho8"], ALU.mult); P.tt("dve", i_i, i_i, T["rho8"], ALU.mult)
    P.tt("dve", Zr[:, :, 0], Zr[:, :, 0], i_r, ALU.add); P.tt("dve", Zi[:, :, 0], Zi[:, :, 0], i_i, ALU.add)
    d0 = self.d0[nb].rearrange("p q n -> p (q n)")
    fr, fi = B["Ta"][:, 0:16 * nb], B["Tb"][:, 0:16 * nb]
    P.op("dve", lambda e: e.tensor_tensor_scan(fr, d0, B["Zr"][:, 0:16 * nb], 0.0, ALU.mult, ALU.add), [d0, B["Zr"][:, 0:16 * nb]], [fr])
    P.op("dve", lambda e: e.tensor_tensor_scan(fi, d0, B["Zi"][:, 0:16 * nb], 0.0, ALU.mult, ALU.add), [d0, B["Zi"][:, 0:16 * nb]], [fi])
    XS = B["XS"]
    xr_o, xi_o = XS[:, 0, :, 0:nb], XS[:, 1, :, 0:nb]
    P.tt("dve", Zr, cj, Ta, ALU.mult); P.tt("dve", Zi, sj, Tb, ALU.mult); P.tt("dve", xr_o, Zr, Zi, ALU.add)
    P.tt("dve", Zr, cj, Tb, ALU.mult); P.tt("dve", Zi, sj, Ta, ALU.mult); P.tt("dve", xi_o, Zr, Zi, ALU.subtract)
    xp = B["xprev"]
    if full:
        P.cp("act", xp[:, :, :, 0], xc)
        if nb > 1:
            P.cp("act", xp[:, :, :, 1:nb], XS[:, :, :, 0:nb - 1])
    P.cp("dve", xc, XS[:, :, :, nb - 1])


def _tile_s5_out(self, ns):
    P, T, B = self.P, self.T, self.B
    N = 128 * ns
    nb = 16 * ns
    uT, ysT, yT, gtmp = B["uT"], B["ysT"], B["yT"], B["gtmp"]
    xp = B["xprev"]
    kc = None
    for ct in range(4):
        kc = self.wchunk("s5K", None)
        yb, _ = P.bank()
        items = []
        for tp in range(8):
            for tau in range(tp, 8):
                items.append((yb[:, tau:N:8], kc[:, ct, tp, :], uT[:, ct, tau - tp:N:8], (tp == 0 and tau == 0), False, dict(skip_group_check=True)))
        P.mmg(items)
        items = []
        yw = self.wchunk("s5Y", ct // 2)
        for q in range(4):
            for tau in range(8):
                for ri in range(2):
                    last = (q == 3 and tau == 7 and ri == 1)
                    items.append((yb[32 * q:32 * q + 32, tau:N:8], yw[:, (4 * ct + q) % 8, tau, ri, :], xp[:, ri, 4 * ct + q, 0:nb],
                                  False, last, dict(tile_position=(0, 32 * q), skip_group_check=True)))
        P.mmg(items)
        P.act(ysT[:, ct, 0:N], yb[:, 0:N], AF.Gelu_apprx_tanh)
    for dt_ in range(8):
        j = dt_ % 4
        wv = self.wchunk("wG", dt_ // 4)
        bv, _ = P.bank()
        P.mmg([(bv[:, 0:N], wv[:, kt, 128 * j:128 * j + 128], ysT[:, kt, 0:N], kt == 0, kt == 3) for kt in range(4)])
        wg = self.wchunk("wG", 2 + dt_ // 4)
        bg, _ = P.bank()
        P.mmg([(bg[:, 0:N], wg[:, kt, 128 * j:128 * j + 128], ysT[:, kt, 0:N], kt == 0, kt == 3) for kt in range(4)])
        P.act(gtmp[:, 1, 0:N], bg[:, 0:N], AF.Sigmoid)
        P.tt("dve", gtmp[:, 2, 0:N], bv[:, 0:N], gtmp[:, 1, 0:N], ALU.mult)
        b2 = self.inproj_tile(28 + dt_, N)
        P.act(gtmp[:, 0, 0:N], b2[:, 0:N], AF.Sigmoid, bias=T["binT"][:, 28 + dt_:29 + dt_])
        P.tt("dve", gtmp[:, 2, 0:N], gtmp[:, 2, 0:N], gtmp[:, 0, 0:N], ALU.mult)
        P.tt("dve", yT[:, dt_, 0:N], yT[:, dt_, 0:N], gtmp[:, 2, 0:N], ALU.add)


Builder.tile_mix_out = _tile_mix_out
Builder.tile_s5 = _tile_s5
Builder.tile_s5_out = _tile_s5_out


def _post_ln(self, x, gb_key):
    P, B = self.P, self.B
    rstd, nmr = gb_key
    P.act(x, x, AF.Identity, bias=nmr, scale=rstd)
    P.tt("dve", x, x, B["lnt"][:, 0, :], ALU.mult)
    P.tt("dve", x, x, B["lnt"][:, 1, :], ALU.add)


def _tile_post(self, ns, xs, store):
    P, I, T, S, B = self.P, self.I, self.T, self.S, self.B
    N = 128 * ns
    yT, hT, prodF = B["yT"], B["hT"], B["prodF"]
    cs = lambda s: slice(128 * s, 128 * s + 128)
    lnt = B["lnt"]
    P.dma(lnt[:, 0, :], self.ap_bcast(I["ln1_gain"], 0, D))
    P.dma(lnt[:, 1, :], self.ap_bcast(I["ln1_bias"], 0, D))
    for hf in range(2):
        w = self.wchunk("wM", hf)
        for s in range(ns):
            bk, _ = P.bank()
            P.mmg([(bk[:, 0:512], yT[:, kt, cs(s)], w[:, kt, :], kt == 0, kt == 7) for kt in range(8)])
            xh = xs[s][:, 512 * hf:512 * hf + 512]
            P.stt(xh, xh, ALPHA, bk[:, 0:512], ALU.mult, ALU.add)
    st_ = [self.ln_stats(xs[s]) for s in range(ns)]
    for s in range(ns):
        self.post_ln(xs[s], st_[s])
    st_ = [self.ln_stats(xs[s]) for s in range(ns)]
    for s in range(ns):
        self.ln_to_T(xs[s], s, 32, 24, stats=st_[s])
    fh = T["fhalo"]
    gpre, gact = B["gpre"], B["gact"]
    pend = None

    def conv_stage(t, par, bv, bc_):
        fd = self.wchunk("fdiag", t // 10, src=S["fdiag"][10 * (t // 10):min(22, 10 * (t // 10) + 10)].rearrange("t p j d -> p t j d"))
        tl = t % 10
        P.mmg([(bc_[:, 0:N], fd[:, tl, j, :], gpre[:, par, j:j + N], j == 0, j == 2) for j in range(3)])
        P.act(gact[:, par, 0:N], bc_[:, 0:N], AF.Gelu_apprx_tanh, bias=T["fcb"][:, t:t + 1])
        P.tt("dve", prodF[:, t, 0:N], bv[:, 0:N], gact[:, par, 0:N], ALU.mult)

    for t in range(22):
        par = t % 2
        wv = self.wchunk("wU", t // 4)
        bv, _ = P.bank()
        P.mmg([(bv[:, 0:N], wv[:, kt, 128 * (t % 4):128 * (t % 4) + 128], hT[:, kt, 0:N], kt == 0, kt == 7) for kt in range(8)])
        gt_ = 22 + t
        wg = self.wchunk("wU", gt_ // 4)
        bg, _ = P.bank()
        P.mmg([(bg[:, 0:N], wg[:, kt, 128 * (gt_ % 4):128 * (gt_ % 4) + 128], hT[:, kt, 0:N], kt == 0, kt == 7) for kt in range(8)])
        P.cp("dve", gpre[:, par, 0:2], fh[:, t, :])
        P.cp("act", gpre[:, par, 2:2 + N], bg[:, 0:N])
        P.cp("dve", fh[:, t, :], gpre[:, par, N:N + 2])
        if pend is not None:
            conv_stage(*pend)
        pend = (t, par, bv, bg)
    conv_stage(*pend)
    P.dma(lnt[:, 0, :], self.ap_bcast(I["ln2_gain"], 0, D))
    P.dma(lnt[:, 1, :], self.ap_bcast(I["ln2_bias"], 0, D))
    for hf in range(2):
        acc = [P.bank()[0] for _ in range(ns)]
        for gi, (k0, nk) in enumerate(WF_GROUPS):
            w = self.wchunk("wF", gi * 2 + hf, src=S["wF"][gi * 2 + hf][:, 0:nk, :])
            for s in range(ns):
                P.mmg([(acc[s][:, 0:512], prodF[:, k0 + kk, cs(s)], w[:, kk, :], (gi == 0 and kk == 0), (gi == 2 and kk == nk - 1)) for kk in range(nk)])
        for s in range(ns):
            xh = xs[s][:, 512 * hf:512 * hf + 512]
            P.stt(xh, xh, ALPHA, acc[s][:, 0:512], ALU.mult, ALU.add)
    st_ = [self.ln_stats(xs[s]) for s in range(ns)]
    for s in range(ns):
        self.post_ln(xs[s], st_[s])
        if store is not None:
            P.dma(self.yout[store + 128 * s:store + 128 * s + 128, :], xs[s], q="pool")
    self.dbg_tile = True


Builder.post_ln = _post_ln
Builder.tile_post = _tile_post


_CACHE = {}


def _consts():
    bm = np.zeros((128, 128), np.float32)
    for q in range(4):
        bm[32 * q:32 * q + 32, 32 * q:32 * q + 32] = 1.0
    return np.eye(128, dtype=np.float32), np.triu(np.ones((128, 128), np.float32)), bm


def core_map(inputs, b, xin, flag):
    ident, tri, bm = _consts()
    m = {"xin": np.ascontiguousarray(xin, dtype=np.float32), "cvec": np.ascontiguousarray(inputs["c"][b], dtype=np.float32),
         "flagv": np.full((128, 1), flag, np.float32), "c_ident": ident, "c_tri": tri, "c_bmask": bm}
    for k, v in inputs.items():
        if k in ("x", "c"):
            continue
        m[k] = np.ascontiguousarray(np.asarray(v)[0], dtype=np.float32)
    return m


def kernel(**inputs):
    inputs = {k: np.asarray(v) for k, v in inputs.items()}
    x = inputs["x"]
    Bn, Sq, _ = x.shape
    half = Sq // 2
    n_state = (half - 128) // 128
    state_ns = [4] * (n_state // 4) + ([n_state % 4] if n_state % 4 else [])
    full_ns = [1] + [4] * (half // 512)
    key = (tuple(state_ns), tuple(full_ns))
    if key not in _CACHE:
        bld = Builder(state_ns, full_ns, 1)
        _CACHE[key] = bld.build()
    nc = _CACHE[key]
    in_maps = []
    for core in range(2 * Bn):
        b, h = core // 2, core % 2
        if h == 0:
            xin = np.concatenate([np.zeros((half, D), np.float32), x[b, :half]], axis=0)
        else:
            xin = x[b]
        in_maps.append(core_map(inputs, b, xin, float(h)))
    res = run_bass_kernel_spmd(nc, in_maps, core_ids=list(range(2 * Bn)))
    out = np.empty((Bn, Sq, D), np.float32)
    for core in range(2 * Bn):
        b, h = core // 2, core % 2
        out[b, h * half:(h + 1) * half] = res.results[core]["yout"]
    return out
```
